# Optimizing a Trainium2 kernel written in Bass

```python
import math
import jax, jax.numpy as jnp
from jax import lax
import numpy as np

D_MODEL = 1024
BATCH = 4
SEQ = 8192
DEPTH = 2

NSA_HEADS = 8
NSA_KV_HEADS = 2
NSA_HEAD_DIM = 64
NSA_CMP_BLOCK = 32
NSA_CMP_STRIDE = 16
NSA_SEL_BLOCK = 64
NSA_TOP_N = 16
NSA_WINDOW = 512
NSA_CMP_HIDDEN = 256
NSA_Q_BLOCK = 128
NSA_BIG = 1e9
ROPE_THETA = 500000.0
ROPE_DIM = NSA_HEAD_DIM // 4
NSA_WIDTH = NSA_HEADS * NSA_HEAD_DIM
NSA_KV_WIDTH = NSA_KV_HEADS * NSA_HEAD_DIM
HG_HEADS = 4
HG_HEAD_DIM = 128
HG_WIDTH = HG_HEADS * HG_HEAD_DIM
HG_CHUNK = 64
M2_HEADS = 16
M2_HEAD_DIM = 64
M2_INNER = M2_HEADS * M2_HEAD_DIM
M2_GROUPS = 2
M2_STATE = 128
M2_CONV = 4
M2_CHUNK = 128
M2_XBC = M2_INNER + 2 * M2_GROUPS * M2_STATE
FFN_DIM = 2816
FFN_CONV = 3
EPS = 1e-6
IN_SPLITS = (NSA_WIDTH, 6 * NSA_KV_WIDTH, 3 * NSA_HEADS,
             HG_WIDTH, HG_WIDTH, HG_WIDTH, HG_WIDTH,
             M2_INNER, M2_XBC, M2_HEADS,
             3 * D_MODEL)
IN_DIM = NSA_WIDTH + 6 * NSA_KV_WIDTH + 3 * NSA_HEADS + 4 * HG_WIDTH + M2_INNER + M2_XBC + M2_HEADS + 3 * D_MODEL

kernel_name = 'hybrid_nsa_hgrn2_ssd_convffn'


def rms_norm(x, g):
    xf = x.astype(jnp.float32)
    y = xf * lax.rsqrt(jnp.mean(xf * xf, axis=-1, keepdims=True) + EPS)
    return (y * g).astype(x.dtype)


def split_cols(a, sizes):
    offsets = np.cumsum(np.array(sizes))[:-1].tolist()
    return jnp.split(a, offsets, axis=-1)


def causal_dwconv(x, w, b):
    width, ch = w.shape
    y = lax.conv_general_dilated(x, w.astype(x.dtype)[:, None, :], window_strides=(1,),
                                 padding=[(width - 1, 0)],
                                 dimension_numbers=('NWC', 'WIO', 'NWC'),
                                 feature_group_count=ch)
    return y + b.astype(y.dtype)


def partial_rope(x, positions):
    half = ROPE_DIM // 2
    inv_freq = ROPE_THETA ** (-jnp.arange(0, ROPE_DIM, 2, dtype=jnp.float32) / ROPE_DIM)
    ang = positions.astype(jnp.float32)[..., None] * inv_freq
    cos, sin = jnp.cos(ang)[:, :, None, :], jnp.sin(ang)[:, :, None, :]
    x1, x2, rest = x[..., :half], x[..., half:ROPE_DIM], x[..., ROPE_DIM:]
    rot = jnp.concatenate([x1 * cos - x2 * sin, x2 * cos + x1 * sin], axis=-1)
    return jnp.concatenate([rot.astype(x.dtype), rest], axis=-1)


def masked_softmax(s, mask):
    s = jnp.where(mask, s.astype(jnp.float32), -jnp.inf)
    m = jnp.max(s, axis=-1, keepdims=True)
    m = jnp.where(jnp.isfinite(m), m, 0.0)
    e = jnp.exp(s - m)
    return e / jnp.maximum(jnp.sum(e, axis=-1, keepdims=True), 1e-30)


def segsum(a):
    t = a.shape[-1]
    cs = jnp.cumsum(a, axis=-1)
    diff = cs[..., :, None] - cs[..., None, :]
    return jnp.where(jnp.tril(jnp.ones((t, t), dtype=bool)), diff, -jnp.inf)


def nsa_mixer(q_cols, kv_cols, gate_cols, positions, q_norm_g, k_norm_g, cmp_pos_k, cmp_pos_v,
              cmp_k_w1, cmp_k_w2, cmp_v_w1, cmp_v_w2):
    b, s, _ = q_cols.shape
    g, r, dh, tq = NSA_KV_HEADS, NSA_HEADS // NSA_KV_HEADS, NSA_HEAD_DIM, NSA_Q_BLOCK
    scale = dh ** -0.5
    q = rms_norm(q_cols.reshape(b, s, NSA_HEADS, dh), q_norm_g)
    q_rot = partial_rope(q, positions)
    kv = kv_cols.reshape(b, s, 3, 2, g, dh)

    n_cmp = (s - NSA_CMP_BLOCK) // NSA_CMP_STRIDE + 1
    c_start_np = np.arange(n_cmp) * NSA_CMP_STRIDE
    blk = c_start_np[:, None] + np.arange(NSA_CMP_BLOCK)[None, :]

    def compress(t, pos_emb, w1, w2):
        tb = t[:, blk] + pos_emb[None, None, :, None, :]
        tb = tb.transpose(0, 1, 3, 2, 4).reshape(b, n_cmp, g, NSA_CMP_BLOCK * dh)
        return jax.nn.gelu(tb @ w1) @ w2

    k_cmp = rms_norm(compress(kv[:, :, 0, 0], cmp_pos_k, cmp_k_w1, cmp_k_w2), k_norm_g[0])
    v_cmp = compress(kv[:, :, 0, 1], cmp_pos_v, cmp_v_w1, cmp_v_w2)
    k_sel = partial_rope(rms_norm(kv[:, :, 1, 0], k_norm_g[1]), positions).transpose(0, 2, 1, 3)
    v_sel = kv[:, :, 1, 1].transpose(0, 2, 1, 3)
    pad = ((0, 0), (NSA_WINDOW, 0), (0, 0), (0, 0))
    k_win = jnp.pad(partial_rope(rms_norm(kv[:, :, 2, 0], k_norm_g[2]), positions), pad)
    v_win = jnp.pad(kv[:, :, 2, 1], pad)
    gates = jax.nn.sigmoid(gate_cols.astype(jnp.float32)).reshape(b, s, g, r, 3)

    n_sel = s // NSA_SEL_BLOCK
    top_n = min(NSA_TOP_N, n_sel)
    s_start_np = np.arange(n_sel) * NSA_SEL_BLOCK
    overlap = np.clip(np.minimum(c_start_np[:, None] + NSA_CMP_BLOCK, s_start_np[None, :] + NSA_SEL_BLOCK)
                      - np.maximum(c_start_np[:, None], s_start_np[None, :]), 0, None)
    cmp_to_sel = jnp.asarray(overlap / NSA_CMP_BLOCK, dtype=jnp.float32)
    c_end = jnp.asarray(c_start_np + NSA_CMP_BLOCK - 1)
    s_start = jnp.asarray(s_start_np)
    blk_ids = jnp.arange(n_sel)
    within = jnp.arange(NSA_SEL_BLOCK)
    b_idx = jnp.arange(b)[:, None, None]
    g_idx = jnp.arange(g)[None, :, None]

    def query_block(i):
        s0 = i * tq
        t = s0 + jnp.arange(tq)
        qb = lax.dynamic_slice_in_dim(q, s0, tq, 1).reshape(b, tq, g, r, dh)
        qr = lax.dynamic_slice_in_dim(q_rot, s0, tq, 1).reshape(b, tq, g, r, dh)
        gb = lax.dynamic_slice_in_dim(gates, s0, tq, 1)
        sc = jnp.einsum('btgrd,bjgd->bgrtj', qb, k_cmp) * scale
        p_cmp = masked_softmax(sc, c_end[None, :] <= t[:, None])
        o_cmp = jnp.einsum('bgrtj,bjgd->btgrd', p_cmp, v_cmp)
        imp = jnp.einsum('bgrtj,jn->bgtn', p_cmp, cmp_to_sel)
        cur = t // NSA_SEL_BLOCK
        forced = (blk_ids[None, :] == 0) | (blk_ids[None, :] == cur[:, None]) | (blk_ids[None, :] == cur[:, None] - 1)
        valid = s_start[None, :] <= t[:, None]
        imp = jnp.where(forced, NSA_BIG, jnp.where(valid, imp, -NSA_BIG))
        _, sel = lax.top_k(imp, top_n)
        key_pos = (sel[..., None] * NSA_SEL_BLOCK + within).reshape(b, g, tq * top_n * NSA_SEL_BLOCK)
        ks = k_sel[b_idx, g_idx, key_pos].reshape(b, g, tq, top_n * NSA_SEL_BLOCK, dh)
        vs = v_sel[b_idx, g_idx, key_pos].reshape(b, g, tq, top_n * NSA_SEL_BLOCK, dh)
        ss = jnp.einsum('btgrd,bgtkd->bgrtk', qr, ks) * scale
        sel_mask = (key_pos.reshape(b, g, tq, -1) <= t[:, None])[:, :, None]
        o_sel = jnp.einsum('bgrtk,bgtkd->btgrd', masked_softmax(ss, sel_mask), vs)
        kw = lax.dynamic_slice_in_dim(k_win, s0, NSA_WINDOW + tq, 1)
        vw = lax.dynamic_slice_in_dim(v_win, s0, NSA_WINDOW + tq, 1)
        kpos = s0 - NSA_WINDOW + jnp.arange(NSA_WINDOW + tq)
        wmask = (kpos[None, :] >= 0) & (kpos[None, :] <= t[:, None]) & (t[:, None] - kpos[None, :] < NSA_WINDOW)
        sw = jnp.einsum('btgrd,bkgd->bgrtk', qr, kw) * scale
        o_win = jnp.einsum('bgrtk,bkgd->btgrd', masked_softmax(sw, wmask), vw)
        o = gb[..., 0:1] * o_cmp + gb[..., 1:2] * o_sel + gb[..., 2:3] * o_win
        return o.reshape(b, tq, NSA_WIDTH)

    out = lax.map(query_block, jnp.arange(s // tq))
    return out.transpose(1, 0, 2, 3).reshape(b, s, NSA_WIDTH)


def hgrn2_mixer(q_cols, f_cols, i_cols, g_cols, lower_bound, norm_g):
    b, s, _ = q_cols.shape
    h, dk, c = HG_HEADS, HG_HEAD_DIM, HG_CHUNK
    n_ch = s // c
    f = lower_bound + (1.0 - lower_bound) * jax.nn.sigmoid(f_cols.astype(jnp.float32))
    log_f = jnp.log(f)
    k = 1.0 - f

    def chunks(a):
        return a.astype(jnp.float32).reshape(b, n_ch, c, h, dk).transpose(1, 0, 3, 2, 4)

    causal = jnp.tril(jnp.ones((c, c), dtype=bool))

    def step(state, inp):
        qc, kc, vc, lf = inp
        gcum = jnp.cumsum(lf, axis=2)
        diff = jnp.where(causal[:, :, None], gcum[:, :, :, None, :] - gcum[:, :, None, :, :], -jnp.inf)
        scores = jnp.einsum('bhtsk,bhsk->bhts', qc[:, :, :, None, :] * jnp.exp(diff), kc)
        o = (jnp.einsum('bhts,bhsv->bhtv', scores, vc)
             + jnp.einsum('bhtk,bhkv->bhtv', qc * jnp.exp(gcum), state))
        g_last = gcum[:, :, -1]
        state = (jnp.exp(g_last)[..., None] * state
                 + jnp.einsum('bhsk,bhsv->bhkv', kc * jnp.exp(g_last[:, :, None] - gcum), vc))
        return state, o

    s_init = jnp.zeros((b, h, dk, dk), jnp.float32)
    _, o = lax.scan(step, s_init, (chunks(q_cols), chunks(k), chunks(i_cols), chunks(log_f)))
    o = o.transpose(1, 0, 3, 2, 4).reshape(b, s, h, dk)
    o = rms_norm(o, norm_g).reshape(b, s, HG_WIDTH)
    return o * jax.nn.silu(g_cols.astype(jnp.float32))


def ssd_chunked(x, a, bm, cm):
    b, s, h, p = x.shape
    g, n = bm.shape[2], bm.shape[3]
    r, q = h // g, M2_CHUNK
    c = s // q
    x = x.reshape(b, c, q, g, r, p)
    a = a.reshape(b, c, q, g, r).transpose(0, 3, 4, 1, 2)
    bm = bm.reshape(b, c, q, g, n)
    cm = cm.reshape(b, c, q, g, n)
    a_cs = jnp.cumsum(a, axis=-1)
    lmat = jnp.exp(segsum(a))
    cb = jnp.einsum('bclgn,bcsgn->bgcls', cm, bm)
    y_diag = jnp.einsum('bgrcls,bcsgrp->bclgrp', cb[:, :, None] * lmat, x)
    decay_states = jnp.exp(a_cs[..., -1:] - a_cs).transpose(0, 3, 4, 1, 2)
    states = jnp.einsum('bclgn,bclgrp->bcgrpn', bm, x * decay_states[..., None])
    states = jnp.concatenate([jnp.zeros_like(states[:, :1]), states], axis=1)
    decay_chunk = jnp.exp(segsum(jnp.pad(a_cs[..., -1], ((0, 0), (0, 0), (0, 0), (1, 0)))))
    states = jnp.einsum('bgrzc,bcgrpn->bzgrpn', decay_chunk, states)[:, :-1]
    state_decay_out = jnp.exp(a_cs).transpose(0, 3, 4, 1, 2)
    y_off = jnp.einsum('bclgn,bcgrpn->bclgrp', cm, states) * state_decay_out[..., None]
    return (y_diag + y_off).reshape(b, s, h, p)


def mamba2_mixer(z, xbc, dt_cols, conv_w, conv_b, dt_bias, a_log, d_skip, norm_g):
    b, s, _ = z.shape
    xbc = jax.nn.silu(causal_dwconv(xbc, conv_w, conv_b))
    xs, bm, cm = split_cols(xbc, (M2_INNER, M2_GROUPS * M2_STATE, M2_GROUPS * M2_STATE))
    dt = jax.nn.softplus(dt_cols.astype(jnp.float32) + dt_bias)
    a = -jnp.exp(a_log.astype(jnp.float32))
    xh = xs.astype(jnp.float32).reshape(b, s, M2_HEADS, M2_HEAD_DIM)
    y = ssd_chunked(xh * dt[..., None], dt * a,
                    bm.astype(jnp.float32).reshape(b, s, M2_GROUPS, M2_STATE),
                    cm.astype(jnp.float32).reshape(b, s, M2_GROUPS, M2_STATE))
    y = y + d_skip[:, None] * xh
    y = y.reshape(b, s, M2_INNER) * jax.nn.silu(z.astype(jnp.float32))
    y = rms_norm(y.reshape(b, s, M2_GROUPS, M2_INNER // M2_GROUPS), norm_g.reshape(M2_GROUPS, -1))
    return y.reshape(b, s, M2_INNER)


def conv_ffn(h, w_up, conv_w, conv_b, w_down):
    u = causal_dwconv(h @ w_up, conv_w, conv_b)
    gate, up = jnp.split(u, 2, axis=-1)
    return (jax.nn.silu(gate) * up) @ w_down


def setup_inputs(seed: int = 0) -> dict:
    key = jax.random.key(seed)
    ks = jax.random.split(key, 32)
    f32 = jnp.float32
    lyr = DEPTH

    def nrm(k, shape, scale):
        return jax.random.normal(k, shape, f32) * scale

    cin = NSA_CMP_BLOCK * NSA_HEAD_DIM
    dt0 = jnp.exp(jax.random.uniform(ks[17], (lyr, M2_HEADS), f32, math.log(1e-3), math.log(1e-1)))
    return {
        'x': nrm(ks[0], (BATCH, SEQ, D_MODEL), 1.0),
        'positions': (jax.random.randint(ks[1], (BATCH, 1), 0, 1024) + jnp.arange(SEQ)[None, :]).astype(jnp.int32),
        'attn_norm_g': 1.0 + nrm(ks[2], (lyr, D_MODEL), 0.05),
        'ffn_norm_g': 1.0 + nrm(ks[3], (lyr, D_MODEL), 0.05),
        'w_in': nrm(ks[4], (lyr, D_MODEL, IN_DIM), D_MODEL ** -0.5),
        'nsa_q_norm_g': 1.0 + nrm(ks[5], (lyr, NSA_HEAD_DIM), 0.05),
        'nsa_k_norm_g': 1.0 + nrm(ks[6], (lyr, 3, NSA_HEAD_DIM), 0.05),
        'nsa_cmp_pos_k': nrm(ks[7], (lyr, NSA_CMP_BLOCK, NSA_HEAD_DIM), 0.5),
        'nsa_cmp_pos_v': nrm(ks[8], (lyr, NSA_CMP_BLOCK, NSA_HEAD_DIM), 0.5),
        'nsa_cmp_k_w1': nrm(ks[9], (lyr, cin, NSA_CMP_HIDDEN), cin ** -0.5),
        'nsa_cmp_k_w2': nrm(ks[10], (lyr, NSA_CMP_HIDDEN, NSA_HEAD_DIM), NSA_CMP_HIDDEN ** -0.5),
        'nsa_cmp_v_w1': nrm(ks[11], (lyr, cin, NSA_CMP_HIDDEN), cin ** -0.5),
        'nsa_cmp_v_w2': nrm(ks[12], (lyr, NSA_CMP_HIDDEN, NSA_HEAD_DIM), NSA_CMP_HIDDEN ** -0.5),
        'hgrn_lb_logits': nrm(ks[13], (lyr, HG_WIDTH), 1.0),
        'hgrn_norm_g': 1.0 + nrm(ks[14], (lyr, HG_HEAD_DIM), 0.05),
        'm2_conv_w': nrm(ks[15], (lyr, M2_CONV, M2_XBC), M2_CONV ** -0.5),
        'm2_conv_b': nrm(ks[16], (lyr, M2_XBC), 0.01),
        'm2_dt_bias': dt0 + jnp.log(-jnp.expm1(-dt0)),
        'm2_a_log': jnp.log(jax.random.uniform(ks[18], (lyr, M2_HEADS), f32, 1.0, 16.0)),
        'm2_d_skip': 1.0 + nrm(ks[19], (lyr, M2_HEADS), 0.1),
        'm2_norm_g': 1.0 + nrm(ks[20], (lyr, M2_INNER), 0.05),
        'w_branch_nsa': nrm(ks[21], (lyr, NSA_WIDTH, D_MODEL), NSA_WIDTH ** -0.5),
        'w_branch_hgrn': nrm(ks[22], (lyr, HG_WIDTH, D_MODEL), HG_WIDTH ** -0.5),
        'w_branch_m2': nrm(ks[23], (lyr, M2_INNER, D_MODEL), M2_INNER ** -0.5),
        'w_out': nrm(ks[24], (lyr, D_MODEL, D_MODEL), D_MODEL ** -0.5),
        'ffn_w_up': nrm(ks[25], (lyr, D_MODEL, 2 * FFN_DIM), D_MODEL ** -0.5),
        'ffn_conv_w': nrm(ks[26], (lyr, FFN_CONV, 2 * FFN_DIM), FFN_CONV ** -0.5),
        'ffn_conv_b': nrm(ks[27], (lyr, 2 * FFN_DIM), 0.01),
        'ffn_w_down': nrm(ks[28], (lyr, FFN_DIM, D_MODEL), FFN_DIM ** -0.5),
    }


def reference(x, positions, attn_norm_g, ffn_norm_g, w_in, nsa_q_norm_g, nsa_k_norm_g,
              nsa_cmp_pos_k, nsa_cmp_pos_v, nsa_cmp_k_w1, nsa_cmp_k_w2, nsa_cmp_v_w1, nsa_cmp_v_w2,
              hgrn_lb_logits, hgrn_norm_g, m2_conv_w, m2_conv_b, m2_dt_bias, m2_a_log, m2_d_skip,
              m2_norm_g, w_branch_nsa, w_branch_hgrn, w_branch_m2, w_out,
              ffn_w_up, ffn_conv_w, ffn_conv_b, ffn_w_down):
    lb_soft = jax.nn.softmax(hgrn_lb_logits.astype(jnp.float32), axis=0)
    lower_bounds = jnp.cumsum(lb_soft, axis=0) - lb_soft[0]
    for l in range(DEPTH):
        h = rms_norm(x, attn_norm_g[l])
        (nsa_q, nsa_kv, nsa_gate, hg_q, hg_f, hg_i, hg_g,
         m2_z, m2_xbc, m2_dt, merge_gate) = split_cols(h @ w_in[l], IN_SPLITS)
        y_a = nsa_mixer(nsa_q, nsa_kv, nsa_gate, positions, nsa_q_norm_g[l], nsa_k_norm_g[l],
                        nsa_cmp_pos_k[l], nsa_cmp_pos_v[l], nsa_cmp_k_w1[l], nsa_cmp_k_w2[l],
                        nsa_cmp_v_w1[l], nsa_cmp_v_w2[l])
        y_b = hgrn2_mixer(hg_q, hg_f, hg_i, hg_g, lower_bounds[l], hgrn_norm_g[l])
        y_c = mamba2_mixer(m2_z, m2_xbc, m2_dt, m2_conv_w[l], m2_conv_b[l], m2_dt_bias[l],
                           m2_a_log[l], m2_d_skip[l], m2_norm_g[l])
        g_a, g_b, g_c = jnp.split(jax.nn.sigmoid(merge_gate.astype(jnp.float32)), 3, axis=-1)
        merged = (g_a * (y_a @ w_branch_nsa[l]) + g_b * (y_b @ w_branch_hgrn[l])
                  + g_c * (y_c @ w_branch_m2[l]))
        x = x + (merged @ w_out[l]).astype(x.dtype)
        x = x + conv_ffn(rms_norm(x, ffn_norm_g[l]), ffn_w_up[l], ffn_conv_w[l], ffn_conv_b[l],
                         ffn_w_down[l]).astype(x.dtype)
    return x
```

```python
import numpy as np
import concourse.bass as bass
import concourse.mybir as mybir
from concourse.bass_utils import run_bass_kernel_spmd
from contextlib import ExitStack

F32 = mybir.dt.float32
BF16 = mybir.dt.bfloat16
I32 = mybir.dt.int32
AF = mybir.ActivationFunctionType
ALU = mybir.AluOpType
AX = mybir.AxisListType

SAME_ENG_SYNC = True


_UQ = [0]


def SBT(nc, name, shape, dt):
    _UQ[0] += 1
    return nc.sbuf_tensor("%s_u%d" % (name, _UQ[0]), shape, dt)


class Buf:
    __slots__ = ("name", "w", "r")

    def __init__(self, name=""):
        self.name = name
        self.w = None
        self.r = {}


class Em:
    def __init__(self, nc, es, n_dma_sems=12):
        self.nc = nc
        self.es = es
        self.eng = dict(pe=nc.tensor, act=nc.scalar, dve=nc.vector, pool=nc.gpsimd, sp=nc.sync)
        self.semh = {}
        self.cnt = {}
        self.seen = {k: {} for k in self.eng}
        for k in ("pe", "act", "dve", "pool"):
            self.semh[k] = es.enter_context(nc.semaphore("c_" + k))
            self.cnt[k] = 0
        self.dring = {}
        for q in ("sp", "pool", "act"):
            ring = []
            for i in range(n_dma_sems):
                key = "d_%s%d" % (q, i)
                self.semh[key] = es.enter_context(nc.semaphore(key))
                ring.append(key)
            self.dring[q] = [ring, 0, {k: 0 for k in ring}]
        self.n_inst = 0

    def _wait(self, eng, toks):
        E = self.eng[eng]
        seen = self.seen[eng]
        for (k, v) in toks:
            if k == eng:
                if eng == "pe" or not SAME_ENG_SYNC:
                    continue
                if v > self.cnt[eng]:
                    continue
            if seen.get(k, 0) < v:
                E.wait_ge(self.semh[k], v)
                seen[k] = v
                self.n_inst += 1

    def _deps(self, reads, writes):
        toks = []
        for b in reads:
            if b.w is not None:
                toks.append(b.w)
        for b in writes:
            if b.w is not None:
                toks.append(b.w)
            toks.extend(b.r.items())
        return toks

    def op(self, eng, fn, reads=(), writes=(), inc=True):
        self._wait(eng, self._deps(reads, writes))
        ins = fn(self.eng[eng])
        self.n_inst += 1
        if inc:
            self.cnt[eng] += 1
            ins.then_inc(self.semh[eng], 1)
            tok = (eng, self.cnt[eng])
        else:
            tok = (eng, self.cnt[eng] + 1)
        for b in reads:
            if b.r.get(tok[0], 0) < tok[1]:
                b.r[tok[0]] = tok[1]
        for b in writes:
            b.w = tok
            b.r = {}
        return ins

    def dma(self, q, out, in_, reads=(), writes=(), **kw):
        ring, idx, tgt = self.dring[q]
        key = ring[idx]
        self.dring[q][1] = (idx + 1) % len(ring)
        toks = self._deps(reads, writes)
        if tgt[key] > 0:
            toks.append((key, tgt[key]))
        self._wait(q, toks)
        ins = self.eng[q].dma_start(out=out, in_=in_, **kw)
        self.n_inst += 1
        tgt[key] += 16
        ins.then_inc(self.semh[key], 16)
        tok = (key, tgt[key])
        for b in reads:
            b.r[key] = tok[1]
        for b in writes:
            b.w = tok
            b.r = {}
        return ins

    def drain(self, engs=("sp", "pool", "act", "pe", "dve")):
        toks = [(k, self.cnt[k]) for k in ("pe", "act", "dve", "pool") if self.cnt[k] > 0]
        for q in self.dring:
            for key, t in self.dring[q][2].items():
                if t > 0:
                    toks.append((key, t))
        for e in engs:
            self._wait(e, toks)


    def dyn_init(self, es, width=1024, nslots=4):
        pass

    def dyn_store(self, dram_aps, src_ap, src_bufs):
        if not isinstance(dram_aps, (list, tuple)):
            dram_aps = [dram_aps]
        for d in dram_aps:
            self.dma("sp", d, src_ap, reads=list(src_bufs))

    def dyn_load(self, dst_ap, dst_bufs, dram_ap):
        self.dma("sp", dst_ap, dram_ap, writes=list(dst_bufs))

STOP = ''

def load_weight_bf16(em, es, name, w_dram, K, N, scale_sb=None, stage_cols=1024):
    nc = em.nc
    KC = K // 128
    w_sb = es.enter_context(SBT(nc, name, [128, KC, N], BF16))
    wb = Buf(name)
    stg = [es.enter_context(SBT(nc, name + "_stg%d" % i, [128, stage_cols], F32)) for i in range(2)]
    stgb = [Buf(name + "_stg%d" % i) for i in range(2)]
    i = 0
    for kc in range(KC):
        for c0 in range(0, N, stage_cols):
            c1 = min(N, c0 + stage_cols)
            s, sb = stg[i % 2], stgb[i % 2]
            em.dma("sp", s[:, : c1 - c0], w_dram[kc * 128:(kc + 1) * 128, c0:c1], writes=[sb])
            if scale_sb is not None:
                sc = scale_sb[0][:, kc:kc + 1]
                em.op("pool", lambda e, s=s, sc=sc, c0=c0, c1=c1, kc=kc: e.tensor_scalar(
                    out=w_sb[:, kc, c0:c1], in0=s[:, : c1 - c0], scalar1=sc, scalar2=None, op0=ALU.mult),
                    reads=[sb, scale_sb[1]], writes=[wb])
            else:
                em.op("pool", lambda e, s=s, c0=c0, c1=c1, kc=kc: e.tensor_copy(
                    out=w_sb[:, kc, c0:c1], in_=s[:, : c1 - c0]), reads=[sb], writes=[wb])
            i += 1
    return w_sb, wb


def rmsnorm_T(em, x_tile, xb, hT, hTb, col0, scr, ident, ps_t, eps=1e-6, D=1024):
    KC = D // 128
    sq, sqb = scr["sq"]
    ss, ssb = scr["ss"]
    hb, hbb = scr["hb"]
    em.op("act", lambda e: e.activation(out=sq[:], in_=x_tile[:], func=AF.Square, accum_out=ss[:, 0:1]),
          reads=[xb], writes=[sqb, ssb])
    em.op("dve", lambda e: e.tensor_scalar(out=ss[:, 1:2], in0=ss[:, 0:1], scalar1=1.0 / D, scalar2=eps,
                                           op0=ALU.mult, op1=ALU.add), reads=[ssb], writes=[ssb])
    em.op("act", lambda e: e.sqrt(out=ss[:, 3:4], in_=ss[:, 1:2]), reads=[ssb], writes=[ssb])
    em.op("dve", lambda e: e.reciprocal(out=ss[:, 2:3], in_=ss[:, 3:4]), reads=[ssb], writes=[ssb])
    em.op("dve", lambda e: e.tensor_scalar(out=hb[:], in0=x_tile[:], scalar1=ss[:, 2:3], scalar2=None,
                                           op0=ALU.mult), reads=[xb, ssb], writes=[hbb])
    pt, ptb = ps_t
    for kc in range(KC):
        em.op("pe", lambda e, kc=kc: e.transpose(out=pt[:, kc, :], in_=hb[:, kc * 128:(kc + 1) * 128],
                                                 identity=ident[0][:]),
              reads=[hbb, ident[1]], writes=[ptb], inc=(kc == KC - 1))
    em.op("act", lambda e: e.copy(out=hT[:, :, col0:col0 + 128], in_=pt[:]), reads=[ptb], writes=[hTb])


def stage_proj(em, T, x_dram, g_sb, w_tm, w_cm, NT, NC, out_tm, out_cm, ident, psum):
    nc = em.nc
    KC = 8
    with ExitStack() as es:
        xs = [(es.enter_context(SBT(nc, "pj_x%d" % i, [128, 1024], F32)), Buf()) for i in range(3)]
        scr = {
            "sq": (es.enter_context(SBT(nc, "pj_sq", [128, 1024], F32)), Buf()),
            "ss": (es.enter_context(SBT(nc, "pj_ss", [128, 4], F32)), Buf()),
            "hb": (es.enter_context(SBT(nc, "pj_hb", [128, 1024], BF16)), Buf()),
        }
        hTs = [(es.enter_context(SBT(nc, "pj_hT%d" % i, [128, KC, 512], BF16)), Buf()) for i in range(2)]
        otm = [(es.enter_context(SBT(nc, "pj_otm%d" % i, [128, NT], F32)), Buf()) for i in range(2)]
        ocm = [(es.enter_context(SBT(nc, "pj_ocm%d" % i, [128, 512], F32)), Buf()) for i in range(3)]
        n_x = 0
        n_otm = 0
        n_ocm = 0
        n_ps = 0
        NST = T // 512
        for st in range(NST):
            hT, hTb = hTs[st % 2]
            for sub in range(4):
                t0 = st * 512 + sub * 128
                x_t, x_b = xs[n_x % 3]
                n_x += 1
                em.dma("sp", x_t[:], (x_dram(t0) if callable(x_dram) else x_dram[t0:t0 + 128, :]), writes=[x_b])
                rmsnorm_T(em, x_t, x_b, hT, hTb, sub * 128, scr, ident, psum[7])
            if STOP == "n":
                continue
            for sub in range(4):
                t0 = st * 512 + sub * 128
                o_t, o_b = otm[n_otm % 2]
                n_otm += 1
                for c0 in range(0, NT, 512):
                    c1 = min(NT, c0 + 512)
                    ps, psb = psum[n_ps % 6]
                    n_ps += 1
                    for kc in range(KC):
                        em.op("pe", lambda e, kc=kc, ps=ps, c0=c0, c1=c1, sub=sub: e.matmul(
                            ps[:, : c1 - c0], lhsT=hT[:, kc, sub * 128:(sub + 1) * 128], rhs=w_tm[0][:, kc, c0:c1],
                            start=(kc == 0), stop=(kc == KC - 1)),
                            reads=[hTb, w_tm[1]], writes=[psb], inc=(kc == KC - 1))
                    ev = "act" if (n_ps % 2) else "dve"
                    if ev == "act":
                        em.op("act", lambda e, ps=ps, c0=c0, c1=c1, o_t=o_t: e.copy(out=o_t[:, c0:c1], in_=ps[:, : c1 - c0]),
                              reads=[psb], writes=[o_b])
                    else:
                        em.op("dve", lambda e, ps=ps, c0=c0, c1=c1, o_t=o_t: e.tensor_copy(out=o_t[:, c0:c1], in_=ps[:, : c1 - c0]),
                              reads=[psb], writes=[o_b])
                em.dma("sp", out_tm[t0:t0 + 128, :], o_t[:], reads=[o_b])
            if STOP == "t":
                continue
            for cb in range(NC // 128):
                ps, psb = psum[n_ps % 6]
                n_ps += 1
                o_t, o_b = ocm[n_ocm % 3]
                n_ocm += 1
                for kc in range(KC):
                    em.op("pe", lambda e, kc=kc, ps=ps, cb=cb: e.matmul(
                        ps[:, :], lhsT=w_cm[0][:, kc, cb * 128:(cb + 1) * 128], rhs=hT[:, kc, :],
                        start=(kc == 0), stop=(kc == KC - 1)),
                        reads=[hTb, w_cm[1]], writes=[psb], inc=(kc == KC - 1))
                ev = "act" if (n_ps % 2) else "dve"
                if ev == "act":
                    em.op("act", lambda e, ps=ps, o_t=o_t: e.copy(out=o_t[:], in_=ps[:]), reads=[psb], writes=[o_b])
                else:
                    em.op("dve", lambda e, ps=ps, o_t=o_t: e.tensor_copy(out=o_t[:], in_=ps[:]), reads=[psb], writes=[o_b])
                em.dma("sp", out_cm[cb * 128:(cb + 1) * 128, st * 512:(st + 1) * 512], o_t[:], reads=[o_b])


def mkT(es, nc, name, shape, dt=F32):
    return (es.enter_context(SBT(nc, name, shape, dt)), Buf(name))


def ld(em, t, src):
    em.dma("sp", t[0][:], src, writes=[t[1]])


def stage_mamba(em, T, p_tm, p_cm, prm, cst, y_dram, psum, TM_Z=1036, TM_DT=1548, CM_X=640, YOFF=512):
    nc = em.nc
    NCH = T // 128
    with ExitStack() as es:
        cw = mkT(es, nc, "m2cw", [128, 6, 4]); ld(em, cw, prm["m2_cw"])
        cb = mkT(es, nc, "m2cb", [128, 6]); ld(em, cb, prm["m2_cb"])
        dtb = mkT(es, nc, "m2dtb", [128, 8]); ld(em, dtb, prm["m2_dtb"])
        alog = mkT(es, nc, "m2alog", [128, 8]); ld(em, alog, prm["m2_alog"])
        dsk = mkT(es, nc, "m2dsk", [128, 8]); ld(em, dsk, prm["m2_dsk"])
        ng = mkT(es, nc, "m2ng", [128, 512]); ld(em, ng, prm["m2_ng"])
        Aneg = mkT(es, nc, "m2A", [128, 8])
        em.op("act", lambda e: e.activation(out=Aneg[0][:], in_=alog[0][:], func=AF.Exp), reads=[alog[1]], writes=[Aneg[1]])
        em.op("dve", lambda e: e.tensor_scalar(out=Aneg[0][:], in0=Aneg[0][:], scalar1=-1.0, scalar2=None, op0=ALU.mult),
              reads=[Aneg[1]], writes=[Aneg[1]])
        stateT = mkT(es, nc, "m2st", [128, 512])
        state_bf = mkT(es, nc, "m2stb", [128, 512], BF16)
        em.op("dve", lambda e: e.memset(stateT[0][:], 0.0), writes=[stateT[1]])
        em.op("dve", lambda e: e.memset(state_bf[0][:], 0.0), writes=[state_bf[1]])
        xin = mkT(es, nc, "m2xin", [128, 6, 131])
        acc = mkT(es, nc, "m2acc", [128, 6, 128])
        xact = mkT(es, nc, "m2xact", [128, 6, 128])
        bT = mkT(es, nc, "m2bT", [128, 128], BF16)
        cT = mkT(es, nc, "m2cT", [128, 128], BF16)
        xs_tm = mkT(es, nc, "m2xs", [128, 512])
        b_tm = mkT(es, nc, "m2btm", [128, 128], BF16)
        dtr = mkT(es, nc, "m2dtr", [128, 8])
        dtv = mkT(es, nc, "m2dtv", [128, 8])
        av = mkT(es, nc, "m2a", [128, 8])
        acs = mkT(es, nc, "m2acs", [128, 8])
        tot = mkT(es, nc, "m2tot", [128, 8])
        dsl = mkT(es, nc, "m2dsl", [128, 8])
        dout = mkT(es, nc, "m2dout", [128, 8])
        dtot = mkT(es, nc, "m2dtot", [128, 8])
        xdt = mkT(es, nc, "m2xdt", [128, 512])
        xdt_bf = mkT(es, nc, "m2xdtb", [128, 512], BF16)
        xdd_bf = mkT(es, nc, "m2xddb", [128, 512], BF16)
        amat = mkT(es, nc, "m2amat", [128, 8, 128])
        LT = mkT(es, nc, "m2LT", [128, 8, 128])
        cbm = mkT(es, nc, "m2cbm", [128, 128])
        MT = mkT(es, nc, "m2MT", [128, 8, 128], BF16)
        yv = mkT(es, nc, "m2y", [128, 512])
        t2 = mkT(es, nc, "m2t2", [128, 512])
        zt = mkT(es, nc, "m2z", [128, 512])
        sq = mkT(es, nc, "m2sq", [128, 512])
        ss = mkT(es, nc, "m2ss", [128, 4])

        def bc8(t):
            return t[0][:, :].unsqueeze(2).to_broadcast([128, 8, 64])

        def v3(t):
            return t[0][:].rearrange("p (h d) -> p h d", h=8)

        for c in range(NCH):
            t0 = c * 128
            if c == 0:
                em.op("dve", lambda e: e.memset(xin[0][:, :, 0:3], 0.0), writes=[xin[1]])
                for blk in range(6):
                    em.dma("sp", xin[0][:, blk, 3:131], p_cm[CM_X + blk * 128:CM_X + (blk + 1) * 128, 0:128], writes=[xin[1]])
            else:
                for blk in range(6):
                    em.dma("sp", xin[0][:, blk, :], p_cm[CM_X + blk * 128:CM_X + (blk + 1) * 128, t0 - 3:t0 + 128], writes=[xin[1]])
            for blk in range(6):
                em.op("dve", lambda e, blk=blk: e.tensor_scalar(out=acc[0][:, blk, :], in0=xin[0][:, blk, 0:128],
                                                                 scalar1=cw[0][:, blk, 0:1], scalar2=None, op0=ALU.mult),
                      reads=[xin[1], cw[1]], writes=[acc[1]])
                for k in range(1, 4):
                    em.op("dve", lambda e, blk=blk, k=k: e.scalar_tensor_tensor(
                        out=acc[0][:, blk, :], in0=xin[0][:, blk, k:k + 128], scalar=cw[0][:, blk, k:k + 1],
                        in1=acc[0][:, blk, :], op0=ALU.mult, op1=ALU.add), reads=[xin[1], cw[1], acc[1]], writes=[acc[1]])
                em.op("act", lambda e, blk=blk: e.activation(out=xact[0][:, blk, :], in_=acc[0][:, blk, :], func=AF.Silu,
                                                             bias=cb[0][:, blk:blk + 1]), reads=[acc[1], cb[1]], writes=[xact[1]])
            em.op("dve", lambda e: e.tensor_copy(out=bT[0][:], in_=xact[0][:, 4, :]), reads=[xact[1]], writes=[bT[1]])
            em.op("dve", lambda e: e.tensor_copy(out=cT[0][:], in_=xact[0][:, 5, :]), reads=[xact[1]], writes=[cT[1]])
            pA, pAb = psum[0]
            for blk in range(4):
                em.op("pe", lambda e, blk=blk: e.transpose(out=pA[:, blk * 128:(blk + 1) * 128], in_=xact[0][:, blk, :],
                                                           identity=cst["ident_f"][0][:]),
                      reads=[xact[1], cst["ident_f"][1]], writes=[pAb], inc=(blk == 3))
            em.op("act", lambda e: e.copy(out=xs_tm[0][:], in_=pA[:, :]), reads=[pAb], writes=[xs_tm[1]])
            pB, pBb = psum[1]
            em.op("pe", lambda e: e.transpose(out=pB[:, 0:128], in_=xact[0][:, 4, :], identity=cst["ident_f"][0][:]),
                  reads=[xact[1], cst["ident_f"][1]], writes=[pBb])
            em.op("act", lambda e: e.copy(out=b_tm[0][:], in_=pB[:, 0:128]), reads=[pBb], writes=[b_tm[1]])
            em.dma("sp", dtr[0][:], p_tm[t0:t0 + 128, TM_DT:TM_DT + 8], writes=[dtr[1]])
            em.op("dve", lambda e: e.tensor_tensor(out=dtv[0][:], in0=dtr[0][:], in1=dtb[0][:], op=ALU.add),
                  reads=[dtr[1], dtb[1]], writes=[dtv[1]])
            em.op("act", lambda e: e.activation(out=dtv[0][:], in_=dtv[0][:], func=AF.Exp), reads=[dtv[1]], writes=[dtv[1]])
            em.op("act", lambda e: e.activation(out=dtv[0][:], in_=dtv[0][:], func=AF.Ln, bias=1.0), reads=[dtv[1]], writes=[dtv[1]])
            em.op("dve", lambda e: e.tensor_tensor(out=av[0][:], in0=dtv[0][:], in1=Aneg[0][:], op=ALU.mult),
                  reads=[dtv[1], Aneg[1]], writes=[av[1]])
            pC, pCb = psum[2]
            em.op("pe", lambda e: e.matmul(pC[:, 0:8], lhsT=cst["triu_f"][0][:], rhs=av[0][:], start=True, stop=True),
                  reads=[cst["triu_f"][1], av[1]], writes=[pCb])
            em.op("pe", lambda e: e.matmul(pC[:, 8:16], lhsT=cst["ones_f"][0][:], rhs=av[0][:], start=True, stop=True),
                  reads=[cst["ones_f"][1], av[1]], writes=[pCb])
            em.op("act", lambda e: e.copy(out=acs[0][:], in_=pC[:, 0:8]), reads=[pCb], writes=[acs[1]])
            em.op("act", lambda e: e.copy(out=tot[0][:], in_=pC[:, 8:16]), reads=[pCb], writes=[tot[1]])
            em.op("dve", lambda e: e.tensor_tensor(out=dsl[0][:], in0=tot[0][:], in1=acs[0][:], op=ALU.subtract),
                  reads=[tot[1], acs[1]], writes=[dsl[1]])
            em.op("act", lambda e: e.activation(out=dsl[0][:], in_=dsl[0][:], func=AF.Exp), reads=[dsl[1]], writes=[dsl[1]])
            em.op("act", lambda e: e.activation(out=dout[0][:], in_=acs[0][:], func=AF.Exp), reads=[acs[1]], writes=[dout[1]])
            em.op("act", lambda e: e.activation(out=dtot[0][:], in_=tot[0][:], func=AF.Exp), reads=[tot[1]], writes=[dtot[1]])
            em.op("dve", lambda e: e.tensor_tensor(out=v3(xdt), in0=v3(xs_tm), in1=bc8(dtv), op=ALU.mult),
                  reads=[xs_tm[1], dtv[1]], writes=[xdt[1]])
            em.op("act", lambda e: e.copy(out=xdt_bf[0][:], in_=xdt[0][:]), reads=[xdt[1]], writes=[xdt_bf[1]])
            em.op("dve", lambda e: e.tensor_tensor(out=xdd_bf[0][:].rearrange("p (h d) -> p h d", h=8), in0=v3(xdt), in1=bc8(dsl), op=ALU.mult),
                  reads=[xdt[1], dsl[1]], writes=[xdd_bf[1]])
            for h in range(8):
                em.op("dve", lambda e, h=h: e.tensor_scalar(out=amat[0][:, h, :], in0=cst["mstrict_f"][0][:], scalar1=av[0][:, h:h + 1],
                                                            scalar2=None, op0=ALU.mult), reads=[cst["mstrict_f"][1], av[1]], writes=[amat[1]])
            pL0, pL0b = psum[3]
            pL1, pL1b = psum[4]
            for h in range(8):
                pl, plb = (pL0, pL0b) if h < 4 else (pL1, pL1b)
                em.op("pe", lambda e, h=h, pl=pl: e.matmul(pl[:, (h % 4) * 128:(h % 4 + 1) * 128], lhsT=amat[0][:, h, :],
                                                          rhs=cst["triu_f"][0][:], start=True, stop=True),
                      reads=[amat[1], cst["triu_f"][1]], writes=[plb])
            em.op("act", lambda e: e.activation(out=LT[0][:, 0:4, :], in_=pL0[:, :].rearrange("p (h l) -> p h l", h=4), func=AF.Exp),
                  reads=[pL0b], writes=[LT[1]])
            em.op("act", lambda e: e.activation(out=LT[0][:, 4:8, :], in_=pL1[:, :].rearrange("p (h l) -> p h l", h=4), func=AF.Exp),
                  reads=[pL1b], writes=[LT[1]])
            pD, pDb = psum[5]
            em.op("pe", lambda e: e.matmul(pD[:, 0:128], lhsT=bT[0][:], rhs=cT[0][:], start=True, stop=True),
                  reads=[bT[1], cT[1]], writes=[pDb])
            em.op("dve", lambda e: e.tensor_tensor(out=cbm[0][:], in0=pD[:, 0:128], in1=cst["mcausT_f"][0][:], op=ALU.mult),
                  reads=[pDb, cst["mcausT_f"][1]], writes=[cbm[1]])
            em.op("dve", lambda e: e.tensor_tensor(out=MT[0][:], in0=LT[0][:], in1=cbm[0][:, :].unsqueeze(1).to_broadcast([128, 8, 128]),
                                                   op=ALU.mult), reads=[LT[1], cbm[1]], writes=[MT[1]])
            pY, pYb = psum[6]
            em.op("pe", lambda e: e.matmul(pY[:, :], lhsT=cT[0][:], rhs=state_bf[0][:], start=True, stop=True),
                  reads=[cT[1], state_bf[1]], writes=[pYb])
            em.op("dve", lambda e: e.tensor_tensor(out=v3(yv), in0=pY[:, :].rearrange("p (h d) -> p h d", h=8), in1=bc8(dout), op=ALU.mult),
                  reads=[pYb, dout[1]], writes=[yv[1]])
            pZ, pZb = psum[0]
            for h in range(8):
                em.op("pe", lambda e, h=h: e.matmul(pZ[:, h * 64:(h + 1) * 64], lhsT=MT[0][:, h, :], rhs=xdt_bf[0][:, h * 64:(h + 1) * 64],
                                                    start=True, stop=True), reads=[MT[1], xdt_bf[1]], writes=[pZb], inc=(h == 7))
            em.op("dve", lambda e: e.tensor_tensor(out=yv[0][:], in0=yv[0][:], in1=pZ[:, :], op=ALU.add), reads=[yv[1], pZb], writes=[yv[1]])
            pS, pSb = psum[1]
            em.op("pe", lambda e: e.matmul(pS[:, :], lhsT=b_tm[0][:], rhs=xdd_bf[0][:], start=True, stop=True),
                  reads=[b_tm[1], xdd_bf[1]], writes=[pSb])
            em.op("dve", lambda e: e.tensor_tensor(out=v3(stateT), in0=v3(stateT), in1=bc8(dtot), op=ALU.mult),
                  reads=[stateT[1], dtot[1]], writes=[stateT[1]])
            em.op("dve", lambda e: e.tensor_tensor(out=stateT[0][:], in0=stateT[0][:], in1=pS[:, :], op=ALU.add),
                  reads=[stateT[1], pSb], writes=[stateT[1]])
            em.op("act", lambda e: e.copy(out=state_bf[0][:], in_=stateT[0][:]), reads=[stateT[1]], writes=[state_bf[1]])
            em.op("dve", lambda e: e.tensor_tensor(out=v3(t2), in0=v3(xs_tm), in1=bc8(dsk), op=ALU.mult),
                  reads=[xs_tm[1], dsk[1]], writes=[t2[1]])
            em.op("dve", lambda e: e.tensor_tensor(out=yv[0][:], in0=yv[0][:], in1=t2[0][:], op=ALU.add), reads=[yv[1], t2[1]], writes=[yv[1]])
            em.dma("sp", zt[0][:], p_tm[t0:t0 + 128, TM_Z:TM_Z + 512], writes=[zt[1]])
            em.op("act", lambda e: e.activation(out=zt[0][:], in_=zt[0][:], func=AF.Silu), reads=[zt[1]], writes=[zt[1]])
            em.op("dve", lambda e: e.tensor_tensor(out=yv[0][:], in0=yv[0][:], in1=zt[0][:], op=ALU.mult), reads=[yv[1], zt[1]], writes=[yv[1]])
            em.op("act", lambda e: e.activation(out=sq[0][:], in_=yv[0][:], func=AF.Square, accum_out=ss[0][:, 0:1]),
                  reads=[yv[1]], writes=[sq[1], ss[1]])
            em.op("dve", lambda e: e.tensor_scalar(out=ss[0][:, 1:2], in0=ss[0][:, 0:1], scalar1=1.0 / 512, scalar2=1e-6,
                                                   op0=ALU.mult, op1=ALU.add), reads=[ss[1]], writes=[ss[1]])
            em.op("act", lambda e: e.sqrt(out=ss[0][:, 3:4], in_=ss[0][:, 1:2]), reads=[ss[1]], writes=[ss[1]])
            em.op("dve", lambda e: e.reciprocal(out=ss[0][:, 2:3], in_=ss[0][:, 3:4]), reads=[ss[1]], writes=[ss[1]])
            em.op("dve", lambda e: e.scalar_tensor_tensor(out=t2[0][:], in0=yv[0][:], scalar=ss[0][:, 2:3], in1=ng[0][:],
                                                          op0=ALU.mult, op1=ALU.mult), reads=[yv[1], ss[1], ng[1]], writes=[t2[1]])
            em.dyn_store(y_dram[t0:t0 + 128, YOFF:YOFF + 512], t2[0][:], [t2[1]])


def stage_hgrn(em, T, layer, p_tm, p_cm, prm, cst, y_dram, psum, TM_I=524, TM_G=780, CM_Q=128, CM_F=384, YOFF=256):
    nc = em.nc
    NT = T // 128
    with ExitStack() as es:
        lbl = mkT(es, nc, "hglbl", [128, 2, 2]); ld(em, lbl, prm["hg_lbl"])
        ng = mkT(es, nc, "hgng", [128, 128]); ld(em, ng, prm["hg_ng"])
        lb = mkT(es, nc, "hglb", [128, 2])
        oml = mkT(es, nc, "hgoml", [128, 2])
        em.op("dve", lambda e: e.tensor_tensor(out=lb[0][:], in0=lbl[0][:, :, 1], in1=lbl[0][:, :, 0], op=ALU.subtract),
              reads=[lbl[1]], writes=[lb[1]])
        em.op("act", lambda e: e.activation(out=lb[0][:], in_=lb[0][:], func=AF.Sigmoid), reads=[lb[1]], writes=[lb[1]])
        lfl = mkT(es, nc, "hglfl", [128, 2]); ld(em, lfl, prm["hg_lflag"])
        em.op("dve", lambda e: e.tensor_tensor(out=lb[0][:], in0=lb[0][:], in1=lfl[0][:], op=ALU.mult),
              reads=[lb[1], lfl[1]], writes=[lb[1]])
        em.op("dve", lambda e: e.tensor_scalar(out=oml[0][:], in0=lb[0][:], scalar1=-1.0, scalar2=1.0, op0=ALU.mult, op1=ALU.add),
              reads=[lb[1]], writes=[oml[1]])
        W = 512
        for hd in range(2):
            S = mkT(es, nc, "hgS%d" % hd, [128, 128])
            Sref = mkT(es, nc, "hgSr%d" % hd, [128, 128], BF16)
            em.op("dve", lambda e: e.memset(S[0][:], 0.0), writes=[S[1]])
            em.op("dve", lambda e: e.memset(Sref[0][:], 0.0), writes=[Sref[1]])
            qT = mkT(es, nc, "hgq%d" % hd, [128, W])
            fT = mkT(es, nc, "hgf%d" % hd, [128, W])
            kk = mkT(es, nc, "hgk%d" % hd, [128, W])
            g0 = mkT(es, nc, "hgg0%d" % hd, [128, W])
            g1 = mkT(es, nc, "hgg1%d" % hd, [128, W])
            d1 = mkT(es, nc, "hgd1%d" % hd, [128, W])
            e1 = mkT(es, nc, "hge1%d" % hd, [128, W])
            qt_bf = mkT(es, nc, "hgqt%d" % hd, [128, W], BF16)
            kt_bf = mkT(es, nc, "hgkt%d" % hd, [128, W], BF16)
            qz = mkT(es, nc, "hgqz%d" % hd, [128, 2, 128], BF16)
            sc = mkT(es, nc, "hgsc%d" % hd, [128, 3, 8])
            ktm = mkT(es, nc, "hgktm%d" % hd, [128, 128], BF16)
            v_f = mkT(es, nc, "hgvf%d" % hd, [128, 128])
            v_bf = mkT(es, nc, "hgvb%d" % hd, [128, 128], BF16)
            AT = mkT(es, nc, "hgAT%d" % hd, [128, 128], BF16)
            tU = mkT(es, nc, "hgtU%d" % hd, [128, 128])
            ot = mkT(es, nc, "hgo%d" % hd, [128, 128])
            sq = mkT(es, nc, "hgsq%d" % hd, [128, 128])
            ss = mkT(es, nc, "hgss%d" % hd, [128, 4])
            gt = mkT(es, nc, "hggt%d" % hd, [128, 128])
            em.op("dve", lambda e: e.memset(qz[0][:], 0.0), writes=[qz[1]])
            for sc_i in range(T // W):
                c0 = sc_i * W
                em.dma("sp", qT[0][:], p_cm[CM_Q + hd * 128:CM_Q + (hd + 1) * 128, c0:c0 + W], writes=[qT[1]])
                em.dma("sp", fT[0][:], p_cm[CM_F + hd * 128:CM_F + (hd + 1) * 128, c0:c0 + W], writes=[fT[1]])
                em.op("act", lambda e: e.activation(out=fT[0][:], in_=fT[0][:], func=AF.Sigmoid), reads=[fT[1]], writes=[fT[1]])
                em.op("dve", lambda e: e.tensor_scalar(out=fT[0][:], in0=fT[0][:], scalar1=oml[0][:, hd:hd + 1], scalar2=lb[0][:, hd:hd + 1],
                                                       op0=ALU.mult, op1=ALU.add), reads=[fT[1], oml[1], lb[1]], writes=[fT[1]])
                em.op("dve", lambda e: e.tensor_scalar(out=kk[0][:], in0=fT[0][:], scalar1=-1.0, scalar2=1.0, op0=ALU.mult, op1=ALU.add),
                      reads=[fT[1]], writes=[kk[1]])
                em.op("act", lambda e: e.activation(out=g0[0][:], in_=fT[0][:], func=AF.Ln), reads=[fT[1]], writes=[g0[1]])
                src, dst = g0, g1
                for d in (1, 2, 4, 8, 16, 32):
                    sv = src[0][:].rearrange("p (c t) -> p c t", t=64)
                    dv = dst[0][:].rearrange("p (c t) -> p c t", t=64)
                    em.op("dve", lambda e, sv=sv, dv=dv, d=d: e.tensor_copy(out=dv[:, :, 0:d], in_=sv[:, :, 0:d]), reads=[src[1]], writes=[dst[1]])
                    em.op("dve", lambda e, sv=sv, dv=dv, d=d: e.tensor_tensor(out=dv[:, :, d:64], in0=sv[:, :, d:64], in1=sv[:, :, 0:64 - d], op=ALU.add),
                          reads=[src[1]], writes=[dst[1]])
                    src, dst = dst, src
                gc = src
                gv = gc[0][:].rearrange("p (c t) -> p c t", t=64)
                em.op("dve", lambda e: e.tensor_tensor(out=d1[0][:].rearrange("p (c t) -> p c t", t=64), in0=gv,
                                                       in1=gv[:, :, 31:32].to_broadcast([128, 8, 64]), op=ALU.subtract),
                      reads=[gc[1]], writes=[d1[1]])
                em.op("act", lambda e: e.activation(out=e1[0][:], in_=d1[0][:], func=AF.Exp), reads=[d1[1]], writes=[e1[1]])
                em.op("dve", lambda e: e.tensor_tensor(out=qt_bf[0][:], in0=qT[0][:], in1=e1[0][:], op=ALU.mult), reads=[qT[1], e1[1]], writes=[qt_bf[1]])
                em.op("act", lambda e: e.activation(out=e1[0][:], in_=d1[0][:], func=AF.Exp, scale=-1.0), reads=[d1[1]], writes=[e1[1]])
                em.op("dve", lambda e: e.tensor_tensor(out=kt_bf[0][:], in0=kk[0][:], in1=e1[0][:], op=ALU.mult), reads=[kk[1], e1[1]], writes=[kt_bf[1]])
                em.op("act", lambda e: e.activation(out=sc[0][:, 0, :], in_=gv[:, :, 31], func=AF.Exp), reads=[gc[1]], writes=[sc[1]])
                em.op("act", lambda e: e.activation(out=sc[0][:, 1, :], in_=gv[:, :, 63], func=AF.Exp), reads=[gc[1]], writes=[sc[1]])
                em.op("act", lambda e: e.activation(out=sc[0][:, 2, :], in_=d1[0][:].rearrange("p (c t) -> p c t", t=64)[:, :, 63], func=AF.Exp),
                      reads=[d1[1]], writes=[sc[1]])
                em.op("dve", lambda e: e.tensor_scalar(out=Sref[0][:], in0=S[0][:], scalar1=sc[0][:, 0, 0:1], scalar2=None, op0=ALU.mult),
                      reads=[S[1], sc[1]], writes=[Sref[1]])
                for tl in range(W // 128):
                    t0 = c0 + tl * 128
                    cols = slice(tl * 128, (tl + 1) * 128)
                    em.dma("sp", v_f[0][:], p_tm[t0:t0 + 128, TM_I + hd * 128:TM_I + (hd + 1) * 128], writes=[v_f[1]])
                    em.op("act", lambda e: e.copy(out=v_bf[0][:], in_=v_f[0][:]), reads=[v_f[1]], writes=[v_bf[1]])
                    pA, pAb = psum[0]
                    em.op("pe", lambda e, cols=cols: e.matmul(pA[:, 0:128], lhsT=kt_bf[0][:, cols], rhs=qt_bf[0][:, cols], start=True, stop=True),
                          reads=[kt_bf[1], qt_bf[1]], writes=[pAb])
                    em.op("dve", lambda e: e.tensor_tensor(out=AT[0][:], in0=pA[:, 0:128], in1=cst["mask2_f"][0][:], op=ALU.mult),
                          reads=[pAb, cst["mask2_f"][1]], writes=[AT[1]])
                    pT, pTb = psum[7]
                    em.op("pe", lambda e, cols=cols: e.transpose(out=pT[:, 0, :], in_=kt_bf[0][:, cols], identity=cst["ident_bf"][0][:]),
                          reads=[kt_bf[1], cst["ident_bf"][1]], writes=[pTb])
                    em.op("act", lambda e: e.copy(out=ktm[0][:], in_=pT[:, 0, :]), reads=[pTb], writes=[ktm[1]])
                    em.op("dve", lambda e, cols=cols: e.tensor_copy(out=qz[0][:, 0, 0:64], in_=qt_bf[0][:, tl * 128:tl * 128 + 64]),
                          reads=[qt_bf[1]], writes=[qz[1]])
                    em.op("dve", lambda e, cols=cols: e.tensor_copy(out=qz[0][:, 1, 64:128], in_=qt_bf[0][:, tl * 128 + 64:tl * 128 + 128]),
                          reads=[qt_bf[1]], writes=[qz[1]])
                    pO, pOb = psum[1]
                    em.op("pe", lambda e: e.matmul(pO[:, 0:128], lhsT=AT[0][:], rhs=v_bf[0][:], start=True, stop=False),
                          reads=[AT[1], v_bf[1]], writes=[pOb], inc=False)
                    for j in range(2):
                        ch = tl * 2 + j
                        em.op("pe", lambda e, j=j: e.matmul(pO[:, 0:128], lhsT=qz[0][:, j, :], rhs=Sref[0][:], start=False, stop=(j == 1)),
                              reads=[qz[1], Sref[1]], writes=[pOb], inc=True)
                        pU, pUb = psum[2 + j]
                        rows = slice(j * 64, (j + 1) * 64)
                        em.op("pe", lambda e, rows=rows, pU=pU: e.matmul(pU[:, 0:128], lhsT=ktm[0][rows, :], rhs=v_bf[0][rows, :], start=True, stop=True),
                              reads=[ktm[1], v_bf[1]], writes=[pUb])
                        em.op("dve", lambda e, pU=pU, ch=ch: e.tensor_scalar(out=tU[0][:], in0=pU[:, 0:128], scalar1=sc[0][:, 2, ch:ch + 1], scalar2=None, op0=ALU.mult),
                              reads=[pUb, sc[1]], writes=[tU[1]])
                        em.op("dve", lambda e, ch=ch: e.scalar_tensor_tensor(out=S[0][:], in0=S[0][:], scalar=sc[0][:, 1, ch:ch + 1], in1=tU[0][:],
                                                                              op0=ALU.mult, op1=ALU.add), reads=[S[1], sc[1], tU[1]], writes=[S[1]])
                        if ch < 7:
                            em.op("dve", lambda e, ch=ch: e.tensor_scalar(out=Sref[0][:], in0=S[0][:], scalar1=sc[0][:, 0, ch + 1:ch + 2], scalar2=None,
                                                                           op0=ALU.mult), reads=[S[1], sc[1]], writes=[Sref[1]])
                    em.op("act", lambda e: e.copy(out=ot[0][:], in_=pO[:, 0:128]), reads=[pOb], writes=[ot[1]])
                    em.op("act", lambda e: e.activation(out=sq[0][:], in_=ot[0][:], func=AF.Square, accum_out=ss[0][:, 0:1]),
                          reads=[ot[1]], writes=[sq[1], ss[1]])
                    em.op("dve", lambda e: e.tensor_scalar(out=ss[0][:, 1:2], in0=ss[0][:, 0:1], scalar1=1.0 / 128, scalar2=1e-6,
                                                           op0=ALU.mult, op1=ALU.add), reads=[ss[1]], writes=[ss[1]])
                    em.op("act", lambda e: e.sqrt(out=ss[0][:, 3:4], in_=ss[0][:, 1:2]), reads=[ss[1]], writes=[ss[1]])
                    em.op("dve", lambda e: e.reciprocal(out=ss[0][:, 2:3], in_=ss[0][:, 3:4]), reads=[ss[1]], writes=[ss[1]])
                    em.op("dve", lambda e: e.scalar_tensor_tensor(out=ot[0][:], in0=ot[0][:], scalar=ss[0][:, 2:3], in1=ng[0][:],
                                                                  op0=ALU.mult, op1=ALU.mult), reads=[ot[1], ss[1], ng[1]], writes=[ot[1]])
                    em.dma("sp", gt[0][:], p_tm[t0:t0 + 128, TM_G + hd * 128:TM_G + (hd + 1) * 128], writes=[gt[1]])
                    em.op("act", lambda e: e.activation(out=gt[0][:], in_=gt[0][:], func=AF.Silu), reads=[gt[1]], writes=[gt[1]])
                    em.op("dve", lambda e: e.tensor_tensor(out=ot[0][:], in0=ot[0][:], in1=gt[0][:], op=ALU.mult), reads=[ot[1], gt[1]], writes=[ot[1]])
                    em.dyn_store(y_dram[t0:t0 + 128, YOFF + hd * 128:YOFF + (hd + 1) * 128], ot[0][:], [ot[1]])


NEG = -30000.0
TWO_PI = 6.283185307179586


def rope_apply(em, x3, xb, o3, ob, cos2, sin2, csb, tmp, H):
    cb = cos2.unsqueeze(1).to_broadcast([128, H, 8])
    sb = sin2.unsqueeze(1).to_broadcast([128, H, 8])
    t = tmp[0]
    x1 = x3[:, :, 0:8]
    x2 = x3[:, :, 8:16]
    em.op("dve", lambda e: e.tensor_tensor(out=t[:, 0, 0:H, :], in0=x1, in1=cb, op=ALU.mult), reads=[xb, csb], writes=[tmp[1]])
    em.op("dve", lambda e: e.tensor_tensor(out=t[:, 1, 0:H, :], in0=x2, in1=sb, op=ALU.mult), reads=[xb, csb], writes=[tmp[1]])
    em.op("dve", lambda e: e.tensor_tensor(out=t[:, 2, 0:H, :], in0=x2, in1=cb, op=ALU.mult), reads=[xb, csb], writes=[tmp[1]])
    em.op("dve", lambda e: e.tensor_tensor(out=t[:, 3, 0:H, :], in0=x1, in1=sb, op=ALU.mult), reads=[xb, csb], writes=[tmp[1]])
    em.op("dve", lambda e: e.tensor_tensor(out=o3[:, :, 0:8], in0=t[:, 0, 0:H, :], in1=t[:, 1, 0:H, :], op=ALU.subtract), reads=[tmp[1]], writes=[ob])
    em.op("dve", lambda e: e.tensor_tensor(out=o3[:, :, 8:16], in0=t[:, 2, 0:H, :], in1=t[:, 3, 0:H, :], op=ALU.add), reads=[tmp[1]], writes=[ob])


def headnorm(em, x, xb, H, g_ap, gb, sq, ss, out3, outb):
    x3 = x[:, 0:H * 64].rearrange("p (h d) -> p h d", h=H)
    s3 = sq[0][:, 0:H * 64].rearrange("p (h d) -> p h d", h=H)
    em.op("dve", lambda e: e.tensor_tensor(out=s3, in0=x3, in1=x3, op=ALU.mult), reads=[xb], writes=[sq[1]])
    em.op("dve", lambda e: e.tensor_reduce(out=ss[0][:, 0, 0:H], in_=s3, axis=AX.X, op=ALU.add), reads=[sq[1]], writes=[ss[1]])
    em.op("dve", lambda e: e.tensor_scalar(out=ss[0][:, 1, 0:H], in0=ss[0][:, 0, 0:H], scalar1=1.0 / 64, scalar2=1e-6, op0=ALU.mult, op1=ALU.add),
          reads=[ss[1]], writes=[ss[1]])
    em.op("act", lambda e: e.sqrt(out=ss[0][:, 2, 0:H], in_=ss[0][:, 1, 0:H]), reads=[ss[1]], writes=[ss[1]])
    em.op("dve", lambda e: e.reciprocal(out=ss[0][:, 1, 0:H], in_=ss[0][:, 2, 0:H]), reads=[ss[1]], writes=[ss[1]])
    em.op("dve", lambda e: e.tensor_tensor(out=out3, in0=x3, in1=ss[0][:, 1, 0:H].unsqueeze(2).to_broadcast([128, H, 64]), op=ALU.mult),
          reads=[xb, ss[1]], writes=[outb])
    em.op("dve", lambda e: e.tensor_tensor(out=out3, in0=out3, in1=g_ap, op=ALU.mult), reads=[outb, gb], writes=[outb])


def stage_nsa(em, T, p_tm, p_cm, prm, cst, y_dram, psum, YOFF=0):
    nc = em.nc
    NQ = T // 128
    NCMP = (T - 32) // 16 + 1
    NJC = (NCMP + 127) // 128
    DBG = None
    def dbg(name, t, part=128):
        if DBG is None:
            return
        shape = list(t[0].shape)
        d = nc.dram_tensor("dbg_" + name, shape, t[0].dtype, kind="ExternalOutput").ap()
        em.dma("sp", d, t[0][:], reads=[t[1]])
        DBG.append("dbg_" + name)
    with ExitStack() as es:
        def LD(name, shape, dt=F32):
            t = mkT(es, nc, "ns_" + name, shape, dt)
            src = prm[name]
            if len(shape) == 2:
                src = src[0:shape[0], 0:shape[1]]
            else:
                src = src[0:shape[0], 0:shape[1], 0:shape[2]]
            ld(em, t, src)
            return t
        gq = LD("ns_gq", [128, 256]); gk = LD("ns_gk", [128, 2, 64]); gk0 = LD("ns_gk0", [64, 1])
        posT = LD("ns_posT", [128, 32]); w2k = LD("ns_w2k", [128, 2, 64]); w2v = LD("ns_w2v", [128, 2, 64])
        posi = LD("ns_pos", [128, NQ], I32)
        invf = LD("c_invf", [128, 8])
        cmpmask = LD("c_cmpmask", [128, 17, 512], BF16)
        c2s = LD("c_c2s", [128, NJC, 128], BF16)
        ebig = LD("c_ebig", [128, T], BF16)
        causneg = LD("c_causneg", [128, 512], BF16)
        anticaus = LD("c_anticaus", [128, 512], BF16)
        keepb = LD("c_keep", [128, 254]); addb = LD("c_add", [128, 254])
        ident_bf = cst["ident_bf"]; ident_f = cst["ident_f"]; ones_f = cst["ones_f"]
        pst, pstb = psum[7]
        cosT = mkT(es, nc, "ns_cos", [128, NQ, 8]); sinT = mkT(es, nc, "ns_sin", [128, NQ, 8])
        if True:
            es2 = es
            posf = mkT(es2, nc, "ns_posf", [128, NQ])
            ang = mkT(es2, nc, "ns_ang", [128, NQ, 8]); fr = mkT(es2, nc, "ns_fr", [128, NQ, 8])
            ki = mkT(es2, nc, "ns_ki", [128, NQ, 8], I32); kf = mkT(es2, nc, "ns_kf", [128, NQ, 8])
            em.op("dve", lambda e: e.tensor_copy(out=posf[0][:], in_=posi[0][:]), reads=[posi[1]], writes=[posf[1]])
            em.op("dve", lambda e: e.tensor_tensor(out=ang[0][:], in0=posf[0][:, :].unsqueeze(2).to_broadcast([128, NQ, 8]),
                                                   in1=invf[0][:, :].unsqueeze(1).to_broadcast([128, NQ, 8]), op=ALU.mult),
                  reads=[posf[1], invf[1]], writes=[ang[1]])
            for dst, off in ((sinT, 0.0), (cosT, 0.25)):
                em.op("dve", lambda e, off=off: e.tensor_scalar(out=fr[0][:], in0=ang[0][:], scalar1=1.0 / TWO_PI, scalar2=off, op0=ALU.mult, op1=ALU.add),
                      reads=[ang[1]], writes=[fr[1]])
                em.op("dve", lambda e: e.tensor_copy(out=ki[0][:], in_=fr[0][:]), reads=[fr[1]], writes=[ki[1]])
                em.op("dve", lambda e: e.tensor_copy(out=kf[0][:], in_=ki[0][:]), reads=[ki[1]], writes=[kf[1]])
                em.op("dve", lambda e: e.tensor_tensor(out=fr[0][:], in0=fr[0][:], in1=kf[0][:], op=ALU.subtract), reads=[fr[1], kf[1]], writes=[fr[1]])
                em.op("dve", lambda e: e.tensor_scalar(out=kf[0][:], in0=fr[0][:], scalar1=0.5, scalar2=None, op0=ALU.is_gt), reads=[fr[1]], writes=[kf[1]])
                em.op("dve", lambda e: e.tensor_tensor(out=fr[0][:], in0=fr[0][:], in1=kf[0][:], op=ALU.subtract), reads=[fr[1], kf[1]], writes=[fr[1]])
                em.op("dve", lambda e: e.tensor_scalar(out=kf[0][:], in0=fr[0][:], scalar1=-0.5, scalar2=None, op0=ALU.is_lt), reads=[fr[1]], writes=[kf[1]])
                em.op("dve", lambda e: e.tensor_tensor(out=fr[0][:], in0=fr[0][:], in1=kf[0][:], op=ALU.add), reads=[fr[1], kf[1]], writes=[fr[1]])
                em.op("act", lambda e, dst=dst: e.activation(out=dst[0][:], in_=fr[0][:], func=AF.Sin, scale=TWO_PI), reads=[fr[1]], writes=[dst[1]])
        cs_b = Buf("cs")
        kTs = mkT(es, nc, "ns_kTs", [64, T], BF16); kTw = mkT(es, nc, "ns_kTw", [64, T], BF16)
        vs = mkT(es, nc, "ns_vs", [128, NQ, 65], BF16); vw = mkT(es, nc, "ns_vw", [128, NQ, 65], BF16)
        em.op("dve", lambda e: e.memset(vs[0][:, :, 64:65], 1.0), writes=[vs[1]])
        em.op("dve", lambda e: e.memset(vw[0][:, :, 64:65], 1.0), writes=[vw[1]])
        kcT = mkT(es, nc, "ns_kcT", [64, NJC * 128], BF16)
        vc = mkT(es, nc, "ns_vc", [128, NJC, 65], BF16)
        em.op("dve", lambda e: e.memset(kcT[0][:], 0.0), writes=[kcT[1]])
        em.op("dve", lambda e: e.memset(vc[0][:, :, 64:65], 1.0), writes=[vc[1]])
        sq = mkT(es, nc, "ns_sq", [128, 256]); ss = mkT(es, nc, "ns_ss", [128, 3, 4])
        rtmp = mkT(es, nc, "ns_rtmp", [128, 4, 4, 8])
        if True:
            es2 = es
            kvr = mkT(es2, nc, "ns_kvr", [128, 256])
            kn = mkT(es2, nc, "ns_kn", [128, 2, 64]); kr = mkT(es2, nc, "ns_kr", [128, 2, 64]); kb = mkT(es2, nc, "ns_kb", [128, 2, 64], BF16)
            for i in range(NQ):
                t0 = i * 128
                em.dma("sp", kvr[0][:], p_tm[t0:t0 + 128, 256:512], writes=[kvr[1]])
                em.op("act", lambda e: e.copy(out=vs[0][:, i, 0:64], in_=kvr[0][:, 64:128]), reads=[kvr[1]], writes=[vs[1]])
                em.op("act", lambda e: e.copy(out=vw[0][:, i, 0:64], in_=kvr[0][:, 192:256]), reads=[kvr[1]], writes=[vw[1]])
                for w, c0 in ((0, 0), (1, 128)):
                    xs_ = kvr[0][:, c0:c0 + 64]
                    headnorm(em, xs_, kvr[1], 1, gk[0][:, w:w + 1, :], gk[1], sq, ss, kn[0][:, w:w + 1, :], kn[1])
                em.op("act", lambda e: e.copy(out=kr[0][:], in_=kn[0][:]), reads=[kn[1]], writes=[kr[1]])
                rope_apply(em, kn[0][:], kn[1], kr[0][:], kr[1], cosT[0][:, i, :], sinT[0][:, i, :], cosT[1], rtmp, 2)
                em.op("act", lambda e: e.copy(out=kb[0][:], in_=kr[0][:]), reads=[kr[1], sinT[1]], writes=[kb[1]])
                for w, dst in ((0, kTs), (1, kTw)):
                    em.op("pe", lambda e, w=w: e.transpose(out=pst[0:64, w, :], in_=kb[0][:, w, :], identity=ident_bf[0][:]),
                          reads=[kb[1], ident_bf[1]], writes=[pstb])
                    em.op("act", lambda e, w=w, dst=dst: e.copy(out=dst[0][:, t0:t0 + 128], in_=pst[0:64, w, :]), reads=[pstb], writes=[dst[1]])
        if True:
            es2 = es
            w1 = mkT(es2, nc, "ns_w1", [128, 32, 256], BF16)
            stg = mkT(es2, nc, "ns_w1s", [128, 2, 256])
            for l0 in range(0, 32, 2):
                em.dma("sp", stg[0][:], prm["ns_w1kv"][:, l0:l0 + 2, :], writes=[stg[1]])
                em.op("act", lambda e, l0=l0: e.copy(out=w1[0][:, l0:l0 + 2, :], in_=stg[0][:]), reads=[stg[1]], writes=[w1[1]])
            kvc = mkT(es2, nc, "ns_kvc", [128, T], BF16)
            CH = min(512, T)
            stg2 = mkT(es2, nc, "ns_kvs", [128, CH])
            for c0 in range(0, T, CH):
                em.dma("sp", stg2[0][:], p_cm[0:128, c0:c0 + CH], writes=[stg2[1]])
                em.op("act", lambda e, c0=c0: e.copy(out=kvc[0][:, c0:c0 + CH], in_=stg2[0][:]), reads=[stg2[1]], writes=[kvc[1]])
            posb = mkT(es2, nc, "ns_posb", [128, 34], BF16)
            em.op("dve", lambda e: e.memset(posb[0][:], 0.0), writes=[posb[1]])
            em.op("act", lambda e: e.copy(out=posb[0][:, 0:32], in_=posT[0][:]), reads=[posT[1]], writes=[posb[1]])
            w2kb = mkT(es2, nc, "ns_w2kb", [128, 2, 64], BF16); w2vb = mkT(es2, nc, "ns_w2vb", [128, 2, 64], BF16)
            em.op("act", lambda e: e.copy(out=w2kb[0][:], in_=w2k[0][:]), reads=[w2k[1]], writes=[w2kb[1]])
            em.op("act", lambda e: e.copy(out=w2vb[0][:], in_=w2v[0][:]), reads=[w2v[1]], writes=[w2vb[1]])
            hact = mkT(es2, nc, "ns_hact", [128, 2, 2, 512], BF16)
            em.op("dve", lambda e: e.memset(hact[0][:], 0.0), writes=[hact[1]])
            bias = mkT(es2, nc, "ns_hb", [128, 4])
            hx = mkT(es2, nc, "ns_hx", [128, 512]); hu = mkT(es2, nc, "ns_hu", [128, 512])
            NJ = NCMP
            for kv in range(2):
                rows = slice(kv * 64, kv * 64 + 64)
                for half in range(2):
                    ph, phb = psum[kv * 2 + half]
                    for l in range(32):
                        em.op("pe", lambda e, l=l, ph=ph: e.matmul(ph[:, 0:NJ], lhsT=w1[0][rows, l, half * 128:(half + 1) * 128],
                                                                    rhs=kvc[0][rows, l:l + 16 * (NJ - 1) + 1:16], start=(l == 0), stop=(l == 31)),
                              reads=[w1[1], kvc[1]], writes=[phb], inc=(l == 31))
                    pb, pbb = psum[4]
                    for l in range(32):
                        em.op("pe", lambda e, l=l: e.matmul(pb[:, 0:2], lhsT=w1[0][rows, l, half * 128:(half + 1) * 128], rhs=posb[0][rows, l:l + 2],
                                                            start=(l == 0), stop=(l == 31)), reads=[w1[1], posb[1]], writes=[pbb], inc=(l == 31))
                    bi = kv * 2 + half
                    em.op("act", lambda e, bi=bi: e.copy(out=bias[0][:, bi:bi + 1], in_=pb[:, 0:1]), reads=[pbb], writes=[bias[1]])
                    em.op("dve", lambda e, bi=bi, ph=ph: e.tensor_scalar(out=hx[0][:, 0:NJ], in0=ph[:, 0:NJ], scalar1=bias[0][:, bi:bi + 1], scalar2=None, op0=ALU.add),
                          reads=[phb, bias[1]], writes=[hx[1]])
                    em.op("dve", lambda e: e.tensor_tensor(out=hu[0][:, 0:NJ], in0=hx[0][:, 0:NJ], in1=hx[0][:, 0:NJ], op=ALU.mult), reads=[hx[1]], writes=[hu[1]])
                    em.op("dve", lambda e: e.tensor_scalar(out=hu[0][:, 0:NJ], in0=hu[0][:, 0:NJ], scalar1=0.044715, scalar2=1.0, op0=ALU.mult, op1=ALU.add),
                          reads=[hu[1]], writes=[hu[1]])
                    em.op("dve", lambda e: e.tensor_tensor(out=hu[0][:, 0:NJ], in0=hu[0][:, 0:NJ], in1=hx[0][:, 0:NJ], op=ALU.mult), reads=[hu[1], hx[1]], writes=[hu[1]])
                    em.op("act", lambda e: e.activation(out=hu[0][:, 0:NJ], in_=hu[0][:, 0:NJ], func=AF.Sigmoid, scale=1.5957691216057308), reads=[hu[1]], writes=[hu[1]])
                    em.op("dve", lambda e, kv=kv, half=half: e.tensor_tensor(out=hact[0][:, kv, half, 0:NJ], in0=hu[0][:, 0:NJ], in1=hx[0][:, 0:NJ], op=ALU.mult),
                          reads=[hu[1], hx[1]], writes=[hact[1]])
            pk, pkb = psum[5]
            for half in range(2):
                em.op("pe", lambda e, half=half: e.matmul(pk[0:64, 0:512], lhsT=w2kb[0][:, half, :], rhs=hact[0][:, 0, half, :], start=(half == 0), stop=(half == 1)),
                      reads=[w2kb[1], hact[1]], writes=[pkb], inc=(half == 1))
            kc_f = mkT(es2, nc, "ns_kcf", [64, 512]); kc_sq = mkT(es2, nc, "ns_kcsq", [64, 512]); kc_r = mkT(es2, nc, "ns_kcr", [64, 512])
            em.op("act", lambda e: e.copy(out=kc_f[0][:], in_=pk[0:64, 0:512]), reads=[pkb], writes=[kc_f[1]])
            em.op("dve", lambda e: e.tensor_tensor(out=kc_sq[0][:], in0=kc_f[0][:], in1=kc_f[0][:], op=ALU.mult), reads=[kc_f[1]], writes=[kc_sq[1]])
            pk2, pk2b = psum[6]
            em.op("pe", lambda e: e.matmul(pk2[0:64, 0:512], lhsT=ones_f[0][0:64, 0:64], rhs=kc_sq[0][:], start=True, stop=True),
                  reads=[ones_f[1], kc_sq[1]], writes=[pk2b])
            em.op("dve", lambda e: e.tensor_scalar(out=kc_r[0][:], in0=pk2[0:64, 0:512], scalar1=1.0 / 64, scalar2=1e-6, op0=ALU.mult, op1=ALU.add),
                  reads=[pk2b], writes=[kc_r[1]])
            em.op("act", lambda e: e.sqrt(out=kc_r[0][:], in_=kc_r[0][:]), reads=[kc_r[1]], writes=[kc_r[1]])
            em.op("dve", lambda e: e.reciprocal(out=kc_r[0][:], in_=kc_r[0][:]), reads=[kc_r[1]], writes=[kc_r[1]])
            em.op("dve", lambda e: e.tensor_tensor(out=kc_f[0][:], in0=kc_f[0][:], in1=kc_r[0][:], op=ALU.mult), reads=[kc_f[1], kc_r[1]], writes=[kc_f[1]])
            em.op("dve", lambda e: e.tensor_scalar(out=kcT[0][:, 0:NJ], in0=kc_f[0][:, 0:NJ], scalar1=gk0[0][:, 0:1], scalar2=None, op0=ALU.mult),
                  reads=[kc_f[1], gk0[1]], writes=[kcT[1]])
            for jc in range(NJC):
                pv, pvb = psum[jc % 4]
                for half in range(2):
                    em.op("pe", lambda e, half=half, jc=jc, pv=pv: e.matmul(pv[:, 0:64], lhsT=hact[0][:, 1, half, jc * 128:(jc + 1) * 128], rhs=w2vb[0][:, half, :],
                                                                           start=(half == 0), stop=(half == 1)), reads=[hact[1], w2vb[1]], writes=[pvb], inc=(half == 1))
                em.op("act", lambda e, jc=jc, pv=pv: e.copy(out=vc[0][:, jc, 0:64], in_=pv[:, 0:64]), reads=[pvb], writes=[vc[1]])
        qraw = mkT(es, nc, "ns_qraw", [128, 256]); gtr = mkT(es, nc, "ns_gtr", [128, 12]); gts = mkT(es, nc, "ns_gts", [128, 12])
        qn = mkT(es, nc, "ns_qn", [128, 256]); qr = mkT(es, nc, "ns_qr", [128, 256])
        qnb = mkT(es, nc, "ns_qnb", [128, 256], BF16); qrb = mkT(es, nc, "ns_qrb", [128, 256], BF16)
        qTn = mkT(es, nc, "ns_qTn", [64, 512], BF16); qTr = mkT(es, nc, "ns_qTr", [64, 512], BF16)
        eTs = [mkT(es, nc, "ns_eT%d" % k, [128, 512], BF16) for k in range(2)]
        oT_sb = mkT(es, nc, "ns_oT", [65, 512])
        otm = mkT(es, nc, "ns_otm", [128, 4, 65])
        rz = mkT(es, nc, "ns_rz", [128, 4]); wz = mkT(es, nc, "ns_wz", [128, 4])
        imp = mkT(es, nc, "ns_imp", [128, 128]); rp = mkT(es, nc, "ns_rp", [128, 128])
        mx = mkT(es, nc, "ns_mx", [128, 8]); thr = mkT(es, nc, "ns_thr", [128, 1])
        nsel = mkT(es, nc, "ns_nsel", [128, 128], BF16)
        nselT = mkT(es, nc, "ns_nselT", [128, 128], BF16)
        nselT4 = mkT(es, nc, "ns_nselT4", [128, 512], BF16)
        acc = mkT(es, nc, "ns_acc", [128, 256])
        n_s = 0
        for i in range(NQ):
            t0 = i * 128
            em.dma("sp", qraw[0][:], p_tm[t0:t0 + 128, 0:256], writes=[qraw[1]])
            em.dma("sp", gtr[0][:], p_tm[t0:t0 + 128, 512:524], writes=[gtr[1]])
            em.op("act", lambda e: e.activation(out=gts[0][:], in_=gtr[0][:], func=AF.Sigmoid), reads=[gtr[1]], writes=[gts[1]])
            headnorm(em, qraw[0], qraw[1], 4, gq[0][:].rearrange("p (h d) -> p h d", h=4), gq[1], sq, ss,
                     qn[0][:].rearrange("p (h d) -> p h d", h=4), qn[1])
            em.op("act", lambda e: e.copy(out=qr[0][:], in_=qn[0][:]), reads=[qn[1]], writes=[qr[1]])
            rope_apply(em, qn[0][:].rearrange("p (h d) -> p h d", h=4), qn[1], qr[0][:].rearrange("p (h d) -> p h d", h=4), qr[1],
                       cosT[0][:, i, :], sinT[0][:, i, :], cosT[1], rtmp, 4)
            em.op("act", lambda e: e.copy(out=qnb[0][:], in_=qn[0][:]), reads=[qn[1]], writes=[qnb[1]])
            em.op("act", lambda e: e.copy(out=qrb[0][:], in_=qr[0][:]), reads=[qr[1], sinT[1]], writes=[qrb[1]])
            for r in range(4):
                em.op("pe", lambda e, r=r: e.transpose(out=pst[0:64, r, :], in_=qnb[0][:, r * 64:(r + 1) * 64], identity=ident_bf[0][:]),
                      reads=[qnb[1], ident_bf[1]], writes=[pstb], inc=False)
            for r in range(4):
                em.op("pe", lambda e, r=r: e.transpose(out=pst[0:64, 4 + r, :], in_=qrb[0][:, r * 64:(r + 1) * 64], identity=ident_bf[0][:]),
                      reads=[qrb[1], ident_bf[1]], writes=[pstb], inc=(r == 3))
            em.op("act", lambda e: e.copy(out=qTn[0][:].rearrange("p (r t) -> p r t", r=4), in_=pst[0:64, 0:4, :]), reads=[pstb], writes=[qTn[1]])
            em.op("act", lambda e: e.copy(out=qTr[0][:].rearrange("p (r t) -> p r t", r=4), in_=pst[0:64, 4:8, :]), reads=[pstb], writes=[qTr[1]])

            def attend(kT, kb_, c, q_sb, extra, v_ap, vb_, po, first, last):
                nonlocal n_s
                ps, psb = psum[n_s % 2]
                eT = eTs[n_s % 2]
                n_s += 1
                nm = len(extra)
                em.op("pe", lambda e: e.matmul(ps[:, :], lhsT=kT[0][:, c * 128:(c + 1) * 128], rhs=q_sb[0][:, :], start=True, stop=(nm == 0)),
                      reads=[kb_, q_sb[1]], writes=[psb], inc=(nm == 0))
                for k, (l_ap, l_b, r_ap, r_b) in enumerate(extra):
                    em.op("pe", lambda e, l_ap=l_ap, r_ap=r_ap, k=k: e.matmul(ps[:, :], lhsT=l_ap, rhs=r_ap, start=False, stop=(k == nm - 1)),
                          reads=[l_b, r_b], writes=[psb], inc=(k == nm - 1))
                em.op("act", lambda e: e.activation(out=eT[0][:], in_=ps[:, :], func=AF.Exp, scale=0.125), reads=[psb], writes=[eT[1]])
                em.op("pe", lambda e: e.matmul(po[0][0:65, :], lhsT=v_ap, rhs=eT[0][:], start=first, stop=last),
                      reads=[vb_, eT[1]], writes=[po[1]], inc=True)
                return eT

            def finish(po, br, first_branch):
                em.op("act", lambda e: e.copy(out=oT_sb[0][:], in_=po[0][0:65, :]), reads=[po[1]], writes=[oT_sb[1]])
                p6, p6b = psum[6]
                for r in range(4):
                    em.op("pe", lambda e, r=r: e.transpose(out=p6[:, r * 65:(r + 1) * 65], in_=oT_sb[0][:, r * 128:(r + 1) * 128], identity=ident_f[0][0:65, 0:65]),
                          reads=[oT_sb[1], ident_f[1]], writes=[p6b], inc=(r == 3))
                em.op("act", lambda e: e.copy(out=otm[0][:], in_=p6[:, 0:260].rearrange("p (r d) -> p r d", r=4)), reads=[p6b], writes=[otm[1]])
                em.op("dve", lambda e: e.tensor_scalar(out=rz[0][:], in0=otm[0][:, :, 64], scalar1=1e-30, scalar2=None, op0=ALU.max), reads=[otm[1]], writes=[rz[1]])
                em.op("dve", lambda e: e.reciprocal(out=rz[0][:], in_=rz[0][:]), reads=[rz[1]], writes=[rz[1]])
                em.op("dve", lambda e: e.tensor_tensor(out=wz[0][:], in0=rz[0][:], in1=gts[0][:, br:12:3], op=ALU.mult), reads=[rz[1], gts[1]], writes=[wz[1]])
                for r in range(4):
                    if first_branch:
                        em.op("dve", lambda e, r=r: e.tensor_scalar(out=acc[0][:, r * 64:(r + 1) * 64], in0=otm[0][:, r, 0:64], scalar1=wz[0][:, r:r + 1], scalar2=None, op0=ALU.mult),
                              reads=[otm[1], wz[1]], writes=[acc[1]])
                    else:
                        em.op("dve", lambda e, r=r: e.scalar_tensor_tensor(out=acc[0][:, r * 64:(r + 1) * 64], in0=otm[0][:, r, 0:64], scalar=wz[0][:, r:r + 1],
                                                                         in1=acc[0][:, r * 64:(r + 1) * 64], op0=ALU.mult, op1=ALU.add),
                              reads=[otm[1], wz[1], acc[1]], writes=[acc[1]])

            njc = (8 * i + 6) // 128 + 1
            pimp = psum[5]
            for jc in range(njc):
                d = 8 * i - 128 * jc
                extra = []
                if 0 <= d <= 128:
                    extra.append((ident_bf[0][:], ident_bf[1], cmpmask[0][:, d // 8, :], cmpmask[1]))
                eT = attend(kcT, kcT[1], jc, qTn, extra, vc[0][:, jc, :], vc[1], psum[2], jc == 0, jc == njc - 1)
                for r in range(4):
                    em.op("pe", lambda e, r=r, jc=jc, eT=eT: e.matmul(pimp[0][:, r * 128:(r + 1) * 128], lhsT=eT[0][:, r * 128:(r + 1) * 128], rhs=c2s[0][:, jc, :],
                                                                    start=(jc == 0), stop=(jc == njc - 1)), reads=[eT[1], c2s[1]], writes=[pimp[1]], inc=(r == 3))
            finish(psum[2], 0, True)
            if i == 1:
                dbg("cos", cosT); dbg("sin", sinT); dbg("qn", qn); dbg("qr", qr); dbg("qTn", qTn); dbg("kTs", kTs); dbg("kcT", kcT); dbg("vc", vc)
                dbg("otm_cmp", otm); dbg("acc_cmp", acc); dbg("gts", gts); dbg("rz", rz)
            for r in range(4):
                if r == 0:
                    em.op("dve", lambda e: e.tensor_scalar(out=imp[0][:], in0=pimp[0][:, 0:128], scalar1=rz[0][:, 0:1], scalar2=None, op0=ALU.mult),
                          reads=[pimp[1], rz[1]], writes=[imp[1]])
                else:
                    em.op("dve", lambda e, r=r: e.scalar_tensor_tensor(out=imp[0][:], in0=pimp[0][:, r * 128:(r + 1) * 128], scalar=rz[0][:, r:r + 1], in1=imp[0][:],
                                                                       op0=ALU.mult, op1=ALU.add), reads=[pimp[1], rz[1], imp[1]], writes=[imp[1]])
            o0 = 126 - 2 * i
            em.op("dve", lambda e: e.tensor_tensor(out=imp[0][:], in0=imp[0][:], in1=keepb[0][:, o0:o0 + 128], op=ALU.mult), reads=[imp[1], keepb[1]], writes=[imp[1]])
            em.op("dve", lambda e: e.tensor_tensor(out=imp[0][:], in0=imp[0][:], in1=addb[0][:, o0:o0 + 128], op=ALU.add), reads=[imp[1], addb[1]], writes=[imp[1]])
            em.op("dve", lambda e: e.memset(imp[0][:, 0:1], 1000.0), reads=[imp[1]], writes=[imp[1]])
            em.op("dve", lambda e: e.max(out=mx[0][:], in_=imp[0][:]), reads=[imp[1]], writes=[mx[1]])
            em.op("dve", lambda e: e.match_replace(out=rp[0][:], in_to_replace=mx[0][:], in_values=imp[0][:], imm_value=-1e30), reads=[imp[1], mx[1]], writes=[rp[1]])
            em.op("dve", lambda e: e.max(out=mx[0][:], in_=rp[0][:]), reads=[rp[1]], writes=[mx[1]])
            em.op("dve", lambda e: e.tensor_reduce(out=thr[0][:], in_=mx[0][:], axis=AX.X, op=ALU.min), reads=[mx[1]], writes=[thr[1]])
            em.op("dve", lambda e: e.tensor_scalar(out=rp[0][:], in0=imp[0][:], scalar1=thr[0][:, 0:1], scalar2=None, op0=ALU.is_ge), reads=[imp[1], thr[1]], writes=[rp[1]])
            em.op("dve", lambda e: e.tensor_scalar(out=nsel[0][:], in0=rp[0][:], scalar1=-1.0, scalar2=-NEG, op0=ALU.add, op1=ALU.mult), reads=[rp[1]], writes=[nsel[1]])
            em.op("pe", lambda e: e.transpose(out=pst[:, 0, :], in_=nsel[0][:], identity=ident_bf[0][:]), reads=[nsel[1], ident_bf[1]], writes=[pstb])
            em.op("act", lambda e: e.copy(out=nselT4[0][:].rearrange("p (r t) -> p r t", r=4), in_=pst[:, 0:1, :].to_broadcast([128, 4, 128])),
                  reads=[pstb], writes=[nselT4[1]])
            if i == 1:
                dbg("imp", imp); dbg("selmask", rp); dbg("nselT4", nselT4)
            for c in range(i + 1):
                extra = [(ebig[0][:, c * 128:(c + 1) * 128], ebig[1], nselT4[0][:], nselT4[1])]
                if c == i:
                    extra.append((ident_bf[0][:], ident_bf[1], causneg[0][:], causneg[1]))
                attend(kTs, kTs[1], c, qTr, extra, vs[0][:, c, :], vs[1], psum[3], c == 0, c == i)
            finish(psum[3], 1, False)
            if i == 1:
                dbg("otm_sel", otm); dbg("acc_sel", acc)
            cl = max(0, i - 4)
            for c in range(cl, i + 1):
                extra = []
                if c == i:
                    extra.append((ident_bf[0][:], ident_bf[1], causneg[0][:], causneg[1]))
                if c == i - 4:
                    extra.append((ident_bf[0][:], ident_bf[1], anticaus[0][:], anticaus[1]))
                attend(kTw, kTw[1], c, qTr, extra, vw[0][:, c, :], vw[1], psum[4], c == cl, c == i)
            finish(psum[4], 2, False)
            em.dyn_store(y_dram[t0:t0 + 128, YOFF:YOFF + 256], acc[0][:], [acc[1]])


def stage_merge(em, NTOK, x_rows, y_rows, prm, cst, x1_dram, psum, halo_flag=None):
    nc = em.nc
    ident = cst["ident_bf"]
    with ExitStack() as es:
        g_sb = mkT(es, nc, "mg_g", [128, 8]); ld(em, g_sb, prm["attn_g"])
        wg = load_weight_bf16(em, es, "mg_wg", prm["w_gate"], 1024, 3072, scale_sb=g_sb)
        wbr = load_weight_bf16(em, es, "mg_wbr", prm["w_br"], 2048, 1024)
        wo = load_weight_bf16(em, es, "mg_wo", prm["w_out"], 1024, 1024)
        xt = mkT(es, nc, "mg_x", [128, 1024]); yt = mkT(es, nc, "mg_y", [128, 2048]); ytb = mkT(es, nc, "mg_yb", [128, 2048], BF16)
        scr = {"sq": mkT(es, nc, "mg_sq", [128, 1024]), "ss": mkT(es, nc, "mg_ss", [128, 4]), "hb": mkT(es, nc, "mg_hb", [128, 1024], BF16)}
        hT = mkT(es, nc, "mg_hT", [128, 8, 128], BF16)
        yT = mkT(es, nc, "mg_yT", [128, 16, 128], BF16)
        gsb = mkT(es, nc, "mg_gs", [128, 3072])
        mrg = mkT(es, nc, "mg_m", [128, 1024]); tmp = mkT(es, nc, "mg_t", [128, 512]); mrgb = mkT(es, nc, "mg_mb", [128, 1024], BF16)
        mT = mkT(es, nc, "mg_mT", [128, 8, 128], BF16)
        x1 = mkT(es, nc, "mg_x1", [128, 1024])
        pst, pstb = psum[7]
        n_ps = 0
        for ti in range(NTOK // 128):
            t0 = ti * 128
            em.dyn_load(xt[0][:], [xt[1]], x_rows(t0))
            for (dst_fn, src) in y_rows(t0):
                em.dyn_load(dst_fn(yt[0]), [yt[1]], src)
            if ti == 0 and halo_flag is not None:
                em.op("dve", lambda e: e.tensor_scalar(out=xt[0][:], in0=xt[0][:], scalar1=halo_flag[0][:, 0:1], scalar2=None, op0=ALU.mult),
                      reads=[xt[1], halo_flag[1]], writes=[xt[1]])
                em.op("dve", lambda e: e.tensor_scalar(out=yt[0][:], in0=yt[0][:], scalar1=halo_flag[0][:, 0:1], scalar2=None, op0=ALU.mult),
                      reads=[yt[1], halo_flag[1]], writes=[yt[1]])
            rmsnorm_T(em, xt[0], xt[1], hT[0], hT[1], 0, scr, ident, psum[7])
            em.op("act", lambda e: e.copy(out=ytb[0][:], in_=yt[0][:]), reads=[yt[1]], writes=[ytb[1]])
            for half in range(2):
                for kc in range(8):
                    em.op("pe", lambda e, kc=kc, half=half: e.transpose(out=pst[:, kc, :], in_=ytb[0][:, (half * 8 + kc) * 128:(half * 8 + kc + 1) * 128],
                                                                       identity=ident[0][:]), reads=[ytb[1], ident[1]], writes=[pstb], inc=(kc == 7))
                em.op("act", lambda e, half=half: e.copy(out=yT[0][:, half * 8:(half + 1) * 8, :], in_=pst[:, :, :]), reads=[pstb], writes=[yT[1]])
            for cb in range(6):
                ps, psb = psum[n_ps % 6]; n_ps += 1
                for kc in range(8):
                    em.op("pe", lambda e, kc=kc, cb=cb, ps=ps: e.matmul(ps[:, :], lhsT=hT[0][:, kc, :], rhs=wg[0][:, kc, cb * 512:(cb + 1) * 512],
                                                                       start=(kc == 0), stop=(kc == 7)), reads=[hT[1], wg[1]], writes=[psb], inc=(kc == 7))
                em.op("act", lambda e, cb=cb, ps=ps: e.activation(out=gsb[0][:, cb * 512:(cb + 1) * 512], in_=ps[:, :], func=AF.Sigmoid), reads=[psb], writes=[gsb[1]])
            for m, (k0, k1) in enumerate(((0, 4), (4, 8), (8, 16))):
                for nb in range(2):
                    ps, psb = psum[n_ps % 6]; n_ps += 1
                    for kc in range(k0, k1):
                        em.op("pe", lambda e, kc=kc, nb=nb, ps=ps: e.matmul(ps[:, :], lhsT=yT[0][:, kc, :], rhs=wbr[0][:, kc, nb * 512:(nb + 1) * 512],
                                                                           start=(kc == k0), stop=(kc == k1 - 1)), reads=[yT[1], wbr[1]], writes=[psb], inc=(kc == k1 - 1))
                    gsl = gsb[0][:, m * 1024 + nb * 512:m * 1024 + (nb + 1) * 512]
                    if m == 0:
                        em.op("dve", lambda e, nb=nb, ps=ps, gsl=gsl: e.tensor_tensor(out=mrg[0][:, nb * 512:(nb + 1) * 512], in0=ps[:, :], in1=gsl, op=ALU.mult),
                              reads=[psb, gsb[1]], writes=[mrg[1]])
                    else:
                        em.op("dve", lambda e, ps=ps, gsl=gsl: e.tensor_tensor(out=tmp[0][:], in0=ps[:, :], in1=gsl, op=ALU.mult), reads=[psb, gsb[1]], writes=[tmp[1]])
                        em.op("dve", lambda e, nb=nb: e.tensor_tensor(out=mrg[0][:, nb * 512:(nb + 1) * 512], in0=mrg[0][:, nb * 512:(nb + 1) * 512], in1=tmp[0][:], op=ALU.add),
                              reads=[mrg[1], tmp[1]], writes=[mrg[1]])
            em.op("act", lambda e: e.copy(out=mrgb[0][:], in_=mrg[0][:]), reads=[mrg[1]], writes=[mrgb[1]])
            for kc in range(8):
                em.op("pe", lambda e, kc=kc: e.transpose(out=pst[:, kc, :], in_=mrgb[0][:, kc * 128:(kc + 1) * 128], identity=ident[0][:]),
                      reads=[mrgb[1], ident[1]], writes=[pstb], inc=(kc == 7))
            em.op("act", lambda e: e.copy(out=mT[0][:], in_=pst[:, :, :]), reads=[pstb], writes=[mT[1]])
            for nb in range(2):
                ps, psb = psum[n_ps % 6]; n_ps += 1
                for kc in range(8):
                    em.op("pe", lambda e, kc=kc, nb=nb, ps=ps: e.matmul(ps[:, :], lhsT=mT[0][:, kc, :], rhs=wo[0][:, kc, nb * 512:(nb + 1) * 512],
                                                                       start=(kc == 0), stop=(kc == 7)), reads=[mT[1], wo[1]], writes=[psb], inc=(kc == 7))
                em.op("dve", lambda e, nb=nb, ps=ps: e.tensor_tensor(out=x1[0][:, nb * 512:(nb + 1) * 512], in0=ps[:, :], in1=xt[0][:, nb * 512:(nb + 1) * 512], op=ALU.add),
                      reads=[psb, xt[1]], writes=[x1[1]])
            em.dma("sp", x1_dram[t0:t0 + 128, :], x1[0][:], reads=[x1[1]])


def stage_ffn(em, NTOK, x1_dram, prm, cst, out_rows, psum, FF=2816):
    nc = em.nc
    ident = cst["ident_bf"]
    NCB = FF // 128
    with ExitStack() as es:
        g_sb = mkT(es, nc, "ff_g", [128, 8]); ld(em, g_sb, prm["ffn_g"])
        wup = load_weight_bf16(em, es, "ff_wup", prm["w_up"], 1024, 2 * FF, scale_sb=g_sb)
        wdn = load_weight_bf16(em, es, "ff_wdn", prm["w_down"], FF, 1024)
        fcw = mkT(es, nc, "ff_cw", [128, 2 * NCB, 3]); ld(em, fcw, prm["ffn_cw"])
        fcb = mkT(es, nc, "ff_cb", [128, 2 * NCB]); ld(em, fcb, prm["ffn_cb"])
        xts = [mkT(es, nc, "ff_x%d" % k, [128, 1024]) for k in range(2)]
        scr = {"sq": mkT(es, nc, "ff_sq", [128, 1024]), "ss": mkT(es, nc, "ff_ss", [128, 4]), "hb": mkT(es, nc, "ff_hb", [128, 1024], BF16)}
        hT = mkT(es, nc, "ff_hT", [128, 8, 256], BF16)
        ucar = mkT(es, nc, "ff_car", [128, 2 * NCB, 2])
        em.op("dve", lambda e: e.memset(ucar[0][:], 0.0), writes=[ucar[1]])
        ubuf = [mkT(es, nc, "ff_ub%d" % k, [128, 258]) for k in range(2)]
        acc = [mkT(es, nc, "ff_acc%d" % k, [128, 256]) for k in range(2)]
        gact = mkT(es, nc, "ff_ga", [128, 256])
        actT = mkT(es, nc, "ff_aT", [128, NCB, 256], BF16)
        xo = mkT(es, nc, "ff_xo", [128, 1024])
        n_ps = 0
        segs = [(0, 128, True)] + [(128 + k * 256, 256, False) for k in range((NTOK - 128) // 256)]
        for (s0, ntok, halo) in segs:
            nt = ntok // 128
            for k in range(nt):
                em.dma("sp", xts[k][0][:], x1_dram[s0 + k * 128:s0 + (k + 1) * 128, :], writes=[xts[k][1]])
                rmsnorm_T(em, xts[k][0], xts[k][1], hT[0], hT[1], k * 128, scr, ident, psum[7])
            for cb in range(NCB):
                for gi, blk in enumerate((cb, cb + NCB)):
                    ps, psb = psum[n_ps % 4]; n_ps += 1
                    ub = ubuf[gi]; ac = acc[gi]
                    for kc in range(8):
                        em.op("pe", lambda e, kc=kc, blk=blk, ps=ps: e.matmul(ps[:, 0:ntok], lhsT=wup[0][:, kc, blk * 128:(blk + 1) * 128], rhs=hT[0][:, kc, 0:ntok],
                                                                             start=(kc == 0), stop=(kc == 7)), reads=[hT[1], wup[1]], writes=[psb], inc=(kc == 7))
                    em.op("act", lambda e, blk=blk, ub=ub: e.copy(out=ub[0][:, 0:2], in_=ucar[0][:, blk, :]), reads=[ucar[1]], writes=[ub[1]])
                    em.op("act", lambda e, ps=ps, ub=ub: e.copy(out=ub[0][:, 2:2 + ntok], in_=ps[:, 0:ntok]), reads=[psb], writes=[ub[1]])
                    em.op("act", lambda e, blk=blk, ub=ub: e.copy(out=ucar[0][:, blk, :], in_=ub[0][:, ntok:ntok + 2]), reads=[ub[1]], writes=[ucar[1]])
                    if halo:
                        continue
                    em.op("dve", lambda e, blk=blk, ub=ub, ac=ac: e.tensor_scalar(out=ac[0][:, 0:ntok], in0=ub[0][:, 0:ntok], scalar1=fcw[0][:, blk, 0:1], scalar2=None, op0=ALU.mult),
                          reads=[ub[1], fcw[1]], writes=[ac[1]])
                    for k in (1, 2):
                        em.op("dve", lambda e, blk=blk, ub=ub, ac=ac, k=k: e.scalar_tensor_tensor(out=ac[0][:, 0:ntok], in0=ub[0][:, k:k + ntok], scalar=fcw[0][:, blk, k:k + 1],
                                                                                               in1=ac[0][:, 0:ntok], op0=ALU.mult, op1=ALU.add),
                              reads=[ub[1], fcw[1], ac[1]], writes=[ac[1]])
                    if gi == 0:
                        em.op("act", lambda e, blk=blk, ac=ac: e.activation(out=gact[0][:, 0:ntok], in_=ac[0][:, 0:ntok], func=AF.Silu, bias=fcb[0][:, blk:blk + 1]),
                              reads=[ac[1], fcb[1]], writes=[gact[1]])
                    else:
                        em.op("dve", lambda e, blk=blk, ac=ac, cb=cb: e.scalar_tensor_tensor(out=actT[0][:, cb, 0:ntok], in0=ac[0][:, 0:ntok], scalar=fcb[0][:, blk:blk + 1],
                                                                                            in1=gact[0][:, 0:ntok], op0=ALU.add, op1=ALU.mult),
                              reads=[ac[1], fcb[1], gact[1]], writes=[actT[1]])
            if halo:
                continue
            for k in range(nt):
                for nb in range(2):
                    ps, psb = psum[4 + (n_ps % 2)]; n_ps += 1
                    for cb in range(NCB):
                        em.op("pe", lambda e, cb=cb, nb=nb, ps=ps, k=k: e.matmul(ps[:, :], lhsT=actT[0][:, cb, k * 128:(k + 1) * 128], rhs=wdn[0][:, cb, nb * 512:(nb + 1) * 512],
                                                                                start=(cb == 0), stop=(cb == NCB - 1)), reads=[actT[1], wdn[1]], writes=[psb], inc=(cb == NCB - 1))
                    em.op("dve", lambda e, nb=nb, ps=ps, k=k: e.tensor_tensor(out=xo[0][:, nb * 512:(nb + 1) * 512], in0=ps[:, :], in1=xts[k][0][:, nb * 512:(nb + 1) * 512], op=ALU.add),
                          reads=[psb, xts[k][1]], writes=[xo[1]])
                o0 = s0 - 128 + k * 128
                em.dyn_store(out_rows(o0), xo[0][:], [xo[1]])

import ml_dtypes

T_SEQ = 8192
NT_TM = 1556
NC_CM = 1408
OFF = dict(q=0, kv=512, gate=1280, hg_q=1304, hg_f=1816, hg_i=2328, hg_g=2840, z=3352, xbc=4376, dt=5912, merge=5928)


def group_cols(g):
    ar = np.arange
    kv = lambda br, kvi: OFF["kv"] + ((br * 2 + kvi) * 2 + g) * 64 + ar(64)
    tm = np.concatenate([
        OFF["q"] + g * 256 + ar(256), kv(1, 0), kv(1, 1), kv(2, 0), kv(2, 1),
        OFF["gate"] + g * 12 + ar(12), OFF["hg_i"] + g * 256 + ar(256), OFF["hg_g"] + g * 256 + ar(256),
        OFF["z"] + g * 512 + ar(512), OFF["dt"] + g * 8 + ar(8)])
    cm = np.concatenate([
        kv(0, 0), kv(0, 1), OFF["hg_q"] + g * 256 + ar(256), OFF["hg_f"] + g * 256 + ar(256),
        OFF["xbc"] + g * 512 + ar(512), OFF["xbc"] + 1024 + g * 128 + ar(128), OFF["xbc"] + 1280 + g * 128 + ar(128)])
    assert tm.size == NT_TM and cm.size == NC_CM
    return tm, cm


_CONST_CACHE = {}


def consts():
    if _CONST_CACHE:
        return _CONST_CACHE
    bf = ml_dtypes.bfloat16
    f32 = np.float32
    i = np.arange(128)
    c = {}
    c["ident_bf"] = np.eye(128).astype(bf)
    c["ident_f"] = np.eye(128).astype(f32)
    c["triu_f"] = (i[:, None] <= i[None, :]).astype(f32)
    c["ones_f"] = np.ones((128, 128), f32)
    c["mstrict_f"] = (i[:, None] > i[None, :]).astype(f32)
    c["mcausT_f"] = (i[None, :] >= i[:, None]).astype(f32)
    c["mask2_f"] = ((i[:, None] // 64 == i[None, :] // 64) & (i[:, None] <= i[None, :])).astype(f32)
    theta = np.float32(500000.0)
    invf = (theta ** (-np.arange(0, 16, 2, dtype=np.float32) / np.float32(16))).astype(f32)
    c["c_invf"] = np.broadcast_to(invf[None, :], (128, 8)).copy()
    NEG_ = -30000.0
    ds = list(range(0, 128, 8)) + [128]
    cm = np.zeros((128, 17, 4, 128), f32)
    for k, d in enumerate(ds):
        vis = (16 * (i[:, None] - d) + 31) <= i[None, :]
        cm[:, k, :, :] = np.where(vis, 0.0, NEG_)[:, None, :]
    c["c_cmpmask"] = cm.reshape(128, 17, 512).astype(bf)
    n_cmp = (T_SEQ - 32) // 16 + 1
    cs = np.arange(n_cmp) * 16
    s_start = np.arange(128) * 64
    ov = np.clip(np.minimum(cs[:, None] + 32, s_start[None, :] + 64) - np.maximum(cs[:, None], s_start[None, :]), 0, None) / 32.0
    c2s = np.zeros((512, 128), f32)
    c2s[:n_cmp] = ov
    c["c_c2s"] = np.ascontiguousarray(c2s.reshape(4, 128, 128).transpose(1, 0, 2)).astype(bf)
    keys = np.arange(T_SEQ)
    c["c_ebig"] = (i[:, None] == (keys[None, :] // 64)).astype(bf)
    caus = np.where(i[:, None] <= i[None, :], 0.0, NEG_)
    c["c_causneg"] = np.tile(caus, (1, 4)).astype(bf)
    anti = np.where(i[:, None] > i[None, :], 0.0, NEG_)
    c["c_anticaus"] = np.tile(anti, (1, 4)).astype(bf)
    u = np.arange(254) - 126
    curp = (i >= 64).astype(np.int64)[:, None]
    forced = (u[None, :] == curp) | (u[None, :] == curp - 1)
    invalid = u[None, :] > curp
    c["c_keep"] = (~forced & ~invalid).astype(f32)
    c["c_add"] = np.where(forced, 200.0 + u[None, :], np.where(invalid, -(300.0 + u[None, :]), 0.0)).astype(f32)
    _CONST_CACHE.update(c)
    return c


CONST_SB = ["ident_bf", "ident_f", "triu_f", "ones_f", "mstrict_f", "mcausT_f", "mask2_f"]
CONST_NSA = ["c_invf", "c_cmpmask", "c_c2s", "c_ebig", "c_causneg", "c_anticaus", "c_keep", "c_add"]
PRM_B = {
    "ns_gq": ([128, 256], F32), "ns_gk": ([128, 2, 64], F32), "ns_gk0": ([64, 1], F32), "ns_posT": ([128, 32], F32),
    "ns_w1kv": ([128, 32, 256], F32), "ns_w2k": ([128, 2, 64], F32), "ns_w2v": ([128, 2, 64], F32), "ns_pos": ([128, 64], I32),
    "hg_lbl": ([128, 2, 2], F32), "hg_lflag": ([128, 2], F32), "hg_ng": ([128, 128], F32),
    "m2_cw": ([128, 6, 4], F32), "m2_cb": ([128, 6], F32), "m2_dtb": ([128, 8], F32), "m2_alog": ([128, 8], F32),
    "m2_dsk": ([128, 8], F32), "m2_ng": ([128, 512], F32),
}
CONST_SHAPES = {
    "ident_bf": ([128, 128], BF16), "ident_f": ([128, 128], F32), "triu_f": ([128, 128], F32), "ones_f": ([128, 128], F32),
    "mstrict_f": ([128, 128], F32), "mcausT_f": ([128, 128], F32), "mask2_f": ([128, 128], F32),
    "c_invf": ([128, 8], F32), "c_cmpmask": ([128, 17, 512], BF16), "c_c2s": ([128, 4, 128], BF16), "c_ebig": ([128, T_SEQ], BF16),
    "c_causneg": ([128, 512], BF16), "c_anticaus": ([128, 512], BF16), "c_keep": ([128, 254], F32), "c_add": ([128, 254], F32),
}


def bcast(v, rows=128):
    v = np.asarray(v, np.float32).reshape(1, -1)
    return np.ascontiguousarray(np.broadcast_to(v, (rows, v.shape[1])))


def prep_B(inp, l, b, g, x_b):
    tm, cm = group_cols(g)
    w_in = inp["w_in"][l]
    m = {"x": (None if x_b is None else np.ascontiguousarray(x_b)), "g_attn": np.ascontiguousarray(inp["attn_norm_g"][l].reshape(8, 128).T),
         "w_tm": np.ascontiguousarray(w_in[:, tm]), "w_cm": np.ascontiguousarray(w_in[:, cm])}
    m.update(consts())
    m["ns_gq"] = bcast(np.tile(inp["nsa_q_norm_g"][l], 4))
    m["ns_gk"] = np.ascontiguousarray(np.broadcast_to(inp["nsa_k_norm_g"][l][1:3][None], (128, 2, 64))).astype(np.float32)
    m["ns_gk0"] = np.ascontiguousarray(inp["nsa_k_norm_g"][l][0].reshape(64, 1))
    m["ns_posT"] = np.ascontiguousarray(np.concatenate([inp["nsa_cmp_pos_k"][l].T, inp["nsa_cmp_pos_v"][l].T], 0))
    w1k = inp["nsa_cmp_k_w1"][l].reshape(32, 64, 256).transpose(1, 0, 2)
    w1v = inp["nsa_cmp_v_w1"][l].reshape(32, 64, 256).transpose(1, 0, 2)
    m["ns_w1kv"] = np.ascontiguousarray(np.concatenate([w1k, w1v], 0))
    m["ns_w2k"] = np.ascontiguousarray(inp["nsa_cmp_k_w2"][l].reshape(2, 128, 64).transpose(1, 0, 2))
    m["ns_w2v"] = np.ascontiguousarray(inp["nsa_cmp_v_w2"][l].reshape(2, 128, 64).transpose(1, 0, 2))
    m["ns_pos"] = np.ascontiguousarray(inp["positions"][b].reshape(64, 128).T.astype(np.int32))
    lbl = inp["hgrn_lb_logits"][:, g * 256:(g + 1) * 256].reshape(2, 2, 128)
    m["hg_lbl"] = np.ascontiguousarray(lbl.transpose(2, 1, 0))
    m["hg_ng"] = bcast(inp["hgrn_norm_g"][l])
    m["hg_lflag"] = np.full((128, 2), float(l), np.float32)
    chs = np.concatenate([g * 512 + np.arange(512), 1024 + g * 128 + np.arange(128), 1280 + g * 128 + np.arange(128)])
    m["m2_cw"] = np.ascontiguousarray(inp["m2_conv_w"][l][:, chs].reshape(4, 6, 128).transpose(2, 1, 0))
    m["m2_cb"] = np.ascontiguousarray(inp["m2_conv_b"][l][chs].reshape(6, 128).T)
    m["m2_dtb"] = bcast(inp["m2_dt_bias"][l][g * 8:(g + 1) * 8])
    m["m2_alog"] = bcast(inp["m2_a_log"][l][g * 8:(g + 1) * 8])
    m["m2_dsk"] = bcast(inp["m2_d_skip"][l][g * 8:(g + 1) * 8])
    m["m2_ng"] = bcast(inp["m2_norm_g"][l][g * 512:(g + 1) * 512])
    return m


def build_B(layer, stages=("proj", "m2", "hg", "nsa"), T=T_SEQ, debug_out=False):
    nc = bass.Bass("TRN2", target_bir_lowering=False)
    D = lambda name, shape, dt=F32, kind="ExternalInput": nc.dram_tensor(name, shape, dt, kind=kind).ap()
    x = D("x", [T, 1024]); g_attn = D("g_attn", [128, 8]); w_tm_d = D("w_tm", [1024, NT_TM]); w_cm_d = D("w_cm", [1024, NC_CM])
    cd = {k: D(k, *CONST_SHAPES[k]) for k in CONST_SHAPES}
    prm = {k: D(k, *PRM_B[k]) for k in PRM_B}
    prm.update({k: cd[k] for k in CONST_NSA})
    y = D("y", [T, 1024], kind="ExternalOutput")
    if debug_out:
        p_tm = D("p_tm", [T, NT_TM], kind="ExternalOutput"); p_cm = D("p_cm", [NC_CM, T], kind="ExternalOutput")
    else:
        p_tm = D("p_tm", [T, NT_TM], kind="Internal"); p_cm = D("p_cm", [NC_CM, T], kind="Internal")
    with ExitStack() as es:
        em = Em(nc, es)
        psum = [(es.enter_context(nc.psum_tensor("ps%d" % i, [128, 512], F32)), Buf()) for i in range(7)]
        psum.append((es.enter_context(nc.psum_tensor("pst", [128, 8, 128], BF16)), Buf()))
        cst = {}
        for k in CONST_SB:
            cst[k] = mkT(es, nc, "k_" + k, *CONST_SHAPES[k])
            ld(em, cst[k], cd[k])
        em.dyn_init(es)
        if "proj" in stages:
            with ExitStack() as es2:
                g_sb = mkT(es2, nc, "g_sb", [128, 8]); ld(em, g_sb, g_attn)
                w_tm = load_weight_bf16(em, es2, "w_tm_sb", w_tm_d, 1024, NT_TM, scale_sb=g_sb)
                w_cm = load_weight_bf16(em, es2, "w_cm_sb", w_cm_d, 1024, NC_CM, scale_sb=g_sb)
                stage_proj(em, T, x, g_sb, w_tm, w_cm, NT_TM, NC_CM, p_tm, p_cm, cst["ident_bf"], psum)
            em.drain()
        if "m2" in stages:
            stage_mamba(em, T, p_tm, p_cm, prm, cst, y, psum)
            em.drain()
        if "hg" in stages:
            stage_hgrn(em, T, layer, p_tm, p_cm, prm, cst, y, psum)
            em.drain()
        if "nsa" in stages:
            stage_nsa(em, T, p_tm, p_cm, prm, cst, y, psum)
        em.drain()
        print("build_B n_inst", em.n_inst, flush=True)
    return nc


PRM_C = {
    "attn_g": ([128, 8], F32), "w_gate": ([1024, 3072], F32), "w_br": ([2048, 1024], F32), "w_out": ([1024, 1024], F32),
    "ffn_g": ([128, 8], F32), "w_up": ([1024, 5632], F32), "w_down": ([2816, 1024], F32), "ffn_cw": ([128, 44, 3], F32), "ffn_cb": ([128, 44], F32),
}
NTOK_C = 4096 + 128


def prep_C(inp, l, xh, yh):
    m = {"xh": xh, "yh": yh}
    m["ident_bf"] = consts()["ident_bf"]
    m["attn_g"] = np.ascontiguousarray(inp["attn_norm_g"][l].reshape(8, 128).T)
    m["w_gate"] = np.ascontiguousarray(inp["w_in"][l][:, OFF["merge"]:OFF["merge"] + 3072])
    m["w_br"] = np.ascontiguousarray(np.concatenate([inp["w_branch_nsa"][l], inp["w_branch_hgrn"][l], inp["w_branch_m2"][l]], 0))
    m["w_out"] = np.ascontiguousarray(inp["w_out"][l])
    m["ffn_g"] = np.ascontiguousarray(inp["ffn_norm_g"][l].reshape(8, 128).T)
    m["w_up"] = np.ascontiguousarray(inp["ffn_w_up"][l])
    m["w_down"] = np.ascontiguousarray(inp["ffn_w_down"][l])
    m["ffn_cw"] = np.ascontiguousarray(inp["ffn_conv_w"][l].reshape(3, 44, 128).transpose(2, 1, 0))
    m["ffn_cb"] = np.ascontiguousarray(inp["ffn_conv_b"][l].reshape(44, 128).T)
    return m


def build_C(NTOK=NTOK_C):
    nc = bass.Bass("TRN2", target_bir_lowering=False)
    D = lambda name, shape, dt=F32, kind="ExternalInput": nc.dram_tensor(name, shape, dt, kind=kind).ap()
    xh = D("xh", [NTOK, 1024]); yh = D("yh", [NTOK, 2048])
    idd = D("ident_bf", [128, 128], BF16)
    prm = {k: D(k, *PRM_C[k]) for k in PRM_C}
    out = D("out", [NTOK - 128, 1024], kind="ExternalOutput")
    x1d = D("x1d", [NTOK, 1024], kind="Internal")
    with ExitStack() as es:
        em = Em(nc, es)
        psum = [(es.enter_context(nc.psum_tensor("ps%d" % i, [128, 512], F32)), Buf()) for i in range(7)]
        psum.append((es.enter_context(nc.psum_tensor("pst", [128, 8, 128], BF16)), Buf()))
        cst = {"ident_bf": mkT(es, nc, "k_ident", [128, 128], BF16)}
        ld(em, cst["ident_bf"], idd)
        em.dyn_init(es)
        stage_merge(em, NTOK, lambda t0: xh[t0:t0 + 128, :], lambda t0: [(lambda yt: yt[:, 0:2048], yh[t0:t0 + 128, :])], prm, cst, x1d, psum)
        em.drain()
        stage_ffn(em, NTOK, x1d, prm, cst, lambda o0: out[o0:o0 + 128, :], psum)
        em.drain()
    return nc


def prep_fused(inp, b, s):
    m = {"x": np.ascontiguousarray(inp["x"][b]), "halo_flag": np.full((128, 1), float(s), np.float32)}
    m.update(consts())
    for l in range(2):
        mb = prep_B(inp, l, b, s, None)
        mc = prep_C(inp, l, None, None)
        for k, v in list(mb.items()) + list(mc.items()):
            if k in CONST_SHAPES or k in ("x", "xh", "yh") or v is None:
                continue
            m["%s_l%d" % (k, l)] = v
    return m


class YDst:
    def __init__(self, fams, halos, par, ntile_half):
        self.fams = fams; self.halos = halos; self.par = par; self.nth = ntile_half

    def __getitem__(self, key):
        rows, cols = key
        i = rows.start // 128
        h, ti = i // self.nth, i % self.nth + 1
        c0, w = cols.start, cols.stop - cols.start
        if c0 < 256:
            fam, coff = "n", c0
        elif c0 < 512:
            fam, coff = "h", c0 - 256
        else:
            fam, coff = "m", c0 - 512
        out = [self.fams[fam][ti][h, bass.ds(self.par, 1)][0, :, coff:coff + w]]
        if i == self.nth - 1:
            out.append(self.halos[fam][bass.ds(self.par, 1)][0, :, coff:coff + w])
        return out


def build_fused(T=T_SEQ, n_layers=2):
    nc = bass.Bass("TRN2", target_bir_lowering=False, num_devices=8)
    D = lambda name, shape, dt=F32, kind="ExternalInput", **kw: nc.dram_tensor(name, shape, dt, kind=kind, **kw).ap()
    SH = lambda name, shape: D(name, shape, kind="Internal", addr_space="Shared")
    HALF = T // 2
    NTH = HALF // 128
    x_ext = D("x", [T, 1024])
    hflag_d = D("halo_flag", [128, 1])
    cd = {k: D(k, *CONST_SHAPES[k]) for k in CONST_SHAPES}
    out_ext = D("out", [HALF, 1024], kind="ExternalOutput")
    XT = [[None] + [SH("xt%d_%d" % (bf, ti), [2, 128, 1024]) for ti in range(1, NTH + 1)] for bf in range(2)]
    XH = [SH("xh%d" % bf, [2, 128, 1024]) for bf in range(2)]
    WID = {"m": 512, "h": 256, "n": 256}
    FAM = {f: [None] + [SH("y%s_%d" % (f, ti), [2, 2, 128, WID[f]]) for ti in range(1, NTH + 1)] for f in WID}
    FAMH = {f: SH("y%s_halo" % f, [2, 128, WID[f]]) for f in WID}
    p_tm = D("p_tm", [T, NT_TM], kind="Internal"); p_cm = D("p_cm", [NC_CM, T], kind="Internal")
    x1d = D("x1d", [HALF + 128, 1024], kind="Internal")
    LP = []
    for l in range(n_layers):
        d = {"g_attn": D("g_attn_l%d" % l, [128, 8]), "w_tm": D("w_tm_l%d" % l, [1024, NT_TM]), "w_cm": D("w_cm_l%d" % l, [1024, NC_CM])}
        d.update({k: D("%s_l%d" % (k, l), *PRM_B[k]) for k in PRM_B})
        d.update({k: D("%s_l%d" % (k, l), *PRM_C[k]) for k in PRM_C})
        d.update({k: cd[k] for k in CONST_NSA})
        LP.append(d)
    with ExitStack() as es:
        em = Em(nc, es)
        psum = [(es.enter_context(nc.psum_tensor("ps%d" % i, [128, 512], F32)), Buf()) for i in range(7)]
        psum.append((es.enter_context(nc.psum_tensor("pst", [128, 8, 128], BF16)), Buf()))
        cst = {}
        for k in CONST_SB:
            cst[k] = mkT(es, nc, "k_" + k, *CONST_SHAPES[k])
            ld(em, cst[k], cd[k])
        hflag = mkT(es, nc, "hflag", [128, 1]); ld(em, hflag, hflag_d)
        for i in range(T // 128):
            h, ti = i // NTH, i % NTH + 1
            em.dma("sp", XT[0][ti][h, :, :], x_ext[i * 128:(i + 1) * 128, :])
        em.dma("sp", XH[0][0, :, :], x_ext[HALF - 128:HALF, :])
        em.drain()
        nc.all_core_barrier()
        par = nc.sync.partition_id() % 2

        def x_tile_static(bf):
            def f(t0):
                i = t0 // 128
                return XT[bf][i % NTH + 1][i // NTH, :, :]
            return f

        for l in range(n_layers):
            prm = LP[l]
            bf, bn = l % 2, (l + 1) % 2
            y_dst = YDst(FAM, FAMH, par, NTH)
            with ExitStack() as es2:
                g_sb = mkT(es2, nc, "g_sb%d" % l, [128, 8]); ld(em, g_sb, prm["g_attn"])
                w_tm = load_weight_bf16(em, es2, "w_tm_sb%d" % l, prm["w_tm"], 1024, NT_TM, scale_sb=g_sb)
                w_cm = load_weight_bf16(em, es2, "w_cm_sb%d" % l, prm["w_cm"], 1024, NC_CM, scale_sb=g_sb)
                stage_proj(em, T, x_tile_static(bf), g_sb, w_tm, w_cm, NT_TM, NC_CM, p_tm, p_cm, cst["ident_bf"], psum)
                em.drain()
            stage_mamba(em, T, p_tm, p_cm, prm, cst, y_dst, psum)
            em.drain()
            stage_hgrn(em, T, l, p_tm, p_cm, prm, cst, y_dst, psum)
            em.drain()
            stage_nsa(em, T, p_tm, p_cm, prm, cst, y_dst, psum)
            em.drain()
            nc.all_core_barrier()

            def x_rows(t0, bf=bf):
                ti = t0 // 128
                if ti == 0:
                    return XH[bf][0, :, :]
                return XT[bf][ti][bass.ds(par, 1)][0, :, :]

            def y_rows(t0):
                ti = t0 // 128
                res = []
                for f, c0 in (("n", 0), ("h", 512), ("m", 1024)):
                    w = WID[f]
                    dst_fn = (lambda yt, c0=c0, w=w: yt[:, c0:c0 + 2 * w].rearrange("p (g c) -> p g c", g=2))
                    if ti == 0:
                        src = FAMH[f].rearrange("g r c -> r g c")
                    else:
                        src = FAM[f][ti][bass.ds(par, 1)][0].rearrange("g r c -> r g c")
                    res.append((dst_fn, src))
                return res
            stage_merge(em, HALF + 128, x_rows, y_rows, prm, cst, x1d, psum, halo_flag=hflag)
            em.drain()
            if l == n_layers - 1:
                out_rows = lambda o0: out_ext[o0:o0 + 128, :]
            else:
                def out_rows(o0, bn=bn):
                    ti = o0 // 128 + 1
                    r = [XT[bn][ti][bass.ds(par, 1)][0, :, :]]
                    if ti == NTH:
                        r.append(XH[bn][bass.ds(par, 1)][0, :, :])
                    return r
            stage_ffn(em, HALF + 128, x1d, prm, cst, out_rows, psum)
            em.drain()
            if l != n_layers - 1:
                nc.all_core_barrier()
        print("build_fused n_inst", em.n_inst, flush=True)
    return nc


_NC_CACHE = {}


def kernel(**inputs):
    inp = {k: np.asarray(v) for k, v in inputs.items()}
    B, T, Dm = inp["x"].shape
    if "F" not in _NC_CACHE:
        _NC_CACHE["F"] = build_fused()
    maps = [prep_fused(inp, b, s) for b in range(B) for s in range(2)]
    res = run_bass_kernel_spmd(_NC_CACHE["F"], maps, core_ids=list(range(8)))
    out = np.empty((B, T, Dm), np.float32)
    for b in range(B):
        for s in range(2):
            out[b, s * (T // 2):(s + 1) * (T // 2)] = np.asarray(res.results[2 * b + s]["out"])
    return out
```

```python
import numpy as np
import concourse.bass as bass
import concourse.mybir as mybir
from concourse.bass_utils import run_bass_kernel_spmd
from contextlib import ExitStack

F32 = mybir.dt.float32
BF16 = mybir.dt.bfloat16
I32 = mybir.dt.int32
AF = mybir.ActivationFunctionType
ALU = mybir.AluOpType
AX = mybir.AxisListType

SAME_ENG_SYNC = True
STORE_Q = "sp"


_UQ = [0]


def SBT(nc, name, shape, dt):
    _UQ[0] += 1
    return nc.sbuf_tensor("%s_u%d" % (name, _UQ[0]), shape, dt)


class Buf:
    __slots__ = ("name", "w", "r")

    def __init__(self, name=""):
        self.name = name
        self.w = None
        self.r = {}


class Em:
    def __init__(self, nc, es, n_dma_sems=12):
        self.nc = nc
        self.es = es
        self.eng = dict(pe=nc.tensor, act=nc.scalar, dve=nc.vector, pool=nc.gpsimd, sp=nc.sync)
        self.semh = {}
        self.cnt = {}
        self.seen = {k: {} for k in self.eng}
        for k in ("pe", "act", "dve", "pool"):
            self.semh[k] = es.enter_context(nc.semaphore("c_" + k))
            self.cnt[k] = 0
        self.dring = {}
        for q in ("sp", "pool", "act"):
            ring = []
            for i in range({"sp": 16, "pool": 48, "act": 2}[q]):
                key = "d_%s%d" % (q, i)
                self.semh[key] = es.enter_context(nc.semaphore(key))
                ring.append(key)
            self.dring[q] = [ring, 0, {k: 0 for k in ring}]
        self.n_inst = 0

    def _wait(self, eng, toks):
        E = self.eng[eng]
        seen = self.seen[eng]
        for (k, v) in toks:
            if k == eng:
                if eng == "pe" or not SAME_ENG_SYNC:
                    continue
                if v > self.cnt[eng]:
                    continue
            if seen.get(k, 0) < v:
                E.wait_ge(self.semh[k], v)
                seen[k] = v
                self.n_inst += 1

    def _deps(self, reads, writes):
        toks = []
        for b in reads:
            if b.w is not None:
                toks.append(b.w)
        for b in writes:
            if b.w is not None:
                toks.append(b.w)
            toks.extend(b.r.items())
        return toks

    def op(self, eng, fn, reads=(), writes=(), inc=True):
        self._wait(eng, self._deps(reads, writes))
        ins = fn(self.eng[eng])
        self.n_inst += 1
        if inc:
            self.cnt[eng] += 1
            ins.then_inc(self.semh[eng], 1)
            tok = (eng, self.cnt[eng])
        else:
            tok = (eng, self.cnt[eng] + 1)
        for b in reads:
            if b.r.get(tok[0], 0) < tok[1]:
                b.r[tok[0]] = tok[1]
        for b in writes:
            b.w = tok
            b.r = {}
        return ins

    def dma(self, q, out, in_, reads=(), writes=(), **kw):
        ring, idx, tgt = self.dring[q]
        key = ring[idx]
        if q == "pool" and tgt[key] > 0 and not any(self.seen[e].get(key, 0) >= tgt[key] for e in self.seen):
            return self.dma("sp", out, in_, reads=reads, writes=writes, **kw)
        self.dring[q][1] = (idx + 1) % len(ring)
        toks = self._deps(reads, writes)
        if tgt[key] > 0 and q != "pool":
            toks.append((key, tgt[key]))
        self._wait(q, toks)
        ins = self.eng[q].dma_start(out=out, in_=in_, **kw)
        self.n_inst += 1
        tgt[key] += 16
        ins.then_inc(self.semh[key], 16)
        tok = (key, tgt[key])
        for b in reads:
            b.r[key] = tok[1]
        for b in writes:
            b.w = tok
            b.r = {}
        return ins

    def drain(self, engs=("sp", "pool", "act", "pe", "dve")):
        toks = [(k, self.cnt[k]) for k in ("pe", "act", "dve", "pool") if self.cnt[k] > 0]
        for q in self.dring:
            for key, t in self.dring[q][2].items():
                if t > 0:
                    toks.append((key, t))
        for e in engs:
            self._wait(e, toks)


    def dyn_init(self, es, width=1024, nslots=4):
        pass

    def dyn_store(self, dram_aps, src_ap, src_bufs):
        if not isinstance(dram_aps, (list, tuple)):
            dram_aps = [dram_aps]
        for d in dram_aps:
            self.dma(STORE_Q, d, src_ap, reads=list(src_bufs))

    def dyn_load(self, dst_ap, dst_bufs, dram_ap):
        self.dma("sp", dst_ap, dram_ap, writes=list(dst_bufs))

STOP = ''

def load_weight_bf16(em, es, name, w_dram, K, N, scale_sb=None, stage_cols=1024, stage=None):
    nc = em.nc
    KC = K // 128
    w_sb = es.enter_context(SBT(nc, name, [128, KC, N], BF16))
    wb = Buf(name)
    if stage is None:
        stg = [es.enter_context(SBT(nc, name + "_stg%d" % i, [128, stage_cols], F32)) for i in range(2)]
        stgb = [Buf(name + "_stg%d" % i) for i in range(2)]
    else:
        stg = [t[0] for t in stage]
        stgb = [t[1] for t in stage]
    i = 0
    for kc in range(KC):
        for c0 in range(0, N, stage_cols):
            c1 = min(N, c0 + stage_cols)
            s, sb = stg[i % 2], stgb[i % 2]
            em.dma("sp", s[:, : c1 - c0], w_dram[kc * 128:(kc + 1) * 128, c0:c1], writes=[sb])
            if scale_sb is not None:
                sc = scale_sb[0][:, kc:kc + 1]
                em.op("pool", lambda e, s=s, sc=sc, c0=c0, c1=c1, kc=kc: e.tensor_scalar(
                    out=w_sb[:, kc, c0:c1], in0=s[:, : c1 - c0], scalar1=sc, scalar2=None, op0=ALU.mult),
                    reads=[sb, scale_sb[1]], writes=[wb])
            else:
                em.op("pool", lambda e, s=s, c0=c0, c1=c1, kc=kc: e.tensor_copy(
                    out=w_sb[:, kc, c0:c1], in_=s[:, : c1 - c0]), reads=[sb], writes=[wb])
            i += 1
    return w_sb, wb


def rmsnorm_T(em, x_tile, xb, hT, hTb, col0, scr, ident, ps_t, eps=1e-6, D=1024):
    KC = D // 128
    sq, sqb = scr["sq"]
    ss, ssb = scr["ss"]
    hb, hbb = scr["hb"]
    em.op("act", lambda e: e.activation(out=sq[:], in_=x_tile[:], func=AF.Square, accum_out=ss[:, 0:1]),
          reads=[xb], writes=[sqb, ssb])
    em.op("dve", lambda e: e.tensor_scalar(out=ss[:, 1:2], in0=ss[:, 0:1], scalar1=1.0 / D, scalar2=eps,
                                           op0=ALU.mult, op1=ALU.add), reads=[ssb], writes=[ssb])
    em.op("act", lambda e: e.sqrt(out=ss[:, 3:4], in_=ss[:, 1:2]), reads=[ssb], writes=[ssb])
    em.op("dve", lambda e: e.reciprocal(out=ss[:, 2:3], in_=ss[:, 3:4]), reads=[ssb], writes=[ssb])
    em.op("dve", lambda e: e.tensor_scalar(out=hb[:], in0=x_tile[:], scalar1=ss[:, 2:3], scalar2=None,
                                           op0=ALU.mult), reads=[xb, ssb], writes=[hbb])
    pt, ptb = ps_t
    for kc in range(KC):
        em.op("pe", lambda e, kc=kc: e.transpose(out=pt[:, kc, :], in_=hb[:, kc * 128:(kc + 1) * 128],
                                                 identity=ident[0][:]),
              reads=[hbb, ident[1]], writes=[ptb], inc=(kc == KC - 1))
    em.op("act", lambda e: e.copy(out=hT[:, :, col0:col0 + 128], in_=pt[:]), reads=[ptb], writes=[hTb])


def stage_proj(em, T, x_dram, g_sb, w_tm, w_cm, NT, NC, out_tm, out_cm, ident, psum):
    nc = em.nc
    KC = 8
    with ExitStack() as es:
        xs = [(es.enter_context(SBT(nc, "pj_x%d" % i, [128, 1024], F32)), Buf()) for i in range(3)]
        scr = {
            "sq": (es.enter_context(SBT(nc, "pj_sq", [128, 1024], F32)), Buf()),
            "ss": (es.enter_context(SBT(nc, "pj_ss", [128, 4], F32)), Buf()),
            "hb": (es.enter_context(SBT(nc, "pj_hb", [128, 1024], BF16)), Buf()),
        }
        hTs = [(es.enter_context(SBT(nc, "pj_hT%d" % i, [128, KC, 512], BF16)), Buf()) for i in range(2)]
        otm = [(es.enter_context(SBT(nc, "pj_otm%d" % i, [128, NT], F32)), Buf()) for i in range(2)]
        ocm = [(es.enter_context(SBT(nc, "pj_ocm%d" % i, [128, 512], F32)), Buf()) for i in range(3)]
        n_x = 0
        n_otm = 0
        n_ocm = 0
        n_ps = 0
        NST = T // 512
        def prep_st(st):
            nonlocal n_x
            hT, hTb = hTs[st % 2]
            for sub in range(4):
                t0 = st * 512 + sub * 128
                x_t, x_b = xs[n_x % 3]
                n_x += 1
                em.dma("sp", x_t[:], (x_dram(t0) if callable(x_dram) else x_dram[t0:t0 + 128, :]), writes=[x_b])
                rmsnorm_T(em, x_t, x_b, hT, hTb, sub * 128, scr, ident, psum[7])

        prep_st(0)
        for st in range(NST):
            hT, hTb = hTs[st % 2]
            if st + 1 < NST:
                prep_st(st + 1)
            if STOP == "n":
                continue
            for sub in range(4):
                t0 = st * 512 + sub * 128
                o_t, o_b = otm[n_otm % 2]
                n_otm += 1
                for c0 in range(0, NT, 512):
                    c1 = min(NT, c0 + 512)
                    ps, psb = psum[n_ps % 6]
                    n_ps += 1
                    for kc in range(KC):
                        em.op("pe", lambda e, kc=kc, ps=ps, c0=c0, c1=c1, sub=sub: e.matmul(
                            ps[:, : c1 - c0], lhsT=hT[:, kc, sub * 128:(sub + 1) * 128], rhs=w_tm[0][:, kc, c0:c1],
                            start=(kc == 0), stop=(kc == KC - 1)),
                            reads=[hTb, w_tm[1]], writes=[psb], inc=(kc == KC - 1))
                    ev = "act" if (n_ps % 2) else "dve"
                    if ev == "act":
                        em.op("act", lambda e, ps=ps, c0=c0, c1=c1, o_t=o_t: e.copy(out=o_t[:, c0:c1], in_=ps[:, : c1 - c0]),
                              reads=[psb], writes=[o_b])
                    else:
                        em.op("dve", lambda e, ps=ps, c0=c0, c1=c1, o_t=o_t: e.tensor_copy(out=o_t[:, c0:c1], in_=ps[:, : c1 - c0]),
                              reads=[psb], writes=[o_b])
                em.dma(STORE_Q, out_tm[t0:t0 + 128, :], o_t[:], reads=[o_b])
            if STOP == "t":
                continue
            for cb in range(NC // 128):
                ps, psb = psum[n_ps % 6]
                n_ps += 1
                o_t, o_b = ocm[n_ocm % 3]
                n_ocm += 1
                for kc in range(KC):
                    em.op("pe", lambda e, kc=kc, ps=ps, cb=cb: e.matmul(
                        ps[:, :], lhsT=w_cm[0][:, kc, cb * 128:(cb + 1) * 128], rhs=hT[:, kc, :],
                        start=(kc == 0), stop=(kc == KC - 1)),
                        reads=[hTb, w_cm[1]], writes=[psb], inc=(kc == KC - 1))
                ev = "act" if (n_ps % 2) else "dve"
                if ev == "act":
                    em.op("act", lambda e, ps=ps, o_t=o_t: e.copy(out=o_t[:], in_=ps[:]), reads=[psb], writes=[o_b])
                else:
                    em.op("dve", lambda e, ps=ps, o_t=o_t: e.tensor_copy(out=o_t[:], in_=ps[:]), reads=[psb], writes=[o_b])
                em.dma(STORE_Q, out_cm[cb * 128:(cb + 1) * 128, st * 512:(st + 1) * 512], o_t[:], reads=[o_b])


def mkT(es, nc, name, shape, dt=F32):
    return (es.enter_context(SBT(nc, name, shape, dt)), Buf(name))


def ld(em, t, src):
    em.dma("sp", t[0][:], src, writes=[t[1]])


def stage_mamba(em, T, p_tm, p_cm, prm, cst, y_dram, psum, TM_Z=1036, TM_DT=1548, CM_X=640, YOFF=512):
    nc = em.nc
    NCH = T // 128
    with ExitStack() as es:
        cw = mkT(es, nc, "m2cw", [128, 6, 4]); ld(em, cw, prm["m2_cw"])
        cb = mkT(es, nc, "m2cb", [128, 6]); ld(em, cb, prm["m2_cb"])
        dtb = mkT(es, nc, "m2dtb", [128, 8]); ld(em, dtb, prm["m2_dtb"])
        alog = mkT(es, nc, "m2alog", [128, 8]); ld(em, alog, prm["m2_alog"])
        dsk = mkT(es, nc, "m2dsk", [128, 8]); ld(em, dsk, prm["m2_dsk"])
        ng = mkT(es, nc, "m2ng", [128, 512]); ld(em, ng, prm["m2_ng"])
        Aneg = mkT(es, nc, "m2A", [128, 8])
        em.op("act", lambda e: e.activation(out=Aneg[0][:], in_=alog[0][:], func=AF.Exp), reads=[alog[1]], writes=[Aneg[1]])
        em.op("dve", lambda e: e.tensor_scalar(out=Aneg[0][:], in0=Aneg[0][:], scalar1=-1.0, scalar2=None, op0=ALU.mult),
              reads=[Aneg[1]], writes=[Aneg[1]])
        stateT = mkT(es, nc, "m2st", [128, 512])
        state_bf = mkT(es, nc, "m2stb", [128, 512], BF16)
        em.op("dve", lambda e: e.memset(stateT[0][:], 0.0), writes=[stateT[1]])
        em.op("dve", lambda e: e.memset(state_bf[0][:], 0.0), writes=[state_bf[1]])
        sets = []
        for sfx in ("a", "b"):
            S = {}
            S["xin"] = mkT(es, nc, "m2xin" + sfx, [128, 6, 131])
            S["acc"] = mkT(es, nc, "m2acc" + sfx, [128, 6, 128])
            S["xact"] = mkT(es, nc, "m2xact" + sfx, [128, 6, 128])
            S["bT"] = mkT(es, nc, "m2bT" + sfx, [128, 128], BF16)
            S["cT"] = mkT(es, nc, "m2cT" + sfx, [128, 128], BF16)
            S["xs_tm"] = mkT(es, nc, "m2xs" + sfx, [128, 512])
            S["b_tm"] = mkT(es, nc, "m2btm" + sfx, [128, 128], BF16)
            S["dtr"] = mkT(es, nc, "m2dtr" + sfx, [128, 8])
            S["dtv"] = mkT(es, nc, "m2dtv" + sfx, [128, 8])
            S["av"] = mkT(es, nc, "m2a" + sfx, [128, 8])
            S["acs"] = mkT(es, nc, "m2acs" + sfx, [128, 8])
            S["tot"] = mkT(es, nc, "m2tot" + sfx, [128, 8])
            S["dsl"] = mkT(es, nc, "m2dsl" + sfx, [128, 8])
            S["dout"] = mkT(es, nc, "m2dout" + sfx, [128, 8])
            S["dtot"] = mkT(es, nc, "m2dtot" + sfx, [128, 8])
            S["xdt"] = mkT(es, nc, "m2xdt" + sfx, [128, 512])
            S["xdt_bf"] = mkT(es, nc, "m2xdtb" + sfx, [128, 512], BF16)
            S["xdd_bf"] = mkT(es, nc, "m2xddb" + sfx, [128, 512], BF16)
            S["amat"] = mkT(es, nc, "m2amat" + sfx, [128, 8, 128])
            S["LT"] = mkT(es, nc, "m2LT" + sfx, [128, 8, 128])
            S["cbm"] = mkT(es, nc, "m2cbm" + sfx, [128, 128])
            S["MT"] = mkT(es, nc, "m2MT" + sfx, [128, 8, 128], BF16)
            S["yv"] = mkT(es, nc, "m2y" + sfx, [128, 512])
            S["t2"] = mkT(es, nc, "m2t2" + sfx, [128, 512])
            S["zt"] = mkT(es, nc, "m2z" + sfx, [128, 512])
            S["sq"] = mkT(es, nc, "m2sq" + sfx, [128, 512])
            S["ss"] = mkT(es, nc, "m2ss" + sfx, [128, 4])
            sets.append(S)
        def bc8(t):
            return t[0][:, :].unsqueeze(2).to_broadcast([128, 8, 64])

        def v3(t):
            return t[0][:].rearrange("p (h d) -> p h d", h=8)

        def load_chunk(c):
            S = sets[c % 2]
            xin, dtr, zt = S["xin"], S["dtr"], S["zt"]
            t0 = c * 128
            if c == 0:
                em.op("dve", lambda e: e.memset(xin[0][:, :, 0:3], 0.0), writes=[xin[1]])
                for blk in range(6):
                    em.dma("sp", xin[0][:, blk, 3:131], p_cm[CM_X + blk * 128:CM_X + (blk + 1) * 128, 0:128], writes=[xin[1]])
            else:
                for blk in range(6):
                    em.dma("sp", xin[0][:, blk, :], p_cm[CM_X + blk * 128:CM_X + (blk + 1) * 128, t0 - 3:t0 + 128], writes=[xin[1]])
            em.dma("sp", dtr[0][:], p_tm[t0:t0 + 128, TM_DT:TM_DT + 8], writes=[dtr[1]])
            em.dma("sp", zt[0][:], p_tm[t0:t0 + 128, TM_Z:TM_Z + 512], writes=[zt[1]])

        load_chunk(0)
        for c in range(NCH):
            t0 = c * 128
            if c + 1 < NCH:
                load_chunk(c + 1)
            S = sets[c % 2]
            xin = S["xin"]
            acc = S["acc"]
            xact = S["xact"]
            bT = S["bT"]
            cT = S["cT"]
            xs_tm = S["xs_tm"]
            b_tm = S["b_tm"]
            dtr = S["dtr"]
            dtv = S["dtv"]
            av = S["av"]
            acs = S["acs"]
            tot = S["tot"]
            dsl = S["dsl"]
            dout = S["dout"]
            dtot = S["dtot"]
            xdt = S["xdt"]
            xdt_bf = S["xdt_bf"]
            xdd_bf = S["xdd_bf"]
            amat = S["amat"]
            LT = S["LT"]
            cbm = S["cbm"]
            MT = S["MT"]
            yv = S["yv"]
            t2 = S["t2"]
            zt = S["zt"]
            sq = S["sq"]
            ss = S["ss"]
            for blk in range(6):
                em.op("dve", lambda e, blk=blk: e.tensor_scalar(out=acc[0][:, blk, :], in0=xin[0][:, blk, 0:128],
                                                                 scalar1=cw[0][:, blk, 0:1], scalar2=None, op0=ALU.mult),
                      reads=[xin[1], cw[1]], writes=[acc[1]])
                for k in range(1, 4):
                    em.op("dve", lambda e, blk=blk, k=k: e.scalar_tensor_tensor(
                        out=acc[0][:, blk, :], in0=xin[0][:, blk, k:k + 128], scalar=cw[0][:, blk, k:k + 1],
                        in1=acc[0][:, blk, :], op0=ALU.mult, op1=ALU.add), reads=[xin[1], cw[1], acc[1]], writes=[acc[1]])
                em.op("act", lambda e, blk=blk: e.activation(out=xact[0][:, blk, :], in_=acc[0][:, blk, :], func=AF.Silu,
                                                             bias=cb[0][:, blk:blk + 1]), reads=[acc[1], cb[1]], writes=[xact[1]])
            em.op("dve", lambda e: e.tensor_copy(out=bT[0][:], in_=xact[0][:, 4, :]), reads=[xact[1]], writes=[bT[1]])
            em.op("dve", lambda e: e.tensor_copy(out=cT[0][:], in_=xact[0][:, 5, :]), reads=[xact[1]], writes=[cT[1]])
            pA, pAb = psum[0]
            for blk in range(4):
                em.op("pe", lambda e, blk=blk: e.transpose(out=pA[:, blk * 128:(blk + 1) * 128], in_=xact[0][:, blk, :],
                                                           identity=cst["ident_f"][0][:]),
                      reads=[xact[1], cst["ident_f"][1]], writes=[pAb], inc=(blk == 3))
            em.op("act", lambda e: e.copy(out=xs_tm[0][:], in_=pA[:, :]), reads=[pAb], writes=[xs_tm[1]])
            pB, pBb = psum[1]
            em.op("pe", lambda e: e.transpose(out=pB[:, 0:128], in_=xact[0][:, 4, :], identity=cst["ident_f"][0][:]),
                  reads=[xact[1], cst["ident_f"][1]], writes=[pBb])
            em.op("act", lambda e: e.copy(out=b_tm[0][:], in_=pB[:, 0:128]), reads=[pBb], writes=[b_tm[1]])
            em.op("dve", lambda e: e.tensor_tensor(out=dtv[0][:], in0=dtr[0][:], in1=dtb[0][:], op=ALU.add),
                  reads=[dtr[1], dtb[1]], writes=[dtv[1]])
            em.op("act", lambda e: e.activation(out=dtv[0][:], in_=dtv[0][:], func=AF.Exp), reads=[dtv[1]], writes=[dtv[1]])
            em.op("act", lambda e: e.activation(out=dtv[0][:], in_=dtv[0][:], func=AF.Ln, bias=1.0), reads=[dtv[1]], writes=[dtv[1]])
            em.op("dve", lambda e: e.tensor_tensor(out=av[0][:], in0=dtv[0][:], in1=Aneg[0][:], op=ALU.mult),
                  reads=[dtv[1], Aneg[1]], writes=[av[1]])
            pC, pCb = psum[2]
            em.op("pe", lambda e: e.matmul(pC[:, 0:8], lhsT=cst["triu_f"][0][:], rhs=av[0][:], start=True, stop=True),
                  reads=[cst["triu_f"][1], av[1]], writes=[pCb])
            em.op("pe", lambda e: e.matmul(pC[:, 8:16], lhsT=cst["ones_f"][0][:], rhs=av[0][:], start=True, stop=True),
                  reads=[cst["ones_f"][1], av[1]], writes=[pCb])
            em.op("act", lambda e: e.copy(out=acs[0][:], in_=pC[:, 0:8]), reads=[pCb], writes=[acs[1]])
            em.op("act", lambda e: e.copy(out=tot[0][:], in_=pC[:, 8:16]), reads=[pCb], writes=[tot[1]])
            em.op("dve", lambda e: e.tensor_tensor(out=dsl[0][:], in0=tot[0][:], in1=acs[0][:], op=ALU.subtract),
                  reads=[tot[1], acs[1]], writes=[dsl[1]])
            em.op("act", lambda e: e.activation(out=dsl[0][:], in_=dsl[0][:], func=AF.Exp), reads=[dsl[1]], writes=[dsl[1]])
            em.op("act", lambda e: e.activation(out=dout[0][:], in_=acs[0][:], func=AF.Exp), reads=[acs[1]], writes=[dout[1]])
            em.op("act", lambda e: e.activation(out=dtot[0][:], in_=tot[0][:], func=AF.Exp), reads=[tot[1]], writes=[dtot[1]])
            em.op("dve", lambda e: e.tensor_tensor(out=v3(xdt), in0=v3(xs_tm), in1=bc8(dtv), op=ALU.mult),
                  reads=[xs_tm[1], dtv[1]], writes=[xdt[1]])
            em.op("act", lambda e: e.copy(out=xdt_bf[0][:], in_=xdt[0][:]), reads=[xdt[1]], writes=[xdt_bf[1]])
            em.op("dve", lambda e: e.tensor_tensor(out=xdd_bf[0][:].rearrange("p (h d) -> p h d", h=8), in0=v3(xdt), in1=bc8(dsl), op=ALU.mult),
                  reads=[xdt[1], dsl[1]], writes=[xdd_bf[1]])
            for h in range(8):
                em.op("dve", lambda e, h=h: e.tensor_scalar(out=amat[0][:, h, :], in0=cst["mstrict_f"][0][:], scalar1=av[0][:, h:h + 1],
                                                            scalar2=None, op0=ALU.mult), reads=[cst["mstrict_f"][1], av[1]], writes=[amat[1]])
            pL0, pL0b = psum[3]
            pL1, pL1b = psum[4]
            for h in range(8):
                pl, plb = (pL0, pL0b) if h < 4 else (pL1, pL1b)
                em.op("pe", lambda e, h=h, pl=pl: e.matmul(pl[:, (h % 4) * 128:(h % 4 + 1) * 128], lhsT=amat[0][:, h, :],
                                                          rhs=cst["triu_f"][0][:], start=True, stop=True),
                      reads=[amat[1], cst["triu_f"][1]], writes=[plb])
            em.op("act", lambda e: e.activation(out=LT[0][:, 0:4, :], in_=pL0[:, :].rearrange("p (h l) -> p h l", h=4), func=AF.Exp),
                  reads=[pL0b], writes=[LT[1]])
            em.op("act", lambda e: e.activation(out=LT[0][:, 4:8, :], in_=pL1[:, :].rearrange("p (h l) -> p h l", h=4), func=AF.Exp),
                  reads=[pL1b], writes=[LT[1]])
            pD, pDb = psum[5]
            em.op("pe", lambda e: e.matmul(pD[:, 0:128], lhsT=bT[0][:], rhs=cT[0][:], start=True, stop=True),
                  reads=[bT[1], cT[1]], writes=[pDb])
            em.op("dve", lambda e: e.tensor_tensor(out=cbm[0][:], in0=pD[:, 0:128], in1=cst["mcausT_f"][0][:], op=ALU.mult),
                  reads=[pDb, cst["mcausT_f"][1]], writes=[cbm[1]])
            em.op("dve", lambda e: e.tensor_tensor(out=MT[0][:], in0=LT[0][:], in1=cbm[0][:, :].unsqueeze(1).to_broadcast([128, 8, 128]),
                                                   op=ALU.mult), reads=[LT[1], cbm[1]], writes=[MT[1]])
            pY, pYb = psum[6]
            em.op("pe", lambda e: e.matmul(pY[:, :], lhsT=cT[0][:], rhs=state_bf[0][:], start=True, stop=True),
                  reads=[cT[1], state_bf[1]], writes=[pYb])
            em.op("dve", lambda e: e.tensor_tensor(out=v3(yv), in0=pY[:, :].rearrange("p (h d) -> p h d", h=8), in1=bc8(dout), op=ALU.mult),
                  reads=[pYb, dout[1]], writes=[yv[1]])
            pZ, pZb = psum[0]
            for h in range(8):
                em.op("pe", lambda e, h=h: e.matmul(pZ[:, h * 64:(h + 1) * 64], lhsT=MT[0][:, h, :], rhs=xdt_bf[0][:, h * 64:(h + 1) * 64],
                                                    start=True, stop=True), reads=[MT[1], xdt_bf[1]], writes=[pZb], inc=(h == 7))
            em.op("dve", lambda e: e.tensor_tensor(out=yv[0][:], in0=yv[0][:], in1=pZ[:, :], op=ALU.add), reads=[yv[1], pZb], writes=[yv[1]])
            pS, pSb = psum[1]
            em.op("pe", lambda e: e.matmul(pS[:, :], lhsT=b_tm[0][:], rhs=xdd_bf[0][:], start=True, stop=True),
                  reads=[b_tm[1], xdd_bf[1]], writes=[pSb])
            em.op("dve", lambda e: e.tensor_tensor(out=v3(stateT), in0=v3(stateT), in1=bc8(dtot), op=ALU.mult),
                  reads=[stateT[1], dtot[1]], writes=[stateT[1]])
            em.op("dve", lambda e: e.tensor_tensor(out=stateT[0][:], in0=stateT[0][:], in1=pS[:, :], op=ALU.add),
                  reads=[stateT[1], pSb], writes=[stateT[1]])
            em.op("act", lambda e: e.copy(out=state_bf[0][:], in_=stateT[0][:]), reads=[stateT[1]], writes=[state_bf[1]])
            em.op("dve", lambda e: e.tensor_tensor(out=v3(t2), in0=v3(xs_tm), in1=bc8(dsk), op=ALU.mult),
                  reads=[xs_tm[1], dsk[1]], writes=[t2[1]])
            em.op("dve", lambda e: e.tensor_tensor(out=yv[0][:], in0=yv[0][:], in1=t2[0][:], op=ALU.add), reads=[yv[1], t2[1]], writes=[yv[1]])
            em.op("act", lambda e: e.activation(out=zt[0][:], in_=zt[0][:], func=AF.Silu), reads=[zt[1]], writes=[zt[1]])
            em.op("dve", lambda e: e.tensor_tensor(out=yv[0][:], in0=yv[0][:], in1=zt[0][:], op=ALU.mult), reads=[yv[1], zt[1]], writes=[yv[1]])
            em.op("act", lambda e: e.activation(out=sq[0][:], in_=yv[0][:], func=AF.Square, accum_out=ss[0][:, 0:1]),
                  reads=[yv[1]], writes=[sq[1], ss[1]])
            em.op("dve", lambda e: e.tensor_scalar(out=ss[0][:, 1:2], in0=ss[0][:, 0:1], scalar1=1.0 / 512, scalar2=1e-6,
                                                   op0=ALU.mult, op1=ALU.add), reads=[ss[1]], writes=[ss[1]])
            em.op("act", lambda e: e.sqrt(out=ss[0][:, 3:4], in_=ss[0][:, 1:2]), reads=[ss[1]], writes=[ss[1]])
            em.op("dve", lambda e: e.reciprocal(out=ss[0][:, 2:3], in_=ss[0][:, 3:4]), reads=[ss[1]], writes=[ss[1]])
            em.op("dve", lambda e: e.scalar_tensor_tensor(out=t2[0][:], in0=yv[0][:], scalar=ss[0][:, 2:3], in1=ng[0][:],
                                                          op0=ALU.mult, op1=ALU.mult), reads=[yv[1], ss[1], ng[1]], writes=[t2[1]])
            em.dyn_store(y_dram[t0:t0 + 128, YOFF:YOFF + 512], t2[0][:], [t2[1]])


def stage_hgrn(em, T, layer, p_tm, p_cm, prm, cst, y_dram, psum, TM_I=524, TM_G=780, CM_Q=128, CM_F=384, YOFF=256):
    nc = em.nc
    NT = T // 128
    with ExitStack() as es:
        lbl = mkT(es, nc, "hglbl", [128, 2, 2]); ld(em, lbl, prm["hg_lbl"])
        ng = mkT(es, nc, "hgng", [128, 128]); ld(em, ng, prm["hg_ng"])
        lb = mkT(es, nc, "hglb", [128, 2])
        oml = mkT(es, nc, "hgoml", [128, 2])
        em.op("dve", lambda e: e.tensor_tensor(out=lb[0][:], in0=lbl[0][:, :, 1], in1=lbl[0][:, :, 0], op=ALU.subtract),
              reads=[lbl[1]], writes=[lb[1]])
        em.op("act", lambda e: e.activation(out=lb[0][:], in_=lb[0][:], func=AF.Sigmoid), reads=[lb[1]], writes=[lb[1]])
        lfl = mkT(es, nc, "hglfl", [128, 2]); ld(em, lfl, prm["hg_lflag"])
        em.op("dve", lambda e: e.tensor_tensor(out=lb[0][:], in0=lb[0][:], in1=lfl[0][:], op=ALU.mult),
              reads=[lb[1], lfl[1]], writes=[lb[1]])
        em.op("dve", lambda e: e.tensor_scalar(out=oml[0][:], in0=lb[0][:], scalar1=-1.0, scalar2=1.0, op0=ALU.mult, op1=ALU.add),
              reads=[lb[1]], writes=[oml[1]])
        W = 512
        HD = []
        for hd in range(2):
            S = mkT(es, nc, "hgS%d" % hd, [128, 128])
            Sref = mkT(es, nc, "hgSr%d" % hd, [128, 128], BF16)
            em.op("dve", lambda e: e.memset(S[0][:], 0.0), writes=[S[1]])
            em.op("dve", lambda e: e.memset(Sref[0][:], 0.0), writes=[Sref[1]])
            qT = [mkT(es, nc, "hgq%d_%d" % (hd, k), [128, W]) for k in range(2)]
            fT = [mkT(es, nc, "hgf%d_%d" % (hd, k), [128, W]) for k in range(2)]
            kk = mkT(es, nc, "hgk%d" % hd, [128, W])
            g0 = mkT(es, nc, "hgg0%d" % hd, [128, W])
            g1 = mkT(es, nc, "hgg1%d" % hd, [128, W])
            d1 = mkT(es, nc, "hgd1%d" % hd, [128, W])
            e1 = mkT(es, nc, "hge1%d" % hd, [128, W])
            qt_bf = mkT(es, nc, "hgqt%d" % hd, [128, W], BF16)
            kt_bf = mkT(es, nc, "hgkt%d" % hd, [128, W], BF16)
            qz = mkT(es, nc, "hgqz%d" % hd, [128, 2, 128], BF16)
            sc = mkT(es, nc, "hgsc%d" % hd, [128, 3, 8])
            ktm = mkT(es, nc, "hgktm%d" % hd, [128, 128], BF16)
            v_f = [mkT(es, nc, "hgvf%d_%d" % (hd, k), [128, 128]) for k in range(2)]
            v_bf = mkT(es, nc, "hgvb%d" % hd, [128, 128], BF16)
            AT = mkT(es, nc, "hgAT%d" % hd, [128, 128], BF16)
            tU = mkT(es, nc, "hgtU%d" % hd, [128, 128])
            ot = mkT(es, nc, "hgo%d" % hd, [128, 128])
            sq = mkT(es, nc, "hgsq%d" % hd, [128, 128])
            ss = mkT(es, nc, "hgss%d" % hd, [128, 4])
            gt = [mkT(es, nc, "hggt%d_%d" % (hd, k), [128, 128]) for k in range(2)]
            em.op("dve", lambda e: e.memset(qz[0][:], 0.0), writes=[qz[1]])
            HD.append(dict(S=S, Sref=Sref, qT=qT, fT=fT, kk=kk, g0=g0, g1=g1, d1=d1, e1=e1, qt_bf=qt_bf, kt_bf=kt_bf, qz=qz, sc=sc, ktm=ktm, v_f=v_f, v_bf=v_bf, AT=AT, tU=tU, ot=ot, sq=sq, ss=ss, gt=gt))
        def load_sc(hd, sc_i):
            H = HD[hd]
            c0 = sc_i * W
            q_, f_ = H["qT"][sc_i % 2], H["fT"][sc_i % 2]
            em.dma("sp", q_[0][:], p_cm[CM_Q + hd * 128:CM_Q + (hd + 1) * 128, c0:c0 + W], writes=[q_[1]])
            em.dma("sp", f_[0][:], p_cm[CM_F + hd * 128:CM_F + (hd + 1) * 128, c0:c0 + W], writes=[f_[1]])

        def load_tile(hd, gti):
            H = HD[hd]
            t0 = gti * 128
            v_, g_ = H["v_f"][gti % 2], H["gt"][gti % 2]
            em.dma("sp", v_[0][:], p_tm[t0:t0 + 128, TM_I + hd * 128:TM_I + (hd + 1) * 128], writes=[v_[1]])
            em.dma("sp", g_[0][:], p_tm[t0:t0 + 128, TM_G + hd * 128:TM_G + (hd + 1) * 128], writes=[g_[1]])

        for hd in range(2):
            load_sc(hd, 0)
            load_tile(hd, 0)
        for sc_i in range(T // W):
          for hd in range(2):
            H = HD[hd]
            if sc_i + 1 < T // W:
                load_sc(hd, sc_i + 1)
            S = H["S"]
            Sref = H["Sref"]
            qT = H["qT"][sc_i % 2]
            fT = H["fT"][sc_i % 2]
            kk = H["kk"]
            g0 = H["g0"]
            g1 = H["g1"]
            d1 = H["d1"]
            e1 = H["e1"]
            qt_bf = H["qt_bf"]
            kt_bf = H["kt_bf"]
            qz = H["qz"]
            sc = H["sc"]
            ktm = H["ktm"]
            v_bf = H["v_bf"]
            AT = H["AT"]
            tU = H["tU"]
            ot = H["ot"]
            sq = H["sq"]
            ss = H["ss"]
            PB = hd * 4
            if True:
                c0 = sc_i * W
                em.op("act", lambda e: e.activation(out=fT[0][:], in_=fT[0][:], func=AF.Sigmoid), reads=[fT[1]], writes=[fT[1]])
                em.op("dve", lambda e: e.tensor_scalar(out=fT[0][:], in0=fT[0][:], scalar1=oml[0][:, hd:hd + 1], scalar2=lb[0][:, hd:hd + 1],
                                                       op0=ALU.mult, op1=ALU.add), reads=[fT[1], oml[1], lb[1]], writes=[fT[1]])
                em.op("dve", lambda e: e.tensor_scalar(out=kk[0][:], in0=fT[0][:], scalar1=-1.0, scalar2=1.0, op0=ALU.mult, op1=ALU.add),
                      reads=[fT[1]], writes=[kk[1]])
                em.op("act", lambda e: e.activation(out=g0[0][:], in_=fT[0][:], func=AF.Ln), reads=[fT[1]], writes=[g0[1]])
                src, dst = g0, g1
                for d in (1, 2, 4, 8, 16, 32):
                    sv = src[0][:].rearrange("p (c t) -> p c t", t=64)
                    dv = dst[0][:].rearrange("p (c t) -> p c t", t=64)
                    em.op("dve", lambda e, sv=sv, dv=dv, d=d: e.tensor_copy(out=dv[:, :, 0:d], in_=sv[:, :, 0:d]), reads=[src[1]], writes=[dst[1]])
                    em.op("dve", lambda e, sv=sv, dv=dv, d=d: e.tensor_tensor(out=dv[:, :, d:64], in0=sv[:, :, d:64], in1=sv[:, :, 0:64 - d], op=ALU.add),
                          reads=[src[1]], writes=[dst[1]])
                    src, dst = dst, src
                gc = src
                gv = gc[0][:].rearrange("p (c t) -> p c t", t=64)
                em.op("dve", lambda e: e.tensor_tensor(out=d1[0][:].rearrange("p (c t) -> p c t", t=64), in0=gv,
                                                       in1=gv[:, :, 31:32].to_broadcast([128, 8, 64]), op=ALU.subtract),
                      reads=[gc[1]], writes=[d1[1]])
                em.op("act", lambda e: e.activation(out=e1[0][:], in_=d1[0][:], func=AF.Exp), reads=[d1[1]], writes=[e1[1]])
                em.op("dve", lambda e: e.tensor_tensor(out=qt_bf[0][:], in0=qT[0][:], in1=e1[0][:], op=ALU.mult), reads=[qT[1], e1[1]], writes=[qt_bf[1]])
                em.op("act", lambda e: e.activation(out=e1[0][:], in_=d1[0][:], func=AF.Exp, scale=-1.0), reads=[d1[1]], writes=[e1[1]])
                em.op("dve", lambda e: e.tensor_tensor(out=kt_bf[0][:], in0=kk[0][:], in1=e1[0][:], op=ALU.mult), reads=[kk[1], e1[1]], writes=[kt_bf[1]])
                em.op("act", lambda e: e.activation(out=sc[0][:, 0, :], in_=gv[:, :, 31], func=AF.Exp), reads=[gc[1]], writes=[sc[1]])
                em.op("act", lambda e: e.activation(out=sc[0][:, 1, :], in_=gv[:, :, 63], func=AF.Exp), reads=[gc[1]], writes=[sc[1]])
                em.op("act", lambda e: e.activation(out=sc[0][:, 2, :], in_=d1[0][:].rearrange("p (c t) -> p c t", t=64)[:, :, 63], func=AF.Exp),
                      reads=[d1[1]], writes=[sc[1]])
                em.op("dve", lambda e: e.tensor_scalar(out=Sref[0][:], in0=S[0][:], scalar1=sc[0][:, 0, 0:1], scalar2=None, op0=ALU.mult),
                      reads=[S[1], sc[1]], writes=[Sref[1]])
                for tl in range(W // 128):
                    t0 = c0 + tl * 128
                    gti = sc_i * (W // 128) + tl
                    v_f = H["v_f"][gti % 2]
                    gt = H["gt"][gti % 2]
                    if gti + 1 < T // 128:
                        load_tile(hd, gti + 1)
                    cols = slice(tl * 128, (tl + 1) * 128)
                    em.op("act", lambda e: e.copy(out=v_bf[0][:], in_=v_f[0][:]), reads=[v_f[1]], writes=[v_bf[1]])
                    pA, pAb = psum[PB + 0]
                    em.op("pe", lambda e, cols=cols: e.matmul(pA[:, 0:128], lhsT=kt_bf[0][:, cols], rhs=qt_bf[0][:, cols], start=True, stop=True),
                          reads=[kt_bf[1], qt_bf[1]], writes=[pAb])
                    em.op("dve", lambda e: e.tensor_tensor(out=AT[0][:], in0=pA[:, 0:128], in1=cst["mask2_f"][0][:], op=ALU.mult),
                          reads=[pAb, cst["mask2_f"][1]], writes=[AT[1]])
                    pT, pTb = psum[7]
                    em.op("pe", lambda e, cols=cols: e.transpose(out=pT[:, 0, :], in_=kt_bf[0][:, cols], identity=cst["ident_bf"][0][:]),
                          reads=[kt_bf[1], cst["ident_bf"][1]], writes=[pTb])
                    em.op("act", lambda e: e.copy(out=ktm[0][:], in_=pT[:, 0, :]), reads=[pTb], writes=[ktm[1]])
                    em.op("dve", lambda e, cols=cols: e.tensor_copy(out=qz[0][:, 0, 0:64], in_=qt_bf[0][:, tl * 128:tl * 128 + 64]),
                          reads=[qt_bf[1]], writes=[qz[1]])
                    em.op("dve", lambda e, cols=cols: e.tensor_copy(out=qz[0][:, 1, 64:128], in_=qt_bf[0][:, tl * 128 + 64:tl * 128 + 128]),
                          reads=[qt_bf[1]], writes=[qz[1]])
                    pO, pOb = psum[PB + 1]
                    em.op("pe", lambda e: e.matmul(pO[:, 0:128], lhsT=AT[0][:], rhs=v_bf[0][:], start=True, stop=False),
                          reads=[AT[1], v_bf[1]], writes=[pOb], inc=False)
                    for j in range(2):
                        ch = tl * 2 + j
                        em.op("pe", lambda e, j=j: e.matmul(pO[:, 0:128], lhsT=qz[0][:, j, :], rhs=Sref[0][:], start=False, stop=(j == 1)),
                              reads=[qz[1], Sref[1]], writes=[pOb], inc=True)
                        pU, pUb = psum[PB + 2 + (j if hd == 0 else 0)]
                        rows = slice(j * 64, (j + 1) * 64)
                        em.op("pe", lambda e, rows=rows, pU=pU: e.matmul(pU[:, 0:128], lhsT=ktm[0][rows, :], rhs=v_bf[0][rows, :], start=True, stop=True),
                              reads=[ktm[1], v_bf[1]], writes=[pUb])
                        em.op("dve", lambda e, pU=pU, ch=ch: e.tensor_scalar(out=tU[0][:], in0=pU[:, 0:128], scalar1=sc[0][:, 2, ch:ch + 1], scalar2=None, op0=ALU.mult),
                              reads=[pUb, sc[1]], writes=[tU[1]])
                        em.op("dve", lambda e, ch=ch: e.scalar_tensor_tensor(out=S[0][:], in0=S[0][:], scalar=sc[0][:, 1, ch:ch + 1], in1=tU[0][:],
                                                                              op0=ALU.mult, op1=ALU.add), reads=[S[1], sc[1], tU[1]], writes=[S[1]])
                        if ch < 7:
                            em.op("dve", lambda e, ch=ch: e.tensor_scalar(out=Sref[0][:], in0=S[0][:], scalar1=sc[0][:, 0, ch + 1:ch + 2], scalar2=None,
                                                                           op0=ALU.mult), reads=[S[1], sc[1]], writes=[Sref[1]])
                    em.op("act", lambda e: e.copy(out=ot[0][:], in_=pO[:, 0:128]), reads=[pOb], writes=[ot[1]])
                    em.op("act", lambda e: e.activation(out=sq[0][:], in_=ot[0][:], func=AF.Square, accum_out=ss[0][:, 0:1]),
                          reads=[ot[1]], writes=[sq[1], ss[1]])
                    em.op("dve", lambda e: e.tensor_scalar(out=ss[0][:, 1:2], in0=ss[0][:, 0:1], scalar1=1.0 / 128, scalar2=1e-6,
                                                           op0=ALU.mult, op1=ALU.add), reads=[ss[1]], writes=[ss[1]])
                    em.op("act", lambda e: e.sqrt(out=ss[0][:, 3:4], in_=ss[0][:, 1:2]), reads=[ss[1]], writes=[ss[1]])
                    em.op("dve", lambda e: e.reciprocal(out=ss[0][:, 2:3], in_=ss[0][:, 3:4]), reads=[ss[1]], writes=[ss[1]])
                    em.op("dve", lambda e: e.scalar_tensor_tensor(out=ot[0][:], in0=ot[0][:], scalar=ss[0][:, 2:3], in1=ng[0][:],
                                                                  op0=ALU.mult, op1=ALU.mult), reads=[ot[1], ss[1], ng[1]], writes=[ot[1]])
                    em.op("act", lambda e: e.activation(out=gt[0][:], in_=gt[0][:], func=AF.Silu), reads=[gt[1]], writes=[gt[1]])
                    em.op("dve", lambda e: e.tensor_tensor(out=ot[0][:], in0=ot[0][:], in1=gt[0][:], op=ALU.mult), reads=[ot[1], gt[1]], writes=[ot[1]])
                    em.dyn_store(y_dram[t0:t0 + 128, YOFF + hd * 128:YOFF + (hd + 1) * 128], ot[0][:], [ot[1]])


NEG = -30000.0
TWO_PI = 6.283185307179586


def rope_apply(em, x3, xb, o3, ob, cos2, sin2, csb, tmp, H):
    cb = cos2.unsqueeze(1).to_broadcast([128, H, 8])
    sb = sin2.unsqueeze(1).to_broadcast([128, H, 8])
    t = tmp[0]
    x1 = x3[:, :, 0:8]
    x2 = x3[:, :, 8:16]
    em.op("dve", lambda e: e.tensor_tensor(out=t[:, 0, 0:H, :], in0=x1, in1=cb, op=ALU.mult), reads=[xb, csb], writes=[tmp[1]])
    em.op("dve", lambda e: e.tensor_tensor(out=t[:, 1, 0:H, :], in0=x2, in1=sb, op=ALU.mult), reads=[xb, csb], writes=[tmp[1]])
    em.op("dve", lambda e: e.tensor_tensor(out=t[:, 2, 0:H, :], in0=x2, in1=cb, op=ALU.mult), reads=[xb, csb], writes=[tmp[1]])
    em.op("dve", lambda e: e.tensor_tensor(out=t[:, 3, 0:H, :], in0=x1, in1=sb, op=ALU.mult), reads=[xb, csb], writes=[tmp[1]])
    em.op("dve", lambda e: e.tensor_tensor(out=o3[:, :, 0:8], in0=t[:, 0, 0:H, :], in1=t[:, 1, 0:H, :], op=ALU.subtract), reads=[tmp[1]], writes=[ob])
    em.op("dve", lambda e: e.tensor_tensor(out=o3[:, :, 8:16], in0=t[:, 2, 0:H, :], in1=t[:, 3, 0:H, :], op=ALU.add), reads=[tmp[1]], writes=[ob])


def headnorm(em, x, xb, H, g_ap, gb, sq, ss, out3, outb):
    x3 = x[:, 0:H * 64].rearrange("p (h d) -> p h d", h=H)
    s3 = sq[0][:, 0:H * 64].rearrange("p (h d) -> p h d", h=H)
    em.op("dve", lambda e: e.tensor_tensor(out=s3, in0=x3, in1=x3, op=ALU.mult), reads=[xb], writes=[sq[1]])
    em.op("dve", lambda e: e.tensor_reduce(out=ss[0][:, 0, 0:H], in_=s3, axis=AX.X, op=ALU.add), reads=[sq[1]], writes=[ss[1]])
    em.op("dve", lambda e: e.tensor_scalar(out=ss[0][:, 1, 0:H], in0=ss[0][:, 0, 0:H], scalar1=1.0 / 64, scalar2=1e-6, op0=ALU.mult, op1=ALU.add),
          reads=[ss[1]], writes=[ss[1]])
    em.op("act", lambda e: e.sqrt(out=ss[0][:, 2, 0:H], in_=ss[0][:, 1, 0:H]), reads=[ss[1]], writes=[ss[1]])
    em.op("dve", lambda e: e.reciprocal(out=ss[0][:, 1, 0:H], in_=ss[0][:, 2, 0:H]), reads=[ss[1]], writes=[ss[1]])
    em.op("dve", lambda e: e.tensor_tensor(out=out3, in0=x3, in1=ss[0][:, 1, 0:H].unsqueeze(2).to_broadcast([128, H, 64]), op=ALU.mult),
          reads=[xb, ss[1]], writes=[outb])
    em.op("dve", lambda e: e.tensor_tensor(out=out3, in0=out3, in1=g_ap, op=ALU.mult), reads=[outb, gb], writes=[outb])


def stage_nsa(em, T, p_tm, p_cm, prm, cst, y_dram, psum, YOFF=0):
    nc = em.nc
    NQ = T // 128
    NCMP = (T - 32) // 16 + 1
    NJC = (NCMP + 127) // 128
    DBG = None
    def dbg(name, t, part=128):
        if DBG is None:
            return
        shape = list(t[0].shape)
        d = nc.dram_tensor("dbg_" + name, shape, t[0].dtype, kind="ExternalOutput").ap()
        em.dma("sp", d, t[0][:], reads=[t[1]])
        DBG.append("dbg_" + name)
    with ExitStack() as es:
        def LD(name, shape, dt=F32):
            t = mkT(es, nc, "ns_" + name, shape, dt)
            src = prm[name]
            if len(shape) == 2:
                src = src[0:shape[0], 0:shape[1]]
            else:
                src = src[0:shape[0], 0:shape[1], 0:shape[2]]
            ld(em, t, src)
            return t
        gq = LD("ns_gq", [128, 256]); gk = LD("ns_gk", [128, 2, 64]); gk0 = LD("ns_gk0", [64, 1])
        posT = LD("ns_posT", [128, 32]); w2k = LD("ns_w2k", [128, 2, 64]); w2v = LD("ns_w2v", [128, 2, 64])
        posi = LD("ns_pos", [128, NQ], I32)
        invf = LD("c_invf", [128, 8])
        cmpmask = LD("c_cmpmask", [128, 17, 512], BF16)
        c2s = LD("c_c2s", [128, NJC, 128], BF16)
        ebig = LD("c_ebig", [128, T], BF16)
        causneg = LD("c_causneg", [128, 512], BF16)
        anticaus = LD("c_anticaus", [128, 512], BF16)
        keepb = LD("c_keep", [128, 254]); addb = LD("c_add", [128, 254])
        ident_bf = cst["ident_bf"]; ident_f = cst["ident_f"]; ones_f = cst["ones_f"]
        pst, pstb = psum[7]
        cosT = mkT(es, nc, "ns_cos", [128, NQ, 8]); sinT = mkT(es, nc, "ns_sin", [128, NQ, 8])
        if True:
            es2 = es
            posf = mkT(es2, nc, "ns_posf", [128, NQ])
            ang = mkT(es2, nc, "ns_ang", [128, NQ, 8]); fr = mkT(es2, nc, "ns_fr", [128, NQ, 8])
            ki = mkT(es2, nc, "ns_ki", [128, NQ, 8], I32); kf = mkT(es2, nc, "ns_kf", [128, NQ, 8])
            em.op("dve", lambda e: e.tensor_copy(out=posf[0][:], in_=posi[0][:]), reads=[posi[1]], writes=[posf[1]])
            em.op("dve", lambda e: e.tensor_tensor(out=ang[0][:], in0=posf[0][:, :].unsqueeze(2).to_broadcast([128, NQ, 8]),
                                                   in1=invf[0][:, :].unsqueeze(1).to_broadcast([128, NQ, 8]), op=ALU.mult),
                  reads=[posf[1], invf[1]], writes=[ang[1]])
            for dst, off in ((sinT, 0.0), (cosT, 0.25)):
                em.op("dve", lambda e, off=off: e.tensor_scalar(out=fr[0][:], in0=ang[0][:], scalar1=1.0 / TWO_PI, scalar2=off, op0=ALU.mult, op1=ALU.add),
                      reads=[ang[1]], writes=[fr[1]])
                em.op("dve", lambda e: e.tensor_copy(out=ki[0][:], in_=fr[0][:]), reads=[fr[1]], writes=[ki[1]])
                em.op("dve", lambda e: e.tensor_copy(out=kf[0][:], in_=ki[0][:]), reads=[ki[1]], writes=[kf[1]])
                em.op("dve", lambda e: e.tensor_tensor(out=fr[0][:], in0=fr[0][:], in1=kf[0][:], op=ALU.subtract), reads=[fr[1], kf[1]], writes=[fr[1]])
                em.op("dve", lambda e: e.tensor_scalar(out=kf[0][:], in0=fr[0][:], scalar1=0.5, scalar2=None, op0=ALU.is_gt), reads=[fr[1]], writes=[kf[1]])
                em.op("dve", lambda e: e.tensor_tensor(out=fr[0][:], in0=fr[0][:], in1=kf[0][:], op=ALU.subtract), reads=[fr[1], kf[1]], writes=[fr[1]])
                em.op("dve", lambda e: e.tensor_scalar(out=kf[0][:], in0=fr[0][:], scalar1=-0.5, scalar2=None, op0=ALU.is_lt), reads=[fr[1]], writes=[kf[1]])
                em.op("dve", lambda e: e.tensor_tensor(out=fr[0][:], in0=fr[0][:], in1=kf[0][:], op=ALU.add), reads=[fr[1], kf[1]], writes=[fr[1]])
                em.op("act", lambda e, dst=dst: e.activation(out=dst[0][:], in_=fr[0][:], func=AF.Sin, scale=TWO_PI), reads=[fr[1]], writes=[dst[1]])
        cs_b = Buf("cs")
        kTs = mkT(es, nc, "ns_kTs", [64, T], BF16); kTw = mkT(es, nc, "ns_kTw", [64, T], BF16)
        vs = mkT(es, nc, "ns_vs", [128, NQ, 65], BF16); vw = mkT(es, nc, "ns_vw", [128, NQ, 65], BF16)
        em.op("dve", lambda e: e.memset(vs[0][:, :, 64:65], 1.0), writes=[vs[1]])
        em.op("dve", lambda e: e.memset(vw[0][:, :, 64:65], 1.0), writes=[vw[1]])
        kcT = mkT(es, nc, "ns_kcT", [64, NJC * 128], BF16)
        vc = mkT(es, nc, "ns_vc", [128, NJC, 65], BF16)
        em.op("dve", lambda e: e.memset(kcT[0][:], 0.0), writes=[kcT[1]])
        em.op("dve", lambda e: e.memset(vc[0][:, :, 64:65], 1.0), writes=[vc[1]])
        sq = mkT(es, nc, "ns_sq", [128, 256]); ss = mkT(es, nc, "ns_ss", [128, 3, 4])
        rtmp = mkT(es, nc, "ns_rtmp", [128, 4, 4, 8])
        if True:
            es2 = es
            KS = [dict(kvr=mkT(es2, nc, "ns_kvr%d" % k, [128, 256]), kn=mkT(es2, nc, "ns_kn%d" % k, [128, 2, 64]), kr=mkT(es2, nc, "ns_kr%d" % k, [128, 2, 64]),
                       kb=mkT(es2, nc, "ns_kb%d" % k, [128, 2, 64], BF16), sq=mkT(es2, nc, "ns_ksq%d" % k, [128, 256]), ss=mkT(es2, nc, "ns_kss%d" % k, [128, 3, 4]),
                       rtmp=mkT(es2, nc, "ns_krt%d" % k, [128, 4, 4, 8])) for k in range(2)]
            for i in range(NQ):
                t0 = i * 128
                kvr, kn, kr, kb, sq, ss, rtmp = (KS[i % 2][k] for k in ("kvr", "kn", "kr", "kb", "sq", "ss", "rtmp"))
                em.dma("sp", kvr[0][:], p_tm[t0:t0 + 128, 256:512], writes=[kvr[1]])
                em.op("act", lambda e: e.copy(out=vs[0][:, i, 0:64], in_=kvr[0][:, 64:128]), reads=[kvr[1]], writes=[vs[1]])
                em.op("act", lambda e: e.copy(out=vw[0][:, i, 0:64], in_=kvr[0][:, 192:256]), reads=[kvr[1]], writes=[vw[1]])
                for w, c0 in ((0, 0), (1, 128)):
                    xs_ = kvr[0][:, c0:c0 + 64]
                    headnorm(em, xs_, kvr[1], 1, gk[0][:, w:w + 1, :], gk[1], sq, ss, kn[0][:, w:w + 1, :], kn[1])
                em.op("act", lambda e: e.copy(out=kr[0][:], in_=kn[0][:]), reads=[kn[1]], writes=[kr[1]])
                rope_apply(em, kn[0][:], kn[1], kr[0][:], kr[1], cosT[0][:, i, :], sinT[0][:, i, :], cosT[1], rtmp, 2)
                em.op("act", lambda e: e.copy(out=kb[0][:], in_=kr[0][:]), reads=[kr[1], sinT[1]], writes=[kb[1]])
                for w, dst in ((0, kTs), (1, kTw)):
                    em.op("pe", lambda e, w=w: e.transpose(out=pst[0:64, w, :], in_=kb[0][:, w, :], identity=ident_bf[0][:]),
                          reads=[kb[1], ident_bf[1]], writes=[pstb])
                    em.op("act", lambda e, w=w, dst=dst: e.copy(out=dst[0][:, t0:t0 + 128], in_=pst[0:64, w, :]), reads=[pstb], writes=[dst[1]])
        if True:
            es2 = es
            w1 = mkT(es2, nc, "ns_w1", [128, 32, 256], BF16)
            stgs = [mkT(es2, nc, "ns_w1s%d" % k, [128, 2, 256]) for k in range(2)]
            for l0 in range(0, 32, 2):
                stg = stgs[(l0 // 2) % 2]
                em.dma("sp", stg[0][:], prm["ns_w1kv"][:, l0:l0 + 2, :], writes=[stg[1]])
                em.op("pool", lambda e, l0=l0, stg=stg: e.tensor_copy(out=w1[0][:, l0:l0 + 2, :], in_=stg[0][:]), reads=[stg[1]], writes=[w1[1]])
            kvc = mkT(es2, nc, "ns_kvc", [128, T], BF16)
            CH = min(512, T)
            stg2s = [mkT(es2, nc, "ns_kvs%d" % k, [128, CH]) for k in range(2)]
            for c0 in range(0, T, CH):
                stg2 = stg2s[(c0 // CH) % 2]
                em.dma("sp", stg2[0][:], p_cm[0:128, c0:c0 + CH], writes=[stg2[1]])
                em.op("pool", lambda e, c0=c0, stg2=stg2: e.tensor_copy(out=kvc[0][:, c0:c0 + CH], in_=stg2[0][:]), reads=[stg2[1]], writes=[kvc[1]])
            posb = mkT(es2, nc, "ns_posb", [128, 34], BF16)
            em.op("dve", lambda e: e.memset(posb[0][:], 0.0), writes=[posb[1]])
            em.op("act", lambda e: e.copy(out=posb[0][:, 0:32], in_=posT[0][:]), reads=[posT[1]], writes=[posb[1]])
            w2kb = mkT(es2, nc, "ns_w2kb", [128, 2, 64], BF16); w2vb = mkT(es2, nc, "ns_w2vb", [128, 2, 64], BF16)
            em.op("act", lambda e: e.copy(out=w2kb[0][:], in_=w2k[0][:]), reads=[w2k[1]], writes=[w2kb[1]])
            em.op("act", lambda e: e.copy(out=w2vb[0][:], in_=w2v[0][:]), reads=[w2v[1]], writes=[w2vb[1]])
            hact = mkT(es2, nc, "ns_hact", [128, 2, 2, 512], BF16)
            em.op("dve", lambda e: e.memset(hact[0][:], 0.0), writes=[hact[1]])
            bias = mkT(es2, nc, "ns_hb", [128, 4])
            hx = mkT(es2, nc, "ns_hx", [128, 512]); hu = mkT(es2, nc, "ns_hu", [128, 512])
            NJ = NCMP
            for kv in range(2):
                rows = slice(kv * 64, kv * 64 + 64)
                for half in range(2):
                    ph, phb = psum[kv * 2 + half]
                    for l in range(32):
                        em.op("pe", lambda e, l=l, ph=ph: e.matmul(ph[:, 0:NJ], lhsT=w1[0][rows, l, half * 128:(half + 1) * 128],
                                                                    rhs=kvc[0][rows, l:l + 16 * (NJ - 1) + 1:16], start=(l == 0), stop=(l == 31)),
                              reads=[w1[1], kvc[1]], writes=[phb], inc=(l == 31))
                    pb, pbb = psum[4]
                    for l in range(32):
                        em.op("pe", lambda e, l=l: e.matmul(pb[:, 0:2], lhsT=w1[0][rows, l, half * 128:(half + 1) * 128], rhs=posb[0][rows, l:l + 2],
                                                            start=(l == 0), stop=(l == 31)), reads=[w1[1], posb[1]], writes=[pbb], inc=(l == 31))
                    bi = kv * 2 + half
                    em.op("act", lambda e, bi=bi: e.copy(out=bias[0][:, bi:bi + 1], in_=pb[:, 0:1]), reads=[pbb], writes=[bias[1]])
                    em.op("dve", lambda e, bi=bi, ph=ph: e.tensor_scalar(out=hx[0][:, 0:NJ], in0=ph[:, 0:NJ], scalar1=bias[0][:, bi:bi + 1], scalar2=None, op0=ALU.add),
                          reads=[phb, bias[1]], writes=[hx[1]])
                    em.op("dve", lambda e: e.tensor_tensor(out=hu[0][:, 0:NJ], in0=hx[0][:, 0:NJ], in1=hx[0][:, 0:NJ], op=ALU.mult), reads=[hx[1]], writes=[hu[1]])
                    em.op("dve", lambda e: e.tensor_scalar(out=hu[0][:, 0:NJ], in0=hu[0][:, 0:NJ], scalar1=0.044715, scalar2=1.0, op0=ALU.mult, op1=ALU.add),
                          reads=[hu[1]], writes=[hu[1]])
                    em.op("dve", lambda e: e.tensor_tensor(out=hu[0][:, 0:NJ], in0=hu[0][:, 0:NJ], in1=hx[0][:, 0:NJ], op=ALU.mult), reads=[hu[1], hx[1]], writes=[hu[1]])
                    em.op("act", lambda e: e.activation(out=hu[0][:, 0:NJ], in_=hu[0][:, 0:NJ], func=AF.Sigmoid, scale=1.5957691216057308), reads=[hu[1]], writes=[hu[1]])
                    em.op("dve", lambda e, kv=kv, half=half: e.tensor_tensor(out=hact[0][:, kv, half, 0:NJ], in0=hu[0][:, 0:NJ], in1=hx[0][:, 0:NJ], op=ALU.mult),
                          reads=[hu[1], hx[1]], writes=[hact[1]])
            pk, pkb = psum[5]
            for half in range(2):
                em.op("pe", lambda e, half=half: e.matmul(pk[0:64, 0:512], lhsT=w2kb[0][:, half, :], rhs=hact[0][:, 0, half, :], start=(half == 0), stop=(half == 1)),
                      reads=[w2kb[1], hact[1]], writes=[pkb], inc=(half == 1))
            kc_f = mkT(es2, nc, "ns_kcf", [64, 512]); kc_sq = mkT(es2, nc, "ns_kcsq", [64, 512]); kc_r = mkT(es2, nc, "ns_kcr", [64, 512])
            em.op("act", lambda e: e.copy(out=kc_f[0][:], in_=pk[0:64, 0:512]), reads=[pkb], writes=[kc_f[1]])
            em.op("dve", lambda e: e.tensor_tensor(out=kc_sq[0][:], in0=kc_f[0][:], in1=kc_f[0][:], op=ALU.mult), reads=[kc_f[1]], writes=[kc_sq[1]])
            pk2, pk2b = psum[6]
            em.op("pe", lambda e: e.matmul(pk2[0:64, 0:512], lhsT=ones_f[0][0:64, 0:64], rhs=kc_sq[0][:], start=True, stop=True),
                  reads=[ones_f[1], kc_sq[1]], writes=[pk2b])
            em.op("dve", lambda e: e.tensor_scalar(out=kc_r[0][:], in0=pk2[0:64, 0:512], scalar1=1.0 / 64, scalar2=1e-6, op0=ALU.mult, op1=ALU.add),
                  reads=[pk2b], writes=[kc_r[1]])
            em.op("act", lambda e: e.sqrt(out=kc_r[0][:], in_=kc_r[0][:]), reads=[kc_r[1]], writes=[kc_r[1]])
            em.op("dve", lambda e: e.reciprocal(out=kc_r[0][:], in_=kc_r[0][:]), reads=[kc_r[1]], writes=[kc_r[1]])
            em.op("dve", lambda e: e.tensor_tensor(out=kc_f[0][:], in0=kc_f[0][:], in1=kc_r[0][:], op=ALU.mult), reads=[kc_f[1], kc_r[1]], writes=[kc_f[1]])
            em.op("dve", lambda e: e.tensor_scalar(out=kcT[0][:, 0:NJ], in0=kc_f[0][:, 0:NJ], scalar1=gk0[0][:, 0:1], scalar2=None, op0=ALU.mult),
                  reads=[kc_f[1], gk0[1]], writes=[kcT[1]])
            for jc in range(NJC):
                pv, pvb = psum[jc % 4]
                for half in range(2):
                    em.op("pe", lambda e, half=half, jc=jc, pv=pv: e.matmul(pv[:, 0:64], lhsT=hact[0][:, 1, half, jc * 128:(jc + 1) * 128], rhs=w2vb[0][:, half, :],
                                                                           start=(half == 0), stop=(half == 1)), reads=[hact[1], w2vb[1]], writes=[pvb], inc=(half == 1))
                em.op("act", lambda e, jc=jc, pv=pv: e.copy(out=vc[0][:, jc, 0:64], in_=pv[:, 0:64]), reads=[pvb], writes=[vc[1]])
        QS = []
        for k in range(2):
            QS.append(dict(
                qraw=mkT(es, nc, "ns_qraw%d" % k, [128, 256]), gtr=mkT(es, nc, "ns_gtr%d" % k, [128, 12]), gts=mkT(es, nc, "ns_gts%d" % k, [128, 12]),
                qn=mkT(es, nc, "ns_qn%d" % k, [128, 256]), qr=mkT(es, nc, "ns_qr%d" % k, [128, 256]),
                qnb=mkT(es, nc, "ns_qnb%d" % k, [128, 256], BF16), qrb=mkT(es, nc, "ns_qrb%d" % k, [128, 256], BF16),
                qTn=mkT(es, nc, "ns_qTn%d" % k, [64, 512], BF16), qTr=mkT(es, nc, "ns_qTr%d" % k, [64, 512], BF16),
                acc=mkT(es, nc, "ns_acc%d" % k, [128, 256]), sq=mkT(es, nc, "ns_sqq%d" % k, [128, 256]), ss=mkT(es, nc, "ns_ssq%d" % k, [128, 3, 4]),
                rtmp=mkT(es, nc, "ns_rtq%d" % k, [128, 4, 4, 8])))
        FS = [dict(oT_sb=mkT(es, nc, "ns_oT%d" % k, [65, 512]), otm=mkT(es, nc, "ns_otm%d" % k, [128, 4, 65]),
                   rz=mkT(es, nc, "ns_rz%d" % k, [128, 4]), wz=mkT(es, nc, "ns_wz%d" % k, [128, 4])) for k in range(3)]
        eTs = [mkT(es, nc, "ns_eT%d" % k, [128, 512], BF16) for k in range(2)]
        imp = mkT(es, nc, "ns_imp", [128, 128]); rp = mkT(es, nc, "ns_rp", [128, 128])
        mx = mkT(es, nc, "ns_mx", [128, 8]); thr = mkT(es, nc, "ns_thr", [128, 1])
        nsel = mkT(es, nc, "ns_nsel", [128, 128], BF16)
        nselT4 = mkT(es, nc, "ns_nselT4", [128, 512], BF16)
        st = {"n_s": 0, "n_f": 0}

        def prep_a(i):
            Q = QS[i % 2]
            t0 = i * 128
            qraw, gtr, gts, qn, qr, qnb, qrb, qTn, qTr = (Q[k] for k in ("qraw", "gtr", "gts", "qn", "qr", "qnb", "qrb", "qTn", "qTr"))
            em.dma("sp", qraw[0][:], p_tm[t0:t0 + 128, 0:256], writes=[qraw[1]])
            em.dma("sp", gtr[0][:], p_tm[t0:t0 + 128, 512:524], writes=[gtr[1]])
            em.op("act", lambda e: e.activation(out=gts[0][:], in_=gtr[0][:], func=AF.Sigmoid), reads=[gtr[1]], writes=[gts[1]])
            headnorm(em, qraw[0], qraw[1], 4, gq[0][:].rearrange("p (h d) -> p h d", h=4), gq[1], Q["sq"], Q["ss"],
                     qn[0][:].rearrange("p (h d) -> p h d", h=4), qn[1])
            em.op("dve", lambda e: e.tensor_copy(out=qr[0][:], in_=qn[0][:]), reads=[qn[1]], writes=[qr[1]])
            rope_apply(em, qn[0][:].rearrange("p (h d) -> p h d", h=4), qn[1], qr[0][:].rearrange("p (h d) -> p h d", h=4), qr[1],
                       cosT[0][:, i, :], sinT[0][:, i, :], cosT[1], Q["rtmp"], 4)
            em.op("dve", lambda e: e.tensor_copy(out=qnb[0][:], in_=qn[0][:]), reads=[qn[1]], writes=[qnb[1]])
            em.op("dve", lambda e: e.tensor_copy(out=qrb[0][:], in_=qr[0][:]), reads=[qr[1], sinT[1]], writes=[qrb[1]])

        def prep_b(i):
            Q = QS[i % 2]
            qnb, qrb, qTn, qTr = (Q[k] for k in ("qnb", "qrb", "qTn", "qTr"))
            for r in range(4):
                em.op("pe", lambda e, r=r: e.transpose(out=pst[0:64, r, :], in_=qnb[0][:, r * 64:(r + 1) * 64], identity=ident_bf[0][:]),
                      reads=[qnb[1], ident_bf[1]], writes=[pstb], inc=False)
            for r in range(4):
                em.op("pe", lambda e, r=r: e.transpose(out=pst[0:64, 4 + r, :], in_=qrb[0][:, r * 64:(r + 1) * 64], identity=ident_bf[0][:]),
                      reads=[qrb[1], ident_bf[1]], writes=[pstb], inc=(r == 3))
            em.op("dve", lambda e: e.tensor_copy(out=qTn[0][:].rearrange("p (r t) -> p r t", r=4), in_=pst[0:64, 0:4, :]), reads=[pstb], writes=[qTn[1]])
            em.op("dve", lambda e: e.tensor_copy(out=qTr[0][:].rearrange("p (r t) -> p r t", r=4), in_=pst[0:64, 4:8, :]), reads=[pstb], writes=[qTr[1]])

        def job_scores(job):
            kT, c, q_sb, extra = job["kT"], job["c"], job["q"], job["extra"]
            k = st["n_s"] % 2
            st["n_s"] += 1
            ps, psb = psum[k]
            job["ps"] = (ps, psb); job["eT"] = eTs[k]
            nm = len(extra)
            em.op("pe", lambda e: e.matmul(ps[:, :], lhsT=kT[0][:, c * 128:(c + 1) * 128], rhs=q_sb[0][:, :], start=True, stop=(nm == 0)),
                  reads=[kT[1], q_sb[1]], writes=[psb], inc=(nm == 0))
            for kk_, (l_ap, l_b, r_ap, r_b) in enumerate(extra):
                em.op("pe", lambda e, l_ap=l_ap, r_ap=r_ap, kk_=kk_: e.matmul(ps[:, :], lhsT=l_ap, rhs=r_ap, start=False, stop=(kk_ == nm - 1)),
                      reads=[l_b, r_b], writes=[psb], inc=(kk_ == nm - 1))

        def job_finish(job):
            ps, psb = job["ps"]; eT = job["eT"]; po = job["po"]
            em.op("act", lambda e: e.activation(out=eT[0][:], in_=ps[:, :], func=AF.Exp, scale=0.125), reads=[psb], writes=[eT[1]])
            em.op("pe", lambda e: e.matmul(po[0][0:65, :], lhsT=job["v"], rhs=eT[0][:], start=job["first"], stop=job["last"]),
                  reads=[job["vb"], eT[1]], writes=[po[1]], inc=True)
            if job.get("imp") is not None:
                jc, njc = job["imp"]
                for r in range(4):
                    em.op("pe", lambda e, r=r: e.matmul(psum[5][0][:, r * 128:(r + 1) * 128], lhsT=eT[0][:, r * 128:(r + 1) * 128], rhs=c2s[0][:, jc, :],
                                                        start=(jc == 0), stop=(jc == njc - 1)), reads=[eT[1], c2s[1]], writes=[psum[5][1]], inc=(r == 3))
            if job.get("after") is not None:
                job["after"]()

        def finish(po, br, first_branch, Q):
            F = FS[st["n_f"] % 3]
            st["n_f"] += 1
            oT_sb, otm, rz, wz = F["oT_sb"], F["otm"], F["rz"], F["wz"]
            acc, gts = Q["acc"], Q["gts"]
            em.op("act", lambda e: e.copy(out=oT_sb[0][:], in_=po[0][0:65, :]), reads=[po[1]], writes=[oT_sb[1]])
            p6, p6b = psum[6]
            for r in range(4):
                em.op("pe", lambda e, r=r: e.transpose(out=p6[:, r * 65:(r + 1) * 65], in_=oT_sb[0][:, r * 128:(r + 1) * 128], identity=ident_f[0][0:65, 0:65]),
                      reads=[oT_sb[1], ident_f[1]], writes=[p6b], inc=(r == 3))
            em.op("dve", lambda e: e.tensor_copy(out=otm[0][:], in_=p6[:, 0:260].rearrange("p (r d) -> p r d", r=4)), reads=[p6b], writes=[otm[1]])
            em.op("dve", lambda e: e.tensor_scalar(out=rz[0][:], in0=otm[0][:, :, 64], scalar1=1e-30, scalar2=None, op0=ALU.max), reads=[otm[1]], writes=[rz[1]])
            em.op("dve", lambda e: e.reciprocal(out=rz[0][:], in_=rz[0][:]), reads=[rz[1]], writes=[rz[1]])
            em.op("dve", lambda e: e.tensor_tensor(out=wz[0][:], in0=rz[0][:], in1=gts[0][:, br:12:3], op=ALU.mult), reads=[rz[1], gts[1]], writes=[wz[1]])
            for r in range(4):
                if first_branch:
                    em.op("dve", lambda e, r=r: e.tensor_scalar(out=acc[0][:, r * 64:(r + 1) * 64], in0=otm[0][:, r, 0:64], scalar1=wz[0][:, r:r + 1], scalar2=None, op0=ALU.mult),
                          reads=[otm[1], wz[1]], writes=[acc[1]])
                else:
                    em.op("dve", lambda e, r=r: e.scalar_tensor_tensor(out=acc[0][:, r * 64:(r + 1) * 64], in0=otm[0][:, r, 0:64], scalar=wz[0][:, r:r + 1],
                                                                     in1=acc[0][:, r * 64:(r + 1) * 64], op0=ALU.mult, op1=ALU.add),
                          reads=[otm[1], wz[1], acc[1]], writes=[acc[1]])
            return F

        def topk(i, F):
            rz = F["rz"]
            pimp = psum[5]
            for r in range(4):
                if r == 0:
                    em.op("dve", lambda e: e.tensor_scalar(out=imp[0][:], in0=pimp[0][:, 0:128], scalar1=rz[0][:, 0:1], scalar2=None, op0=ALU.mult),
                          reads=[pimp[1], rz[1]], writes=[imp[1]])
                else:
                    em.op("dve", lambda e, r=r: e.scalar_tensor_tensor(out=imp[0][:], in0=pimp[0][:, r * 128:(r + 1) * 128], scalar=rz[0][:, r:r + 1], in1=imp[0][:],
                                                                       op0=ALU.mult, op1=ALU.add), reads=[pimp[1], rz[1], imp[1]], writes=[imp[1]])
            o0 = 126 - 2 * i
            em.op("dve", lambda e: e.tensor_tensor(out=imp[0][:], in0=imp[0][:], in1=keepb[0][:, o0:o0 + 128], op=ALU.mult), reads=[imp[1], keepb[1]], writes=[imp[1]])
            em.op("dve", lambda e: e.tensor_tensor(out=imp[0][:], in0=imp[0][:], in1=addb[0][:, o0:o0 + 128], op=ALU.add), reads=[imp[1], addb[1]], writes=[imp[1]])
            em.op("dve", lambda e: e.memset(imp[0][:, 0:1], 1000.0), reads=[imp[1]], writes=[imp[1]])
            em.op("dve", lambda e: e.max(out=mx[0][:], in_=imp[0][:]), reads=[imp[1]], writes=[mx[1]])
            em.op("dve", lambda e: e.match_replace(out=rp[0][:], in_to_replace=mx[0][:], in_values=imp[0][:], imm_value=-1e30), reads=[imp[1], mx[1]], writes=[rp[1]])
            em.op("dve", lambda e: e.max(out=mx[0][:], in_=rp[0][:]), reads=[rp[1]], writes=[mx[1]])
            em.op("dve", lambda e: e.tensor_reduce(out=thr[0][:], in_=mx[0][:], axis=AX.X, op=ALU.min), reads=[mx[1]], writes=[thr[1]])
            em.op("dve", lambda e: e.tensor_scalar(out=rp[0][:], in0=imp[0][:], scalar1=thr[0][:, 0:1], scalar2=None, op0=ALU.is_ge), reads=[imp[1], thr[1]], writes=[rp[1]])
            em.op("dve", lambda e: e.tensor_scalar(out=nsel[0][:], in0=rp[0][:], scalar1=-1.0, scalar2=-NEG, op0=ALU.add, op1=ALU.mult), reads=[rp[1]], writes=[nsel[1]])
            em.op("pe", lambda e: e.transpose(out=pst[:, 0, :], in_=nsel[0][:], identity=ident_bf[0][:]), reads=[nsel[1], ident_bf[1]], writes=[pstb])
            em.op("dve", lambda e: e.tensor_copy(out=nselT4[0][:].rearrange("p (r t) -> p r t", r=4), in_=pst[:, 0:1, :].to_broadcast([128, 4, 128])),
                  reads=[pstb], writes=[nselT4[1]])

        prep_a(0)
        prep_b(0)
        for i in range(NQ):
            t0 = i * 128
            Q = QS[i % 2]
            qTn, qTr = Q["qTn"], Q["qTr"]
            jobs = []
            njc = (8 * i + 6) // 128 + 1
            for jc in range(njc):
                d = 8 * i - 128 * jc
                extra = []
                if 0 <= d <= 128:
                    extra.append((ident_bf[0][:], ident_bf[1], cmpmask[0][:, d // 8, :], cmpmask[1]))
                jobs.append(dict(kT=kcT, c=jc, q=qTn, extra=extra, v=vc[0][:, jc, :], vb=vc[1], po=psum[2], first=(jc == 0), last=(jc == njc - 1),
                                 imp=(jc, njc)))

            def after_cmp(i=i, Q=Q):
                F = finish(psum[2], 0, True, Q)
                topk(i, F)
            jobs[-1]["after"] = after_cmp
            cl = max(0, i - 4)
            for c in range(cl, i + 1):
                extra = []
                if c == i:
                    extra.append((ident_bf[0][:], ident_bf[1], causneg[0][:], causneg[1]))
                if c == i - 4:
                    extra.append((ident_bf[0][:], ident_bf[1], anticaus[0][:], anticaus[1]))
                jobs.append(dict(kT=kTw, c=c, q=qTr, extra=extra, v=vw[0][:, c, :], vb=vw[1], po=psum[4], first=(c == cl), last=(c == i)))
            jobs[-1]["after"] = (lambda Q=Q: finish(psum[4], 2, False, Q))
            for c in range(i + 1):
                extra = [(ebig[0][:, c * 128:(c + 1) * 128], ebig[1], nselT4[0][:], nselT4[1])]
                if c == i:
                    extra.append((ident_bf[0][:], ident_bf[1], causneg[0][:], causneg[1]))
                jobs.append(dict(kT=kTs, c=c, q=qTr, extra=extra, v=vs[0][:, c, :], vb=vs[1], po=psum[3], first=(c == 0), last=(c == i)))

            def after_sel(i=i, Q=Q, t0=t0):
                finish(psum[3], 1, False, Q)
                em.dyn_store(y_dram[t0:t0 + 128, YOFF:YOFF + 256], Q["acc"][0][:], [Q["acc"][1]])
            jobs[-1]["after"] = after_sel
            job_scores(jobs[0])
            if i + 1 < NQ:
                prep_a(i + 1)
            kb = max(0, len(jobs) - 3)
            for k in range(len(jobs)):
                if k + 1 < len(jobs):
                    job_scores(jobs[k + 1])
                if k == kb and i + 1 < NQ:
                    prep_b(i + 1)
                job_finish(jobs[k])


def stage_merge(em, NTOK, x_rows, y_rows, prm, cst, x1_dram, psum, halo_flag=None):
    nc = em.nc
    ident = cst["ident_bf"]
    with ExitStack() as es:
        g_sb = mkT(es, nc, "mg_g", [128, 8]); ld(em, g_sb, prm["attn_g"])
        stage = [mkT(es, nc, "mg_stg%d" % k, [128, 1024]) for k in range(2)]
        wg = load_weight_bf16(em, es, "mg_wg", prm["w_gate"], 1024, 3072, scale_sb=g_sb, stage=stage)
        wbr = load_weight_bf16(em, es, "mg_wbr", prm["w_br"], 2048, 1024, stage=stage)
        wo = load_weight_bf16(em, es, "mg_wo", prm["w_out"], 1024, 1024, stage=stage)
        xts = [mkT(es, nc, "mg_x%d" % k, [128, 1024]) for k in range(2)]
        yt = mkT(es, nc, "mg_y", [128, 2048]); ytb = mkT(es, nc, "mg_yb", [128, 2048], BF16)
        scr = {"sq": mkT(es, nc, "mg_sq", [128, 1024], BF16), "ss": mkT(es, nc, "mg_ss", [128, 4]), "hb": mkT(es, nc, "mg_hb", [128, 1024], BF16)}
        hTs = [mkT(es, nc, "mg_hT%d" % k, [128, 8, 128], BF16) for k in range(2)]
        yTs = [mkT(es, nc, "mg_yT%d" % k, [128, 16, 128], BF16) for k in range(2)]
        gsb = mkT(es, nc, "mg_gs", [128, 3072])
        mrg = mkT(es, nc, "mg_m", [128, 1024]); tmp = mkT(es, nc, "mg_t", [128, 512]); mrgb = mkT(es, nc, "mg_mb", [128, 1024], BF16)
        mT = mkT(es, nc, "mg_mT", [128, 8, 128], BF16)
        x1 = mkT(es, nc, "mg_x1", [128, 1024])
        pst, pstb = psum[7]
        st = {"n_ps": 0}
        NTI = NTOK // 128

        def front(ti):
            t0 = ti * 128
            xt, hT, yT = xts[ti % 2], hTs[ti % 2], yTs[ti % 2]
            em.dyn_load(xt[0][:], [xt[1]], x_rows(t0))
            for (dst_fn, src) in y_rows(t0):
                em.dyn_load(dst_fn(yt[0]), [yt[1]], src)
            if ti == 0 and halo_flag is not None:
                em.op("dve", lambda e: e.tensor_scalar(out=xt[0][:], in0=xt[0][:], scalar1=halo_flag[0][:, 0:1], scalar2=None, op0=ALU.mult),
                      reads=[xt[1], halo_flag[1]], writes=[xt[1]])
                em.op("dve", lambda e: e.tensor_scalar(out=yt[0][:], in0=yt[0][:], scalar1=halo_flag[0][:, 0:1], scalar2=None, op0=ALU.mult),
                      reads=[yt[1], halo_flag[1]], writes=[yt[1]])
            rmsnorm_T(em, xt[0], xt[1], hT[0], hT[1], 0, scr, ident, psum[7])
            em.op("dve", lambda e: e.tensor_copy(out=ytb[0][:], in_=yt[0][:]), reads=[yt[1]], writes=[ytb[1]])
            for half in range(2):
                for kc in range(8):
                    em.op("pe", lambda e, kc=kc, half=half: e.transpose(out=pst[:, kc, :], in_=ytb[0][:, (half * 8 + kc) * 128:(half * 8 + kc + 1) * 128],
                                                                       identity=ident[0][:]), reads=[ytb[1], ident[1]], writes=[pstb], inc=(kc == 7))
                em.op("dve", lambda e, half=half: e.tensor_copy(out=yT[0][:, half * 8:(half + 1) * 8, :], in_=pst[:, :, :]), reads=[pstb], writes=[yT[1]])

        def back(ti):
            t0 = ti * 128
            xt, hT, yT = xts[ti % 2], hTs[ti % 2], yTs[ti % 2]
            for cb in range(6):
                ps, psb = psum[st["n_ps"] % 6]; st["n_ps"] += 1
                for kc in range(8):
                    em.op("pe", lambda e, kc=kc, cb=cb, ps=ps: e.matmul(ps[:, :], lhsT=hT[0][:, kc, :], rhs=wg[0][:, kc, cb * 512:(cb + 1) * 512],
                                                                       start=(kc == 0), stop=(kc == 7)), reads=[hT[1], wg[1]], writes=[psb], inc=(kc == 7))
                em.op("act", lambda e, cb=cb, ps=ps: e.activation(out=gsb[0][:, cb * 512:(cb + 1) * 512], in_=ps[:, :], func=AF.Sigmoid), reads=[psb], writes=[gsb[1]])
            for m, (k0, k1) in enumerate(((0, 4), (4, 8), (8, 16))):
                for nb in range(2):
                    ps, psb = psum[st["n_ps"] % 6]; st["n_ps"] += 1
                    for kc in range(k0, k1):
                        em.op("pe", lambda e, kc=kc, nb=nb, ps=ps: e.matmul(ps[:, :], lhsT=yT[0][:, kc, :], rhs=wbr[0][:, kc, nb * 512:(nb + 1) * 512],
                                                                           start=(kc == k0), stop=(kc == k1 - 1)), reads=[yT[1], wbr[1]], writes=[psb], inc=(kc == k1 - 1))
                    gsl = gsb[0][:, m * 1024 + nb * 512:m * 1024 + (nb + 1) * 512]
                    if m == 0:
                        em.op("dve", lambda e, nb=nb, ps=ps, gsl=gsl: e.tensor_tensor(out=mrg[0][:, nb * 512:(nb + 1) * 512], in0=ps[:, :], in1=gsl, op=ALU.mult),
                              reads=[psb, gsb[1]], writes=[mrg[1]])
                    else:
                        em.op("dve", lambda e, ps=ps, gsl=gsl: e.tensor_tensor(out=tmp[0][:], in0=ps[:, :], in1=gsl, op=ALU.mult), reads=[psb, gsb[1]], writes=[tmp[1]])
                        em.op("dve", lambda e, nb=nb: e.tensor_tensor(out=mrg[0][:, nb * 512:(nb + 1) * 512], in0=mrg[0][:, nb * 512:(nb + 1) * 512], in1=tmp[0][:], op=ALU.add),
                              reads=[mrg[1], tmp[1]], writes=[mrg[1]])
            em.op("act", lambda e: e.copy(out=mrgb[0][:], in_=mrg[0][:]), reads=[mrg[1]], writes=[mrgb[1]])
            for kc in range(8):
                em.op("pe", lambda e, kc=kc: e.transpose(out=pst[:, kc, :], in_=mrgb[0][:, kc * 128:(kc + 1) * 128], identity=ident[0][:]),
                      reads=[mrgb[1], ident[1]], writes=[pstb], inc=(kc == 7))
            em.op("act", lambda e: e.copy(out=mT[0][:], in_=pst[:, :, :]), reads=[pstb], writes=[mT[1]])
            for nb in range(2):
                ps, psb = psum[st["n_ps"] % 6]; st["n_ps"] += 1
                for kc in range(8):
                    em.op("pe", lambda e, kc=kc, nb=nb, ps=ps: e.matmul(ps[:, :], lhsT=mT[0][:, kc, :], rhs=wo[0][:, kc, nb * 512:(nb + 1) * 512],
                                                                       start=(kc == 0), stop=(kc == 7)), reads=[mT[1], wo[1]], writes=[psb], inc=(kc == 7))
                em.op("dve", lambda e, nb=nb, ps=ps: e.tensor_tensor(out=x1[0][:, nb * 512:(nb + 1) * 512], in0=ps[:, :], in1=xt[0][:, nb * 512:(nb + 1) * 512], op=ALU.add),
                      reads=[psb, xt[1]], writes=[x1[1]])
            em.dma(STORE_Q, x1_dram[t0:t0 + 128, :], x1[0][:], reads=[x1[1]])

        front(0)
        for ti in range(NTI):
            if ti + 1 < NTI:
                front(ti + 1)
            back(ti)


def stage_ffn(em, NTOK, x1_dram, prm, cst, out_rows, psum, FF=2816):
    nc = em.nc
    ident = cst["ident_bf"]
    NCB = FF // 128
    with ExitStack() as es:
        g_sb = mkT(es, nc, "ff_g", [128, 8]); ld(em, g_sb, prm["ffn_g"])
        stage = [mkT(es, nc, "ff_stg%d" % k, [128, 1024]) for k in range(2)]
        wup = load_weight_bf16(em, es, "ff_wup", prm["w_up"], 1024, 2 * FF, scale_sb=g_sb, stage=stage)
        wdn = load_weight_bf16(em, es, "ff_wdn", prm["w_down"], FF, 1024, stage=stage)
        fcw = mkT(es, nc, "ff_cw", [128, 2 * NCB, 3]); ld(em, fcw, prm["ffn_cw"])
        fcb = mkT(es, nc, "ff_cb", [128, 2 * NCB]); ld(em, fcb, prm["ffn_cb"])
        xts = [mkT(es, nc, "ff_x%d" % k, [128, 1024]) for k in range(2)]
        scr = {"sq": mkT(es, nc, "ff_sq", [128, 1024], BF16), "ss": mkT(es, nc, "ff_ss", [128, 4]), "hb": mkT(es, nc, "ff_hb", [128, 1024], BF16)}
        hTs = [mkT(es, nc, "ff_hT%d" % k, [128, 8, 256], BF16) for k in range(2)]
        ucar = mkT(es, nc, "ff_car", [128, 2 * NCB, 2])
        em.op("dve", lambda e: e.memset(ucar[0][:], 0.0), writes=[ucar[1]])
        ubuf = [mkT(es, nc, "ff_ub%d" % k, [128, 258]) for k in range(4)]
        ubufc = [Buf("ff_ubc%d" % k) for k in range(4)]
        acc = [mkT(es, nc, "ff_acc%d" % k, [128, 256]) for k in range(4)]
        gact = [mkT(es, nc, "ff_ga%d" % k, [128, 256]) for k in range(2)]
        ptmp = [mkT(es, nc, "ff_pt%d" % k, [128, 256]) for k in range(2)]
        actT = mkT(es, nc, "ff_aT", [128, NCB, 256], BF16)
        xo = mkT(es, nc, "ff_xo", [128, 1024])
        xr = mkT(es, nc, "ff_xr", [128, 1024])
        st = {"n_ps": 0, "n_ub": 0}
        segs = [(0, 128, True)] + [(128 + k * 256, 256, False) for k in range((NTOK - 128) // 256)]

        def front(si):
            s0, ntok, halo = segs[si]
            hT = hTs[si % 2]
            for k in range(ntok // 128):
                em.dma("sp", xts[k][0][:], x1_dram[s0 + k * 128:s0 + (k + 1) * 128, :], writes=[xts[k][1]])
                rmsnorm_T(em, xts[k][0], xts[k][1], hT[0], hT[1], k * 128, scr, ident, psum[7])

        def back(si):
            s0, ntok, halo = segs[si]
            hT = hTs[si % 2]
            nt = ntok // 128
            for cb in range(NCB):
                blks = (cb, cb + NCB)
                pss = []
                for gi, blk in enumerate(blks):
                    ps, psb = psum[st["n_ps"] % 4]; st["n_ps"] += 1
                    pss.append((ps, psb))
                    for kc in range(8):
                        em.op("pe", lambda e, kc=kc, blk=blk, ps=ps: e.matmul(ps[:, 0:ntok], lhsT=wup[0][:, kc, blk * 128:(blk + 1) * 128], rhs=hT[0][:, kc, 0:ntok],
                                                                             start=(kc == 0), stop=(kc == 7)), reads=[hT[1], wup[1]], writes=[psb], inc=(kc == 7))
                r = st["n_ub"] % 2; st["n_ub"] += 1
                ubs = [ubuf[2 * r], ubuf[2 * r + 1]]
                ubcs = [ubufc[2 * r], ubufc[2 * r + 1]]
                acs = [acc[2 * r], acc[2 * r + 1]]
                for gi, blk in enumerate(blks):
                    em.op("dve", lambda e, blk=blk, gi=gi: e.tensor_copy(out=ubs[gi][0][:, 0:2], in_=ucar[0][:, blk, :]), reads=[ucar[1]], writes=[ubcs[gi]])
                for gi, blk in enumerate(blks):
                    ps, psb = pss[gi]
                    em.op("act", lambda e, ps=ps, gi=gi: e.copy(out=ubs[gi][0][:, 2:2 + ntok], in_=ps[:, 0:ntok]), reads=[psb], writes=[ubs[gi][1]])
                for gi, blk in enumerate(blks):
                    em.op("dve", lambda e, blk=blk, gi=gi: e.tensor_copy(out=ucar[0][:, blk, :], in_=ubs[gi][0][:, ntok:ntok + 2]), reads=[ubs[gi][1], ubcs[gi]], writes=[ucar[1]])
                if halo:
                    continue
                for k in range(3):
                    for gi, blk in enumerate(blks):
                        eng = "dve"
                        ac, ub, ubc = acs[gi], ubs[gi], ubcs[gi]
                        if k == 0:
                            em.op(eng, lambda e, blk=blk, ub=ub, ac=ac: e.tensor_scalar(out=ac[0][:, 0:ntok], in0=ub[0][:, 0:ntok], scalar1=fcw[0][:, blk, 0:1], scalar2=None, op0=ALU.mult),
                                  reads=[ub[1], ubc, fcw[1]], writes=[ac[1]])
                        elif eng == "dve":
                            em.op(eng, lambda e, blk=blk, ub=ub, ac=ac, k=k: e.scalar_tensor_tensor(out=ac[0][:, 0:ntok], in0=ub[0][:, k:k + ntok], scalar=fcw[0][:, blk, k:k + 1],
                                                                                                  in1=ac[0][:, 0:ntok], op0=ALU.mult, op1=ALU.add),
                                  reads=[ub[1], ubc, fcw[1], ac[1]], writes=[ac[1]])
                        else:
                            tp = ptmp[r]
                            em.op(eng, lambda e, blk=blk, ub=ub, k=k, tp=tp: e.tensor_scalar(out=tp[0][:, 0:ntok], in0=ub[0][:, k:k + ntok], scalar1=fcw[0][:, blk, k:k + 1], scalar2=None, op0=ALU.mult),
                                  reads=[ub[1], ubc, fcw[1]], writes=[tp[1]])
                            em.op(eng, lambda e, ac=ac, tp=tp: e.tensor_tensor(out=ac[0][:, 0:ntok], in0=ac[0][:, 0:ntok], in1=tp[0][:, 0:ntok], op=ALU.add),
                                  reads=[ac[1], tp[1]], writes=[ac[1]])
                em.op("act", lambda e: e.activation(out=gact[r][0][:, 0:ntok], in_=acs[0][0][:, 0:ntok], func=AF.Silu, bias=fcb[0][:, blks[0]:blks[0] + 1]),
                      reads=[acs[0][1], fcb[1]], writes=[gact[r][1]])
                em.op("dve", lambda e: e.scalar_tensor_tensor(out=actT[0][:, cb, 0:ntok], in0=acs[1][0][:, 0:ntok], scalar=fcb[0][:, blks[1]:blks[1] + 1],
                                                              in1=gact[r][0][:, 0:ntok], op0=ALU.add, op1=ALU.mult),
                      reads=[acs[1][1], fcb[1], gact[r][1]], writes=[actT[1]])
            if halo:
                return
            for k in range(nt):
                em.dma("sp", xr[0][:], x1_dram[s0 + k * 128:s0 + (k + 1) * 128, :], writes=[xr[1]])
                for nb in range(2):
                    ps, psb = psum[4 + (st["n_ps"] % 2)]; st["n_ps"] += 1
                    for cb in range(NCB):
                        em.op("pe", lambda e, cb=cb, nb=nb, ps=ps, k=k: e.matmul(ps[:, :], lhsT=actT[0][:, cb, k * 128:(k + 1) * 128], rhs=wdn[0][:, cb, nb * 512:(nb + 1) * 512],
                                                                                start=(cb == 0), stop=(cb == NCB - 1)), reads=[actT[1], wdn[1]], writes=[psb], inc=(cb == NCB - 1))
                    em.op("dve", lambda e, nb=nb, ps=ps: e.tensor_tensor(out=xo[0][:, nb * 512:(nb + 1) * 512], in0=ps[:, :], in1=xr[0][:, nb * 512:(nb + 1) * 512], op=ALU.add),
                          reads=[psb, xr[1]], writes=[xo[1]])
                o0 = s0 - 128 + k * 128
                em.dyn_store(out_rows(o0), xo[0][:], [xo[1]])

        front(0)
        for si in range(len(segs)):
            if si + 1 < len(segs):
                front(si + 1)
            back(si)

import ml_dtypes

T_SEQ = 8192
NT_TM = 1556
NC_CM = 1408
OFF = dict(q=0, kv=512, gate=1280, hg_q=1304, hg_f=1816, hg_i=2328, hg_g=2840, z=3352, xbc=4376, dt=5912, merge=5928)


def group_cols(g):
    ar = np.arange
    kv = lambda br, kvi: OFF["kv"] + ((br * 2 + kvi) * 2 + g) * 64 + ar(64)
    tm = np.concatenate([
        OFF["q"] + g * 256 + ar(256), kv(1, 0), kv(1, 1), kv(2, 0), kv(2, 1),
        OFF["gate"] + g * 12 + ar(12), OFF["hg_i"] + g * 256 + ar(256), OFF["hg_g"] + g * 256 + ar(256),
        OFF["z"] + g * 512 + ar(512), OFF["dt"] + g * 8 + ar(8)])
    cm = np.concatenate([
        kv(0, 0), kv(0, 1), OFF["hg_q"] + g * 256 + ar(256), OFF["hg_f"] + g * 256 + ar(256),
        OFF["xbc"] + g * 512 + ar(512), OFF["xbc"] + 1024 + g * 128 + ar(128), OFF["xbc"] + 1280 + g * 128 + ar(128)])
    assert tm.size == NT_TM and cm.size == NC_CM
    return tm, cm


_CONST_CACHE = {}


def consts():
    if _CONST_CACHE:
        return _CONST_CACHE
    bf = ml_dtypes.bfloat16
    f32 = np.float32
    i = np.arange(128)
    c = {}
    c["ident_bf"] = np.eye(128).astype(bf)
    c["ident_f"] = np.eye(128).astype(f32)
    c["triu_f"] = (i[:, None] <= i[None, :]).astype(f32)
    c["ones_f"] = np.ones((128, 128), f32)
    c["mstrict_f"] = (i[:, None] > i[None, :]).astype(f32)
    c["mcausT_f"] = (i[None, :] >= i[:, None]).astype(f32)
    c["mask2_f"] = ((i[:, None] // 64 == i[None, :] // 64) & (i[:, None] <= i[None, :])).astype(f32)
    theta = np.float32(500000.0)
    invf = (theta ** (-np.arange(0, 16, 2, dtype=np.float32) / np.float32(16))).astype(f32)
    c["c_invf"] = np.broadcast_to(invf[None, :], (128, 8)).copy()
    NEG_ = -30000.0
    ds = list(range(0, 128, 8)) + [128]
    cm = np.zeros((128, 17, 4, 128), f32)
    for k, d in enumerate(ds):
        vis = (16 * (i[:, None] - d) + 31) <= i[None, :]
        cm[:, k, :, :] = np.where(vis, 0.0, NEG_)[:, None, :]
    c["c_cmpmask"] = cm.reshape(128, 17, 512).astype(bf)
    n_cmp = (T_SEQ - 32) // 16 + 1
    cs = np.arange(n_cmp) * 16
    s_start = np.arange(128) * 64
    ov = np.clip(np.minimum(cs[:, None] + 32, s_start[None, :] + 64) - np.maximum(cs[:, None], s_start[None, :]), 0, None) / 32.0
    c2s = np.zeros((512, 128), f32)
    c2s[:n_cmp] = ov
    c["c_c2s"] = np.ascontiguousarray(c2s.reshape(4, 128, 128).transpose(1, 0, 2)).astype(bf)
    keys = np.arange(T_SEQ)
    c["c_ebig"] = (i[:, None] == (keys[None, :] // 64)).astype(bf)
    caus = np.where(i[:, None] <= i[None, :], 0.0, NEG_)
    c["c_causneg"] = np.tile(caus, (1, 4)).astype(bf)
    anti = np.where(i[:, None] > i[None, :], 0.0, NEG_)
    c["c_anticaus"] = np.tile(anti, (1, 4)).astype(bf)
    u = np.arange(254) - 126
    curp = (i >= 64).astype(np.int64)[:, None]
    forced = (u[None, :] == curp) | (u[None, :] == curp - 1)
    invalid = u[None, :] > curp
    c["c_keep"] = (~forced & ~invalid).astype(f32)
    c["c_add"] = np.where(forced, 200.0 + u[None, :], np.where(invalid, -(300.0 + u[None, :]), 0.0)).astype(f32)
    _CONST_CACHE.update(c)
    return c


CONST_SB = ["ident_bf", "ident_f", "triu_f", "ones_f", "mstrict_f", "mcausT_f", "mask2_f"]
CONST_NSA = ["c_invf", "c_cmpmask", "c_c2s", "c_ebig", "c_causneg", "c_anticaus", "c_keep", "c_add"]
PRM_B = {
    "ns_gq": ([128, 256], F32), "ns_gk": ([128, 2, 64], F32), "ns_gk0": ([64, 1], F32), "ns_posT": ([128, 32], F32),
    "ns_w1kv": ([128, 32, 256], F32), "ns_w2k": ([128, 2, 64], F32), "ns_w2v": ([128, 2, 64], F32), "ns_pos": ([128, 64], I32),
    "hg_lbl": ([128, 2, 2], F32), "hg_lflag": ([128, 2], F32), "hg_ng": ([128, 128], F32),
    "m2_cw": ([128, 6, 4], F32), "m2_cb": ([128, 6], F32), "m2_dtb": ([128, 8], F32), "m2_alog": ([128, 8], F32),
    "m2_dsk": ([128, 8], F32), "m2_ng": ([128, 512], F32),
}
CONST_SHAPES = {
    "ident_bf": ([128, 128], BF16), "ident_f": ([128, 128], F32), "triu_f": ([128, 128], F32), "ones_f": ([128, 128], F32),
    "mstrict_f": ([128, 128], F32), "mcausT_f": ([128, 128], F32), "mask2_f": ([128, 128], F32),
    "c_invf": ([128, 8], F32), "c_cmpmask": ([128, 17, 512], BF16), "c_c2s": ([128, 4, 128], BF16), "c_ebig": ([128, T_SEQ], BF16),
    "c_causneg": ([128, 512], BF16), "c_anticaus": ([128, 512], BF16), "c_keep": ([128, 254], F32), "c_add": ([128, 254], F32),
}


def bcast(v, rows=128):
    v = np.asarray(v, np.float32).reshape(1, -1)
    return np.ascontiguousarray(np.broadcast_to(v, (rows, v.shape[1])))


def prep_B(inp, l, b, g, x_b):
    tm, cm = group_cols(g)
    w_in = inp["w_in"][l]
    m = {"x": (None if x_b is None else np.ascontiguousarray(x_b)), "g_attn": np.ascontiguousarray(inp["attn_norm_g"][l].reshape(8, 128).T),
         "w_tm": np.ascontiguousarray(w_in[:, tm]), "w_cm": np.ascontiguousarray(w_in[:, cm])}
    m.update(consts())
    m["ns_gq"] = bcast(np.tile(inp["nsa_q_norm_g"][l], 4))
    m["ns_gk"] = np.ascontiguousarray(np.broadcast_to(inp["nsa_k_norm_g"][l][1:3][None], (128, 2, 64))).astype(np.float32)
    m["ns_gk0"] = np.ascontiguousarray(inp["nsa_k_norm_g"][l][0].reshape(64, 1))
    m["ns_posT"] = np.ascontiguousarray(np.concatenate([inp["nsa_cmp_pos_k"][l].T, inp["nsa_cmp_pos_v"][l].T], 0))
    w1k = inp["nsa_cmp_k_w1"][l].reshape(32, 64, 256).transpose(1, 0, 2)
    w1v = inp["nsa_cmp_v_w1"][l].reshape(32, 64, 256).transpose(1, 0, 2)
    m["ns_w1kv"] = np.ascontiguousarray(np.concatenate([w1k, w1v], 0))
    m["ns_w2k"] = np.ascontiguousarray(inp["nsa_cmp_k_w2"][l].reshape(2, 128, 64).transpose(1, 0, 2))
    m["ns_w2v"] = np.ascontiguousarray(inp["nsa_cmp_v_w2"][l].reshape(2, 128, 64).transpose(1, 0, 2))
    m["ns_pos"] = np.ascontiguousarray(inp["positions"][b].reshape(64, 128).T.astype(np.int32))
    lbl = inp["hgrn_lb_logits"][:, g * 256:(g + 1) * 256].reshape(2, 2, 128)
    m["hg_lbl"] = np.ascontiguousarray(lbl.transpose(2, 1, 0))
    m["hg_ng"] = bcast(inp["hgrn_norm_g"][l])
    m["hg_lflag"] = np.full((128, 2), float(l), np.float32)
    chs = np.concatenate([g * 512 + np.arange(512), 1024 + g * 128 + np.arange(128), 1280 + g * 128 + np.arange(128)])
    m["m2_cw"] = np.ascontiguousarray(inp["m2_conv_w"][l][:, chs].reshape(4, 6, 128).transpose(2, 1, 0))
    m["m2_cb"] = np.ascontiguousarray(inp["m2_conv_b"][l][chs].reshape(6, 128).T)
    m["m2_dtb"] = bcast(inp["m2_dt_bias"][l][g * 8:(g + 1) * 8])
    m["m2_alog"] = bcast(inp["m2_a_log"][l][g * 8:(g + 1) * 8])
    m["m2_dsk"] = bcast(inp["m2_d_skip"][l][g * 8:(g + 1) * 8])
    m["m2_ng"] = bcast(inp["m2_norm_g"][l][g * 512:(g + 1) * 512])
    return m


def build_B(layer, stages=("proj", "m2", "hg", "nsa"), T=T_SEQ, debug_out=False):
    nc = bass.Bass("TRN2", target_bir_lowering=False)
    D = lambda name, shape, dt=F32, kind="ExternalInput": nc.dram_tensor(name, shape, dt, kind=kind).ap()
    x = D("x", [T, 1024]); g_attn = D("g_attn", [128, 8]); w_tm_d = D("w_tm", [1024, NT_TM]); w_cm_d = D("w_cm", [1024, NC_CM])
    cd = {k: D(k, *CONST_SHAPES[k]) for k in CONST_SHAPES}
    prm = {k: D(k, *PRM_B[k]) for k in PRM_B}
    prm.update({k: cd[k] for k in CONST_NSA})
    y = D("y", [T, 1024], kind="ExternalOutput")
    if debug_out:
        p_tm = D("p_tm", [T, NT_TM], kind="ExternalOutput"); p_cm = D("p_cm", [NC_CM, T], kind="ExternalOutput")
    else:
        p_tm = D("p_tm", [T, NT_TM], kind="Internal"); p_cm = D("p_cm", [NC_CM, T], kind="Internal")
    with ExitStack() as es:
        em = Em(nc, es)
        psum = [(es.enter_context(nc.psum_tensor("ps%d" % i, [128, 512], F32)), Buf()) for i in range(7)]
        psum.append((es.enter_context(nc.psum_tensor("pst", [128, 8, 128], BF16)), Buf()))
        cst = {}
        for k in CONST_SB:
            cst[k] = mkT(es, nc, "k_" + k, *CONST_SHAPES[k])
            ld(em, cst[k], cd[k])
        em.dyn_init(es)
        if "proj" in stages:
            with ExitStack() as es2:
                g_sb = mkT(es2, nc, "g_sb", [128, 8]); ld(em, g_sb, g_attn)
                w_tm = load_weight_bf16(em, es2, "w_tm_sb", w_tm_d, 1024, NT_TM, scale_sb=g_sb)
                w_cm = load_weight_bf16(em, es2, "w_cm_sb", w_cm_d, 1024, NC_CM, scale_sb=g_sb)
                stage_proj(em, T, x, g_sb, w_tm, w_cm, NT_TM, NC_CM, p_tm, p_cm, cst["ident_bf"], psum)
            em.drain()
        if "m2" in stages:
            stage_mamba(em, T, p_tm, p_cm, prm, cst, y, psum)
            em.drain()
        if "hg" in stages:
            stage_hgrn(em, T, layer, p_tm, p_cm, prm, cst, y, psum)
            em.drain()
        if "nsa" in stages:
            stage_nsa(em, T, p_tm, p_cm, prm, cst, y, psum)
        em.drain()
        print("build_B n_inst", em.n_inst, flush=True)
    return nc


PRM_C = {
    "attn_g": ([128, 8], F32), "w_gate": ([1024, 3072], F32), "w_br": ([2048, 1024], F32), "w_out": ([1024, 1024], F32),
    "ffn_g": ([128, 8], F32), "w_up": ([1024, 5632], F32), "w_down": ([2816, 1024], F32), "ffn_cw": ([128, 44, 3], F32), "ffn_cb": ([128, 44], F32),
}
NTOK_C = 4096 + 128
CSTAGES = "both"


def prep_C(inp, l, xh, yh):
    m = {"xh": xh, "yh": yh}
    m["ident_bf"] = consts()["ident_bf"]
    m["attn_g"] = np.ascontiguousarray(inp["attn_norm_g"][l].reshape(8, 128).T)
    m["w_gate"] = np.ascontiguousarray(inp["w_in"][l][:, OFF["merge"]:OFF["merge"] + 3072])
    m["w_br"] = np.ascontiguousarray(np.concatenate([inp["w_branch_nsa"][l], inp["w_branch_hgrn"][l], inp["w_branch_m2"][l]], 0))
    m["w_out"] = np.ascontiguousarray(inp["w_out"][l])
    m["ffn_g"] = np.ascontiguousarray(inp["ffn_norm_g"][l].reshape(8, 128).T)
    m["w_up"] = np.ascontiguousarray(inp["ffn_w_up"][l])
    m["w_down"] = np.ascontiguousarray(inp["ffn_w_down"][l])
    m["ffn_cw"] = np.ascontiguousarray(inp["ffn_conv_w"][l].reshape(3, 44, 128).transpose(2, 1, 0))
    m["ffn_cb"] = np.ascontiguousarray(inp["ffn_conv_b"][l].reshape(44, 128).T)
    return m


def build_C(NTOK=NTOK_C):
    nc = bass.Bass("TRN2", target_bir_lowering=False)
    D = lambda name, shape, dt=F32, kind="ExternalInput": nc.dram_tensor(name, shape, dt, kind=kind).ap()
    xh = D("xh", [NTOK, 1024]); yh = D("yh", [NTOK, 2048])
    idd = D("ident_bf", [128, 128], BF16)
    prm = {k: D(k, *PRM_C[k]) for k in PRM_C}
    out = D("out", [NTOK - 128, 1024], kind="ExternalOutput")
    x1d = D("x1d", [NTOK, 1024], kind="Internal")
    with ExitStack() as es:
        em = Em(nc, es)
        psum = [(es.enter_context(nc.psum_tensor("ps%d" % i, [128, 512], F32)), Buf()) for i in range(7)]
        psum.append((es.enter_context(nc.psum_tensor("pst", [128, 8, 128], BF16)), Buf()))
        cst = {"ident_bf": mkT(es, nc, "k_ident", [128, 128], BF16)}
        ld(em, cst["ident_bf"], idd)
        em.dyn_init(es)
        if CSTAGES in ("both", "merge"):
            stage_merge(em, NTOK, lambda t0: xh[t0:t0 + 128, :], lambda t0: [(lambda yt: yt[:, 0:2048], yh[t0:t0 + 128, :])], prm, cst, x1d, psum)
        em.drain()
        if CSTAGES in ("both", "ffn"):
            stage_ffn(em, NTOK, x1d, prm, cst, lambda o0: out[o0:o0 + 128, :], psum)
        em.drain()
    return nc


def prep_fused(inp, b, s):
    m = {"x": np.ascontiguousarray(inp["x"][b]), "halo_flag": np.full((128, 1), float(s), np.float32)}
    m.update(consts())
    for l in range(2):
        mb = prep_B(inp, l, b, s, None)
        mc = prep_C(inp, l, None, None)
        for k, v in list(mb.items()) + list(mc.items()):
            if k in CONST_SHAPES or k in ("x", "xh", "yh") or v is None:
                continue
            m["%s_l%d" % (k, l)] = v
    return m


class YDst:
    def __init__(self, fams, halos, par, ntile_half):
        self.fams = fams; self.halos = halos; self.par = par; self.nth = ntile_half

    def __getitem__(self, key):
        rows, cols = key
        i = rows.start // 128
        h, ti = i // self.nth, i % self.nth + 1
        c0, w = cols.start, cols.stop - cols.start
        if c0 < 256:
            fam, coff = "n", c0
        elif c0 < 512:
            fam, coff = "h", c0 - 256
        else:
            fam, coff = "m", c0 - 512
        out = [self.fams[fam][ti][h, bass.ds(self.par, 1)][0, :, coff:coff + w]]
        if i == self.nth - 1:
            out.append(self.halos[fam][bass.ds(self.par, 1)][0, :, coff:coff + w])
        return out


def build_fused(T=T_SEQ, n_layers=2):
    nc = bass.Bass("TRN2", target_bir_lowering=False, num_devices=8)
    D = lambda name, shape, dt=F32, kind="ExternalInput", **kw: nc.dram_tensor(name, shape, dt, kind=kind, **kw).ap()
    SH = lambda name, shape: D(name, shape, kind="Internal", addr_space="Shared")
    HALF = T // 2
    NTH = HALF // 128
    x_ext = D("x", [T, 1024])
    hflag_d = D("halo_flag", [128, 1])
    cd = {k: D(k, *CONST_SHAPES[k]) for k in CONST_SHAPES}
    out_ext = D("out", [HALF, 1024], kind="ExternalOutput")
    XT = [[None] + [SH("xt%d_%d" % (bf, ti), [2, 128, 1024]) for ti in range(1, NTH + 1)] for bf in range(2)]
    XH = [SH("xh%d" % bf, [2, 128, 1024]) for bf in range(2)]
    WID = {"m": 512, "h": 256, "n": 256}
    FAM = {f: [None] + [SH("y%s_%d" % (f, ti), [2, 2, 128, WID[f]]) for ti in range(1, NTH + 1)] for f in WID}
    FAMH = {f: SH("y%s_halo" % f, [2, 128, WID[f]]) for f in WID}
    p_tm = D("p_tm", [T, NT_TM], kind="Internal"); p_cm = D("p_cm", [NC_CM, T], kind="Internal")
    x1d = D("x1d", [HALF + 128, 1024], kind="Internal")
    LP = []
    for l in range(n_layers):
        d = {"g_attn": D("g_attn_l%d" % l, [128, 8]), "w_tm": D("w_tm_l%d" % l, [1024, NT_TM]), "w_cm": D("w_cm_l%d" % l, [1024, NC_CM])}
        d.update({k: D("%s_l%d" % (k, l), *PRM_B[k]) for k in PRM_B})
        d.update({k: D("%s_l%d" % (k, l), *PRM_C[k]) for k in PRM_C})
        d.update({k: cd[k] for k in CONST_NSA})
        LP.append(d)
    with ExitStack() as es:
        em = Em(nc, es)
        psum = [(es.enter_context(nc.psum_tensor("ps%d" % i, [128, 512], F32)), Buf()) for i in range(7)]
        psum.append((es.enter_context(nc.psum_tensor("pst", [128, 8, 128], BF16)), Buf()))
        cst = {}
        for k in CONST_SB:
            cst[k] = mkT(es, nc, "k_" + k, *CONST_SHAPES[k])
            ld(em, cst[k], cd[k])
        hflag = mkT(es, nc, "hflag", [128, 1]); ld(em, hflag, hflag_d)
        for i in range(T // 128):
            h, ti = i // NTH, i % NTH + 1
            em.dma("sp", XT[0][ti][h, :, :], x_ext[i * 128:(i + 1) * 128, :])
        em.dma("sp", XH[0][0, :, :], x_ext[HALF - 128:HALF, :])
        em.drain()
        nc.all_core_barrier()
        par = nc.sync.partition_id() % 2
        par_st = (nc.gpsimd.partition_id() % 2) if STORE_Q == "pool" else par

        def x_tile_static(bf):
            def f(t0):
                i = t0 // 128
                return XT[bf][i % NTH + 1][i // NTH, :, :]
            return f

        for l in range(n_layers):
            prm = LP[l]
            bf, bn = l % 2, (l + 1) % 2
            y_dst = YDst(FAM, FAMH, par_st, NTH)
            with ExitStack() as es2:
                g_sb = mkT(es2, nc, "g_sb%d" % l, [128, 8]); ld(em, g_sb, prm["g_attn"])
                w_tm = load_weight_bf16(em, es2, "w_tm_sb%d" % l, prm["w_tm"], 1024, NT_TM, scale_sb=g_sb)
                w_cm = load_weight_bf16(em, es2, "w_cm_sb%d" % l, prm["w_cm"], 1024, NC_CM, scale_sb=g_sb)
                stage_proj(em, T, x_tile_static(bf), g_sb, w_tm, w_cm, NT_TM, NC_CM, p_tm, p_cm, cst["ident_bf"], psum)
                em.drain()
            stage_mamba(em, T, p_tm, p_cm, prm, cst, y_dst, psum)
            em.drain()
            stage_hgrn(em, T, l, p_tm, p_cm, prm, cst, y_dst, psum)
            em.drain()
            stage_nsa(em, T, p_tm, p_cm, prm, cst, y_dst, psum)
            em.drain()
            nc.all_core_barrier()

            def x_rows(t0, bf=bf):
                ti = t0 // 128
                if ti == 0:
                    return XH[bf][0, :, :]
                return XT[bf][ti][bass.ds(par, 1)][0, :, :]

            def y_rows(t0):
                ti = t0 // 128
                res = []
                for f, c0 in (("n", 0), ("h", 512), ("m", 1024)):
                    w = WID[f]
                    dst_fn = (lambda yt, c0=c0, w=w: yt[:, c0:c0 + 2 * w].rearrange("p (g c) -> p g c", g=2))
                    if ti == 0:
                        src = FAMH[f].rearrange("g r c -> r g c")
                    else:
                        src = FAM[f][ti][bass.ds(par, 1)][0].rearrange("g r c -> r g c")
                    res.append((dst_fn, src))
                return res
            stage_merge(em, HALF + 128, x_rows, y_rows, prm, cst, x1d, psum, halo_flag=hflag)
            em.drain()
            if l == n_layers - 1:
                out_rows = lambda o0: out_ext[o0:o0 + 128, :]
            else:
                def out_rows(o0, bn=bn):
                    ti = o0 // 128 + 1
                    r = [XT[bn][ti][bass.ds(par_st, 1)][0, :, :]]
                    if ti == NTH:
                        r.append(XH[bn][bass.ds(par_st, 1)][0, :, :])
                    return r
            stage_ffn(em, HALF + 128, x1d, prm, cst, out_rows, psum)
            em.drain()
            if l != n_layers - 1:
                nc.all_core_barrier()
        print("build_fused n_inst", em.n_inst, flush=True)
    return nc


_NC_CACHE = {}


def kernel(**inputs):
    inp = {k: np.asarray(v) for k, v in inputs.items()}
    B, T, Dm = inp["x"].shape
    if "F" not in _NC_CACHE:
        _NC_CACHE["F"] = build_fused()
    maps = [prep_fused(inp, b, s) for b in range(B) for s in range(2)]
    res = run_bass_kernel_spmd(_NC_CACHE["F"], maps, core_ids=list(range(8)))
    out = np.empty((B, T, Dm), np.float32)
    for b in range(B):
        for s in range(2):
            out[b, s * (T // 2):(s + 1) * (T // 2)] = np.asarray(res.results[2 * b + s]["out"])
    return out
```

```python
import numpy as np
import concourse.bass as bass
import concourse.mybir as mybir
from concourse.bass_utils import run_bass_kernel_spmd
from contextlib import ExitStack

F32 = mybir.dt.float32
BF16 = mybir.dt.bfloat16
I32 = mybir.dt.int32
AF = mybir.ActivationFunctionType
ALU = mybir.AluOpType
AX = mybir.AxisListType

SAME_ENG_SYNC = True
STORE_Q = "sp"


_UQ = [0]


def SBT(nc, name, shape, dt):
    _UQ[0] += 1
    return nc.sbuf_tensor("%s_u%d" % (name, _UQ[0]), shape, dt)


class _Keep:
    def __init__(self, es):
        self.es = es

    def __enter__(self):
        return self.es

    def __exit__(self, *a):
        return False


class Buf:
    __slots__ = ("name", "w", "r")

    def __init__(self, name=""):
        self.name = name
        self.w = None
        self.r = {}


class Em:
    def __init__(self, nc, es, n_dma_sems=12):
        self.nc = nc
        self.es = es
        self.eng = dict(pe=nc.tensor, act=nc.scalar, dve=nc.vector, pool=nc.gpsimd, sp=nc.sync)
        self.semh = {}
        self.cnt = {}
        self.seen = {k: {} for k in self.eng}
        for k in ("pe", "act", "dve", "pool"):
            self.semh[k] = es.enter_context(nc.semaphore("c_" + k))
            self.cnt[k] = 0
        self.dring = {}
        for q in ("sp", "pool", "act"):
            ring = []
            for i in range({"sp": 16, "pool": 48, "act": 2}[q]):
                key = "d_%s%d" % (q, i)
                self.semh[key] = es.enter_context(nc.semaphore(key))
                ring.append(key)
            self.dring[q] = [ring, 0, {k: 0 for k in ring}]
        self.n_inst = 0

    def _wait(self, eng, toks):
        E = self.eng[eng]
        seen = self.seen[eng]
        for (k, v) in toks:
            if k == eng:
                if eng == "pe" or not SAME_ENG_SYNC:
                    continue
                if v > self.cnt[eng]:
                    continue
            if seen.get(k, 0) < v:
                E.wait_ge(self.semh[k], v)
                seen[k] = v
                self.n_inst += 1

    def _deps(self, reads, writes):
        toks = []
        for b in reads:
            if b.w is not None:
                toks.append(b.w)
        for b in writes:
            if b.w is not None:
                toks.append(b.w)
            toks.extend(b.r.items())
        return toks

    def op(self, eng, fn, reads=(), writes=(), inc=True):
        self._wait(eng, self._deps(reads, writes))
        ins = fn(self.eng[eng])
        self.n_inst += 1
        if inc:
            self.cnt[eng] += 1
            ins.then_inc(self.semh[eng], 1)
            tok = (eng, self.cnt[eng])
        else:
            tok = (eng, self.cnt[eng] + 1)
        for b in reads:
            if b.r.get(tok[0], 0) < tok[1]:
                b.r[tok[0]] = tok[1]
        for b in writes:
            b.w = tok
            b.r = {}
        return ins

    def dma(self, q, out, in_, reads=(), writes=(), **kw):
        ring, idx, tgt = self.dring[q]
        key = ring[idx]
        if q == "pool" and tgt[key] > 0 and not any(self.seen[e].get(key, 0) >= tgt[key] for e in self.seen):
            return self.dma("sp", out, in_, reads=reads, writes=writes, **kw)
        self.dring[q][1] = (idx + 1) % len(ring)
        toks = self._deps(reads, writes)
        if tgt[key] > 0 and q != "pool":
            toks.append((key, tgt[key]))
        self._wait(q, toks)
        ins = self.eng[q].dma_start(out=out, in_=in_, **kw)
        self.n_inst += 1
        tgt[key] += 16
        ins.then_inc(self.semh[key], 16)
        tok = (key, tgt[key])
        for b in reads:
            b.r[key] = tok[1]
        for b in writes:
            b.w = tok
            b.r = {}
        return ins

    def drain(self, engs=("sp", "pool", "act", "pe", "dve")):
        toks = [(k, self.cnt[k]) for k in ("pe", "act", "dve", "pool") if self.cnt[k] > 0]
        for q in self.dring:
            for key, t in self.dring[q][2].items():
                if t > 0:
                    toks.append((key, t))
        for e in engs:
            self._wait(e, toks)


    def dyn_init(self, es, width=1024, nslots=4):
        pass

    def dyn_store(self, dram_aps, src_ap, src_bufs):
        if not isinstance(dram_aps, (list, tuple)):
            dram_aps = [dram_aps]
        for d in dram_aps:
            self.dma(STORE_Q, d, src_ap, reads=list(src_bufs))

    def dyn_load(self, dst_ap, dst_bufs, dram_ap):
        self.dma("sp", dst_ap, dram_ap, writes=list(dst_bufs))

STOP = ''

def load_weight_bf16(em, es, name, w_dram, K, N, scale_sb=None, stage_cols=1024, stage=None):
    nc = em.nc
    KC = K // 128
    w_sb = es.enter_context(SBT(nc, name, [128, KC, N], BF16))
    wb = Buf(name)
    if stage is None:
        stg = [es.enter_context(SBT(nc, name + "_stg%d" % i, [128, stage_cols], F32)) for i in range(2)]
        stgb = [Buf(name + "_stg%d" % i) for i in range(2)]
    else:
        stg = [t[0] for t in stage]
        stgb = [t[1] for t in stage]
    i = 0
    for kc in range(KC):
        for c0 in range(0, N, stage_cols):
            c1 = min(N, c0 + stage_cols)
            s, sb = stg[i % 2], stgb[i % 2]
            em.dma("sp", s[:, : c1 - c0], w_dram[kc * 128:(kc + 1) * 128, c0:c1], writes=[sb])
            if scale_sb is not None:
                sc = scale_sb[0][:, kc:kc + 1]
                em.op("pool", lambda e, s=s, sc=sc, c0=c0, c1=c1, kc=kc: e.tensor_scalar(
                    out=w_sb[:, kc, c0:c1], in0=s[:, : c1 - c0], scalar1=sc, scalar2=None, op0=ALU.mult),
                    reads=[sb, scale_sb[1]], writes=[wb])
            else:
                em.op("pool", lambda e, s=s, c0=c0, c1=c1, kc=kc: e.tensor_copy(
                    out=w_sb[:, kc, c0:c1], in_=s[:, : c1 - c0]), reads=[sb], writes=[wb])
            i += 1
    return w_sb, wb


def rmsnorm_T(em, x_tile, xb, hT, hTb, col0, scr, ident, ps_t, eps=1e-6, D=1024):
    KC = D // 128
    sq, sqb = scr["sq"]
    ss, ssb = scr["ss"]
    hb, hbb = scr["hb"]
    em.op("act", lambda e: e.activation(out=sq[:], in_=x_tile[:], func=AF.Square, accum_out=ss[:, 0:1]),
          reads=[xb], writes=[sqb, ssb])
    em.op("dve", lambda e: e.tensor_scalar(out=ss[:, 1:2], in0=ss[:, 0:1], scalar1=1.0 / D, scalar2=eps,
                                           op0=ALU.mult, op1=ALU.add), reads=[ssb], writes=[ssb])
    em.op("act", lambda e: e.sqrt(out=ss[:, 3:4], in_=ss[:, 1:2]), reads=[ssb], writes=[ssb])
    em.op("dve", lambda e: e.reciprocal(out=ss[:, 2:3], in_=ss[:, 3:4]), reads=[ssb], writes=[ssb])
    em.op("dve", lambda e: e.tensor_scalar(out=hb[:], in0=x_tile[:], scalar1=ss[:, 2:3], scalar2=None,
                                           op0=ALU.mult), reads=[xb, ssb], writes=[hbb])
    pt, ptb = ps_t
    for kc in range(KC):
        em.op("pe", lambda e, kc=kc: e.transpose(out=pt[:, kc, :], in_=hb[:, kc * 128:(kc + 1) * 128],
                                                 identity=ident[0][:]),
              reads=[hbb, ident[1]], writes=[ptb], inc=(kc == KC - 1))
    em.op("act", lambda e: e.copy(out=hT[:, :, col0:col0 + 128], in_=pt[:]), reads=[ptb], writes=[hTb])


def stage_proj(em, T, x_dram, g_sb, w_tm, w_cm, NT, NC, out_tm, out_cm, ident, psum):
    nc = em.nc
    KC = 8
    with ExitStack() as es:
        xs = [(es.enter_context(SBT(nc, "pj_x%d" % i, [128, 1024], F32)), Buf()) for i in range(3)]
        scr = {
            "sq": (es.enter_context(SBT(nc, "pj_sq", [128, 1024], F32)), Buf()),
            "ss": (es.enter_context(SBT(nc, "pj_ss", [128, 4], F32)), Buf()),
            "hb": (es.enter_context(SBT(nc, "pj_hb", [128, 1024], BF16)), Buf()),
        }
        hTs = [(es.enter_context(SBT(nc, "pj_hT%d" % i, [128, KC, 512], BF16)), Buf()) for i in range(2)]
        otm = [(es.enter_context(SBT(nc, "pj_otm%d" % i, [128, NT], F32)), Buf()) for i in range(2)]
        ocm = [(es.enter_context(SBT(nc, "pj_ocm%d" % i, [128, 512], F32)), Buf()) for i in range(3)]
        n_x = 0
        n_otm = 0
        n_ocm = 0
        n_ps = 0
        NST = T // 512
        def prep_st(st):
            nonlocal n_x
            hT, hTb = hTs[st % 2]
            for sub in range(4):
                t0 = st * 512 + sub * 128
                x_t, x_b = xs[n_x % 3]
                n_x += 1
                em.dma("sp", x_t[:], (x_dram(t0) if callable(x_dram) else x_dram[t0:t0 + 128, :]), writes=[x_b])
                rmsnorm_T(em, x_t, x_b, hT, hTb, sub * 128, scr, ident, psum[7])

        prep_st(0)
        for st in range(NST):
            hT, hTb = hTs[st % 2]
            if st + 1 < NST:
                prep_st(st + 1)
            if STOP == "n":
                continue
            for sub in range(4):
                t0 = st * 512 + sub * 128
                o_t, o_b = otm[n_otm % 2]
                n_otm += 1
                for c0 in range(0, NT, 512):
                    c1 = min(NT, c0 + 512)
                    ps, psb = psum[n_ps % 6]
                    n_ps += 1
                    for kc in range(KC):
                        em.op("pe", lambda e, kc=kc, ps=ps, c0=c0, c1=c1, sub=sub: e.matmul(
                            ps[:, : c1 - c0], lhsT=hT[:, kc, sub * 128:(sub + 1) * 128], rhs=w_tm[0][:, kc, c0:c1],
                            start=(kc == 0), stop=(kc == KC - 1)),
                            reads=[hTb, w_tm[1]], writes=[psb], inc=(kc == KC - 1))
                    ev = "act" if (n_ps % 2) else "dve"
                    if ev == "act":
                        em.op("act", lambda e, ps=ps, c0=c0, c1=c1, o_t=o_t: e.copy(out=o_t[:, c0:c1], in_=ps[:, : c1 - c0]),
                              reads=[psb], writes=[o_b])
                    else:
                        em.op("dve", lambda e, ps=ps, c0=c0, c1=c1, o_t=o_t: e.tensor_copy(out=o_t[:, c0:c1], in_=ps[:, : c1 - c0]),
                              reads=[psb], writes=[o_b])
                em.dma(STORE_Q, out_tm[t0:t0 + 128, :], o_t[:], reads=[o_b])
            if STOP == "t":
                continue
            for cb in range(NC // 128):
                ps, psb = psum[n_ps % 6]
                n_ps += 1
                o_t, o_b = ocm[n_ocm % 3]
                n_ocm += 1
                for kc in range(KC):
                    em.op("pe", lambda e, kc=kc, ps=ps, cb=cb: e.matmul(
                        ps[:, :], lhsT=w_cm[0][:, kc, cb * 128:(cb + 1) * 128], rhs=hT[:, kc, :],
                        start=(kc == 0), stop=(kc == KC - 1)),
                        reads=[hTb, w_cm[1]], writes=[psb], inc=(kc == KC - 1))
                ev = "act" if (n_ps % 2) else "dve"
                if ev == "act":
                    em.op("act", lambda e, ps=ps, o_t=o_t: e.copy(out=o_t[:], in_=ps[:]), reads=[psb], writes=[o_b])
                else:
                    em.op("dve", lambda e, ps=ps, o_t=o_t: e.tensor_copy(out=o_t[:], in_=ps[:]), reads=[psb], writes=[o_b])
                em.dma(STORE_Q, out_cm[cb * 128:(cb + 1) * 128, st * 512:(st + 1) * 512], o_t[:], reads=[o_b])


def mkT(es, nc, name, shape, dt=F32):
    return (es.enter_context(SBT(nc, name, shape, dt)), Buf(name))


def ld(em, t, src):
    em.dma("sp", t[0][:], src, writes=[t[1]])


def stage_mamba_gen(em, T, p_tm, p_cm, prm, cst, y_dram, psum, TM_Z=1036, TM_DT=1548, CM_X=640, YOFF=512, es_outer=None):
    nc = em.nc
    NCH = T // 128
    with (_Keep(es_outer) if es_outer is not None else ExitStack()) as es:
        cw = mkT(es, nc, "m2cw", [128, 6, 4]); ld(em, cw, prm["m2_cw"])
        cb = mkT(es, nc, "m2cb", [128, 6]); ld(em, cb, prm["m2_cb"])
        dtb = mkT(es, nc, "m2dtb", [128, 8]); ld(em, dtb, prm["m2_dtb"])
        alog = mkT(es, nc, "m2alog", [128, 8]); ld(em, alog, prm["m2_alog"])
        dsk = mkT(es, nc, "m2dsk", [128, 8]); ld(em, dsk, prm["m2_dsk"])
        ng = mkT(es, nc, "m2ng", [128, 512]); ld(em, ng, prm["m2_ng"])
        Aneg = mkT(es, nc, "m2A", [128, 8])
        em.op("act", lambda e: e.activation(out=Aneg[0][:], in_=alog[0][:], func=AF.Exp), reads=[alog[1]], writes=[Aneg[1]])
        em.op("dve", lambda e: e.tensor_scalar(out=Aneg[0][:], in0=Aneg[0][:], scalar1=-1.0, scalar2=None, op0=ALU.mult),
              reads=[Aneg[1]], writes=[Aneg[1]])
        stateT = mkT(es, nc, "m2st", [128, 512])
        state_bf = mkT(es, nc, "m2stb", [128, 512], BF16)
        em.op("dve", lambda e: e.memset(stateT[0][:], 0.0), writes=[stateT[1]])
        em.op("dve", lambda e: e.memset(state_bf[0][:], 0.0), writes=[state_bf[1]])
        sets = []
        for sfx in ("a", "b"):
            S = {}
            S["xin"] = mkT(es, nc, "m2xin" + sfx, [128, 6, 131])
            S["acc"] = mkT(es, nc, "m2acc" + sfx, [128, 6, 128])
            S["xact"] = mkT(es, nc, "m2xact" + sfx, [128, 6, 128])
            S["bT"] = mkT(es, nc, "m2bT" + sfx, [128, 128], BF16)
            S["cT"] = mkT(es, nc, "m2cT" + sfx, [128, 128], BF16)
            S["xs_tm"] = mkT(es, nc, "m2xs" + sfx, [128, 512])
            S["b_tm"] = mkT(es, nc, "m2btm" + sfx, [128, 128], BF16)
            S["dtr"] = mkT(es, nc, "m2dtr" + sfx, [128, 8])
            S["dtv"] = mkT(es, nc, "m2dtv" + sfx, [128, 8])
            S["av"] = mkT(es, nc, "m2a" + sfx, [128, 8])
            S["acs"] = mkT(es, nc, "m2acs" + sfx, [128, 8])
            S["tot"] = mkT(es, nc, "m2tot" + sfx, [128, 8])
            S["dsl"] = mkT(es, nc, "m2dsl" + sfx, [128, 8])
            S["dout"] = mkT(es, nc, "m2dout" + sfx, [128, 8])
            S["dtot"] = mkT(es, nc, "m2dtot" + sfx, [128, 8])
            S["xdt"] = mkT(es, nc, "m2xdt" + sfx, [128, 512])
            S["xdt_bf"] = mkT(es, nc, "m2xdtb" + sfx, [128, 512], BF16)
            S["xdd_bf"] = mkT(es, nc, "m2xddb" + sfx, [128, 512], BF16)
            S["amat"] = mkT(es, nc, "m2amat" + sfx, [128, 8, 128])
            S["LT"] = mkT(es, nc, "m2LT" + sfx, [128, 8, 128])
            S["cbm"] = mkT(es, nc, "m2cbm" + sfx, [128, 128])
            S["MT"] = mkT(es, nc, "m2MT" + sfx, [128, 8, 128], BF16)
            S["yv"] = mkT(es, nc, "m2y" + sfx, [128, 512])
            S["t2"] = mkT(es, nc, "m2t2" + sfx, [128, 512])
            S["zt"] = mkT(es, nc, "m2z" + sfx, [128, 512])
            S["sq"] = mkT(es, nc, "m2sq" + sfx, [128, 512])
            S["ss"] = mkT(es, nc, "m2ss" + sfx, [128, 4])
            sets.append(S)
        def bc8(t):
            return t[0][:, :].unsqueeze(2).to_broadcast([128, 8, 64])

        def v3(t):
            return t[0][:].rearrange("p (h d) -> p h d", h=8)

        def load_chunk(c):
            S = sets[c % 2]
            xin, dtr, zt = S["xin"], S["dtr"], S["zt"]
            t0 = c * 128
            if c == 0:
                em.op("dve", lambda e: e.memset(xin[0][:, :, 0:3], 0.0), writes=[xin[1]])
                for blk in range(6):
                    em.dma("sp", xin[0][:, blk, 3:131], p_cm[CM_X + blk * 128:CM_X + (blk + 1) * 128, 0:128], writes=[xin[1]])
            else:
                for blk in range(6):
                    em.dma("sp", xin[0][:, blk, :], p_cm[CM_X + blk * 128:CM_X + (blk + 1) * 128, t0 - 3:t0 + 128], writes=[xin[1]])
            em.dma("sp", dtr[0][:], p_tm[t0:t0 + 128, TM_DT:TM_DT + 8], writes=[dtr[1]])
            em.dma("sp", zt[0][:], p_tm[t0:t0 + 128, TM_Z:TM_Z + 512], writes=[zt[1]])

        load_chunk(0)
        for c in range(NCH):
            t0 = c * 128
            if c + 1 < NCH:
                load_chunk(c + 1)
            S = sets[c % 2]
            xin = S["xin"]
            acc = S["acc"]
            xact = S["xact"]
            bT = S["bT"]
            cT = S["cT"]
            xs_tm = S["xs_tm"]
            b_tm = S["b_tm"]
            dtr = S["dtr"]
            dtv = S["dtv"]
            av = S["av"]
            acs = S["acs"]
            tot = S["tot"]
            dsl = S["dsl"]
            dout = S["dout"]
            dtot = S["dtot"]
            xdt = S["xdt"]
            xdt_bf = S["xdt_bf"]
            xdd_bf = S["xdd_bf"]
            amat = S["amat"]
            LT = S["LT"]
            cbm = S["cbm"]
            MT = S["MT"]
            yv = S["yv"]
            t2 = S["t2"]
            zt = S["zt"]
            sq = S["sq"]
            ss = S["ss"]
            for k in range(4):
                wk = cw[0][:, :, k:k + 1].to_broadcast([128, 6, 128])
                if k == 0:
                    em.op("dve", lambda e, wk=wk: e.tensor_tensor(out=acc[0][:], in0=xin[0][:, :, 0:128], in1=wk, op=ALU.mult),
                          reads=[xin[1], cw[1]], writes=[acc[1]])
                else:
                    em.op("dve", lambda e, wk=wk, k=k: e.tensor_tensor(out=xact[0][:], in0=xin[0][:, :, k:k + 128], in1=wk, op=ALU.mult),
                          reads=[xin[1], cw[1]], writes=[xact[1]])
                    em.op("dve", lambda e: e.tensor_tensor(out=acc[0][:], in0=acc[0][:], in1=xact[0][:], op=ALU.add),
                          reads=[acc[1], xact[1]], writes=[acc[1]])
            em.op("dve", lambda e: e.tensor_tensor(out=acc[0][:], in0=acc[0][:], in1=cb[0][:, :].unsqueeze(2).to_broadcast([128, 6, 128]), op=ALU.add),
                  reads=[acc[1], cb[1]], writes=[acc[1]])
            em.op("act", lambda e: e.activation(out=xact[0][:], in_=acc[0][:], func=AF.Silu), reads=[acc[1]], writes=[xact[1]])
            em.op("dve", lambda e: e.tensor_copy(out=bT[0][:], in_=xact[0][:, 4, :]), reads=[xact[1]], writes=[bT[1]])
            em.op("dve", lambda e: e.tensor_copy(out=cT[0][:], in_=xact[0][:, 5, :]), reads=[xact[1]], writes=[cT[1]])
            pA, pAb = psum[0]
            for blk in range(4):
                em.op("pe", lambda e, blk=blk: e.transpose(out=pA[:, blk * 128:(blk + 1) * 128], in_=xact[0][:, blk, :],
                                                           identity=cst["ident_f"][0][:]),
                      reads=[xact[1], cst["ident_f"][1]], writes=[pAb], inc=(blk == 3))
            em.op("act", lambda e: e.copy(out=xs_tm[0][:], in_=pA[:, :]), reads=[pAb], writes=[xs_tm[1]])
            pB, pBb = psum[1]
            em.op("pe", lambda e: e.transpose(out=pB[:, 0:128], in_=xact[0][:, 4, :], identity=cst["ident_f"][0][:]),
                  reads=[xact[1], cst["ident_f"][1]], writes=[pBb])
            em.op("act", lambda e: e.copy(out=b_tm[0][:], in_=pB[:, 0:128]), reads=[pBb], writes=[b_tm[1]])
            em.op("dve", lambda e: e.tensor_tensor(out=dtv[0][:], in0=dtr[0][:], in1=dtb[0][:], op=ALU.add),
                  reads=[dtr[1], dtb[1]], writes=[dtv[1]])
            em.op("act", lambda e: e.activation(out=dtv[0][:], in_=dtv[0][:], func=AF.Exp), reads=[dtv[1]], writes=[dtv[1]])
            em.op("act", lambda e: e.activation(out=dtv[0][:], in_=dtv[0][:], func=AF.Ln, bias=1.0), reads=[dtv[1]], writes=[dtv[1]])
            em.op("dve", lambda e: e.tensor_tensor(out=av[0][:], in0=dtv[0][:], in1=Aneg[0][:], op=ALU.mult),
                  reads=[dtv[1], Aneg[1]], writes=[av[1]])
            pC, pCb = psum[2]
            em.op("pe", lambda e: e.matmul(pC[:, 0:8], lhsT=cst["triu_f"][0][:], rhs=av[0][:], start=True, stop=True),
                  reads=[cst["triu_f"][1], av[1]], writes=[pCb])
            em.op("pe", lambda e: e.matmul(pC[:, 8:16], lhsT=cst["ones_f"][0][:], rhs=av[0][:], start=True, stop=True),
                  reads=[cst["ones_f"][1], av[1]], writes=[pCb])
            em.op("act", lambda e: e.copy(out=acs[0][:], in_=pC[:, 0:8]), reads=[pCb], writes=[acs[1]])
            em.op("act", lambda e: e.copy(out=tot[0][:], in_=pC[:, 8:16]), reads=[pCb], writes=[tot[1]])
            em.op("dve", lambda e: e.tensor_tensor(out=dsl[0][:], in0=tot[0][:], in1=acs[0][:], op=ALU.subtract),
                  reads=[tot[1], acs[1]], writes=[dsl[1]])
            em.op("act", lambda e: e.activation(out=dsl[0][:], in_=dsl[0][:], func=AF.Exp), reads=[dsl[1]], writes=[dsl[1]])
            em.op("act", lambda e: e.activation(out=dout[0][:], in_=acs[0][:], func=AF.Exp), reads=[acs[1]], writes=[dout[1]])
            em.op("act", lambda e: e.activation(out=dtot[0][:], in_=tot[0][:], func=AF.Exp), reads=[tot[1]], writes=[dtot[1]])
            em.op("dve", lambda e: e.tensor_tensor(out=v3(xdt), in0=v3(xs_tm), in1=bc8(dtv), op=ALU.mult),
                  reads=[xs_tm[1], dtv[1]], writes=[xdt[1]])
            em.op("act", lambda e: e.copy(out=xdt_bf[0][:], in_=xdt[0][:]), reads=[xdt[1]], writes=[xdt_bf[1]])
            em.op("dve", lambda e: e.tensor_tensor(out=xdd_bf[0][:].rearrange("p (h d) -> p h d", h=8), in0=v3(xdt), in1=bc8(dsl), op=ALU.mult),
                  reads=[xdt[1], dsl[1]], writes=[xdd_bf[1]])
            em.op("dve", lambda e: e.tensor_tensor(out=amat[0][:], in0=cst["mstrict_f"][0][:, :].unsqueeze(1).to_broadcast([128, 8, 128]),
                                                   in1=av[0][:, :].unsqueeze(2).to_broadcast([128, 8, 128]), op=ALU.mult),
                  reads=[cst["mstrict_f"][1], av[1]], writes=[amat[1]])
            pL0, pL0b = psum[3]
            pL1, pL1b = psum[4]
            for h in range(8):
                pl, plb = (pL0, pL0b) if h < 4 else (pL1, pL1b)
                em.op("pe", lambda e, h=h, pl=pl: e.matmul(pl[:, (h % 4) * 128:(h % 4 + 1) * 128], lhsT=amat[0][:, h, :],
                                                          rhs=cst["triu_f"][0][:], start=True, stop=True),
                      reads=[amat[1], cst["triu_f"][1]], writes=[plb])
            em.op("act", lambda e: e.activation(out=LT[0][:, 0:4, :], in_=pL0[:, :].rearrange("p (h l) -> p h l", h=4), func=AF.Exp),
                  reads=[pL0b], writes=[LT[1]])
            em.op("act", lambda e: e.activation(out=LT[0][:, 4:8, :], in_=pL1[:, :].rearrange("p (h l) -> p h l", h=4), func=AF.Exp),
                  reads=[pL1b], writes=[LT[1]])
            pD, pDb = psum[5]
            em.op("pe", lambda e: e.matmul(pD[:, 0:128], lhsT=bT[0][:], rhs=cT[0][:], start=True, stop=True),
                  reads=[bT[1], cT[1]], writes=[pDb])
            em.op("dve", lambda e: e.tensor_tensor(out=cbm[0][:], in0=pD[:, 0:128], in1=cst["mcausT_f"][0][:], op=ALU.mult),
                  reads=[pDb, cst["mcausT_f"][1]], writes=[cbm[1]])
            em.op("dve", lambda e: e.tensor_tensor(out=MT[0][:], in0=LT[0][:], in1=cbm[0][:, :].unsqueeze(1).to_broadcast([128, 8, 128]),
                                                   op=ALU.mult), reads=[LT[1], cbm[1]], writes=[MT[1]])
            pY, pYb = psum[6]
            em.op("pe", lambda e: e.matmul(pY[:, :], lhsT=cT[0][:], rhs=state_bf[0][:], start=True, stop=True),
                  reads=[cT[1], state_bf[1]], writes=[pYb])
            em.op("dve", lambda e: e.tensor_tensor(out=v3(yv), in0=pY[:, :].rearrange("p (h d) -> p h d", h=8), in1=bc8(dout), op=ALU.mult),
                  reads=[pYb, dout[1]], writes=[yv[1]])
            pZ, pZb = psum[0]
            for h in range(8):
                em.op("pe", lambda e, h=h: e.matmul(pZ[:, h * 64:(h + 1) * 64], lhsT=MT[0][:, h, :], rhs=xdt_bf[0][:, h * 64:(h + 1) * 64],
                                                    start=True, stop=True), reads=[MT[1], xdt_bf[1]], writes=[pZb], inc=(h == 7))
            em.op("dve", lambda e: e.tensor_tensor(out=yv[0][:], in0=yv[0][:], in1=pZ[:, :], op=ALU.add), reads=[yv[1], pZb], writes=[yv[1]])
            pS, pSb = psum[1]
            em.op("pe", lambda e: e.matmul(pS[:, :], lhsT=b_tm[0][:], rhs=xdd_bf[0][:], start=True, stop=True),
                  reads=[b_tm[1], xdd_bf[1]], writes=[pSb])
            em.op("dve", lambda e: e.tensor_tensor(out=v3(stateT), in0=v3(stateT), in1=bc8(dtot), op=ALU.mult),
                  reads=[stateT[1], dtot[1]], writes=[stateT[1]])
            em.op("dve", lambda e: e.tensor_tensor(out=stateT[0][:], in0=stateT[0][:], in1=pS[:, :], op=ALU.add),
                  reads=[stateT[1], pSb], writes=[stateT[1]])
            em.op("act", lambda e: e.copy(out=state_bf[0][:], in_=stateT[0][:]), reads=[stateT[1]], writes=[state_bf[1]])
            em.op("dve", lambda e: e.tensor_tensor(out=v3(t2), in0=v3(xs_tm), in1=bc8(dsk), op=ALU.mult),
                  reads=[xs_tm[1], dsk[1]], writes=[t2[1]])
            em.op("dve", lambda e: e.tensor_tensor(out=yv[0][:], in0=yv[0][:], in1=t2[0][:], op=ALU.add), reads=[yv[1], t2[1]], writes=[yv[1]])
            em.op("act", lambda e: e.activation(out=zt[0][:], in_=zt[0][:], func=AF.Silu), reads=[zt[1]], writes=[zt[1]])
            em.op("dve", lambda e: e.tensor_tensor(out=yv[0][:], in0=yv[0][:], in1=zt[0][:], op=ALU.mult), reads=[yv[1], zt[1]], writes=[yv[1]])
            em.op("act", lambda e: e.activation(out=sq[0][:], in_=yv[0][:], func=AF.Square, accum_out=ss[0][:, 0:1]),
                  reads=[yv[1]], writes=[sq[1], ss[1]])
            em.op("dve", lambda e: e.tensor_scalar(out=ss[0][:, 1:2], in0=ss[0][:, 0:1], scalar1=1.0 / 512, scalar2=1e-6,
                                                   op0=ALU.mult, op1=ALU.add), reads=[ss[1]], writes=[ss[1]])
            em.op("act", lambda e: e.sqrt(out=ss[0][:, 3:4], in_=ss[0][:, 1:2]), reads=[ss[1]], writes=[ss[1]])
            em.op("dve", lambda e: e.reciprocal(out=ss[0][:, 2:3], in_=ss[0][:, 3:4]), reads=[ss[1]], writes=[ss[1]])
            em.op("dve", lambda e: e.scalar_tensor_tensor(out=t2[0][:], in0=yv[0][:], scalar=ss[0][:, 2:3], in1=ng[0][:],
                                                          op0=ALU.mult, op1=ALU.mult), reads=[yv[1], ss[1], ng[1]], writes=[t2[1]])
            em.dyn_store(y_dram[t0:t0 + 128, YOFF:YOFF + 512], t2[0][:], [t2[1]])
            yield c


def stage_mamba(*a, **k):
    for _ in stage_mamba_gen(*a, **k):
        pass


def stage_hgrn_gen(em, T, layer, p_tm, p_cm, prm, cst, y_dram, psum, TM_I=524, TM_G=780, CM_Q=128, CM_F=384, YOFF=256, es_outer=None):
    nc = em.nc
    NT = T // 128
    with (_Keep(es_outer) if es_outer is not None else ExitStack()) as es:
        lbl = mkT(es, nc, "hglbl", [128, 2, 2]); ld(em, lbl, prm["hg_lbl"])
        ng = mkT(es, nc, "hgng", [128, 128]); ld(em, ng, prm["hg_ng"])
        lb = mkT(es, nc, "hglb", [128, 2])
        oml = mkT(es, nc, "hgoml", [128, 2])
        em.op("dve", lambda e: e.tensor_tensor(out=lb[0][:], in0=lbl[0][:, :, 1], in1=lbl[0][:, :, 0], op=ALU.subtract),
              reads=[lbl[1]], writes=[lb[1]])
        em.op("act", lambda e: e.activation(out=lb[0][:], in_=lb[0][:], func=AF.Sigmoid), reads=[lb[1]], writes=[lb[1]])
        lfl = mkT(es, nc, "hglfl", [128, 2]); ld(em, lfl, prm["hg_lflag"])
        em.op("dve", lambda e: e.tensor_tensor(out=lb[0][:], in0=lb[0][:], in1=lfl[0][:], op=ALU.mult),
              reads=[lb[1], lfl[1]], writes=[lb[1]])
        em.op("dve", lambda e: e.tensor_scalar(out=oml[0][:], in0=lb[0][:], scalar1=-1.0, scalar2=1.0, op0=ALU.mult, op1=ALU.add),
              reads=[lb[1]], writes=[oml[1]])
        W = 512
        HD = []
        for hd in range(2):
            S = mkT(es, nc, "hgS%d" % hd, [128, 128])
            Sref = mkT(es, nc, "hgSr%d" % hd, [128, 128], BF16)
            em.op("dve", lambda e: e.memset(S[0][:], 0.0), writes=[S[1]])
            em.op("dve", lambda e: e.memset(Sref[0][:], 0.0), writes=[Sref[1]])
            qT = [mkT(es, nc, "hgq%d_%d" % (hd, k), [128, W]) for k in range(2)]
            fT = [mkT(es, nc, "hgf%d_%d" % (hd, k), [128, W]) for k in range(2)]
            kk = mkT(es, nc, "hgk%d" % hd, [128, W])
            g0 = mkT(es, nc, "hgg0%d" % hd, [128, W])
            g1 = mkT(es, nc, "hgg1%d" % hd, [128, W])
            d1 = mkT(es, nc, "hgd1%d" % hd, [128, W])
            e1 = mkT(es, nc, "hge1%d" % hd, [128, W])
            qt_bf = mkT(es, nc, "hgqt%d" % hd, [128, W], BF16)
            kt_bf = mkT(es, nc, "hgkt%d" % hd, [128, W], BF16)
            qz = mkT(es, nc, "hgqz%d" % hd, [128, 2, 128], BF16)
            sc = mkT(es, nc, "hgsc%d" % hd, [128, 3, 8])
            ktm = mkT(es, nc, "hgktm%d" % hd, [128, 128], BF16)
            v_f = [mkT(es, nc, "hgvf%d_%d" % (hd, k), [128, 128]) for k in range(2)]
            v_bf = mkT(es, nc, "hgvb%d" % hd, [128, 128], BF16)
            AT = mkT(es, nc, "hgAT%d" % hd, [128, 128], BF16)
            tU = mkT(es, nc, "hgtU%d" % hd, [128, 128])
            ot = mkT(es, nc, "hgo%d" % hd, [128, 128])
            sq = mkT(es, nc, "hgsq%d" % hd, [128, 128])
            ss = mkT(es, nc, "hgss%d" % hd, [128, 4])
            gt = [mkT(es, nc, "hggt%d_%d" % (hd, k), [128, 128]) for k in range(2)]
            em.op("dve", lambda e: e.memset(qz[0][:], 0.0), writes=[qz[1]])
            HD.append(dict(S=S, Sref=Sref, qT=qT, fT=fT, kk=kk, g0=g0, g1=g1, d1=d1, e1=e1, qt_bf=qt_bf, kt_bf=kt_bf, qz=qz, sc=sc, ktm=ktm, v_f=v_f, v_bf=v_bf, AT=AT, tU=tU, ot=ot, sq=sq, ss=ss, gt=gt))
        def load_sc(hd, sc_i):
            H = HD[hd]
            c0 = sc_i * W
            q_, f_ = H["qT"][sc_i % 2], H["fT"][sc_i % 2]
            em.dma("sp", q_[0][:], p_cm[CM_Q + hd * 128:CM_Q + (hd + 1) * 128, c0:c0 + W], writes=[q_[1]])
            em.dma("sp", f_[0][:], p_cm[CM_F + hd * 128:CM_F + (hd + 1) * 128, c0:c0 + W], writes=[f_[1]])

        def load_tile(hd, gti):
            H = HD[hd]
            t0 = gti * 128
            v_, g_ = H["v_f"][gti % 2], H["gt"][gti % 2]
            em.dma("sp", v_[0][:], p_tm[t0:t0 + 128, TM_I + hd * 128:TM_I + (hd + 1) * 128], writes=[v_[1]])
            em.dma("sp", g_[0][:], p_tm[t0:t0 + 128, TM_G + hd * 128:TM_G + (hd + 1) * 128], writes=[g_[1]])

        for hd in range(2):
            load_sc(hd, 0)
            load_tile(hd, 0)
        for sc_i in range(T // W):
          for hd in range(2):
            H = HD[hd]
            if sc_i + 1 < T // W:
                load_sc(hd, sc_i + 1)
            S = H["S"]
            Sref = H["Sref"]
            qT = H["qT"][sc_i % 2]
            fT = H["fT"][sc_i % 2]
            kk = H["kk"]
            g0 = H["g0"]
            g1 = H["g1"]
            d1 = H["d1"]
            e1 = H["e1"]
            qt_bf = H["qt_bf"]
            kt_bf = H["kt_bf"]
            qz = H["qz"]
            sc = H["sc"]
            ktm = H["ktm"]
            v_bf = H["v_bf"]
            AT = H["AT"]
            tU = H["tU"]
            ot = H["ot"]
            sq = H["sq"]
            ss = H["ss"]
            PB = hd * 4
            if True:
                c0 = sc_i * W
                em.op("act", lambda e: e.activation(out=fT[0][:], in_=fT[0][:], func=AF.Sigmoid), reads=[fT[1]], writes=[fT[1]])
                em.op("dve", lambda e: e.tensor_scalar(out=fT[0][:], in0=fT[0][:], scalar1=oml[0][:, hd:hd + 1], scalar2=lb[0][:, hd:hd + 1],
                                                       op0=ALU.mult, op1=ALU.add), reads=[fT[1], oml[1], lb[1]], writes=[fT[1]])
                em.op("dve", lambda e: e.tensor_scalar(out=kk[0][:], in0=fT[0][:], scalar1=-1.0, scalar2=1.0, op0=ALU.mult, op1=ALU.add),
                      reads=[fT[1]], writes=[kk[1]])
                em.op("act", lambda e: e.activation(out=g0[0][:], in_=fT[0][:], func=AF.Ln), reads=[fT[1]], writes=[g0[1]])
                src, dst = g0, g1
                for d in (1, 2, 4, 8, 16, 32):
                    sv = src[0][:].rearrange("p (c t) -> p c t", t=64)
                    dv = dst[0][:].rearrange("p (c t) -> p c t", t=64)
                    em.op("dve", lambda e, sv=sv, dv=dv, d=d: e.tensor_copy(out=dv[:, :, 0:d], in_=sv[:, :, 0:d]), reads=[src[1]], writes=[dst[1]])
                    em.op("dve", lambda e, sv=sv, dv=dv, d=d: e.tensor_tensor(out=dv[:, :, d:64], in0=sv[:, :, d:64], in1=sv[:, :, 0:64 - d], op=ALU.add),
                          reads=[src[1]], writes=[dst[1]])
                    src, dst = dst, src
                gc = src
                gv = gc[0][:].rearrange("p (c t) -> p c t", t=64)
                em.op("dve", lambda e: e.tensor_tensor(out=d1[0][:].rearrange("p (c t) -> p c t", t=64), in0=gv,
                                                       in1=gv[:, :, 31:32].to_broadcast([128, 8, 64]), op=ALU.subtract),
                      reads=[gc[1]], writes=[d1[1]])
                em.op("act", lambda e: e.activation(out=e1[0][:], in_=d1[0][:], func=AF.Exp), reads=[d1[1]], writes=[e1[1]])
                em.op("dve", lambda e: e.tensor_tensor(out=qt_bf[0][:], in0=qT[0][:], in1=e1[0][:], op=ALU.mult), reads=[qT[1], e1[1]], writes=[qt_bf[1]])
                em.op("act", lambda e: e.activation(out=e1[0][:], in_=d1[0][:], func=AF.Exp, scale=-1.0), reads=[d1[1]], writes=[e1[1]])
                em.op("dve", lambda e: e.tensor_tensor(out=kt_bf[0][:], in0=kk[0][:], in1=e1[0][:], op=ALU.mult), reads=[kk[1], e1[1]], writes=[kt_bf[1]])
                em.op("act", lambda e: e.activation(out=sc[0][:, 0, :], in_=gv[:, :, 31], func=AF.Exp), reads=[gc[1]], writes=[sc[1]])
                em.op("act", lambda e: e.activation(out=sc[0][:, 1, :], in_=gv[:, :, 63], func=AF.Exp), reads=[gc[1]], writes=[sc[1]])
                em.op("act", lambda e: e.activation(out=sc[0][:, 2, :], in_=d1[0][:].rearrange("p (c t) -> p c t", t=64)[:, :, 63], func=AF.Exp),
                      reads=[d1[1]], writes=[sc[1]])
                em.op("dve", lambda e: e.tensor_scalar(out=Sref[0][:], in0=S[0][:], scalar1=sc[0][:, 0, 0:1], scalar2=None, op0=ALU.mult),
                      reads=[S[1], sc[1]], writes=[Sref[1]])
                for tl in range(W // 128):
                    t0 = c0 + tl * 128
                    gti = sc_i * (W // 128) + tl
                    v_f = H["v_f"][gti % 2]
                    gt = H["gt"][gti % 2]
                    if gti + 1 < T // 128:
                        load_tile(hd, gti + 1)
                    cols = slice(tl * 128, (tl + 1) * 128)
                    em.op("act", lambda e: e.copy(out=v_bf[0][:], in_=v_f[0][:]), reads=[v_f[1]], writes=[v_bf[1]])
                    pA, pAb = psum[PB + 0]
                    em.op("pe", lambda e, cols=cols: e.matmul(pA[:, 0:128], lhsT=kt_bf[0][:, cols], rhs=qt_bf[0][:, cols], start=True, stop=True),
                          reads=[kt_bf[1], qt_bf[1]], writes=[pAb])
                    em.op("dve", lambda e: e.tensor_tensor(out=AT[0][:], in0=pA[:, 0:128], in1=cst["mask2_f"][0][:], op=ALU.mult),
                          reads=[pAb, cst["mask2_f"][1]], writes=[AT[1]])
                    pT, pTb = psum[7]
                    em.op("pe", lambda e, cols=cols: e.transpose(out=pT[:, 0, :], in_=kt_bf[0][:, cols], identity=cst["ident_bf"][0][:]),
                          reads=[kt_bf[1], cst["ident_bf"][1]], writes=[pTb])
                    em.op("act", lambda e: e.copy(out=ktm[0][:], in_=pT[:, 0, :]), reads=[pTb], writes=[ktm[1]])
                    em.op("dve", lambda e, cols=cols: e.tensor_copy(out=qz[0][:, 0, 0:64], in_=qt_bf[0][:, tl * 128:tl * 128 + 64]),
                          reads=[qt_bf[1]], writes=[qz[1]])
                    em.op("dve", lambda e, cols=cols: e.tensor_copy(out=qz[0][:, 1, 64:128], in_=qt_bf[0][:, tl * 128 + 64:tl * 128 + 128]),
                          reads=[qt_bf[1]], writes=[qz[1]])
                    pO, pOb = psum[PB + 1]
                    em.op("pe", lambda e: e.matmul(pO[:, 0:128], lhsT=AT[0][:], rhs=v_bf[0][:], start=True, stop=False),
                          reads=[AT[1], v_bf[1]], writes=[pOb], inc=False)
                    for j in range(2):
                        ch = tl * 2 + j
                        em.op("pe", lambda e, j=j: e.matmul(pO[:, 0:128], lhsT=qz[0][:, j, :], rhs=Sref[0][:], start=False, stop=(j == 1)),
                              reads=[qz[1], Sref[1]], writes=[pOb], inc=True)
                        pU, pUb = psum[PB + 2 + (j if hd == 0 else 0)]
                        rows = slice(j * 64, (j + 1) * 64)
                        em.op("pe", lambda e, rows=rows, pU=pU: e.matmul(pU[:, 0:128], lhsT=ktm[0][rows, :], rhs=v_bf[0][rows, :], start=True, stop=True),
                              reads=[ktm[1], v_bf[1]], writes=[pUb])
                        em.op("dve", lambda e, pU=pU, ch=ch: e.tensor_scalar(out=tU[0][:], in0=pU[:, 0:128], scalar1=sc[0][:, 2, ch:ch + 1], scalar2=None, op0=ALU.mult),
                              reads=[pUb, sc[1]], writes=[tU[1]])
                        em.op("dve", lambda e, ch=ch: e.scalar_tensor_tensor(out=S[0][:], in0=S[0][:], scalar=sc[0][:, 1, ch:ch + 1], in1=tU[0][:],
                                                                              op0=ALU.mult, op1=ALU.add), reads=[S[1], sc[1], tU[1]], writes=[S[1]])
                        if ch < 7:
                            em.op("dve", lambda e, ch=ch: e.tensor_scalar(out=Sref[0][:], in0=S[0][:], scalar1=sc[0][:, 0, ch + 1:ch + 2], scalar2=None,
                                                                           op0=ALU.mult), reads=[S[1], sc[1]], writes=[Sref[1]])
                    em.op("act", lambda e: e.copy(out=ot[0][:], in_=pO[:, 0:128]), reads=[pOb], writes=[ot[1]])
                    em.op("act", lambda e: e.activation(out=sq[0][:], in_=ot[0][:], func=AF.Square, accum_out=ss[0][:, 0:1]),
                          reads=[ot[1]], writes=[sq[1], ss[1]])
                    em.op("dve", lambda e: e.tensor_scalar(out=ss[0][:, 1:2], in0=ss[0][:, 0:1], scalar1=1.0 / 128, scalar2=1e-6,
                                                           op0=ALU.mult, op1=ALU.add), reads=[ss[1]], writes=[ss[1]])
                    em.op("act", lambda e: e.sqrt(out=ss[0][:, 3:4], in_=ss[0][:, 1:2]), reads=[ss[1]], writes=[ss[1]])
                    em.op("dve", lambda e: e.reciprocal(out=ss[0][:, 2:3], in_=ss[0][:, 3:4]), reads=[ss[1]], writes=[ss[1]])
                    em.op("dve", lambda e: e.scalar_tensor_tensor(out=ot[0][:], in0=ot[0][:], scalar=ss[0][:, 2:3], in1=ng[0][:],
                                                                  op0=ALU.mult, op1=ALU.mult), reads=[ot[1], ss[1], ng[1]], writes=[ot[1]])
                    em.op("act", lambda e: e.activation(out=gt[0][:], in_=gt[0][:], func=AF.Silu), reads=[gt[1]], writes=[gt[1]])
                    em.op("dve", lambda e: e.tensor_tensor(out=ot[0][:], in0=ot[0][:], in1=gt[0][:], op=ALU.mult), reads=[ot[1], gt[1]], writes=[ot[1]])
                    em.dyn_store(y_dram[t0:t0 + 128, YOFF + hd * 128:YOFF + (hd + 1) * 128], ot[0][:], [ot[1]])
            yield (sc_i, hd)


def stage_hgrn(*a, **k):
    for _ in stage_hgrn_gen(*a, **k):
        pass


NEG = -30000.0
TWO_PI = 6.283185307179586


def rope_apply(em, x3, xb, o3, ob, cos2, sin2, csb, tmp, H):
    cb = cos2.unsqueeze(1).to_broadcast([128, H, 8])
    sb = sin2.unsqueeze(1).to_broadcast([128, H, 8])
    t = tmp[0]
    x1 = x3[:, :, 0:8]
    x2 = x3[:, :, 8:16]
    em.op("dve", lambda e: e.tensor_tensor(out=t[:, 0, 0:H, :], in0=x1, in1=cb, op=ALU.mult), reads=[xb, csb], writes=[tmp[1]])
    em.op("dve", lambda e: e.tensor_tensor(out=t[:, 1, 0:H, :], in0=x2, in1=sb, op=ALU.mult), reads=[xb, csb], writes=[tmp[1]])
    em.op("dve", lambda e: e.tensor_tensor(out=t[:, 2, 0:H, :], in0=x2, in1=cb, op=ALU.mult), reads=[xb, csb], writes=[tmp[1]])
    em.op("dve", lambda e: e.tensor_tensor(out=t[:, 3, 0:H, :], in0=x1, in1=sb, op=ALU.mult), reads=[xb, csb], writes=[tmp[1]])
    em.op("dve", lambda e: e.tensor_tensor(out=o3[:, :, 0:8], in0=t[:, 0, 0:H, :], in1=t[:, 1, 0:H, :], op=ALU.subtract), reads=[tmp[1]], writes=[ob])
    em.op("dve", lambda e: e.tensor_tensor(out=o3[:, :, 8:16], in0=t[:, 2, 0:H, :], in1=t[:, 3, 0:H, :], op=ALU.add), reads=[tmp[1]], writes=[ob])


def headnorm(em, x, xb, H, g_ap, gb, sq, ss, out3, outb):
    x3 = x[:, 0:H * 64].rearrange("p (h d) -> p h d", h=H)
    s3 = sq[0][:, 0:H * 64].rearrange("p (h d) -> p h d", h=H)
    em.op("dve", lambda e: e.tensor_tensor(out=s3, in0=x3, in1=x3, op=ALU.mult), reads=[xb], writes=[sq[1]])
    em.op("dve", lambda e: e.tensor_reduce(out=ss[0][:, 0, 0:H], in_=s3, axis=AX.X, op=ALU.add), reads=[sq[1]], writes=[ss[1]])
    em.op("dve", lambda e: e.tensor_scalar(out=ss[0][:, 1, 0:H], in0=ss[0][:, 0, 0:H], scalar1=1.0 / 64, scalar2=1e-6, op0=ALU.mult, op1=ALU.add),
          reads=[ss[1]], writes=[ss[1]])
    em.op("act", lambda e: e.sqrt(out=ss[0][:, 2, 0:H], in_=ss[0][:, 1, 0:H]), reads=[ss[1]], writes=[ss[1]])
    em.op("dve", lambda e: e.reciprocal(out=ss[0][:, 1, 0:H], in_=ss[0][:, 2, 0:H]), reads=[ss[1]], writes=[ss[1]])
    em.op("dve", lambda e: e.tensor_tensor(out=out3, in0=x3, in1=ss[0][:, 1, 0:H].unsqueeze(2).to_broadcast([128, H, 64]), op=ALU.mult),
          reads=[xb, ss[1]], writes=[outb])
    em.op("dve", lambda e: e.tensor_tensor(out=out3, in0=out3, in1=g_ap, op=ALU.mult), reads=[outb, gb], writes=[outb])


def stage_nsa(em, T, p_tm, p_cm, prm, cst, y_dram, psum, YOFF=0):
    nc = em.nc
    NQ = T // 128
    NCMP = (T - 32) // 16 + 1
    NJC = (NCMP + 127) // 128
    DBG = None
    def dbg(name, t, part=128):
        if DBG is None:
            return
        shape = list(t[0].shape)
        d = nc.dram_tensor("dbg_" + name, shape, t[0].dtype, kind="ExternalOutput").ap()
        em.dma("sp", d, t[0][:], reads=[t[1]])
        DBG.append("dbg_" + name)
    with ExitStack() as es:
        def LD(name, shape, dt=F32):
            t = mkT(es, nc, "ns_" + name, shape, dt)
            src = prm[name]
            if len(shape) == 2:
                src = src[0:shape[0], 0:shape[1]]
            else:
                src = src[0:shape[0], 0:shape[1], 0:shape[2]]
            ld(em, t, src)
            return t
        gq = LD("ns_gq", [128, 256]); gk = LD("ns_gk", [128, 2, 64]); gk0 = LD("ns_gk0", [64, 1])
        posT = LD("ns_posT", [128, 32]); w2k = LD("ns_w2k", [128, 2, 64]); w2v = LD("ns_w2v", [128, 2, 64])
        posi = LD("ns_pos", [128, NQ], I32)
        invf = LD("c_invf", [128, 8])
        cmpmask = LD("c_cmpmask", [128, 17, 512], BF16)
        c2s = LD("c_c2s", [128, NJC, 128], BF16)
        ebig = LD("c_ebig", [128, T], BF16)
        causneg = LD("c_causneg", [128, 512], BF16)
        anticaus = LD("c_anticaus", [128, 512], BF16)
        keepb = LD("c_keep", [128, 254]); addb = LD("c_add", [128, 254])
        ident_bf = cst["ident_bf"]; ident_f = cst["ident_f"]; ones_f = cst["ones_f"]
        pst, pstb = psum[7]
        cosT = mkT(es, nc, "ns_cos", [128, NQ, 8]); sinT = mkT(es, nc, "ns_sin", [128, NQ, 8])
        if True:
            es2 = es
            posf = mkT(es2, nc, "ns_posf", [128, NQ])
            ang = mkT(es2, nc, "ns_ang", [128, NQ, 8]); fr = mkT(es2, nc, "ns_fr", [128, NQ, 8])
            ki = mkT(es2, nc, "ns_ki", [128, NQ, 8], I32); kf = mkT(es2, nc, "ns_kf", [128, NQ, 8])
            em.op("dve", lambda e: e.tensor_copy(out=posf[0][:], in_=posi[0][:]), reads=[posi[1]], writes=[posf[1]])
            em.op("dve", lambda e: e.tensor_tensor(out=ang[0][:], in0=posf[0][:, :].unsqueeze(2).to_broadcast([128, NQ, 8]),
                                                   in1=invf[0][:, :].unsqueeze(1).to_broadcast([128, NQ, 8]), op=ALU.mult),
                  reads=[posf[1], invf[1]], writes=[ang[1]])
            for dst, off in ((sinT, 0.0), (cosT, 0.25)):
                em.op("dve", lambda e, off=off: e.tensor_scalar(out=fr[0][:], in0=ang[0][:], scalar1=1.0 / TWO_PI, scalar2=off, op0=ALU.mult, op1=ALU.add),
                      reads=[ang[1]], writes=[fr[1]])
                em.op("dve", lambda e: e.tensor_copy(out=ki[0][:], in_=fr[0][:]), reads=[fr[1]], writes=[ki[1]])
                em.op("dve", lambda e: e.tensor_copy(out=kf[0][:], in_=ki[0][:]), reads=[ki[1]], writes=[kf[1]])
                em.op("dve", lambda e: e.tensor_tensor(out=fr[0][:], in0=fr[0][:], in1=kf[0][:], op=ALU.subtract), reads=[fr[1], kf[1]], writes=[fr[1]])
                em.op("dve", lambda e: e.tensor_scalar(out=kf[0][:], in0=fr[0][:], scalar1=0.5, scalar2=None, op0=ALU.is_gt), reads=[fr[1]], writes=[kf[1]])
                em.op("dve", lambda e: e.tensor_tensor(out=fr[0][:], in0=fr[0][:], in1=kf[0][:], op=ALU.subtract), reads=[fr[1], kf[1]], writes=[fr[1]])
                em.op("dve", lambda e: e.tensor_scalar(out=kf[0][:], in0=fr[0][:], scalar1=-0.5, scalar2=None, op0=ALU.is_lt), reads=[fr[1]], writes=[kf[1]])
                em.op("dve", lambda e: e.tensor_tensor(out=fr[0][:], in0=fr[0][:], in1=kf[0][:], op=ALU.add), reads=[fr[1], kf[1]], writes=[fr[1]])
                em.op("act", lambda e, dst=dst: e.activation(out=dst[0][:], in_=fr[0][:], func=AF.Sin, scale=TWO_PI), reads=[fr[1]], writes=[dst[1]])
        cs_b = Buf("cs")
        kTs = mkT(es, nc, "ns_kTs", [64, T], BF16); kTw = mkT(es, nc, "ns_kTw", [64, T], BF16)
        vs = mkT(es, nc, "ns_vs", [128, NQ, 65], BF16); vw = mkT(es, nc, "ns_vw", [128, NQ, 65], BF16)
        em.op("dve", lambda e: e.memset(vs[0][:, :, 64:65], 1.0), writes=[vs[1]])
        em.op("dve", lambda e: e.memset(vw[0][:, :, 64:65], 1.0), writes=[vw[1]])
        kcT = mkT(es, nc, "ns_kcT", [64, NJC * 128], BF16)
        vc = mkT(es, nc, "ns_vc", [128, NJC, 65], BF16)
        em.op("dve", lambda e: e.memset(kcT[0][:], 0.0), writes=[kcT[1]])
        em.op("dve", lambda e: e.memset(vc[0][:, :, 64:65], 1.0), writes=[vc[1]])
        sq = mkT(es, nc, "ns_sq", [128, 256]); ss = mkT(es, nc, "ns_ss", [128, 3, 4])
        rtmp = mkT(es, nc, "ns_rtmp", [128, 4, 4, 8])
        if True:
            es2 = es
            KS = [dict(kvr=mkT(es2, nc, "ns_kvr%d" % k, [128, 256]), kn=mkT(es2, nc, "ns_kn%d" % k, [128, 2, 64]), kr=mkT(es2, nc, "ns_kr%d" % k, [128, 2, 64]),
                       kb=mkT(es2, nc, "ns_kb%d" % k, [128, 2, 64], BF16), sq=mkT(es2, nc, "ns_ksq%d" % k, [128, 256]), ss=mkT(es2, nc, "ns_kss%d" % k, [128, 3, 4]),
                       rtmp=mkT(es2, nc, "ns_krt%d" % k, [128, 4, 4, 8])) for k in range(2)]
            for i in range(NQ):
                t0 = i * 128
                kvr, kn, kr, kb, sq, ss, rtmp = (KS[i % 2][k] for k in ("kvr", "kn", "kr", "kb", "sq", "ss", "rtmp"))
                em.dma("sp", kvr[0][:], p_tm[t0:t0 + 128, 256:512], writes=[kvr[1]])
                em.op("act", lambda e: e.copy(out=vs[0][:, i, 0:64], in_=kvr[0][:, 64:128]), reads=[kvr[1]], writes=[vs[1]])
                em.op("act", lambda e: e.copy(out=vw[0][:, i, 0:64], in_=kvr[0][:, 192:256]), reads=[kvr[1]], writes=[vw[1]])
                for w, c0 in ((0, 0), (1, 128)):
                    xs_ = kvr[0][:, c0:c0 + 64]
                    headnorm(em, xs_, kvr[1], 1, gk[0][:, w:w + 1, :], gk[1], sq, ss, kn[0][:, w:w + 1, :], kn[1])
                em.op("act", lambda e: e.copy(out=kr[0][:], in_=kn[0][:]), reads=[kn[1]], writes=[kr[1]])
                rope_apply(em, kn[0][:], kn[1], kr[0][:], kr[1], cosT[0][:, i, :], sinT[0][:, i, :], cosT[1], rtmp, 2)
                em.op("act", lambda e: e.copy(out=kb[0][:], in_=kr[0][:]), reads=[kr[1], sinT[1]], writes=[kb[1]])
                for w, dst in ((0, kTs), (1, kTw)):
                    em.op("pe", lambda e, w=w: e.transpose(out=pst[0:64, w, :], in_=kb[0][:, w, :], identity=ident_bf[0][:]),
                          reads=[kb[1], ident_bf[1]], writes=[pstb])
                    em.op("act", lambda e, w=w, dst=dst: e.copy(out=dst[0][:, t0:t0 + 128], in_=pst[0:64, w, :]), reads=[pstb], writes=[dst[1]])
        if True:
            es2 = es
            w1 = mkT(es2, nc, "ns_w1", [128, 32, 256], BF16)
            stgs = [mkT(es2, nc, "ns_w1s%d" % k, [128, 2, 256]) for k in range(2)]
            for l0 in range(0, 32, 2):
                stg = stgs[(l0 // 2) % 2]
                em.dma("sp", stg[0][:], prm["ns_w1kv"][:, l0:l0 + 2, :], writes=[stg[1]])
                em.op("pool", lambda e, l0=l0, stg=stg: e.tensor_copy(out=w1[0][:, l0:l0 + 2, :], in_=stg[0][:]), reads=[stg[1]], writes=[w1[1]])
            kvc = mkT(es2, nc, "ns_kvc", [128, T], BF16)
            CH = min(512, T)
            stg2s = [mkT(es2, nc, "ns_kvs%d" % k, [128, CH]) for k in range(2)]
            for c0 in range(0, T, CH):
                stg2 = stg2s[(c0 // CH) % 2]
                em.dma("sp", stg2[0][:], p_cm[0:128, c0:c0 + CH], writes=[stg2[1]])
                em.op("pool", lambda e, c0=c0, stg2=stg2: e.tensor_copy(out=kvc[0][:, c0:c0 + CH], in_=stg2[0][:]), reads=[stg2[1]], writes=[kvc[1]])
            posb = mkT(es2, nc, "ns_posb", [128, 34], BF16)
            em.op("dve", lambda e: e.memset(posb[0][:], 0.0), writes=[posb[1]])
            em.op("act", lambda e: e.copy(out=posb[0][:, 0:32], in_=posT[0][:]), reads=[posT[1]], writes=[posb[1]])
            w2kb = mkT(es2, nc, "ns_w2kb", [128, 2, 64], BF16); w2vb = mkT(es2, nc, "ns_w2vb", [128, 2, 64], BF16)
            em.op("act", lambda e: e.copy(out=w2kb[0][:], in_=w2k[0][:]), reads=[w2k[1]], writes=[w2kb[1]])
            em.op("act", lambda e: e.copy(out=w2vb[0][:], in_=w2v[0][:]), reads=[w2v[1]], writes=[w2vb[1]])
            hact = mkT(es2, nc, "ns_hact", [128, 2, 2, 512], BF16)
            em.op("dve", lambda e: e.memset(hact[0][:], 0.0), writes=[hact[1]])
            bias = mkT(es2, nc, "ns_hb", [128, 4])
            hx = mkT(es2, nc, "ns_hx", [128, 512]); hu = mkT(es2, nc, "ns_hu", [128, 512])
            NJ = NCMP
            for kv in range(2):
                rows = slice(kv * 64, kv * 64 + 64)
                for half in range(2):
                    ph, phb = psum[kv * 2 + half]
                    for l in range(32):
                        em.op("pe", lambda e, l=l, ph=ph: e.matmul(ph[:, 0:NJ], lhsT=w1[0][rows, l, half * 128:(half + 1) * 128],
                                                                    rhs=kvc[0][rows, l:l + 16 * (NJ - 1) + 1:16], start=(l == 0), stop=(l == 31)),
                              reads=[w1[1], kvc[1]], writes=[phb], inc=(l == 31))
                    pb, pbb = psum[4]
                    for l in range(32):
                        em.op("pe", lambda e, l=l: e.matmul(pb[:, 0:2], lhsT=w1[0][rows, l, half * 128:(half + 1) * 128], rhs=posb[0][rows, l:l + 2],
                                                            start=(l == 0), stop=(l == 31)), reads=[w1[1], posb[1]], writes=[pbb], inc=(l == 31))
                    bi = kv * 2 + half
                    em.op("act", lambda e, bi=bi: e.copy(out=bias[0][:, bi:bi + 1], in_=pb[:, 0:1]), reads=[pbb], writes=[bias[1]])
                    em.op("dve", lambda e, bi=bi, ph=ph: e.tensor_scalar(out=hx[0][:, 0:NJ], in0=ph[:, 0:NJ], scalar1=bias[0][:, bi:bi + 1], scalar2=None, op0=ALU.add),
                          reads=[phb, bias[1]], writes=[hx[1]])
                    em.op("dve", lambda e: e.tensor_tensor(out=hu[0][:, 0:NJ], in0=hx[0][:, 0:NJ], in1=hx[0][:, 0:NJ], op=ALU.mult), reads=[hx[1]], writes=[hu[1]])
                    em.op("dve", lambda e: e.tensor_scalar(out=hu[0][:, 0:NJ], in0=hu[0][:, 0:NJ], scalar1=0.044715, scalar2=1.0, op0=ALU.mult, op1=ALU.add),
                          reads=[hu[1]], writes=[hu[1]])
                    em.op("dve", lambda e: e.tensor_tensor(out=hu[0][:, 0:NJ], in0=hu[0][:, 0:NJ], in1=hx[0][:, 0:NJ], op=ALU.mult), reads=[hu[1], hx[1]], writes=[hu[1]])
                    em.op("act", lambda e: e.activation(out=hu[0][:, 0:NJ], in_=hu[0][:, 0:NJ], func=AF.Sigmoid, scale=1.5957691216057308), reads=[hu[1]], writes=[hu[1]])
                    em.op("dve", lambda e, kv=kv, half=half: e.tensor_tensor(out=hact[0][:, kv, half, 0:NJ], in0=hu[0][:, 0:NJ], in1=hx[0][:, 0:NJ], op=ALU.mult),
                          reads=[hu[1], hx[1]], writes=[hact[1]])
            pk, pkb = psum[5]
            for half in range(2):
                em.op("pe", lambda e, half=half: e.matmul(pk[0:64, 0:512], lhsT=w2kb[0][:, half, :], rhs=hact[0][:, 0, half, :], start=(half == 0), stop=(half == 1)),
                      reads=[w2kb[1], hact[1]], writes=[pkb], inc=(half == 1))
            kc_f = mkT(es2, nc, "ns_kcf", [64, 512]); kc_sq = mkT(es2, nc, "ns_kcsq", [64, 512]); kc_r = mkT(es2, nc, "ns_kcr", [64, 512])
            em.op("act", lambda e: e.copy(out=kc_f[0][:], in_=pk[0:64, 0:512]), reads=[pkb], writes=[kc_f[1]])
            em.op("dve", lambda e: e.tensor_tensor(out=kc_sq[0][:], in0=kc_f[0][:], in1=kc_f[0][:], op=ALU.mult), reads=[kc_f[1]], writes=[kc_sq[1]])
            pk2, pk2b = psum[6]
            em.op("pe", lambda e: e.matmul(pk2[0:64, 0:512], lhsT=ones_f[0][0:64, 0:64], rhs=kc_sq[0][:], start=True, stop=True),
                  reads=[ones_f[1], kc_sq[1]], writes=[pk2b])
            em.op("dve", lambda e: e.tensor_scalar(out=kc_r[0][:], in0=pk2[0:64, 0:512], scalar1=1.0 / 64, scalar2=1e-6, op0=ALU.mult, op1=ALU.add),
                  reads=[pk2b], writes=[kc_r[1]])
            em.op("act", lambda e: e.sqrt(out=kc_r[0][:], in_=kc_r[0][:]), reads=[kc_r[1]], writes=[kc_r[1]])
            em.op("dve", lambda e: e.reciprocal(out=kc_r[0][:], in_=kc_r[0][:]), reads=[kc_r[1]], writes=[kc_r[1]])
            em.op("dve", lambda e: e.tensor_tensor(out=kc_f[0][:], in0=kc_f[0][:], in1=kc_r[0][:], op=ALU.mult), reads=[kc_f[1], kc_r[1]], writes=[kc_f[1]])
            em.op("dve", lambda e: e.tensor_scalar(out=kcT[0][:, 0:NJ], in0=kc_f[0][:, 0:NJ], scalar1=gk0[0][:, 0:1], scalar2=None, op0=ALU.mult),
                  reads=[kc_f[1], gk0[1]], writes=[kcT[1]])
            for jc in range(NJC):
                pv, pvb = psum[jc % 4]
                for half in range(2):
                    em.op("pe", lambda e, half=half, jc=jc, pv=pv: e.matmul(pv[:, 0:64], lhsT=hact[0][:, 1, half, jc * 128:(jc + 1) * 128], rhs=w2vb[0][:, half, :],
                                                                           start=(half == 0), stop=(half == 1)), reads=[hact[1], w2vb[1]], writes=[pvb], inc=(half == 1))
                em.op("act", lambda e, jc=jc, pv=pv: e.copy(out=vc[0][:, jc, 0:64], in_=pv[:, 0:64]), reads=[pvb], writes=[vc[1]])
        QS = []
        for k in range(2):
            QS.append(dict(
                qraw=mkT(es, nc, "ns_qraw%d" % k, [128, 256]), gtr=mkT(es, nc, "ns_gtr%d" % k, [128, 12]), gts=mkT(es, nc, "ns_gts%d" % k, [128, 12]),
                qn=mkT(es, nc, "ns_qn%d" % k, [128, 256]), qr=mkT(es, nc, "ns_qr%d" % k, [128, 256]),
                qnb=mkT(es, nc, "ns_qnb%d" % k, [128, 256], BF16), qrb=mkT(es, nc, "ns_qrb%d" % k, [128, 256], BF16),
                qTn=mkT(es, nc, "ns_qTn%d" % k, [64, 512], BF16), qTr=mkT(es, nc, "ns_qTr%d" % k, [64, 512], BF16),
                acc=mkT(es, nc, "ns_acc%d" % k, [128, 256]), sq=mkT(es, nc, "ns_sqq%d" % k, [128, 256]), ss=mkT(es, nc, "ns_ssq%d" % k, [128, 3, 4]),
                rtmp=mkT(es, nc, "ns_rtq%d" % k, [128, 4, 4, 8])))
        FS = [dict(oT_sb=mkT(es, nc, "ns_oT%d" % k, [65, 512]), otm=mkT(es, nc, "ns_otm%d" % k, [128, 4, 65]),
                   rz=mkT(es, nc, "ns_rz%d" % k, [128, 4]), wz=mkT(es, nc, "ns_wz%d" % k, [128, 4])) for k in range(3)]
        eTs = [mkT(es, nc, "ns_eT%d" % k, [128, 512], BF16) for k in range(2)]
        imp = mkT(es, nc, "ns_imp", [128, 128]); rp = mkT(es, nc, "ns_rp", [128, 128])
        mx = mkT(es, nc, "ns_mx", [128, 8]); thr = mkT(es, nc, "ns_thr", [128, 1])
        nsel = mkT(es, nc, "ns_nsel", [128, 128], BF16)
        nselT4 = mkT(es, nc, "ns_nselT4", [128, 512], BF16)
        st = {"n_s": 0, "n_f": 0}

        def prep_a(i):
            Q = QS[i % 2]
            t0 = i * 128
            qraw, gtr, gts, qn, qr, qnb, qrb, qTn, qTr = (Q[k] for k in ("qraw", "gtr", "gts", "qn", "qr", "qnb", "qrb", "qTn", "qTr"))
            em.dma("sp", qraw[0][:], p_tm[t0:t0 + 128, 0:256], writes=[qraw[1]])
            em.dma("sp", gtr[0][:], p_tm[t0:t0 + 128, 512:524], writes=[gtr[1]])
            em.op("act", lambda e: e.activation(out=gts[0][:], in_=gtr[0][:], func=AF.Sigmoid), reads=[gtr[1]], writes=[gts[1]])
            headnorm(em, qraw[0], qraw[1], 4, gq[0][:].rearrange("p (h d) -> p h d", h=4), gq[1], Q["sq"], Q["ss"],
                     qn[0][:].rearrange("p (h d) -> p h d", h=4), qn[1])
            em.op("dve", lambda e: e.tensor_copy(out=qr[0][:], in_=qn[0][:]), reads=[qn[1]], writes=[qr[1]])
            rope_apply(em, qn[0][:].rearrange("p (h d) -> p h d", h=4), qn[1], qr[0][:].rearrange("p (h d) -> p h d", h=4), qr[1],
                       cosT[0][:, i, :], sinT[0][:, i, :], cosT[1], Q["rtmp"], 4)
            em.op("dve", lambda e: e.tensor_copy(out=qnb[0][:], in_=qn[0][:]), reads=[qn[1]], writes=[qnb[1]])
            em.op("dve", lambda e: e.tensor_copy(out=qrb[0][:], in_=qr[0][:]), reads=[qr[1], sinT[1]], writes=[qrb[1]])

        def prep_b(i):
            Q = QS[i % 2]
            qnb, qrb, qTn, qTr = (Q[k] for k in ("qnb", "qrb", "qTn", "qTr"))
            for r in range(4):
                em.op("pe", lambda e, r=r: e.transpose(out=pst[0:64, r, :], in_=qnb[0][:, r * 64:(r + 1) * 64], identity=ident_bf[0][:]),
                      reads=[qnb[1], ident_bf[1]], writes=[pstb], inc=False)
            for r in range(4):
                em.op("pe", lambda e, r=r: e.transpose(out=pst[0:64, 4 + r, :], in_=qrb[0][:, r * 64:(r + 1) * 64], identity=ident_bf[0][:]),
                      reads=[qrb[1], ident_bf[1]], writes=[pstb], inc=(r == 3))
            em.op("dve", lambda e: e.tensor_copy(out=qTn[0][:].rearrange("p (r t) -> p r t", r=4), in_=pst[0:64, 0:4, :]), reads=[pstb], writes=[qTn[1]])
            em.op("dve", lambda e: e.tensor_copy(out=qTr[0][:].rearrange("p (r t) -> p r t", r=4), in_=pst[0:64, 4:8, :]), reads=[pstb], writes=[qTr[1]])

        def job_scores(job):
            kT, c, q_sb, extra = job["kT"], job["c"], job["q"], job["extra"]
            k = st["n_s"] % 2
            st["n_s"] += 1
            ps, psb = psum[k]
            job["ps"] = (ps, psb); job["eT"] = eTs[k]
            nm = len(extra)
            em.op("pe", lambda e: e.matmul(ps[:, :], lhsT=kT[0][:, c * 128:(c + 1) * 128], rhs=q_sb[0][:, :], start=True, stop=(nm == 0)),
                  reads=[kT[1], q_sb[1]], writes=[psb], inc=(nm == 0))
            for kk_, (l_ap, l_b, r_ap, r_b) in enumerate(extra):
                em.op("pe", lambda e, l_ap=l_ap, r_ap=r_ap, kk_=kk_: e.matmul(ps[:, :], lhsT=l_ap, rhs=r_ap, start=False, stop=(kk_ == nm - 1)),
                      reads=[l_b, r_b], writes=[psb], inc=(kk_ == nm - 1))

        def job_finish(job):
            ps, psb = job["ps"]; eT = job["eT"]; po = job["po"]
            em.op("act", lambda e: e.activation(out=eT[0][:], in_=ps[:, :], func=AF.Exp, scale=0.125), reads=[psb], writes=[eT[1]])
            em.op("pe", lambda e: e.matmul(po[0][0:65, :], lhsT=job["v"], rhs=eT[0][:], start=job["first"], stop=job["last"]),
                  reads=[job["vb"], eT[1]], writes=[po[1]], inc=True)
            if job.get("imp") is not None:
                jc, njc = job["imp"]
                for r in range(4):
                    em.op("pe", lambda e, r=r: e.matmul(psum[5][0][:, r * 128:(r + 1) * 128], lhsT=eT[0][:, r * 128:(r + 1) * 128], rhs=c2s[0][:, jc, :],
                                                        start=(jc == 0), stop=(jc == njc - 1)), reads=[eT[1], c2s[1]], writes=[psum[5][1]], inc=(r == 3))
            if job.get("after") is not None:
                job["after"]()

        def finish(po, br, first_branch, Q):
            F = FS[st["n_f"] % 3]
            st["n_f"] += 1
            oT_sb, otm, rz, wz = F["oT_sb"], F["otm"], F["rz"], F["wz"]
            acc, gts = Q["acc"], Q["gts"]
            em.op("act", lambda e: e.copy(out=oT_sb[0][:], in_=po[0][0:65, :]), reads=[po[1]], writes=[oT_sb[1]])
            p6, p6b = psum[6]
            for r in range(4):
                em.op("pe", lambda e, r=r: e.transpose(out=p6[:, r * 65:(r + 1) * 65], in_=oT_sb[0][:, r * 128:(r + 1) * 128], identity=ident_f[0][0:65, 0:65]),
                      reads=[oT_sb[1], ident_f[1]], writes=[p6b], inc=(r == 3))
            em.op("dve", lambda e: e.tensor_copy(out=otm[0][:], in_=p6[:, 0:260].rearrange("p (r d) -> p r d", r=4)), reads=[p6b], writes=[otm[1]])
            em.op("dve", lambda e: e.tensor_scalar(out=rz[0][:], in0=otm[0][:, :, 64], scalar1=1e-30, scalar2=None, op0=ALU.max), reads=[otm[1]], writes=[rz[1]])
            em.op("dve", lambda e: e.reciprocal(out=rz[0][:], in_=rz[0][:]), reads=[rz[1]], writes=[rz[1]])
            em.op("dve", lambda e: e.tensor_tensor(out=wz[0][:], in0=rz[0][:], in1=gts[0][:, br:12:3], op=ALU.mult), reads=[rz[1], gts[1]], writes=[wz[1]])
            for r in range(4):
                if first_branch:
                    em.op("dve", lambda e, r=r: e.tensor_scalar(out=acc[0][:, r * 64:(r + 1) * 64], in0=otm[0][:, r, 0:64], scalar1=wz[0][:, r:r + 1], scalar2=None, op0=ALU.mult),
                          reads=[otm[1], wz[1]], writes=[acc[1]])
                else:
                    em.op("dve", lambda e, r=r: e.scalar_tensor_tensor(out=acc[0][:, r * 64:(r + 1) * 64], in0=otm[0][:, r, 0:64], scalar=wz[0][:, r:r + 1],
                                                                     in1=acc[0][:, r * 64:(r + 1) * 64], op0=ALU.mult, op1=ALU.add),
                          reads=[otm[1], wz[1], acc[1]], writes=[acc[1]])
            return F

        def topk(i, F):
            rz = F["rz"]
            pimp = psum[5]
            for r in range(4):
                if r == 0:
                    em.op("dve", lambda e: e.tensor_scalar(out=imp[0][:], in0=pimp[0][:, 0:128], scalar1=rz[0][:, 0:1], scalar2=None, op0=ALU.mult),
                          reads=[pimp[1], rz[1]], writes=[imp[1]])
                else:
                    em.op("dve", lambda e, r=r: e.scalar_tensor_tensor(out=imp[0][:], in0=pimp[0][:, r * 128:(r + 1) * 128], scalar=rz[0][:, r:r + 1], in1=imp[0][:],
                                                                       op0=ALU.mult, op1=ALU.add), reads=[pimp[1], rz[1], imp[1]], writes=[imp[1]])
            o0 = 126 - 2 * i
            em.op("dve", lambda e: e.tensor_tensor(out=imp[0][:], in0=imp[0][:], in1=keepb[0][:, o0:o0 + 128], op=ALU.mult), reads=[imp[1], keepb[1]], writes=[imp[1]])
            em.op("dve", lambda e: e.tensor_tensor(out=imp[0][:], in0=imp[0][:], in1=addb[0][:, o0:o0 + 128], op=ALU.add), reads=[imp[1], addb[1]], writes=[imp[1]])
            em.op("dve", lambda e: e.memset(imp[0][:, 0:1], 1000.0), reads=[imp[1]], writes=[imp[1]])
            em.op("dve", lambda e: e.max(out=mx[0][:], in_=imp[0][:]), reads=[imp[1]], writes=[mx[1]])
            em.op("dve", lambda e: e.match_replace(out=rp[0][:], in_to_replace=mx[0][:], in_values=imp[0][:], imm_value=-1e30), reads=[imp[1], mx[1]], writes=[rp[1]])
            em.op("dve", lambda e: e.max(out=mx[0][:], in_=rp[0][:]), reads=[rp[1]], writes=[mx[1]])
            em.op("dve", lambda e: e.tensor_reduce(out=thr[0][:], in_=mx[0][:], axis=AX.X, op=ALU.min), reads=[mx[1]], writes=[thr[1]])
            em.op("dve", lambda e: e.tensor_scalar(out=rp[0][:], in0=imp[0][:], scalar1=thr[0][:, 0:1], scalar2=None, op0=ALU.is_ge), reads=[imp[1], thr[1]], writes=[rp[1]])
            em.op("dve", lambda e: e.tensor_scalar(out=nsel[0][:], in0=rp[0][:], scalar1=-1.0, scalar2=-NEG, op0=ALU.add, op1=ALU.mult), reads=[rp[1]], writes=[nsel[1]])
            em.op("pe", lambda e: e.transpose(out=pst[:, 0, :], in_=nsel[0][:], identity=ident_bf[0][:]), reads=[nsel[1], ident_bf[1]], writes=[pstb])
            em.op("dve", lambda e: e.tensor_copy(out=nselT4[0][:].rearrange("p (r t) -> p r t", r=4), in_=pst[:, 0:1, :].to_broadcast([128, 4, 128])),
                  reads=[pstb], writes=[nselT4[1]])

        prep_a(0)
        prep_b(0)
        for i in range(NQ):
            t0 = i * 128
            Q = QS[i % 2]
            qTn, qTr = Q["qTn"], Q["qTr"]
            jobs = []
            njc = (8 * i + 6) // 128 + 1
            for jc in range(njc):
                d = 8 * i - 128 * jc
                extra = []
                if 0 <= d <= 128:
                    extra.append((ident_bf[0][:], ident_bf[1], cmpmask[0][:, d // 8, :], cmpmask[1]))
                jobs.append(dict(kT=kcT, c=jc, q=qTn, extra=extra, v=vc[0][:, jc, :], vb=vc[1], po=psum[2], first=(jc == 0), last=(jc == njc - 1),
                                 imp=(jc, njc)))

            def after_cmp(i=i, Q=Q):
                F = finish(psum[2], 0, True, Q)
                topk(i, F)
            jobs[-1]["after"] = after_cmp
            cl = max(0, i - 4)
            for c in range(cl, i + 1):
                extra = []
                if c == i:
                    extra.append((ident_bf[0][:], ident_bf[1], causneg[0][:], causneg[1]))
                if c == i - 4:
                    extra.append((ident_bf[0][:], ident_bf[1], anticaus[0][:], anticaus[1]))
                jobs.append(dict(kT=kTw, c=c, q=qTr, extra=extra, v=vw[0][:, c, :], vb=vw[1], po=psum[4], first=(c == cl), last=(c == i)))
            jobs[-1]["after"] = (lambda Q=Q: finish(psum[4], 2, False, Q))
            for c in range(i + 1):
                extra = [(ebig[0][:, c * 128:(c + 1) * 128], ebig[1], nselT4[0][:], nselT4[1])]
                if c == i:
                    extra.append((ident_bf[0][:], ident_bf[1], causneg[0][:], causneg[1]))
                jobs.append(dict(kT=kTs, c=c, q=qTr, extra=extra, v=vs[0][:, c, :], vb=vs[1], po=psum[3], first=(c == 0), last=(c == i)))

            def after_sel(i=i, Q=Q, t0=t0):
                finish(psum[3], 1, False, Q)
                em.dyn_store(y_dram[t0:t0 + 128, YOFF:YOFF + 256], Q["acc"][0][:], [Q["acc"][1]])
            jobs[-1]["after"] = after_sel
            job_scores(jobs[0])
            if i + 1 < NQ:
                prep_a(i + 1)
            kb = max(0, len(jobs) - 3)
            for k in range(len(jobs)):
                if k + 1 < len(jobs):
                    job_scores(jobs[k + 1])
                if k == kb and i + 1 < NQ:
                    prep_b(i + 1)
                job_finish(jobs[k])


def stage_merge(em, NTOK, x_rows, y_rows, prm, cst, x1_dram, psum, halo_flag=None):
    nc = em.nc
    ident = cst["ident_bf"]
    with ExitStack() as es:
        g_sb = mkT(es, nc, "mg_g", [128, 8]); ld(em, g_sb, prm["attn_g"])
        stage = [mkT(es, nc, "mg_stg%d" % k, [128, 1024]) for k in range(2)]
        wg = load_weight_bf16(em, es, "mg_wg", prm["w_gate"], 1024, 3072, scale_sb=g_sb, stage=stage)
        wbr = load_weight_bf16(em, es, "mg_wbr", prm["w_br"], 2048, 1024, stage=stage)
        wo = load_weight_bf16(em, es, "mg_wo", prm["w_out"], 1024, 1024, stage=stage)
        xts = [mkT(es, nc, "mg_x%d" % k, [128, 1024]) for k in range(2)]
        yt = mkT(es, nc, "mg_y", [128, 2048]); ytb = mkT(es, nc, "mg_yb", [128, 2048], BF16)
        scr = {"sq": mkT(es, nc, "mg_sq", [128, 1024], BF16), "ss": mkT(es, nc, "mg_ss", [128, 4]), "hb": mkT(es, nc, "mg_hb", [128, 1024], BF16)}
        hTs = [mkT(es, nc, "mg_hT%d" % k, [128, 8, 128], BF16) for k in range(2)]
        yTs = [mkT(es, nc, "mg_yT%d" % k, [128, 16, 128], BF16) for k in range(2)]
        gsb = mkT(es, nc, "mg_gs", [128, 3072])
        mrg = mkT(es, nc, "mg_m", [128, 1024]); tmp = mkT(es, nc, "mg_t", [128, 512]); mrgb = mkT(es, nc, "mg_mb", [128, 1024], BF16)
        mT = mkT(es, nc, "mg_mT", [128, 8, 128], BF16)
        x1 = mkT(es, nc, "mg_x1", [128, 1024])
        pst, pstb = psum[7]
        st = {"n_ps": 0}
        NTI = NTOK // 128

        def front(ti):
            t0 = ti * 128
            xt, hT, yT = xts[ti % 2], hTs[ti % 2], yTs[ti % 2]
            em.dyn_load(xt[0][:], [xt[1]], x_rows(t0))
            for (dst_fn, src) in y_rows(t0):
                em.dyn_load(dst_fn(yt[0]), [yt[1]], src)
            if ti == 0 and halo_flag is not None:
                em.op("dve", lambda e: e.tensor_scalar(out=xt[0][:], in0=xt[0][:], scalar1=halo_flag[0][:, 0:1], scalar2=None, op0=ALU.mult),
                      reads=[xt[1], halo_flag[1]], writes=[xt[1]])
                em.op("dve", lambda e: e.tensor_scalar(out=yt[0][:], in0=yt[0][:], scalar1=halo_flag[0][:, 0:1], scalar2=None, op0=ALU.mult),
                      reads=[yt[1], halo_flag[1]], writes=[yt[1]])
            rmsnorm_T(em, xt[0], xt[1], hT[0], hT[1], 0, scr, ident, psum[7])
            em.op("dve", lambda e: e.tensor_copy(out=ytb[0][:], in_=yt[0][:]), reads=[yt[1]], writes=[ytb[1]])
            for half in range(2):
                for kc in range(8):
                    em.op("pe", lambda e, kc=kc, half=half: e.transpose(out=pst[:, kc, :], in_=ytb[0][:, (half * 8 + kc) * 128:(half * 8 + kc + 1) * 128],
                                                                       identity=ident[0][:]), reads=[ytb[1], ident[1]], writes=[pstb], inc=(kc == 7))
                em.op("dve", lambda e, half=half: e.tensor_copy(out=yT[0][:, half * 8:(half + 1) * 8, :], in_=pst[:, :, :]), reads=[pstb], writes=[yT[1]])

        def back(ti):
            t0 = ti * 128
            xt, hT, yT = xts[ti % 2], hTs[ti % 2], yTs[ti % 2]
            for cb in range(6):
                ps, psb = psum[st["n_ps"] % 6]; st["n_ps"] += 1
                for kc in range(8):
                    em.op("pe", lambda e, kc=kc, cb=cb, ps=ps: e.matmul(ps[:, :], lhsT=hT[0][:, kc, :], rhs=wg[0][:, kc, cb * 512:(cb + 1) * 512],
                                                                       start=(kc == 0), stop=(kc == 7)), reads=[hT[1], wg[1]], writes=[psb], inc=(kc == 7))
                em.op("act", lambda e, cb=cb, ps=ps: e.activation(out=gsb[0][:, cb * 512:(cb + 1) * 512], in_=ps[:, :], func=AF.Sigmoid), reads=[psb], writes=[gsb[1]])
            for m, (k0, k1) in enumerate(((0, 4), (4, 8), (8, 16))):
                for nb in range(2):
                    ps, psb = psum[st["n_ps"] % 6]; st["n_ps"] += 1
                    for kc in range(k0, k1):
                        em.op("pe", lambda e, kc=kc, nb=nb, ps=ps: e.matmul(ps[:, :], lhsT=yT[0][:, kc, :], rhs=wbr[0][:, kc, nb * 512:(nb + 1) * 512],
                                                                           start=(kc == k0), stop=(kc == k1 - 1)), reads=[yT[1], wbr[1]], writes=[psb], inc=(kc == k1 - 1))
                    gsl = gsb[0][:, m * 1024 + nb * 512:m * 1024 + (nb + 1) * 512]
                    if m == 0:
                        em.op("dve", lambda e, nb=nb, ps=ps, gsl=gsl: e.tensor_tensor(out=mrg[0][:, nb * 512:(nb + 1) * 512], in0=ps[:, :], in1=gsl, op=ALU.mult),
                              reads=[psb, gsb[1]], writes=[mrg[1]])
                    else:
                        em.op("dve", lambda e, ps=ps, gsl=gsl: e.tensor_tensor(out=tmp[0][:], in0=ps[:, :], in1=gsl, op=ALU.mult), reads=[psb, gsb[1]], writes=[tmp[1]])
                        em.op("dve", lambda e, nb=nb: e.tensor_tensor(out=mrg[0][:, nb * 512:(nb + 1) * 512], in0=mrg[0][:, nb * 512:(nb + 1) * 512], in1=tmp[0][:], op=ALU.add),
                              reads=[mrg[1], tmp[1]], writes=[mrg[1]])
            em.op("act", lambda e: e.copy(out=mrgb[0][:], in_=mrg[0][:]), reads=[mrg[1]], writes=[mrgb[1]])
            for kc in range(8):
                em.op("pe", lambda e, kc=kc: e.transpose(out=pst[:, kc, :], in_=mrgb[0][:, kc * 128:(kc + 1) * 128], identity=ident[0][:]),
                      reads=[mrgb[1], ident[1]], writes=[pstb], inc=(kc == 7))
            em.op("act", lambda e: e.copy(out=mT[0][:], in_=pst[:, :, :]), reads=[pstb], writes=[mT[1]])
            for nb in range(2):
                ps, psb = psum[st["n_ps"] % 6]; st["n_ps"] += 1
                for kc in range(8):
                    em.op("pe", lambda e, kc=kc, nb=nb, ps=ps: e.matmul(ps[:, :], lhsT=mT[0][:, kc, :], rhs=wo[0][:, kc, nb * 512:(nb + 1) * 512],
                                                                       start=(kc == 0), stop=(kc == 7)), reads=[mT[1], wo[1]], writes=[psb], inc=(kc == 7))
                em.op("dve", lambda e, nb=nb, ps=ps: e.tensor_tensor(out=x1[0][:, nb * 512:(nb + 1) * 512], in0=ps[:, :], in1=xt[0][:, nb * 512:(nb + 1) * 512], op=ALU.add),
                      reads=[psb, xt[1]], writes=[x1[1]])
            em.dma(STORE_Q, x1_dram[t0:t0 + 128, :], x1[0][:], reads=[x1[1]])

        front(0)
        for ti in range(NTI):
            if ti + 1 < NTI:
                front(ti + 1)
            back(ti)


def stage_ffn(em, NTOK, x1_dram, prm, cst, out_rows, psum, FF=2816):
    nc = em.nc
    ident = cst["ident_bf"]
    NCB = FF // 128
    with ExitStack() as es:
        g_sb = mkT(es, nc, "ff_g", [128, 8]); ld(em, g_sb, prm["ffn_g"])
        stage = [mkT(es, nc, "ff_stg%d" % k, [128, 1024]) for k in range(2)]
        wup = load_weight_bf16(em, es, "ff_wup", prm["w_up"], 1024, 2 * FF, scale_sb=g_sb, stage=stage)
        wdn = load_weight_bf16(em, es, "ff_wdn", prm["w_down"], FF, 1024, stage=stage)
        fcw = mkT(es, nc, "ff_cw", [128, 2 * NCB, 3]); ld(em, fcw, prm["ffn_cw"])
        fcb = mkT(es, nc, "ff_cb", [128, 2 * NCB]); ld(em, fcb, prm["ffn_cb"])
        xts = [mkT(es, nc, "ff_x%d" % k, [128, 1024]) for k in range(2)]
        scr = {"sq": mkT(es, nc, "ff_sq", [128, 1024], BF16), "ss": mkT(es, nc, "ff_ss", [128, 4]), "hb": mkT(es, nc, "ff_hb", [128, 1024], BF16)}
        hTs = [mkT(es, nc, "ff_hT%d" % k, [128, 8, 256], BF16) for k in range(2)]
        ucar = mkT(es, nc, "ff_car", [128, 2 * NCB, 2])
        em.op("dve", lambda e: e.memset(ucar[0][:], 0.0), writes=[ucar[1]])
        ubuf = [mkT(es, nc, "ff_ub%d" % k, [128, 258]) for k in range(4)]
        ubufc = [Buf("ff_ubc%d" % k) for k in range(4)]
        acc = [mkT(es, nc, "ff_acc%d" % k, [128, 256]) for k in range(4)]
        gact = [mkT(es, nc, "ff_ga%d" % k, [128, 256]) for k in range(2)]
        ptmp = [mkT(es, nc, "ff_pt%d" % k, [128, 256]) for k in range(2)]
        actT = mkT(es, nc, "ff_aT", [128, NCB, 256], BF16)
        xo = mkT(es, nc, "ff_xo", [128, 1024])
        xr = mkT(es, nc, "ff_xr", [128, 1024])
        st = {"n_ps": 0, "n_ub": 0}
        segs = [(0, 128, True)] + [(128 + k * 256, 256, False) for k in range((NTOK - 128) // 256)]

        def front(si):
            s0, ntok, halo = segs[si]
            hT = hTs[si % 2]
            for k in range(ntok // 128):
                em.dma("sp", xts[k][0][:], x1_dram[s0 + k * 128:s0 + (k + 1) * 128, :], writes=[xts[k][1]])
                rmsnorm_T(em, xts[k][0], xts[k][1], hT[0], hT[1], k * 128, scr, ident, psum[7])

        def back(si):
            s0, ntok, halo = segs[si]
            hT = hTs[si % 2]
            nt = ntok // 128
            for cb in range(NCB):
                blks = (cb, cb + NCB)
                pss = []
                for gi, blk in enumerate(blks):
                    ps, psb = psum[st["n_ps"] % 4]; st["n_ps"] += 1
                    pss.append((ps, psb))
                    for kc in range(8):
                        em.op("pe", lambda e, kc=kc, blk=blk, ps=ps: e.matmul(ps[:, 0:ntok], lhsT=wup[0][:, kc, blk * 128:(blk + 1) * 128], rhs=hT[0][:, kc, 0:ntok],
                                                                             start=(kc == 0), stop=(kc == 7)), reads=[hT[1], wup[1]], writes=[psb], inc=(kc == 7))
                r = st["n_ub"] % 2; st["n_ub"] += 1
                ubs = [ubuf[2 * r], ubuf[2 * r + 1]]
                ubcs = [ubufc[2 * r], ubufc[2 * r + 1]]
                acs = [acc[2 * r], acc[2 * r + 1]]
                for gi, blk in enumerate(blks):
                    em.op("dve", lambda e, blk=blk, gi=gi: e.tensor_copy(out=ubs[gi][0][:, 0:2], in_=ucar[0][:, blk, :]), reads=[ucar[1]], writes=[ubcs[gi]])
                for gi, blk in enumerate(blks):
                    ps, psb = pss[gi]
                    em.op("act", lambda e, ps=ps, gi=gi: e.copy(out=ubs[gi][0][:, 2:2 + ntok], in_=ps[:, 0:ntok]), reads=[psb], writes=[ubs[gi][1]])
                for gi, blk in enumerate(blks):
                    em.op("dve", lambda e, blk=blk, gi=gi: e.tensor_copy(out=ucar[0][:, blk, :], in_=ubs[gi][0][:, ntok:ntok + 2]), reads=[ubs[gi][1], ubcs[gi]], writes=[ucar[1]])
                if halo:
                    continue
                for k in range(3):
                    for gi, blk in enumerate(blks):
                        eng = "dve"
                        ac, ub, ubc = acs[gi], ubs[gi], ubcs[gi]
                        if k == 0:
                            em.op(eng, lambda e, blk=blk, ub=ub, ac=ac: e.tensor_scalar(out=ac[0][:, 0:ntok], in0=ub[0][:, 0:ntok], scalar1=fcw[0][:, blk, 0:1], scalar2=None, op0=ALU.mult),
                                  reads=[ub[1], ubc, fcw[1]], writes=[ac[1]])
                        elif eng == "dve":
                            em.op(eng, lambda e, blk=blk, ub=ub, ac=ac, k=k: e.scalar_tensor_tensor(out=ac[0][:, 0:ntok], in0=ub[0][:, k:k + ntok], scalar=fcw[0][:, blk, k:k + 1],
                                                                                                  in1=ac[0][:, 0:ntok], op0=ALU.mult, op1=ALU.add),
                                  reads=[ub[1], ubc, fcw[1], ac[1]], writes=[ac[1]])
                        else:
                            tp = ptmp[r]
                            em.op(eng, lambda e, blk=blk, ub=ub, k=k, tp=tp: e.tensor_scalar(out=tp[0][:, 0:ntok], in0=ub[0][:, k:k + ntok], scalar1=fcw[0][:, blk, k:k + 1], scalar2=None, op0=ALU.mult),
                                  reads=[ub[1], ubc, fcw[1]], writes=[tp[1]])
                            em.op(eng, lambda e, ac=ac, tp=tp: e.tensor_tensor(out=ac[0][:, 0:ntok], in0=ac[0][:, 0:ntok], in1=tp[0][:, 0:ntok], op=ALU.add),
                                  reads=[ac[1], tp[1]], writes=[ac[1]])
                em.op("act", lambda e: e.activation(out=gact[r][0][:, 0:ntok], in_=acs[0][0][:, 0:ntok], func=AF.Silu, bias=fcb[0][:, blks[0]:blks[0] + 1]),
                      reads=[acs[0][1], fcb[1]], writes=[gact[r][1]])
                em.op("dve", lambda e: e.scalar_tensor_tensor(out=actT[0][:, cb, 0:ntok], in0=acs[1][0][:, 0:ntok], scalar=fcb[0][:, blks[1]:blks[1] + 1],
                                                              in1=gact[r][0][:, 0:ntok], op0=ALU.add, op1=ALU.mult),
                      reads=[acs[1][1], fcb[1], gact[r][1]], writes=[actT[1]])
            if halo:
                return
            for k in range(nt):
                em.dma("sp", xr[0][:], x1_dram[s0 + k * 128:s0 + (k + 1) * 128, :], writes=[xr[1]])
                for nb in range(2):
                    ps, psb = psum[4 + (st["n_ps"] % 2)]; st["n_ps"] += 1
                    for cb in range(NCB):
                        em.op("pe", lambda e, cb=cb, nb=nb, ps=ps, k=k: e.matmul(ps[:, :], lhsT=actT[0][:, cb, k * 128:(k + 1) * 128], rhs=wdn[0][:, cb, nb * 512:(nb + 1) * 512],
                                                                                start=(cb == 0), stop=(cb == NCB - 1)), reads=[actT[1], wdn[1]], writes=[psb], inc=(cb == NCB - 1))
                    em.op("dve", lambda e, nb=nb, ps=ps: e.tensor_tensor(out=xo[0][:, nb * 512:(nb + 1) * 512], in0=ps[:, :], in1=xr[0][:, nb * 512:(nb + 1) * 512], op=ALU.add),
                          reads=[psb, xr[1]], writes=[xo[1]])
                o0 = s0 - 128 + k * 128
                em.dyn_store(out_rows(o0), xo[0][:], [xo[1]])

        front(0)
        for si in range(len(segs)):
            if si + 1 < len(segs):
                front(si + 1)
            back(si)

import ml_dtypes

T_SEQ = 8192
NT_TM = 1556
NC_CM = 1408
OFF = dict(q=0, kv=512, gate=1280, hg_q=1304, hg_f=1816, hg_i=2328, hg_g=2840, z=3352, xbc=4376, dt=5912, merge=5928)


def group_cols(g):
    ar = np.arange
    kv = lambda br, kvi: OFF["kv"] + ((br * 2 + kvi) * 2 + g) * 64 + ar(64)
    tm = np.concatenate([
        OFF["q"] + g * 256 + ar(256), kv(1, 0), kv(1, 1), kv(2, 0), kv(2, 1),
        OFF["gate"] + g * 12 + ar(12), OFF["hg_i"] + g * 256 + ar(256), OFF["hg_g"] + g * 256 + ar(256),
        OFF["z"] + g * 512 + ar(512), OFF["dt"] + g * 8 + ar(8)])
    cm = np.concatenate([
        kv(0, 0), kv(0, 1), OFF["hg_q"] + g * 256 + ar(256), OFF["hg_f"] + g * 256 + ar(256),
        OFF["xbc"] + g * 512 + ar(512), OFF["xbc"] + 1024 + g * 128 + ar(128), OFF["xbc"] + 1280 + g * 128 + ar(128)])
    assert tm.size == NT_TM and cm.size == NC_CM
    return tm, cm


_CONST_CACHE = {}


def consts():
    if _CONST_CACHE:
        return _CONST_CACHE
    bf = ml_dtypes.bfloat16
    f32 = np.float32
    i = np.arange(128)
    c = {}
    c["ident_bf"] = np.eye(128).astype(bf)
    c["ident_f"] = np.eye(128).astype(f32)
    c["triu_f"] = (i[:, None] <= i[None, :]).astype(f32)
    c["ones_f"] = np.ones((128, 128), f32)
    c["mstrict_f"] = (i[:, None] > i[None, :]).astype(f32)
    c["mcausT_f"] = (i[None, :] >= i[:, None]).astype(f32)
    c["mask2_f"] = ((i[:, None] // 64 == i[None, :] // 64) & (i[:, None] <= i[None, :])).astype(f32)
    theta = np.float32(500000.0)
    invf = (theta ** (-np.arange(0, 16, 2, dtype=np.float32) / np.float32(16))).astype(f32)
    c["c_invf"] = np.broadcast_to(invf[None, :], (128, 8)).copy()
    NEG_ = -30000.0
    ds = list(range(0, 128, 8)) + [128]
    cm = np.zeros((128, 17, 4, 128), f32)
    for k, d in enumerate(ds):
        vis = (16 * (i[:, None] - d) + 31) <= i[None, :]
        cm[:, k, :, :] = np.where(vis, 0.0, NEG_)[:, None, :]
    c["c_cmpmask"] = cm.reshape(128, 17, 512).astype(bf)
    n_cmp = (T_SEQ - 32) // 16 + 1
    cs = np.arange(n_cmp) * 16
    s_start = np.arange(128) * 64
    ov = np.clip(np.minimum(cs[:, None] + 32, s_start[None, :] + 64) - np.maximum(cs[:, None], s_start[None, :]), 0, None) / 32.0
    c2s = np.zeros((512, 128), f32)
    c2s[:n_cmp] = ov
    c["c_c2s"] = np.ascontiguousarray(c2s.reshape(4, 128, 128).transpose(1, 0, 2)).astype(bf)
    keys = np.arange(T_SEQ)
    c["c_ebig"] = (i[:, None] == (keys[None, :] // 64)).astype(bf)
    caus = np.where(i[:, None] <= i[None, :], 0.0, NEG_)
    c["c_causneg"] = np.tile(caus, (1, 4)).astype(bf)
    anti = np.where(i[:, None] > i[None, :], 0.0, NEG_)
    c["c_anticaus"] = np.tile(anti, (1, 4)).astype(bf)
    u = np.arange(254) - 126
    curp = (i >= 64).astype(np.int64)[:, None]
    forced = (u[None, :] == curp) | (u[None, :] == curp - 1)
    invalid = u[None, :] > curp
    c["c_keep"] = (~forced & ~invalid).astype(f32)
    c["c_add"] = np.where(forced, 200.0 + u[None, :], np.where(invalid, -(300.0 + u[None, :]), 0.0)).astype(f32)
    _CONST_CACHE.update(c)
    return c


CONST_SB = ["ident_bf", "ident_f", "triu_f", "ones_f", "mstrict_f", "mcausT_f", "mask2_f"]
CONST_NSA = ["c_invf", "c_cmpmask", "c_c2s", "c_ebig", "c_causneg", "c_anticaus", "c_keep", "c_add"]
PRM_B = {
    "ns_gq": ([128, 256], F32), "ns_gk": ([128, 2, 64], F32), "ns_gk0": ([64, 1], F32), "ns_posT": ([128, 32], F32),
    "ns_w1kv": ([128, 32, 256], F32), "ns_w2k": ([128, 2, 64], F32), "ns_w2v": ([128, 2, 64], F32), "ns_pos": ([128, 64], I32),
    "hg_lbl": ([128, 2, 2], F32), "hg_lflag": ([128, 2], F32), "hg_ng": ([128, 128], F32),
    "m2_cw": ([128, 6, 4], F32), "m2_cb": ([128, 6], F32), "m2_dtb": ([128, 8], F32), "m2_alog": ([128, 8], F32),
    "m2_dsk": ([128, 8], F32), "m2_ng": ([128, 512], F32),
}
CONST_SHAPES = {
    "ident_bf": ([128, 128], BF16), "ident_f": ([128, 128], F32), "triu_f": ([128, 128], F32), "ones_f": ([128, 128], F32),
    "mstrict_f": ([128, 128], F32), "mcausT_f": ([128, 128], F32), "mask2_f": ([128, 128], F32),
    "c_invf": ([128, 8], F32), "c_cmpmask": ([128, 17, 512], BF16), "c_c2s": ([128, 4, 128], BF16), "c_ebig": ([128, T_SEQ], BF16),
    "c_causneg": ([128, 512], BF16), "c_anticaus": ([128, 512], BF16), "c_keep": ([128, 254], F32), "c_add": ([128, 254], F32),
}


def bcast(v, rows=128):
    v = np.asarray(v, np.float32).reshape(1, -1)
    return np.ascontiguousarray(np.broadcast_to(v, (rows, v.shape[1])))


def prep_B(inp, l, b, g, x_b):
    tm, cm = group_cols(g)
    w_in = inp["w_in"][l]
    m = {"x": (None if x_b is None else np.ascontiguousarray(x_b)), "g_attn": np.ascontiguousarray(inp["attn_norm_g"][l].reshape(8, 128).T),
         "w_tm": np.ascontiguousarray(w_in[:, tm]), "w_cm": np.ascontiguousarray(w_in[:, cm])}
    m.update(consts())
    m["ns_gq"] = bcast(np.tile(inp["nsa_q_norm_g"][l], 4))
    m["ns_gk"] = np.ascontiguousarray(np.broadcast_to(inp["nsa_k_norm_g"][l][1:3][None], (128, 2, 64))).astype(np.float32)
    m["ns_gk0"] = np.ascontiguousarray(inp["nsa_k_norm_g"][l][0].reshape(64, 1))
    m["ns_posT"] = np.ascontiguousarray(np.concatenate([inp["nsa_cmp_pos_k"][l].T, inp["nsa_cmp_pos_v"][l].T], 0))
    w1k = inp["nsa_cmp_k_w1"][l].reshape(32, 64, 256).transpose(1, 0, 2)
    w1v = inp["nsa_cmp_v_w1"][l].reshape(32, 64, 256).transpose(1, 0, 2)
    m["ns_w1kv"] = np.ascontiguousarray(np.concatenate([w1k, w1v], 0))
    m["ns_w2k"] = np.ascontiguousarray(inp["nsa_cmp_k_w2"][l].reshape(2, 128, 64).transpose(1, 0, 2))
    m["ns_w2v"] = np.ascontiguousarray(inp["nsa_cmp_v_w2"][l].reshape(2, 128, 64).transpose(1, 0, 2))
    m["ns_pos"] = np.ascontiguousarray(inp["positions"][b].reshape(64, 128).T.astype(np.int32))
    lbl = inp["hgrn_lb_logits"][:, g * 256:(g + 1) * 256].reshape(2, 2, 128)
    m["hg_lbl"] = np.ascontiguousarray(lbl.transpose(2, 1, 0))
    m["hg_ng"] = bcast(inp["hgrn_norm_g"][l])
    m["hg_lflag"] = np.full((128, 2), float(l), np.float32)
    chs = np.concatenate([g * 512 + np.arange(512), 1024 + g * 128 + np.arange(128), 1280 + g * 128 + np.arange(128)])
    m["m2_cw"] = np.ascontiguousarray(inp["m2_conv_w"][l][:, chs].reshape(4, 6, 128).transpose(2, 1, 0))
    m["m2_cb"] = np.ascontiguousarray(inp["m2_conv_b"][l][chs].reshape(6, 128).T)
    m["m2_dtb"] = bcast(inp["m2_dt_bias"][l][g * 8:(g + 1) * 8])
    m["m2_alog"] = bcast(inp["m2_a_log"][l][g * 8:(g + 1) * 8])
    m["m2_dsk"] = bcast(inp["m2_d_skip"][l][g * 8:(g + 1) * 8])
    m["m2_ng"] = bcast(inp["m2_norm_g"][l][g * 512:(g + 1) * 512])
    return m


def build_B(layer, stages=("proj", "m2", "hg", "nsa"), T=T_SEQ, debug_out=False):
    nc = bass.Bass("TRN2", target_bir_lowering=False)
    D = lambda name, shape, dt=F32, kind="ExternalInput": nc.dram_tensor(name, shape, dt, kind=kind).ap()
    x = D("x", [T, 1024]); g_attn = D("g_attn", [128, 8]); w_tm_d = D("w_tm", [1024, NT_TM]); w_cm_d = D("w_cm", [1024, NC_CM])
    cd = {k: D(k, *CONST_SHAPES[k]) for k in CONST_SHAPES}
    prm = {k: D(k, *PRM_B[k]) for k in PRM_B}
    prm.update({k: cd[k] for k in CONST_NSA})
    y = D("y", [T, 1024], kind="ExternalOutput")
    if debug_out:
        p_tm = D("p_tm", [T, NT_TM], kind="ExternalOutput"); p_cm = D("p_cm", [NC_CM, T], kind="ExternalOutput")
    else:
        p_tm = D("p_tm", [T, NT_TM], kind="Internal"); p_cm = D("p_cm", [NC_CM, T], kind="Internal")
    with ExitStack() as es:
        em = Em(nc, es)
        psum = [(es.enter_context(nc.psum_tensor("ps%d" % i, [128, 512], F32)), Buf()) for i in range(7)]
        psum.append((es.enter_context(nc.psum_tensor("pst", [128, 8, 128], BF16)), Buf()))
        cst = {}
        for k in CONST_SB:
            cst[k] = mkT(es, nc, "k_" + k, *CONST_SHAPES[k])
            ld(em, cst[k], cd[k])
        em.dyn_init(es)
        if "proj" in stages:
            with ExitStack() as es2:
                g_sb = mkT(es2, nc, "g_sb", [128, 8]); ld(em, g_sb, g_attn)
                w_tm = load_weight_bf16(em, es2, "w_tm_sb", w_tm_d, 1024, NT_TM, scale_sb=g_sb)
                w_cm = load_weight_bf16(em, es2, "w_cm_sb", w_cm_d, 1024, NC_CM, scale_sb=g_sb)
                stage_proj(em, T, x, g_sb, w_tm, w_cm, NT_TM, NC_CM, p_tm, p_cm, cst["ident_bf"], psum)
            em.drain()
        if "m2hg" in stages:
            run_m2_hg(em, T, layer, p_tm, p_cm, prm, cst, y, psum)
            em.drain()
        if "m2" in stages:
            stage_mamba(em, T, p_tm, p_cm, prm, cst, y, psum)
            em.drain()
        if "hg" in stages:
            stage_hgrn(em, T, layer, p_tm, p_cm, prm, cst, y, psum)
            em.drain()
        if "nsa" in stages:
            stage_nsa(em, T, p_tm, p_cm, prm, cst, y, psum)
        em.drain()
        print("build_B n_inst", em.n_inst, flush=True)
    return nc


PRM_C = {
    "attn_g": ([128, 8], F32), "w_gate": ([1024, 3072], F32), "w_br": ([2048, 1024], F32), "w_out": ([1024, 1024], F32),
    "ffn_g": ([128, 8], F32), "w_up": ([1024, 5632], F32), "w_down": ([2816, 1024], F32), "ffn_cw": ([128, 44, 3], F32), "ffn_cb": ([128, 44], F32),
}
NTOK_C = 4096 + 128
CSTAGES = "both"


def prep_C(inp, l, xh, yh):
    m = {"xh": xh, "yh": yh}
    m["ident_bf"] = consts()["ident_bf"]
    m["attn_g"] = np.ascontiguousarray(inp["attn_norm_g"][l].reshape(8, 128).T)
    m["w_gate"] = np.ascontiguousarray(inp["w_in"][l][:, OFF["merge"]:OFF["merge"] + 3072])
    m["w_br"] = np.ascontiguousarray(np.concatenate([inp["w_branch_nsa"][l], inp["w_branch_hgrn"][l], inp["w_branch_m2"][l]], 0))
    m["w_out"] = np.ascontiguousarray(inp["w_out"][l])
    m["ffn_g"] = np.ascontiguousarray(inp["ffn_norm_g"][l].reshape(8, 128).T)
    m["w_up"] = np.ascontiguousarray(inp["ffn_w_up"][l])
    m["w_down"] = np.ascontiguousarray(inp["ffn_w_down"][l])
    m["ffn_cw"] = np.ascontiguousarray(inp["ffn_conv_w"][l].reshape(3, 44, 128).transpose(2, 1, 0))
    m["ffn_cb"] = np.ascontiguousarray(inp["ffn_conv_b"][l].reshape(44, 128).T)
    return m


def build_C(NTOK=NTOK_C):
    nc = bass.Bass("TRN2", target_bir_lowering=False)
    D = lambda name, shape, dt=F32, kind="ExternalInput": nc.dram_tensor(name, shape, dt, kind=kind).ap()
    xh = D("xh", [NTOK, 1024]); yh = D("yh", [NTOK, 2048])
    idd = D("ident_bf", [128, 128], BF16)
    prm = {k: D(k, *PRM_C[k]) for k in PRM_C}
    out = D("out", [NTOK - 128, 1024], kind="ExternalOutput")
    x1d = D("x1d", [NTOK, 1024], kind="Internal")
    with ExitStack() as es:
        em = Em(nc, es)
        psum = [(es.enter_context(nc.psum_tensor("ps%d" % i, [128, 512], F32)), Buf()) for i in range(7)]
        psum.append((es.enter_context(nc.psum_tensor("pst", [128, 8, 128], BF16)), Buf()))
        cst = {"ident_bf": mkT(es, nc, "k_ident", [128, 128], BF16)}
        ld(em, cst["ident_bf"], idd)
        em.dyn_init(es)
        if CSTAGES in ("both", "merge"):
            stage_merge(em, NTOK, lambda t0: xh[t0:t0 + 128, :], lambda t0: [(lambda yt: yt[:, 0:2048], yh[t0:t0 + 128, :])], prm, cst, x1d, psum)
        em.drain()
        if CSTAGES in ("both", "ffn"):
            stage_ffn(em, NTOK, x1d, prm, cst, lambda o0: out[o0:o0 + 128, :], psum)
        em.drain()
    return nc


def run_m2_hg(em, T, layer, p_tm, p_cm, prm, cst, y_dst, psum):
    with ExitStack() as es_sh:
        gm = stage_mamba_gen(em, T, p_tm, p_cm, prm, cst, y_dst, psum, es_outer=es_sh)
        gh = stage_hgrn_gen(em, T, layer, p_tm, p_cm, prm, cst, y_dst, psum, es_outer=es_sh)
        gens = [[gm, 2], [gh, 1]]
        while gens:
            for ent in list(gens):
                for _ in range(ent[1]):
                    try:
                        next(ent[0])
                    except StopIteration:
                        gens.remove(ent)
                        break
        em.drain()


def prep_fused(inp, b, s):
    m = {"x": np.ascontiguousarray(inp["x"][b]), "halo_flag": np.full((128, 1), float(s), np.float32)}
    m.update(consts())
    for l in range(2):
        mb = prep_B(inp, l, b, s, None)
        mc = prep_C(inp, l, None, None)
        for k, v in list(mb.items()) + list(mc.items()):
            if k in CONST_SHAPES or k in ("x", "xh", "yh") or v is None:
                continue
            m["%s_l%d" % (k, l)] = v
    return m


class YDst:
    def __init__(self, fams, halos, par, ntile_half):
        self.fams = fams; self.halos = halos; self.par = par; self.nth = ntile_half

    def __getitem__(self, key):
        rows, cols = key
        i = rows.start // 128
        h, ti = i // self.nth, i % self.nth + 1
        c0, w = cols.start, cols.stop - cols.start
        if c0 < 256:
            fam, coff = "n", c0
        elif c0 < 512:
            fam, coff = "h", c0 - 256
        else:
            fam, coff = "m", c0 - 512
        out = [self.fams[fam][ti][h, bass.ds(self.par, 1)][0, :, coff:coff + w]]
        if i == self.nth - 1:
            out.append(self.halos[fam][bass.ds(self.par, 1)][0, :, coff:coff + w])
        return out


def build_fused(T=T_SEQ, n_layers=2):
    nc = bass.Bass("TRN2", target_bir_lowering=False, num_devices=8)
    D = lambda name, shape, dt=F32, kind="ExternalInput", **kw: nc.dram_tensor(name, shape, dt, kind=kind, **kw).ap()
    SH = lambda name, shape: D(name, shape, kind="Internal", addr_space="Shared")
    HALF = T // 2
    NTH = HALF // 128
    x_ext = D("x", [T, 1024])
    hflag_d = D("halo_flag", [128, 1])
    cd = {k: D(k, *CONST_SHAPES[k]) for k in CONST_SHAPES}
    out_ext = D("out", [HALF, 1024], kind="ExternalOutput")
    XT = [[None] + [SH("xt%d_%d" % (bf, ti), [2, 128, 1024]) for ti in range(1, NTH + 1)] for bf in range(2)]
    XH = [SH("xh%d" % bf, [2, 128, 1024]) for bf in range(2)]
    WID = {"m": 512, "h": 256, "n": 256}
    FAM = {f: [None] + [SH("y%s_%d" % (f, ti), [2, 2, 128, WID[f]]) for ti in range(1, NTH + 1)] for f in WID}
    FAMH = {f: SH("y%s_halo" % f, [2, 128, WID[f]]) for f in WID}
    p_tm = D("p_tm", [T, NT_TM], kind="Internal"); p_cm = D("p_cm", [NC_CM, T], kind="Internal")
    x1d = D("x1d", [HALF + 128, 1024], kind="Internal")
    LP = []
    for l in range(n_layers):
        d = {"g_attn": D("g_attn_l%d" % l, [128, 8]), "w_tm": D("w_tm_l%d" % l, [1024, NT_TM]), "w_cm": D("w_cm_l%d" % l, [1024, NC_CM])}
        d.update({k: D("%s_l%d" % (k, l), *PRM_B[k]) for k in PRM_B})
        d.update({k: D("%s_l%d" % (k, l), *PRM_C[k]) for k in PRM_C})
        d.update({k: cd[k] for k in CONST_NSA})
        LP.append(d)
    with ExitStack() as es:
        em = Em(nc, es)
        psum = [(es.enter_context(nc.psum_tensor("ps%d" % i, [128, 512], F32)), Buf()) for i in range(7)]
        psum.append((es.enter_context(nc.psum_tensor("pst", [128, 8, 128], BF16)), Buf()))
        cst = {}
        for k in CONST_SB:
            cst[k] = mkT(es, nc, "k_" + k, *CONST_SHAPES[k])
            ld(em, cst[k], cd[k])
        hflag = mkT(es, nc, "hflag", [128, 1]); ld(em, hflag, hflag_d)
        for i in range(T // 128):
            h, ti = i // NTH, i % NTH + 1
            em.dma("sp", XT[0][ti][h, :, :], x_ext[i * 128:(i + 1) * 128, :])
        em.dma("sp", XH[0][0, :, :], x_ext[HALF - 128:HALF, :])
        em.drain()
        nc.all_core_barrier()
        par = nc.sync.partition_id() % 2
        par_st = (nc.gpsimd.partition_id() % 2) if STORE_Q == "pool" else par

        def x_tile_static(bf):
            def f(t0):
                i = t0 // 128
                return XT[bf][i % NTH + 1][i // NTH, :, :]
            return f

        for l in range(n_layers):
            prm = LP[l]
            bf, bn = l % 2, (l + 1) % 2
            y_dst = YDst(FAM, FAMH, par_st, NTH)
            with ExitStack() as es2:
                g_sb = mkT(es2, nc, "g_sb%d" % l, [128, 8]); ld(em, g_sb, prm["g_attn"])
                w_tm = load_weight_bf16(em, es2, "w_tm_sb%d" % l, prm["w_tm"], 1024, NT_TM, scale_sb=g_sb)
                w_cm = load_weight_bf16(em, es2, "w_cm_sb%d" % l, prm["w_cm"], 1024, NC_CM, scale_sb=g_sb)
                stage_proj(em, T, x_tile_static(bf), g_sb, w_tm, w_cm, NT_TM, NC_CM, p_tm, p_cm, cst["ident_bf"], psum)
                em.drain()
            run_m2_hg(em, T, l, p_tm, p_cm, prm, cst, y_dst, psum)
            em.drain()
            stage_nsa(em, T, p_tm, p_cm, prm, cst, y_dst, psum)
            em.drain()
            nc.all_core_barrier()

            def x_rows(t0, bf=bf):
                ti = t0 // 128
                if ti == 0:
                    return XH[bf][0, :, :]
                return XT[bf][ti][bass.ds(par, 1)][0, :, :]

            def y_rows(t0):
                ti = t0 // 128
                res = []
                for f, c0 in (("n", 0), ("h", 512), ("m", 1024)):
                    w = WID[f]
                    dst_fn = (lambda yt, c0=c0, w=w: yt[:, c0:c0 + 2 * w].rearrange("p (g c) -> p g c", g=2))
                    if ti == 0:
                        src = FAMH[f].rearrange("g r c -> r g c")
                    else:
                        src = FAM[f][ti][bass.ds(par, 1)][0].rearrange("g r c -> r g c")
                    res.append((dst_fn, src))
                return res
            stage_merge(em, HALF + 128, x_rows, y_rows, prm, cst, x1d, psum, halo_flag=hflag)
            em.drain()
            if l == n_layers - 1:
                out_rows = lambda o0: out_ext[o0:o0 + 128, :]
            else:
                def out_rows(o0, bn=bn):
                    ti = o0 // 128 + 1
                    r = [XT[bn][ti][bass.ds(par_st, 1)][0, :, :]]
                    if ti == NTH:
                        r.append(XH[bn][bass.ds(par_st, 1)][0, :, :])
                    return r
            stage_ffn(em, HALF + 128, x1d, prm, cst, out_rows, psum)
            em.drain()
            if l != n_layers - 1:
                nc.all_core_barrier()
        print("build_fused n_inst", em.n_inst, flush=True)
    return nc


_NC_CACHE = {}


def kernel(**inputs):
    inp = {k: np.asarray(v) for k, v in inputs.items()}
    B, T, Dm = inp["x"].shape
    if "F" not in _NC_CACHE:
        _NC_CACHE["F"] = build_fused()
    maps = [prep_fused(inp, b, s) for b in range(B) for s in range(2)]
    res = run_bass_kernel_spmd(_NC_CACHE["F"], maps, core_ids=list(range(8)))
    out = np.empty((B, T, Dm), np.float32)
    for b in range(B):
        for s in range(2):
            out[b, s * (T // 2):(s + 1) * (T // 2)] = np.asarray(res.results[2 * b + s]["out"])
    return out
```

```python
import numpy as np
import concourse.bass as bass
import concourse.mybir as mybir
from concourse.bass_utils import run_bass_kernel_spmd
from contextlib import ExitStack

F32 = mybir.dt.float32
BF16 = mybir.dt.bfloat16
I32 = mybir.dt.int32
AF = mybir.ActivationFunctionType
ALU = mybir.AluOpType
AX = mybir.AxisListType

SAME_ENG_SYNC = True
STORE_Q = "sp"


_UQ = [0]


def SBT(nc, name, shape, dt):
    _UQ[0] += 1
    return nc.sbuf_tensor("%s_u%d" % (name, _UQ[0]), shape, dt)


class _Keep:
    def __init__(self, es):
        self.es = es

    def __enter__(self):
        return self.es

    def __exit__(self, *a):
        return False


class Buf:
    __slots__ = ("name", "w", "r")

    def __init__(self, name=""):
        self.name = name
        self.w = None
        self.r = {}


class Em:
    def __init__(self, nc, es, n_dma_sems=12):
        self.nc = nc
        self.es = es
        self.eng = dict(pe=nc.tensor, act=nc.scalar, dve=nc.vector, pool=nc.gpsimd, sp=nc.sync)
        self.semh = {}
        self.cnt = {}
        self.seen = {k: {} for k in self.eng}
        for k in ("pe", "act", "dve", "pool"):
            self.semh[k] = es.enter_context(nc.semaphore("c_" + k))
            self.cnt[k] = 0
        self.dring = {}
        for q in ("sp", "pool", "act"):
            ring = []
            for i in range({"sp": 16, "pool": 48, "act": 2}[q]):
                key = "d_%s%d" % (q, i)
                self.semh[key] = es.enter_context(nc.semaphore(key))
                ring.append(key)
            self.dring[q] = [ring, 0, {k: 0 for k in ring}]
        self.n_inst = 0

    def _wait(self, eng, toks):
        E = self.eng[eng]
        seen = self.seen[eng]
        for (k, v) in toks:
            if k == eng:
                if eng == "pe" or not SAME_ENG_SYNC:
                    continue
                if v > self.cnt[eng]:
                    continue
            if seen.get(k, 0) < v:
                E.wait_ge(self.semh[k], v)
                seen[k] = v
                self.n_inst += 1

    def _deps(self, reads, writes):
        toks = []
        for b in reads:
            if b.w is not None:
                toks.append(b.w)
        for b in writes:
            if b.w is not None:
                toks.append(b.w)
            toks.extend(b.r.items())
        return toks

    def op(self, eng, fn, reads=(), writes=(), inc=True):
        self._wait(eng, self._deps(reads, writes))
        ins = fn(self.eng[eng])
        self.n_inst += 1
        if inc:
            self.cnt[eng] += 1
            ins.then_inc(self.semh[eng], 1)
            tok = (eng, self.cnt[eng])
        else:
            tok = (eng, self.cnt[eng] + 1)
        for b in reads:
            if b.r.get(tok[0], 0) < tok[1]:
                b.r[tok[0]] = tok[1]
        for b in writes:
            b.w = tok
            b.r = {}
        return ins

    def dma(self, q, out, in_, reads=(), writes=(), **kw):
        ring, idx, tgt = self.dring[q]
        key = ring[idx]
        if q == "pool" and tgt[key] > 0 and not any(self.seen[e].get(key, 0) >= tgt[key] for e in self.seen):
            return self.dma("sp", out, in_, reads=reads, writes=writes, **kw)
        self.dring[q][1] = (idx + 1) % len(ring)
        toks = self._deps(reads, writes)
        if tgt[key] > 0 and q != "pool":
            toks.append((key, tgt[key]))
        self._wait(q, toks)
        ins = self.eng[q].dma_start(out=out, in_=in_, **kw)
        self.n_inst += 1
        tgt[key] += 16
        ins.then_inc(self.semh[key], 16)
        tok = (key, tgt[key])
        for b in reads:
            b.r[key] = tok[1]
        for b in writes:
            b.w = tok
            b.r = {}
        return ins

    def drain(self, engs=("sp", "pool", "act", "pe", "dve")):
        toks = [(k, self.cnt[k]) for k in ("pe", "act", "dve", "pool") if self.cnt[k] > 0]
        for q in self.dring:
            for key, t in self.dring[q][2].items():
                if t > 0:
                    toks.append((key, t))
        for e in engs:
            self._wait(e, toks)


    def dyn_init(self, es, width=1024, nslots=4):
        pass

    def dyn_store(self, dram_aps, src_ap, src_bufs):
        if not isinstance(dram_aps, (list, tuple)):
            dram_aps = [dram_aps]
        for d in dram_aps:
            self.dma(STORE_Q, d, src_ap, reads=list(src_bufs))

    def dyn_load(self, dst_ap, dst_bufs, dram_ap):
        self.dma("sp", dst_ap, dram_ap, writes=list(dst_bufs))

STOP = ''

def load_weight_bf16(em, es, name, w_dram, K, N, scale_sb=None, stage_cols=1024, stage=None):
    nc = em.nc
    KC = K // 128
    w_sb = es.enter_context(SBT(nc, name, [128, KC, N], BF16))
    wb = Buf(name)
    if stage is None:
        stg = [es.enter_context(SBT(nc, name + "_stg%d" % i, [128, stage_cols], F32)) for i in range(2)]
        stgb = [Buf(name + "_stg%d" % i) for i in range(2)]
    else:
        stg = [t[0] for t in stage]
        stgb = [t[1] for t in stage]
    i = 0
    for kc in range(KC):
        for c0 in range(0, N, stage_cols):
            c1 = min(N, c0 + stage_cols)
            s, sb = stg[i % 2], stgb[i % 2]
            em.dma("sp", s[:, : c1 - c0], w_dram[kc * 128:(kc + 1) * 128, c0:c1], writes=[sb])
            if scale_sb is not None:
                sc = scale_sb[0][:, kc:kc + 1]
                em.op("pool", lambda e, s=s, sc=sc, c0=c0, c1=c1, kc=kc: e.tensor_scalar(
                    out=w_sb[:, kc, c0:c1], in0=s[:, : c1 - c0], scalar1=sc, scalar2=None, op0=ALU.mult),
                    reads=[sb, scale_sb[1]], writes=[wb])
            else:
                em.op("pool", lambda e, s=s, c0=c0, c1=c1, kc=kc: e.tensor_copy(
                    out=w_sb[:, kc, c0:c1], in_=s[:, : c1 - c0]), reads=[sb], writes=[wb])
            i += 1
    return w_sb, wb


def rmsnorm_T(em, x_tile, xb, hT, hTb, col0, scr, ident, ps_t, eps=1e-6, D=1024):
    KC = D // 128
    sq, sqb = scr["sq"]
    ss, ssb = scr["ss"]
    hb, hbb = scr["hb"]
    em.op("act", lambda e: e.activation(out=sq[:], in_=x_tile[:], func=AF.Square, accum_out=ss[:, 0:1]),
          reads=[xb], writes=[sqb, ssb])
    em.op("dve", lambda e: e.tensor_scalar(out=ss[:, 1:2], in0=ss[:, 0:1], scalar1=1.0 / D, scalar2=eps,
                                           op0=ALU.mult, op1=ALU.add), reads=[ssb], writes=[ssb])
    em.op("act", lambda e: e.sqrt(out=ss[:, 3:4], in_=ss[:, 1:2]), reads=[ssb], writes=[ssb])
    em.op("dve", lambda e: e.reciprocal(out=ss[:, 2:3], in_=ss[:, 3:4]), reads=[ssb], writes=[ssb])
    em.op("dve", lambda e: e.tensor_scalar(out=hb[:], in0=x_tile[:], scalar1=ss[:, 2:3], scalar2=None,
                                           op0=ALU.mult), reads=[xb, ssb], writes=[hbb])
    pt, ptb = ps_t
    for kc in range(KC):
        em.op("pe", lambda e, kc=kc: e.transpose(out=pt[:, kc, :], in_=hb[:, kc * 128:(kc + 1) * 128],
                                                 identity=ident[0][:]),
              reads=[hbb, ident[1]], writes=[ptb], inc=(kc == KC - 1))
    em.op("act", lambda e: e.copy(out=hT[:, :, col0:col0 + 128], in_=pt[:]), reads=[ptb], writes=[hTb])


def stage_proj(em, T, x_dram, g_sb, w_tm, w_cm, NT, NC, out_tm, out_cm, ident, psum):
    nc = em.nc
    KC = 8
    with ExitStack() as es:
        xs = [(es.enter_context(SBT(nc, "pj_x%d" % i, [128, 1024], F32)), Buf()) for i in range(3)]
        scr = {
            "sq": (es.enter_context(SBT(nc, "pj_sq", [128, 1024], F32)), Buf()),
            "ss": (es.enter_context(SBT(nc, "pj_ss", [128, 4], F32)), Buf()),
            "hb": (es.enter_context(SBT(nc, "pj_hb", [128, 1024], BF16)), Buf()),
        }
        hTs = [(es.enter_context(SBT(nc, "pj_hT%d" % i, [128, KC, 512], BF16)), Buf()) for i in range(2)]
        otm = [(es.enter_context(SBT(nc, "pj_otm%d" % i, [128, NT], F32)), Buf()) for i in range(3)]
        ocm = [(es.enter_context(SBT(nc, "pj_ocm%d" % i, [128, 512], F32)), Buf()) for i in range(6)]
        n_x = 0
        n_otm = 0
        n_ocm = 0
        n_ps = 0
        NST = T // 512
        def prep_st(st):
            nonlocal n_x
            hT, hTb = hTs[st % 2]
            for sub in range(4):
                t0 = st * 512 + sub * 128
                x_t, x_b = xs[n_x % 3]
                n_x += 1
                em.dma("sp", x_t[:], (x_dram(t0) if callable(x_dram) else x_dram[t0:t0 + 128, :]), writes=[x_b])
                rmsnorm_T(em, x_t, x_b, hT, hTb, sub * 128, scr, ident, psum[7])

        prep_st(0)
        for st in range(NST):
            hT, hTb = hTs[st % 2]
            if st + 1 < NST:
                prep_st(st + 1)
            if STOP == "n":
                continue
            for sub in range(4):
                t0 = st * 512 + sub * 128
                o_t, o_b = otm[n_otm % 3]
                n_otm += 1
                for c0 in range(0, NT, 512):
                    c1 = min(NT, c0 + 512)
                    ps, psb = psum[n_ps % 6]
                    n_ps += 1
                    for kc in range(KC):
                        em.op("pe", lambda e, kc=kc, ps=ps, c0=c0, c1=c1, sub=sub: e.matmul(
                            ps[:, : c1 - c0], lhsT=hT[:, kc, sub * 128:(sub + 1) * 128], rhs=w_tm[0][:, kc, c0:c1],
                            start=(kc == 0), stop=(kc == KC - 1)),
                            reads=[hTb, w_tm[1]], writes=[psb], inc=(kc == KC - 1))
                    ev = "act" if (n_ps % 2) else "dve"
                    if ev == "act":
                        em.op("act", lambda e, ps=ps, c0=c0, c1=c1, o_t=o_t: e.copy(out=o_t[:, c0:c1], in_=ps[:, : c1 - c0]),
                              reads=[psb], writes=[o_b])
                    else:
                        em.op("dve", lambda e, ps=ps, c0=c0, c1=c1, o_t=o_t: e.tensor_copy(out=o_t[:, c0:c1], in_=ps[:, : c1 - c0]),
                              reads=[psb], writes=[o_b])
                em.dma(STORE_Q, out_tm[t0:t0 + 128, :], o_t[:], reads=[o_b])
            if STOP == "t":
                continue
            for cb in range(NC // 128):
                ps, psb = psum[n_ps % 6]
                n_ps += 1
                o_t, o_b = ocm[n_ocm % 6]
                n_ocm += 1
                for kc in range(KC):
                    em.op("pe", lambda e, kc=kc, ps=ps, cb=cb: e.matmul(
                        ps[:, :], lhsT=w_cm[0][:, kc, cb * 128:(cb + 1) * 128], rhs=hT[:, kc, :],
                        start=(kc == 0), stop=(kc == KC - 1)),
                        reads=[hTb, w_cm[1]], writes=[psb], inc=(kc == KC - 1))
                ev = "act" if (n_ps % 2) else "dve"
                if ev == "act":
                    em.op("act", lambda e, ps=ps, o_t=o_t: e.copy(out=o_t[:], in_=ps[:]), reads=[psb], writes=[o_b])
                else:
                    em.op("dve", lambda e, ps=ps, o_t=o_t: e.tensor_copy(out=o_t[:], in_=ps[:]), reads=[psb], writes=[o_b])
                em.dma(STORE_Q, out_cm[cb * 128:(cb + 1) * 128, st * 512:(st + 1) * 512], o_t[:], reads=[o_b])


def mkT(es, nc, name, shape, dt=F32):
    return (es.enter_context(SBT(nc, name, shape, dt)), Buf(name))


def ld(em, t, src):
    em.dma("sp", t[0][:], src, writes=[t[1]])


def stage_mamba_gen(em, T, p_tm, p_cm, prm, cst, y_dram, psum, TM_Z=1036, TM_DT=1548, CM_X=640, YOFF=512, es_outer=None):
    nc = em.nc
    NCH = T // 128
    with (_Keep(es_outer) if es_outer is not None else ExitStack()) as es:
        cw = mkT(es, nc, "m2cw", [128, 6, 4]); ld(em, cw, prm["m2_cw"])
        cb = mkT(es, nc, "m2cb", [128, 6]); ld(em, cb, prm["m2_cb"])
        dtb = mkT(es, nc, "m2dtb", [128, 8]); ld(em, dtb, prm["m2_dtb"])
        alog = mkT(es, nc, "m2alog", [128, 8]); ld(em, alog, prm["m2_alog"])
        dsk = mkT(es, nc, "m2dsk", [128, 8]); ld(em, dsk, prm["m2_dsk"])
        ng = mkT(es, nc, "m2ng", [128, 512]); ld(em, ng, prm["m2_ng"])
        Aneg = mkT(es, nc, "m2A", [128, 8])
        em.op("act", lambda e: e.activation(out=Aneg[0][:], in_=alog[0][:], func=AF.Exp), reads=[alog[1]], writes=[Aneg[1]])
        em.op("dve", lambda e: e.tensor_scalar(out=Aneg[0][:], in0=Aneg[0][:], scalar1=-1.0, scalar2=None, op0=ALU.mult),
              reads=[Aneg[1]], writes=[Aneg[1]])
        stateT = mkT(es, nc, "m2st", [128, 512])
        state_bf = mkT(es, nc, "m2stb", [128, 512], BF16)
        em.op("dve", lambda e: e.memset(stateT[0][:], 0.0), writes=[stateT[1]])
        em.op("dve", lambda e: e.memset(state_bf[0][:], 0.0), writes=[state_bf[1]])
        sets = []
        for sfx in ("a", "b"):
            S = {}
            S["xin"] = mkT(es, nc, "m2xin" + sfx, [128, 6, 131])
            S["acc"] = mkT(es, nc, "m2acc" + sfx, [128, 6, 128])
            S["xact"] = mkT(es, nc, "m2xact" + sfx, [128, 6, 128])
            S["bT"] = mkT(es, nc, "m2bT" + sfx, [128, 128], BF16)
            S["cT"] = mkT(es, nc, "m2cT" + sfx, [128, 128], BF16)
            S["xs_tm"] = mkT(es, nc, "m2xs" + sfx, [128, 512])
            S["b_tm"] = mkT(es, nc, "m2btm" + sfx, [128, 128], BF16)
            S["dtr"] = mkT(es, nc, "m2dtr" + sfx, [128, 8])
            S["dtv"] = mkT(es, nc, "m2dtv" + sfx, [128, 8])
            S["av"] = mkT(es, nc, "m2a" + sfx, [128, 8])
            S["acs"] = mkT(es, nc, "m2acs" + sfx, [128, 8])
            S["tot"] = mkT(es, nc, "m2tot" + sfx, [128, 8])
            S["dsl"] = mkT(es, nc, "m2dsl" + sfx, [128, 8])
            S["dout"] = mkT(es, nc, "m2dout" + sfx, [128, 8])
            S["dtot"] = mkT(es, nc, "m2dtot" + sfx, [128, 8])
            S["xdt"] = mkT(es, nc, "m2xdt" + sfx, [128, 512])
            S["xdt_bf"] = mkT(es, nc, "m2xdtb" + sfx, [128, 512], BF16)
            S["xdd_bf"] = mkT(es, nc, "m2xddb" + sfx, [128, 512], BF16)
            S["amat"] = mkT(es, nc, "m2amat" + sfx, [128, 8, 128])
            S["LT"] = mkT(es, nc, "m2LT" + sfx, [128, 8, 128])
            S["cbm"] = mkT(es, nc, "m2cbm" + sfx, [128, 128])
            S["MT"] = mkT(es, nc, "m2MT" + sfx, [128, 8, 128], BF16)
            S["yv"] = mkT(es, nc, "m2y" + sfx, [128, 512])
            S["t2"] = mkT(es, nc, "m2t2" + sfx, [128, 512])
            S["zt"] = mkT(es, nc, "m2z" + sfx, [128, 512])
            S["sq"] = mkT(es, nc, "m2sq" + sfx, [128, 512])
            S["ss"] = mkT(es, nc, "m2ss" + sfx, [128, 4])
            sets.append(S)
        def bc8(t):
            return t[0][:, :].unsqueeze(2).to_broadcast([128, 8, 64])

        def v3(t):
            return t[0][:].rearrange("p (h d) -> p h d", h=8)

        def load_chunk(c):
            S = sets[c % 2]
            xin, dtr, zt = S["xin"], S["dtr"], S["zt"]
            t0 = c * 128
            if c == 0:
                em.op("dve", lambda e: e.memset(xin[0][:, :, 0:3], 0.0), writes=[xin[1]])
                for blk in range(6):
                    em.dma("sp", xin[0][:, blk, 3:131], p_cm[CM_X + blk * 128:CM_X + (blk + 1) * 128, 0:128], writes=[xin[1]])
            else:
                for blk in range(6):
                    em.dma("sp", xin[0][:, blk, :], p_cm[CM_X + blk * 128:CM_X + (blk + 1) * 128, t0 - 3:t0 + 128], writes=[xin[1]])
            em.dma("sp", dtr[0][:], p_tm[t0:t0 + 128, TM_DT:TM_DT + 8], writes=[dtr[1]])
            em.dma("sp", zt[0][:], p_tm[t0:t0 + 128, TM_Z:TM_Z + 512], writes=[zt[1]])

        load_chunk(0)
        for c in range(NCH):
            t0 = c * 128
            if c + 1 < NCH:
                load_chunk(c + 1)
            S = sets[c % 2]
            xin = S["xin"]
            acc = S["acc"]
            xact = S["xact"]
            bT = S["bT"]
            cT = S["cT"]
            xs_tm = S["xs_tm"]
            b_tm = S["b_tm"]
            dtr = S["dtr"]
            dtv = S["dtv"]
            av = S["av"]
            acs = S["acs"]
            tot = S["tot"]
            dsl = S["dsl"]
            dout = S["dout"]
            dtot = S["dtot"]
            xdt = S["xdt"]
            xdt_bf = S["xdt_bf"]
            xdd_bf = S["xdd_bf"]
            amat = S["amat"]
            LT = S["LT"]
            cbm = S["cbm"]
            MT = S["MT"]
            yv = S["yv"]
            t2 = S["t2"]
            zt = S["zt"]
            sq = S["sq"]
            ss = S["ss"]
            for k in range(4):
                wk = cw[0][:, :, k:k + 1].to_broadcast([128, 6, 128])
                if k == 0:
                    em.op("dve", lambda e, wk=wk: e.tensor_tensor(out=acc[0][:], in0=xin[0][:, :, 0:128], in1=wk, op=ALU.mult),
                          reads=[xin[1], cw[1]], writes=[acc[1]])
                else:
                    em.op("dve", lambda e, wk=wk, k=k: e.tensor_tensor(out=xact[0][:], in0=xin[0][:, :, k:k + 128], in1=wk, op=ALU.mult),
                          reads=[xin[1], cw[1]], writes=[xact[1]])
                    em.op("dve", lambda e: e.tensor_tensor(out=acc[0][:], in0=acc[0][:], in1=xact[0][:], op=ALU.add),
                          reads=[acc[1], xact[1]], writes=[acc[1]])
            em.op("dve", lambda e: e.tensor_tensor(out=acc[0][:], in0=acc[0][:], in1=cb[0][:, :].unsqueeze(2).to_broadcast([128, 6, 128]), op=ALU.add),
                  reads=[acc[1], cb[1]], writes=[acc[1]])
            em.op("act", lambda e: e.activation(out=xact[0][:], in_=acc[0][:], func=AF.Silu), reads=[acc[1]], writes=[xact[1]])
            em.op("dve", lambda e: e.tensor_copy(out=bT[0][:], in_=xact[0][:, 4, :]), reads=[xact[1]], writes=[bT[1]])
            em.op("dve", lambda e: e.tensor_copy(out=cT[0][:], in_=xact[0][:, 5, :]), reads=[xact[1]], writes=[cT[1]])
            pA, pAb = psum[0]
            for blk in range(4):
                em.op("pe", lambda e, blk=blk: e.transpose(out=pA[:, blk * 128:(blk + 1) * 128], in_=xact[0][:, blk, :],
                                                           identity=cst["ident_f"][0][:]),
                      reads=[xact[1], cst["ident_f"][1]], writes=[pAb], inc=(blk == 3))
            em.op("act", lambda e: e.copy(out=xs_tm[0][:], in_=pA[:, :]), reads=[pAb], writes=[xs_tm[1]])
            pB, pBb = psum[1]
            em.op("pe", lambda e: e.transpose(out=pB[:, 0:128], in_=xact[0][:, 4, :], identity=cst["ident_f"][0][:]),
                  reads=[xact[1], cst["ident_f"][1]], writes=[pBb])
            em.op("act", lambda e: e.copy(out=b_tm[0][:], in_=pB[:, 0:128]), reads=[pBb], writes=[b_tm[1]])
            em.op("dve", lambda e: e.tensor_tensor(out=dtv[0][:], in0=dtr[0][:], in1=dtb[0][:], op=ALU.add),
                  reads=[dtr[1], dtb[1]], writes=[dtv[1]])
            em.op("act", lambda e: e.activation(out=dtv[0][:], in_=dtv[0][:], func=AF.Exp), reads=[dtv[1]], writes=[dtv[1]])
            em.op("act", lambda e: e.activation(out=dtv[0][:], in_=dtv[0][:], func=AF.Ln, bias=1.0), reads=[dtv[1]], writes=[dtv[1]])
            em.op("dve", lambda e: e.tensor_tensor(out=av[0][:], in0=dtv[0][:], in1=Aneg[0][:], op=ALU.mult),
                  reads=[dtv[1], Aneg[1]], writes=[av[1]])
            pC, pCb = psum[2]
            em.op("pe", lambda e: e.matmul(pC[:, 0:8], lhsT=cst["triu_f"][0][:], rhs=av[0][:], start=True, stop=True),
                  reads=[cst["triu_f"][1], av[1]], writes=[pCb])
            em.op("pe", lambda e: e.matmul(pC[:, 8:16], lhsT=cst["ones_f"][0][:], rhs=av[0][:], start=True, stop=True),
                  reads=[cst["ones_f"][1], av[1]], writes=[pCb])
            em.op("act", lambda e: e.copy(out=acs[0][:], in_=pC[:, 0:8]), reads=[pCb], writes=[acs[1]])
            em.op("act", lambda e: e.copy(out=tot[0][:], in_=pC[:, 8:16]), reads=[pCb], writes=[tot[1]])
            em.op("dve", lambda e: e.tensor_tensor(out=dsl[0][:], in0=tot[0][:], in1=acs[0][:], op=ALU.subtract),
                  reads=[tot[1], acs[1]], writes=[dsl[1]])
            em.op("act", lambda e: e.activation(out=dsl[0][:], in_=dsl[0][:], func=AF.Exp), reads=[dsl[1]], writes=[dsl[1]])
            em.op("act", lambda e: e.activation(out=dout[0][:], in_=acs[0][:], func=AF.Exp), reads=[acs[1]], writes=[dout[1]])
            em.op("act", lambda e: e.activation(out=dtot[0][:], in_=tot[0][:], func=AF.Exp), reads=[tot[1]], writes=[dtot[1]])
            em.op("dve", lambda e: e.tensor_tensor(out=v3(xdt), in0=v3(xs_tm), in1=bc8(dtv), op=ALU.mult),
                  reads=[xs_tm[1], dtv[1]], writes=[xdt[1]])
            em.op("act", lambda e: e.copy(out=xdt_bf[0][:], in_=xdt[0][:]), reads=[xdt[1]], writes=[xdt_bf[1]])
            em.op("dve", lambda e: e.tensor_tensor(out=xdd_bf[0][:].rearrange("p (h d) -> p h d", h=8), in0=v3(xdt), in1=bc8(dsl), op=ALU.mult),
                  reads=[xdt[1], dsl[1]], writes=[xdd_bf[1]])
            em.op("dve", lambda e: e.tensor_tensor(out=amat[0][:], in0=cst["mstrict_f"][0][:, :].unsqueeze(1).to_broadcast([128, 8, 128]),
                                                   in1=av[0][:, :].unsqueeze(2).to_broadcast([128, 8, 128]), op=ALU.mult),
                  reads=[cst["mstrict_f"][1], av[1]], writes=[amat[1]])
            pL0, pL0b = psum[3]
            pL1, pL1b = psum[4]
            for h in range(8):
                pl, plb = (pL0, pL0b) if h < 4 else (pL1, pL1b)
                em.op("pe", lambda e, h=h, pl=pl: e.matmul(pl[:, (h % 4) * 128:(h % 4 + 1) * 128], lhsT=amat[0][:, h, :],
                                                          rhs=cst["triu_f"][0][:], start=True, stop=True),
                      reads=[amat[1], cst["triu_f"][1]], writes=[plb])
            em.op("act", lambda e: e.activation(out=LT[0][:, 0:4, :], in_=pL0[:, :].rearrange("p (h l) -> p h l", h=4), func=AF.Exp),
                  reads=[pL0b], writes=[LT[1]])
            em.op("act", lambda e: e.activation(out=LT[0][:, 4:8, :], in_=pL1[:, :].rearrange("p (h l) -> p h l", h=4), func=AF.Exp),
                  reads=[pL1b], writes=[LT[1]])
            pD, pDb = psum[5]
            em.op("pe", lambda e: e.matmul(pD[:, 0:128], lhsT=bT[0][:], rhs=cT[0][:], start=True, stop=True),
                  reads=[bT[1], cT[1]], writes=[pDb])
            em.op("dve", lambda e: e.tensor_tensor(out=cbm[0][:], in0=pD[:, 0:128], in1=cst["mcausT_f"][0][:], op=ALU.mult),
                  reads=[pDb, cst["mcausT_f"][1]], writes=[cbm[1]])
            em.op("dve", lambda e: e.tensor_tensor(out=MT[0][:], in0=LT[0][:], in1=cbm[0][:, :].unsqueeze(1).to_broadcast([128, 8, 128]),
                                                   op=ALU.mult), reads=[LT[1], cbm[1]], writes=[MT[1]])
            pY, pYb = psum[6]
            em.op("pe", lambda e: e.matmul(pY[:, :], lhsT=cT[0][:], rhs=state_bf[0][:], start=True, stop=True),
                  reads=[cT[1], state_bf[1]], writes=[pYb])
            em.op("dve", lambda e: e.tensor_tensor(out=v3(yv), in0=pY[:, :].rearrange("p (h d) -> p h d", h=8), in1=bc8(dout), op=ALU.mult),
                  reads=[pYb, dout[1]], writes=[yv[1]])
            pZ, pZb = psum[0]
            for h in range(8):
                em.op("pe", lambda e, h=h: e.matmul(pZ[:, h * 64:(h + 1) * 64], lhsT=MT[0][:, h, :], rhs=xdt_bf[0][:, h * 64:(h + 1) * 64],
                                                    start=True, stop=True), reads=[MT[1], xdt_bf[1]], writes=[pZb], inc=(h == 7))
            em.op("dve", lambda e: e.tensor_tensor(out=yv[0][:], in0=yv[0][:], in1=pZ[:, :], op=ALU.add), reads=[yv[1], pZb], writes=[yv[1]])
            pS, pSb = psum[1]
            em.op("pe", lambda e: e.matmul(pS[:, :], lhsT=b_tm[0][:], rhs=xdd_bf[0][:], start=True, stop=True),
                  reads=[b_tm[1], xdd_bf[1]], writes=[pSb])
            em.op("dve", lambda e: e.tensor_tensor(out=v3(stateT), in0=v3(stateT), in1=bc8(dtot), op=ALU.mult),
                  reads=[stateT[1], dtot[1]], writes=[stateT[1]])
            em.op("dve", lambda e: e.tensor_tensor(out=stateT[0][:], in0=stateT[0][:], in1=pS[:, :], op=ALU.add),
                  reads=[stateT[1], pSb], writes=[stateT[1]])
            em.op("act", lambda e: e.copy(out=state_bf[0][:], in_=stateT[0][:]), reads=[stateT[1]], writes=[state_bf[1]])
            em.op("dve", lambda e: e.tensor_tensor(out=v3(t2), in0=v3(xs_tm), in1=bc8(dsk), op=ALU.mult),
                  reads=[xs_tm[1], dsk[1]], writes=[t2[1]])
            em.op("dve", lambda e: e.tensor_tensor(out=yv[0][:], in0=yv[0][:], in1=t2[0][:], op=ALU.add), reads=[yv[1], t2[1]], writes=[yv[1]])
            em.op("act", lambda e: e.activation(out=zt[0][:], in_=zt[0][:], func=AF.Silu), reads=[zt[1]], writes=[zt[1]])
            em.op("dve", lambda e: e.tensor_tensor(out=yv[0][:], in0=yv[0][:], in1=zt[0][:], op=ALU.mult), reads=[yv[1], zt[1]], writes=[yv[1]])
            em.op("act", lambda e: e.activation(out=sq[0][:], in_=yv[0][:], func=AF.Square, accum_out=ss[0][:, 0:1]),
                  reads=[yv[1]], writes=[sq[1], ss[1]])
            em.op("dve", lambda e: e.tensor_scalar(out=ss[0][:, 1:2], in0=ss[0][:, 0:1], scalar1=1.0 / 512, scalar2=1e-6,
                                                   op0=ALU.mult, op1=ALU.add), reads=[ss[1]], writes=[ss[1]])
            em.op("act", lambda e: e.sqrt(out=ss[0][:, 3:4], in_=ss[0][:, 1:2]), reads=[ss[1]], writes=[ss[1]])
            em.op("dve", lambda e: e.reciprocal(out=ss[0][:, 2:3], in_=ss[0][:, 3:4]), reads=[ss[1]], writes=[ss[1]])
            em.op("dve", lambda e: e.scalar_tensor_tensor(out=t2[0][:], in0=yv[0][:], scalar=ss[0][:, 2:3], in1=ng[0][:],
                                                          op0=ALU.mult, op1=ALU.mult), reads=[yv[1], ss[1], ng[1]], writes=[t2[1]])
            em.dyn_store(y_dram[t0:t0 + 128, YOFF:YOFF + 512], t2[0][:], [t2[1]])
            yield c


def stage_mamba(*a, **k):
    for _ in stage_mamba_gen(*a, **k):
        pass


def stage_hgrn_gen(em, T, layer, p_tm, p_cm, prm, cst, y_dram, psum, TM_I=524, TM_G=780, CM_Q=128, CM_F=384, YOFF=256, es_outer=None):
    nc = em.nc
    NT = T // 128
    with (_Keep(es_outer) if es_outer is not None else ExitStack()) as es:
        lbl = mkT(es, nc, "hglbl", [128, 2, 2]); ld(em, lbl, prm["hg_lbl"])
        ng = mkT(es, nc, "hgng", [128, 128]); ld(em, ng, prm["hg_ng"])
        lb = mkT(es, nc, "hglb", [128, 2])
        oml = mkT(es, nc, "hgoml", [128, 2])
        em.op("dve", lambda e: e.tensor_tensor(out=lb[0][:], in0=lbl[0][:, :, 1], in1=lbl[0][:, :, 0], op=ALU.subtract),
              reads=[lbl[1]], writes=[lb[1]])
        em.op("act", lambda e: e.activation(out=lb[0][:], in_=lb[0][:], func=AF.Sigmoid), reads=[lb[1]], writes=[lb[1]])
        lfl = mkT(es, nc, "hglfl", [128, 2]); ld(em, lfl, prm["hg_lflag"])
        em.op("dve", lambda e: e.tensor_tensor(out=lb[0][:], in0=lb[0][:], in1=lfl[0][:], op=ALU.mult),
              reads=[lb[1], lfl[1]], writes=[lb[1]])
        em.op("dve", lambda e: e.tensor_scalar(out=oml[0][:], in0=lb[0][:], scalar1=-1.0, scalar2=1.0, op0=ALU.mult, op1=ALU.add),
              reads=[lb[1]], writes=[oml[1]])
        W = 512
        HD = []
        for hd in range(2):
            S = mkT(es, nc, "hgS%d" % hd, [128, 128])
            Sref = mkT(es, nc, "hgSr%d" % hd, [128, 128], BF16)
            em.op("dve", lambda e: e.memset(S[0][:], 0.0), writes=[S[1]])
            em.op("dve", lambda e: e.memset(Sref[0][:], 0.0), writes=[Sref[1]])
            qT = [mkT(es, nc, "hgq%d_%d" % (hd, k), [128, W]) for k in range(2)]
            fT = [mkT(es, nc, "hgf%d_%d" % (hd, k), [128, W]) for k in range(2)]
            kk = mkT(es, nc, "hgk%d" % hd, [128, W])
            g0 = mkT(es, nc, "hgg0%d" % hd, [128, W])
            g1 = mkT(es, nc, "hgg1%d" % hd, [128, W])
            d1 = mkT(es, nc, "hgd1%d" % hd, [128, W])
            e1 = mkT(es, nc, "hge1%d" % hd, [128, W])
            qt_bf = mkT(es, nc, "hgqt%d" % hd, [128, W], BF16)
            kt_bf = mkT(es, nc, "hgkt%d" % hd, [128, W], BF16)
            qz = mkT(es, nc, "hgqz%d" % hd, [128, 2, 128], BF16)
            sc = mkT(es, nc, "hgsc%d" % hd, [128, 3, 8])
            ktm = mkT(es, nc, "hgktm%d" % hd, [128, 128], BF16)
            v_f = [mkT(es, nc, "hgvf%d_%d" % (hd, k), [128, 128]) for k in range(2)]
            v_bf = mkT(es, nc, "hgvb%d" % hd, [128, 128], BF16)
            AT = mkT(es, nc, "hgAT%d" % hd, [128, 128], BF16)
            tU = mkT(es, nc, "hgtU%d" % hd, [128, 128])
            ot = mkT(es, nc, "hgo%d" % hd, [128, 128])
            sq = mkT(es, nc, "hgsq%d" % hd, [128, 128])
            ss = mkT(es, nc, "hgss%d" % hd, [128, 4])
            gt = [mkT(es, nc, "hggt%d_%d" % (hd, k), [128, 128]) for k in range(2)]
            em.op("dve", lambda e: e.memset(qz[0][:], 0.0), writes=[qz[1]])
            HD.append(dict(S=S, Sref=Sref, qT=qT, fT=fT, kk=kk, g0=g0, g1=g1, d1=d1, e1=e1, qt_bf=qt_bf, kt_bf=kt_bf, qz=qz, sc=sc, ktm=ktm, v_f=v_f, v_bf=v_bf, AT=AT, tU=tU, ot=ot, sq=sq, ss=ss, gt=gt))
        def load_sc(hd, sc_i):
            H = HD[hd]
            c0 = sc_i * W
            q_, f_ = H["qT"][sc_i % 2], H["fT"][sc_i % 2]
            em.dma("sp", q_[0][:], p_cm[CM_Q + hd * 128:CM_Q + (hd + 1) * 128, c0:c0 + W], writes=[q_[1]])
            em.dma("sp", f_[0][:], p_cm[CM_F + hd * 128:CM_F + (hd + 1) * 128, c0:c0 + W], writes=[f_[1]])

        def load_tile(hd, gti):
            H = HD[hd]
            t0 = gti * 128
            v_, g_ = H["v_f"][gti % 2], H["gt"][gti % 2]
            em.dma("sp", v_[0][:], p_tm[t0:t0 + 128, TM_I + hd * 128:TM_I + (hd + 1) * 128], writes=[v_[1]])
            em.dma("sp", g_[0][:], p_tm[t0:t0 + 128, TM_G + hd * 128:TM_G + (hd + 1) * 128], writes=[g_[1]])

        for hd in range(2):
            load_sc(hd, 0)
            load_tile(hd, 0)
        for sc_i in range(T // W):
          for hd in range(2):
            H = HD[hd]
            if sc_i + 1 < T // W:
                load_sc(hd, sc_i + 1)
            S = H["S"]
            Sref = H["Sref"]
            qT = H["qT"][sc_i % 2]
            fT = H["fT"][sc_i % 2]
            kk = H["kk"]
            g0 = H["g0"]
            g1 = H["g1"]
            d1 = H["d1"]
            e1 = H["e1"]
            qt_bf = H["qt_bf"]
            kt_bf = H["kt_bf"]
            qz = H["qz"]
            sc = H["sc"]
            ktm = H["ktm"]
            v_bf = H["v_bf"]
            AT = H["AT"]
            tU = H["tU"]
            ot = H["ot"]
            sq = H["sq"]
            ss = H["ss"]
            PB = hd * 4
            if True:
                c0 = sc_i * W
                em.op("act", lambda e: e.activation(out=fT[0][:], in_=fT[0][:], func=AF.Sigmoid), reads=[fT[1]], writes=[fT[1]])
                em.op("dve", lambda e: e.tensor_scalar(out=fT[0][:], in0=fT[0][:], scalar1=oml[0][:, hd:hd + 1], scalar2=lb[0][:, hd:hd + 1],
                                                       op0=ALU.mult, op1=ALU.add), reads=[fT[1], oml[1], lb[1]], writes=[fT[1]])
                em.op("dve", lambda e: e.tensor_scalar(out=kk[0][:], in0=fT[0][:], scalar1=-1.0, scalar2=1.0, op0=ALU.mult, op1=ALU.add),
                      reads=[fT[1]], writes=[kk[1]])
                em.op("act", lambda e: e.activation(out=g0[0][:], in_=fT[0][:], func=AF.Ln), reads=[fT[1]], writes=[g0[1]])
                src, dst = g0, g1
                for d in (1, 2, 4, 8, 16, 32):
                    sv = src[0][:].rearrange("p (c t) -> p c t", t=64)
                    dv = dst[0][:].rearrange("p (c t) -> p c t", t=64)
                    em.op("dve", lambda e, sv=sv, dv=dv, d=d: e.tensor_copy(out=dv[:, :, 0:d], in_=sv[:, :, 0:d]), reads=[src[1]], writes=[dst[1]])
                    em.op("dve", lambda e, sv=sv, dv=dv, d=d: e.tensor_tensor(out=dv[:, :, d:64], in0=sv[:, :, d:64], in1=sv[:, :, 0:64 - d], op=ALU.add),
                          reads=[src[1]], writes=[dst[1]])
                    src, dst = dst, src
                gc = src
                gv = gc[0][:].rearrange("p (c t) -> p c t", t=64)
                em.op("dve", lambda e: e.tensor_tensor(out=d1[0][:].rearrange("p (c t) -> p c t", t=64), in0=gv,
                                                       in1=gv[:, :, 31:32].to_broadcast([128, 8, 64]), op=ALU.subtract),
                      reads=[gc[1]], writes=[d1[1]])
                em.op("act", lambda e: e.activation(out=e1[0][:], in_=d1[0][:], func=AF.Exp), reads=[d1[1]], writes=[e1[1]])
                em.op("dve", lambda e: e.tensor_tensor(out=qt_bf[0][:], in0=qT[0][:], in1=e1[0][:], op=ALU.mult), reads=[qT[1], e1[1]], writes=[qt_bf[1]])
                em.op("act", lambda e: e.activation(out=e1[0][:], in_=d1[0][:], func=AF.Exp, scale=-1.0), reads=[d1[1]], writes=[e1[1]])
                em.op("dve", lambda e: e.tensor_tensor(out=kt_bf[0][:], in0=kk[0][:], in1=e1[0][:], op=ALU.mult), reads=[kk[1], e1[1]], writes=[kt_bf[1]])
                em.op("act", lambda e: e.activation(out=sc[0][:, 0, :], in_=gv[:, :, 31], func=AF.Exp), reads=[gc[1]], writes=[sc[1]])
                em.op("act", lambda e: e.activation(out=sc[0][:, 1, :], in_=gv[:, :, 63], func=AF.Exp), reads=[gc[1]], writes=[sc[1]])
                em.op("act", lambda e: e.activation(out=sc[0][:, 2, :], in_=d1[0][:].rearrange("p (c t) -> p c t", t=64)[:, :, 63], func=AF.Exp),
                      reads=[d1[1]], writes=[sc[1]])
                em.op("dve", lambda e: e.tensor_scalar(out=Sref[0][:], in0=S[0][:], scalar1=sc[0][:, 0, 0:1], scalar2=None, op0=ALU.mult),
                      reads=[S[1], sc[1]], writes=[Sref[1]])
                for tl in range(W // 128):
                    t0 = c0 + tl * 128
                    gti = sc_i * (W // 128) + tl
                    v_f = H["v_f"][gti % 2]
                    gt = H["gt"][gti % 2]
                    if gti + 1 < T // 128:
                        load_tile(hd, gti + 1)
                    cols = slice(tl * 128, (tl + 1) * 128)
                    em.op("act", lambda e: e.copy(out=v_bf[0][:], in_=v_f[0][:]), reads=[v_f[1]], writes=[v_bf[1]])
                    pA, pAb = psum[PB + 0]
                    em.op("pe", lambda e, cols=cols: e.matmul(pA[:, 0:128], lhsT=kt_bf[0][:, cols], rhs=qt_bf[0][:, cols], start=True, stop=True),
                          reads=[kt_bf[1], qt_bf[1]], writes=[pAb])
                    em.op("dve", lambda e: e.tensor_tensor(out=AT[0][:], in0=pA[:, 0:128], in1=cst["mask2_f"][0][:], op=ALU.mult),
                          reads=[pAb, cst["mask2_f"][1]], writes=[AT[1]])
                    pT, pTb = psum[7]
                    em.op("pe", lambda e, cols=cols: e.transpose(out=pT[:, 0, :], in_=kt_bf[0][:, cols], identity=cst["ident_bf"][0][:]),
                          reads=[kt_bf[1], cst["ident_bf"][1]], writes=[pTb])
                    em.op("act", lambda e: e.copy(out=ktm[0][:], in_=pT[:, 0, :]), reads=[pTb], writes=[ktm[1]])
                    em.op("dve", lambda e, cols=cols: e.tensor_copy(out=qz[0][:, 0, 0:64], in_=qt_bf[0][:, tl * 128:tl * 128 + 64]),
                          reads=[qt_bf[1]], writes=[qz[1]])
                    em.op("dve", lambda e, cols=cols: e.tensor_copy(out=qz[0][:, 1, 64:128], in_=qt_bf[0][:, tl * 128 + 64:tl * 128 + 128]),
                          reads=[qt_bf[1]], writes=[qz[1]])
                    pO, pOb = psum[PB + 1]
                    em.op("pe", lambda e: e.matmul(pO[:, 0:128], lhsT=AT[0][:], rhs=v_bf[0][:], start=True, stop=False),
                          reads=[AT[1], v_bf[1]], writes=[pOb], inc=False)
                    for j in range(2):
                        ch = tl * 2 + j
                        em.op("pe", lambda e, j=j: e.matmul(pO[:, 0:128], lhsT=qz[0][:, j, :], rhs=Sref[0][:], start=False, stop=(j == 1)),
                              reads=[qz[1], Sref[1]], writes=[pOb], inc=True)
                        pU, pUb = psum[PB + 2 + (j if hd == 0 else 0)]
                        rows = slice(j * 64, (j + 1) * 64)
                        em.op("pe", lambda e, rows=rows, pU=pU: e.matmul(pU[:, 0:128], lhsT=ktm[0][rows, :], rhs=v_bf[0][rows, :], start=True, stop=True),
                              reads=[ktm[1], v_bf[1]], writes=[pUb])
                        em.op("dve", lambda e, pU=pU, ch=ch: e.tensor_scalar(out=tU[0][:], in0=pU[:, 0:128], scalar1=sc[0][:, 2, ch:ch + 1], scalar2=None, op0=ALU.mult),
                              reads=[pUb, sc[1]], writes=[tU[1]])
                        em.op("dve", lambda e, ch=ch: e.scalar_tensor_tensor(out=S[0][:], in0=S[0][:], scalar=sc[0][:, 1, ch:ch + 1], in1=tU[0][:],
                                                                              op0=ALU.mult, op1=ALU.add), reads=[S[1], sc[1], tU[1]], writes=[S[1]])
                        if ch < 7:
                            em.op("dve", lambda e, ch=ch: e.tensor_scalar(out=Sref[0][:], in0=S[0][:], scalar1=sc[0][:, 0, ch + 1:ch + 2], scalar2=None,
                                                                           op0=ALU.mult), reads=[S[1], sc[1]], writes=[Sref[1]])
                    em.op("act", lambda e: e.copy(out=ot[0][:], in_=pO[:, 0:128]), reads=[pOb], writes=[ot[1]])
                    em.op("act", lambda e: e.activation(out=sq[0][:], in_=ot[0][:], func=AF.Square, accum_out=ss[0][:, 0:1]),
                          reads=[ot[1]], writes=[sq[1], ss[1]])
                    em.op("dve", lambda e: e.tensor_scalar(out=ss[0][:, 1:2], in0=ss[0][:, 0:1], scalar1=1.0 / 128, scalar2=1e-6,
                                                           op0=ALU.mult, op1=ALU.add), reads=[ss[1]], writes=[ss[1]])
                    em.op("act", lambda e: e.sqrt(out=ss[0][:, 3:4], in_=ss[0][:, 1:2]), reads=[ss[1]], writes=[ss[1]])
                    em.op("dve", lambda e: e.reciprocal(out=ss[0][:, 2:3], in_=ss[0][:, 3:4]), reads=[ss[1]], writes=[ss[1]])
                    em.op("dve", lambda e: e.scalar_tensor_tensor(out=ot[0][:], in0=ot[0][:], scalar=ss[0][:, 2:3], in1=ng[0][:],
                                                                  op0=ALU.mult, op1=ALU.mult), reads=[ot[1], ss[1], ng[1]], writes=[ot[1]])
                    em.op("act", lambda e: e.activation(out=gt[0][:], in_=gt[0][:], func=AF.Silu), reads=[gt[1]], writes=[gt[1]])
                    em.op("dve", lambda e: e.tensor_tensor(out=ot[0][:], in0=ot[0][:], in1=gt[0][:], op=ALU.mult), reads=[ot[1], gt[1]], writes=[ot[1]])
                    em.dyn_store(y_dram[t0:t0 + 128, YOFF + hd * 128:YOFF + (hd + 1) * 128], ot[0][:], [ot[1]])
            yield (sc_i, hd)


def stage_hgrn(*a, **k):
    for _ in stage_hgrn_gen(*a, **k):
        pass


NEG = -30000.0
TWO_PI = 6.283185307179586


def rope_apply(em, x3, xb, o3, ob, cos2, sin2, csb, tmp, H):
    cb = cos2.unsqueeze(1).to_broadcast([128, H, 8])
    sb = sin2.unsqueeze(1).to_broadcast([128, H, 8])
    t = tmp[0]
    x1 = x3[:, :, 0:8]
    x2 = x3[:, :, 8:16]
    em.op("dve", lambda e: e.tensor_tensor(out=t[:, 0, 0:H, :], in0=x1, in1=cb, op=ALU.mult), reads=[xb, csb], writes=[tmp[1]])
    em.op("dve", lambda e: e.tensor_tensor(out=t[:, 1, 0:H, :], in0=x2, in1=sb, op=ALU.mult), reads=[xb, csb], writes=[tmp[1]])
    em.op("dve", lambda e: e.tensor_tensor(out=t[:, 2, 0:H, :], in0=x2, in1=cb, op=ALU.mult), reads=[xb, csb], writes=[tmp[1]])
    em.op("dve", lambda e: e.tensor_tensor(out=t[:, 3, 0:H, :], in0=x1, in1=sb, op=ALU.mult), reads=[xb, csb], writes=[tmp[1]])
    em.op("dve", lambda e: e.tensor_tensor(out=o3[:, :, 0:8], in0=t[:, 0, 0:H, :], in1=t[:, 1, 0:H, :], op=ALU.subtract), reads=[tmp[1]], writes=[ob])
    em.op("dve", lambda e: e.tensor_tensor(out=o3[:, :, 8:16], in0=t[:, 2, 0:H, :], in1=t[:, 3, 0:H, :], op=ALU.add), reads=[tmp[1]], writes=[ob])


def headnorm(em, x, xb, H, g_ap, gb, sq, ss, out3, outb):
    x3 = x[:, 0:H * 64].rearrange("p (h d) -> p h d", h=H)
    s3 = sq[0][:, 0:H * 64].rearrange("p (h d) -> p h d", h=H)
    em.op("dve", lambda e: e.tensor_tensor(out=s3, in0=x3, in1=x3, op=ALU.mult), reads=[xb], writes=[sq[1]])
    em.op("dve", lambda e: e.tensor_reduce(out=ss[0][:, 0, 0:H], in_=s3, axis=AX.X, op=ALU.add), reads=[sq[1]], writes=[ss[1]])
    em.op("dve", lambda e: e.tensor_scalar(out=ss[0][:, 1, 0:H], in0=ss[0][:, 0, 0:H], scalar1=1.0 / 64, scalar2=1e-6, op0=ALU.mult, op1=ALU.add),
          reads=[ss[1]], writes=[ss[1]])
    em.op("act", lambda e: e.sqrt(out=ss[0][:, 2, 0:H], in_=ss[0][:, 1, 0:H]), reads=[ss[1]], writes=[ss[1]])
    em.op("dve", lambda e: e.reciprocal(out=ss[0][:, 1, 0:H], in_=ss[0][:, 2, 0:H]), reads=[ss[1]], writes=[ss[1]])
    em.op("dve", lambda e: e.tensor_tensor(out=out3, in0=x3, in1=ss[0][:, 1, 0:H].unsqueeze(2).to_broadcast([128, H, 64]), op=ALU.mult),
          reads=[xb, ss[1]], writes=[outb])
    em.op("dve", lambda e: e.tensor_tensor(out=out3, in0=out3, in1=g_ap, op=ALU.mult), reads=[outb, gb], writes=[outb])


def stage_nsa(em, T, p_tm, p_cm, prm, cst, y_dram, psum, YOFF=0):
    nc = em.nc
    NQ = T // 128
    NCMP = (T - 32) // 16 + 1
    NJC = (NCMP + 127) // 128
    DBG = None
    def dbg(name, t, part=128):
        if DBG is None:
            return
        shape = list(t[0].shape)
        d = nc.dram_tensor("dbg_" + name, shape, t[0].dtype, kind="ExternalOutput").ap()
        em.dma("sp", d, t[0][:], reads=[t[1]])
        DBG.append("dbg_" + name)
    with ExitStack() as es:
        def LD(name, shape, dt=F32):
            t = mkT(es, nc, "ns_" + name, shape, dt)
            src = prm[name]
            if len(shape) == 2:
                src = src[0:shape[0], 0:shape[1]]
            else:
                src = src[0:shape[0], 0:shape[1], 0:shape[2]]
            ld(em, t, src)
            return t
        gq = LD("ns_gq", [128, 256]); gk = LD("ns_gk", [128, 2, 64]); gk0 = LD("ns_gk0", [64, 1])
        posT = LD("ns_posT", [128, 32]); w2k = LD("ns_w2k", [128, 2, 64]); w2v = LD("ns_w2v", [128, 2, 64])
        posi = LD("ns_pos", [128, NQ], I32)
        invf = LD("c_invf", [128, 8])
        cmpmask = LD("c_cmpmask", [128, 17, 512], BF16)
        c2s = LD("c_c2s", [128, NJC, 128], BF16)
        ebig = LD("c_ebig", [128, T], BF16)
        causneg = LD("c_causneg", [128, 512], BF16)
        anticaus = LD("c_anticaus", [128, 512], BF16)
        keepb = LD("c_keep", [128, 254]); addb = LD("c_add", [128, 254])
        ident_bf = cst["ident_bf"]; ident_f = cst["ident_f"]; ones_f = cst["ones_f"]
        pst, pstb = psum[7]
        cosT = mkT(es, nc, "ns_cos", [128, NQ, 8]); sinT = mkT(es, nc, "ns_sin", [128, NQ, 8])
        if True:
            es2 = es
            posf = mkT(es2, nc, "ns_posf", [128, NQ])
            ang = mkT(es2, nc, "ns_ang", [128, NQ, 8]); fr = mkT(es2, nc, "ns_fr", [128, NQ, 8])
            ki = mkT(es2, nc, "ns_ki", [128, NQ, 8], I32); kf = mkT(es2, nc, "ns_kf", [128, NQ, 8])
            em.op("dve", lambda e: e.tensor_copy(out=posf[0][:], in_=posi[0][:]), reads=[posi[1]], writes=[posf[1]])
            em.op("dve", lambda e: e.tensor_tensor(out=ang[0][:], in0=posf[0][:, :].unsqueeze(2).to_broadcast([128, NQ, 8]),
                                                   in1=invf[0][:, :].unsqueeze(1).to_broadcast([128, NQ, 8]), op=ALU.mult),
                  reads=[posf[1], invf[1]], writes=[ang[1]])
            for dst, off in ((sinT, 0.0), (cosT, 0.25)):
                em.op("dve", lambda e, off=off: e.tensor_scalar(out=fr[0][:], in0=ang[0][:], scalar1=1.0 / TWO_PI, scalar2=off, op0=ALU.mult, op1=ALU.add),
                      reads=[ang[1]], writes=[fr[1]])
                em.op("dve", lambda e: e.tensor_copy(out=ki[0][:], in_=fr[0][:]), reads=[fr[1]], writes=[ki[1]])
                em.op("dve", lambda e: e.tensor_copy(out=kf[0][:], in_=ki[0][:]), reads=[ki[1]], writes=[kf[1]])
                em.op("dve", lambda e: e.tensor_tensor(out=fr[0][:], in0=fr[0][:], in1=kf[0][:], op=ALU.subtract), reads=[fr[1], kf[1]], writes=[fr[1]])
                em.op("dve", lambda e: e.tensor_scalar(out=kf[0][:], in0=fr[0][:], scalar1=0.5, scalar2=None, op0=ALU.is_gt), reads=[fr[1]], writes=[kf[1]])
                em.op("dve", lambda e: e.tensor_tensor(out=fr[0][:], in0=fr[0][:], in1=kf[0][:], op=ALU.subtract), reads=[fr[1], kf[1]], writes=[fr[1]])
                em.op("dve", lambda e: e.tensor_scalar(out=kf[0][:], in0=fr[0][:], scalar1=-0.5, scalar2=None, op0=ALU.is_lt), reads=[fr[1]], writes=[kf[1]])
                em.op("dve", lambda e: e.tensor_tensor(out=fr[0][:], in0=fr[0][:], in1=kf[0][:], op=ALU.add), reads=[fr[1], kf[1]], writes=[fr[1]])
                em.op("act", lambda e, dst=dst: e.activation(out=dst[0][:], in_=fr[0][:], func=AF.Sin, scale=TWO_PI), reads=[fr[1]], writes=[dst[1]])
        cs_b = Buf("cs")
        kTs = mkT(es, nc, "ns_kTs", [64, T], BF16); kTw = mkT(es, nc, "ns_kTw", [64, T], BF16)
        vs = mkT(es, nc, "ns_vs", [128, NQ, 65], BF16); vw = mkT(es, nc, "ns_vw", [128, NQ, 65], BF16)
        em.op("dve", lambda e: e.memset(vs[0][:, :, 64:65], 1.0), writes=[vs[1]])
        em.op("dve", lambda e: e.memset(vw[0][:, :, 64:65], 1.0), writes=[vw[1]])
        kcT = mkT(es, nc, "ns_kcT", [64, NJC * 128], BF16)
        vc = mkT(es, nc, "ns_vc", [128, NJC, 65], BF16)
        em.op("dve", lambda e: e.memset(kcT[0][:], 0.0), writes=[kcT[1]])
        em.op("dve", lambda e: e.memset(vc[0][:, :, 64:65], 1.0), writes=[vc[1]])
        sq = mkT(es, nc, "ns_sq", [128, 256]); ss = mkT(es, nc, "ns_ss", [128, 3, 4])
        rtmp = mkT(es, nc, "ns_rtmp", [128, 4, 4, 8])
        if True:
            es2 = es
            KS = [dict(kvr=mkT(es2, nc, "ns_kvr%d" % k, [128, 256]), kn=mkT(es2, nc, "ns_kn%d" % k, [128, 2, 64]), kr=mkT(es2, nc, "ns_kr%d" % k, [128, 2, 64]),
                       kb=mkT(es2, nc, "ns_kb%d" % k, [128, 2, 64], BF16), sq=mkT(es2, nc, "ns_ksq%d" % k, [128, 256]), ss=mkT(es2, nc, "ns_kss%d" % k, [128, 3, 4]),
                       rtmp=mkT(es2, nc, "ns_krt%d" % k, [128, 4, 4, 8])) for k in range(2)]
            for i in range(NQ):
                t0 = i * 128
                kvr, kn, kr, kb, sq, ss, rtmp = (KS[i % 2][k] for k in ("kvr", "kn", "kr", "kb", "sq", "ss", "rtmp"))
                em.dma("sp", kvr[0][:], p_tm[t0:t0 + 128, 256:512], writes=[kvr[1]])
                em.op("act", lambda e: e.copy(out=vs[0][:, i, 0:64], in_=kvr[0][:, 64:128]), reads=[kvr[1]], writes=[vs[1]])
                em.op("act", lambda e: e.copy(out=vw[0][:, i, 0:64], in_=kvr[0][:, 192:256]), reads=[kvr[1]], writes=[vw[1]])
                for w, c0 in ((0, 0), (1, 128)):
                    xs_ = kvr[0][:, c0:c0 + 64]
                    headnorm(em, xs_, kvr[1], 1, gk[0][:, w:w + 1, :], gk[1], sq, ss, kn[0][:, w:w + 1, :], kn[1])
                em.op("act", lambda e: e.copy(out=kr[0][:], in_=kn[0][:]), reads=[kn[1]], writes=[kr[1]])
                rope_apply(em, kn[0][:], kn[1], kr[0][:], kr[1], cosT[0][:, i, :], sinT[0][:, i, :], cosT[1], rtmp, 2)
                em.op("act", lambda e: e.copy(out=kb[0][:], in_=kr[0][:]), reads=[kr[1], sinT[1]], writes=[kb[1]])
                for w, dst in ((0, kTs), (1, kTw)):
                    em.op("pe", lambda e, w=w: e.transpose(out=pst[0:64, w, :], in_=kb[0][:, w, :], identity=ident_bf[0][:]),
                          reads=[kb[1], ident_bf[1]], writes=[pstb])
                    em.op("act", lambda e, w=w, dst=dst: e.copy(out=dst[0][:, t0:t0 + 128], in_=pst[0:64, w, :]), reads=[pstb], writes=[dst[1]])
        if True:
            es2 = es
            w1 = mkT(es2, nc, "ns_w1", [128, 32, 256], BF16)
            stgs = [mkT(es2, nc, "ns_w1s%d" % k, [128, 2, 256]) for k in range(2)]
            for l0 in range(0, 32, 2):
                stg = stgs[(l0 // 2) % 2]
                em.dma("sp", stg[0][:], prm["ns_w1kv"][:, l0:l0 + 2, :], writes=[stg[1]])
                em.op("pool", lambda e, l0=l0, stg=stg: e.tensor_copy(out=w1[0][:, l0:l0 + 2, :], in_=stg[0][:]), reads=[stg[1]], writes=[w1[1]])
            kvc = mkT(es2, nc, "ns_kvc", [128, T], BF16)
            CH = min(512, T)
            stg2s = [mkT(es2, nc, "ns_kvs%d" % k, [128, CH]) for k in range(2)]
            for c0 in range(0, T, CH):
                stg2 = stg2s[(c0 // CH) % 2]
                em.dma("sp", stg2[0][:], p_cm[0:128, c0:c0 + CH], writes=[stg2[1]])
                em.op("pool", lambda e, c0=c0, stg2=stg2: e.tensor_copy(out=kvc[0][:, c0:c0 + CH], in_=stg2[0][:]), reads=[stg2[1]], writes=[kvc[1]])
            posb = mkT(es2, nc, "ns_posb", [128, 34], BF16)
            em.op("dve", lambda e: e.memset(posb[0][:], 0.0), writes=[posb[1]])
            em.op("act", lambda e: e.copy(out=posb[0][:, 0:32], in_=posT[0][:]), reads=[posT[1]], writes=[posb[1]])
            w2kb = mkT(es2, nc, "ns_w2kb", [128, 2, 64], BF16); w2vb = mkT(es2, nc, "ns_w2vb", [128, 2, 64], BF16)
            em.op("act", lambda e: e.copy(out=w2kb[0][:], in_=w2k[0][:]), reads=[w2k[1]], writes=[w2kb[1]])
            em.op("act", lambda e: e.copy(out=w2vb[0][:], in_=w2v[0][:]), reads=[w2v[1]], writes=[w2vb[1]])
            hact = mkT(es2, nc, "ns_hact", [128, 2, 2, 512], BF16)
            em.op("dve", lambda e: e.memset(hact[0][:], 0.0), writes=[hact[1]])
            bias = mkT(es2, nc, "ns_hb", [128, 4])
            hx = mkT(es2, nc, "ns_hx", [128, 512]); hu = mkT(es2, nc, "ns_hu", [128, 512])
            NJ = NCMP
            for kv in range(2):
                rows = slice(kv * 64, kv * 64 + 64)
                for half in range(2):
                    ph, phb = psum[kv * 2 + half]
                    for l in range(32):
                        em.op("pe", lambda e, l=l, ph=ph: e.matmul(ph[:, 0:NJ], lhsT=w1[0][rows, l, half * 128:(half + 1) * 128],
                                                                    rhs=kvc[0][rows, l:l + 16 * (NJ - 1) + 1:16], start=(l == 0), stop=(l == 31)),
                              reads=[w1[1], kvc[1]], writes=[phb], inc=(l == 31))
                    pb, pbb = psum[4]
                    for l in range(32):
                        em.op("pe", lambda e, l=l: e.matmul(pb[:, 0:2], lhsT=w1[0][rows, l, half * 128:(half + 1) * 128], rhs=posb[0][rows, l:l + 2],
                                                            start=(l == 0), stop=(l == 31)), reads=[w1[1], posb[1]], writes=[pbb], inc=(l == 31))
                    bi = kv * 2 + half
                    em.op("act", lambda e, bi=bi: e.copy(out=bias[0][:, bi:bi + 1], in_=pb[:, 0:1]), reads=[pbb], writes=[bias[1]])
                    em.op("dve", lambda e, bi=bi, ph=ph: e.tensor_scalar(out=hx[0][:, 0:NJ], in0=ph[:, 0:NJ], scalar1=bias[0][:, bi:bi + 1], scalar2=None, op0=ALU.add),
                          reads=[phb, bias[1]], writes=[hx[1]])
                    em.op("dve", lambda e: e.tensor_tensor(out=hu[0][:, 0:NJ], in0=hx[0][:, 0:NJ], in1=hx[0][:, 0:NJ], op=ALU.mult), reads=[hx[1]], writes=[hu[1]])
                    em.op("dve", lambda e: e.tensor_scalar(out=hu[0][:, 0:NJ], in0=hu[0][:, 0:NJ], scalar1=0.044715, scalar2=1.0, op0=ALU.mult, op1=ALU.add),
                          reads=[hu[1]], writes=[hu[1]])
                    em.op("dve", lambda e: e.tensor_tensor(out=hu[0][:, 0:NJ], in0=hu[0][:, 0:NJ], in1=hx[0][:, 0:NJ], op=ALU.mult), reads=[hu[1], hx[1]], writes=[hu[1]])
                    em.op("act", lambda e: e.activation(out=hu[0][:, 0:NJ], in_=hu[0][:, 0:NJ], func=AF.Sigmoid, scale=1.5957691216057308), reads=[hu[1]], writes=[hu[1]])
                    em.op("dve", lambda e, kv=kv, half=half: e.tensor_tensor(out=hact[0][:, kv, half, 0:NJ], in0=hu[0][:, 0:NJ], in1=hx[0][:, 0:NJ], op=ALU.mult),
                          reads=[hu[1], hx[1]], writes=[hact[1]])
            pk, pkb = psum[5]
            for half in range(2):
                em.op("pe", lambda e, half=half: e.matmul(pk[0:64, 0:512], lhsT=w2kb[0][:, half, :], rhs=hact[0][:, 0, half, :], start=(half == 0), stop=(half == 1)),
                      reads=[w2kb[1], hact[1]], writes=[pkb], inc=(half == 1))
            kc_f = mkT(es2, nc, "ns_kcf", [64, 512]); kc_sq = mkT(es2, nc, "ns_kcsq", [64, 512]); kc_r = mkT(es2, nc, "ns_kcr", [64, 512])
            em.op("act", lambda e: e.copy(out=kc_f[0][:], in_=pk[0:64, 0:512]), reads=[pkb], writes=[kc_f[1]])
            em.op("dve", lambda e: e.tensor_tensor(out=kc_sq[0][:], in0=kc_f[0][:], in1=kc_f[0][:], op=ALU.mult), reads=[kc_f[1]], writes=[kc_sq[1]])
            pk2, pk2b = psum[6]
            em.op("pe", lambda e: e.matmul(pk2[0:64, 0:512], lhsT=ones_f[0][0:64, 0:64], rhs=kc_sq[0][:], start=True, stop=True),
                  reads=[ones_f[1], kc_sq[1]], writes=[pk2b])
            em.op("dve", lambda e: e.tensor_scalar(out=kc_r[0][:], in0=pk2[0:64, 0:512], scalar1=1.0 / 64, scalar2=1e-6, op0=ALU.mult, op1=ALU.add),
                  reads=[pk2b], writes=[kc_r[1]])
            em.op("act", lambda e: e.sqrt(out=kc_r[0][:], in_=kc_r[0][:]), reads=[kc_r[1]], writes=[kc_r[1]])
            em.op("dve", lambda e: e.reciprocal(out=kc_r[0][:], in_=kc_r[0][:]), reads=[kc_r[1]], writes=[kc_r[1]])
            em.op("dve", lambda e: e.tensor_tensor(out=kc_f[0][:], in0=kc_f[0][:], in1=kc_r[0][:], op=ALU.mult), reads=[kc_f[1], kc_r[1]], writes=[kc_f[1]])
            em.op("dve", lambda e: e.tensor_scalar(out=kcT[0][:, 0:NJ], in0=kc_f[0][:, 0:NJ], scalar1=gk0[0][:, 0:1], scalar2=None, op0=ALU.mult),
                  reads=[kc_f[1], gk0[1]], writes=[kcT[1]])
            for jc in range(NJC):
                pv, pvb = psum[jc % 4]
                for half in range(2):
                    em.op("pe", lambda e, half=half, jc=jc, pv=pv: e.matmul(pv[:, 0:64], lhsT=hact[0][:, 1, half, jc * 128:(jc + 1) * 128], rhs=w2vb[0][:, half, :],
                                                                           start=(half == 0), stop=(half == 1)), reads=[hact[1], w2vb[1]], writes=[pvb], inc=(half == 1))
                em.op("act", lambda e, jc=jc, pv=pv: e.copy(out=vc[0][:, jc, 0:64], in_=pv[:, 0:64]), reads=[pvb], writes=[vc[1]])
        QS = []
        for k in range(2):
            QS.append(dict(
                qraw=mkT(es, nc, "ns_qraw%d" % k, [128, 256]), gtr=mkT(es, nc, "ns_gtr%d" % k, [128, 12]), gts=mkT(es, nc, "ns_gts%d" % k, [128, 12]),
                qn=mkT(es, nc, "ns_qn%d" % k, [128, 256]), qr=mkT(es, nc, "ns_qr%d" % k, [128, 256]),
                qnb=mkT(es, nc, "ns_qnb%d" % k, [128, 256], BF16), qrb=mkT(es, nc, "ns_qrb%d" % k, [128, 256], BF16),
                qTn=mkT(es, nc, "ns_qTn%d" % k, [64, 512], BF16), qTr=mkT(es, nc, "ns_qTr%d" % k, [64, 512], BF16),
                acc=mkT(es, nc, "ns_acc%d" % k, [128, 256]), sq=mkT(es, nc, "ns_sqq%d" % k, [128, 256]), ss=mkT(es, nc, "ns_ssq%d" % k, [128, 3, 4]),
                rtmp=mkT(es, nc, "ns_rtq%d" % k, [128, 4, 4, 8])))
        FS = [dict(oT_sb=mkT(es, nc, "ns_oT%d" % k, [65, 512]), otm=mkT(es, nc, "ns_otm%d" % k, [128, 4, 65]),
                   rz=mkT(es, nc, "ns_rz%d" % k, [128, 4]), wz=mkT(es, nc, "ns_wz%d" % k, [128, 4])) for k in range(3)]
        eTs = [mkT(es, nc, "ns_eT%d" % k, [128, 512], BF16) for k in range(3)]
        imp = mkT(es, nc, "ns_imp", [128, 128]); rp = mkT(es, nc, "ns_rp", [128, 128])
        mx = mkT(es, nc, "ns_mx", [128, 8]); thr = mkT(es, nc, "ns_thr", [128, 1])
        nsel = mkT(es, nc, "ns_nsel", [128, 128], BF16)
        nselT4 = mkT(es, nc, "ns_nselT4", [128, 512], BF16)
        st = {"n_s": 0, "n_f": 0}

        def prep_a(i):
            Q = QS[i % 2]
            t0 = i * 128
            qraw, gtr, gts, qn, qr, qnb, qrb, qTn, qTr = (Q[k] for k in ("qraw", "gtr", "gts", "qn", "qr", "qnb", "qrb", "qTn", "qTr"))
            em.dma("sp", qraw[0][:], p_tm[t0:t0 + 128, 0:256], writes=[qraw[1]])
            em.dma("sp", gtr[0][:], p_tm[t0:t0 + 128, 512:524], writes=[gtr[1]])
            em.op("act", lambda e: e.activation(out=gts[0][:], in_=gtr[0][:], func=AF.Sigmoid), reads=[gtr[1]], writes=[gts[1]])
            headnorm(em, qraw[0], qraw[1], 4, gq[0][:].rearrange("p (h d) -> p h d", h=4), gq[1], Q["sq"], Q["ss"],
                     qn[0][:].rearrange("p (h d) -> p h d", h=4), qn[1])
            em.op("dve", lambda e: e.tensor_copy(out=qr[0][:], in_=qn[0][:]), reads=[qn[1]], writes=[qr[1]])
            rope_apply(em, qn[0][:].rearrange("p (h d) -> p h d", h=4), qn[1], qr[0][:].rearrange("p (h d) -> p h d", h=4), qr[1],
                       cosT[0][:, i, :], sinT[0][:, i, :], cosT[1], Q["rtmp"], 4)
            em.op("dve", lambda e: e.tensor_copy(out=qnb[0][:], in_=qn[0][:]), reads=[qn[1]], writes=[qnb[1]])
            em.op("dve", lambda e: e.tensor_copy(out=qrb[0][:], in_=qr[0][:]), reads=[qr[1], sinT[1]], writes=[qrb[1]])

        def prep_b(i):
            Q = QS[i % 2]
            qnb, qrb, qTn, qTr = (Q[k] for k in ("qnb", "qrb", "qTn", "qTr"))
            for r in range(4):
                em.op("pe", lambda e, r=r: e.transpose(out=pst[0:64, r, :], in_=qnb[0][:, r * 64:(r + 1) * 64], identity=ident_bf[0][:]),
                      reads=[qnb[1], ident_bf[1]], writes=[pstb], inc=False)
            for r in range(4):
                em.op("pe", lambda e, r=r: e.transpose(out=pst[0:64, 4 + r, :], in_=qrb[0][:, r * 64:(r + 1) * 64], identity=ident_bf[0][:]),
                      reads=[qrb[1], ident_bf[1]], writes=[pstb], inc=(r == 3))
            em.op("dve", lambda e: e.tensor_copy(out=qTn[0][:].rearrange("p (r t) -> p r t", r=4), in_=pst[0:64, 0:4, :]), reads=[pstb], writes=[qTn[1]])
            em.op("dve", lambda e: e.tensor_copy(out=qTr[0][:].rearrange("p (r t) -> p r t", r=4), in_=pst[0:64, 4:8, :]), reads=[pstb], writes=[qTr[1]])

        def job_scores(job):
            kT, c, q_sb, extra = job["kT"], job["c"], job["q"], job["extra"]
            k = st["n_s"] % 2
            ke = st["n_s"] % 3
            st["n_s"] += 1
            ps, psb = psum[k]
            job["ps"] = (ps, psb); job["eT"] = eTs[ke]
            nm = len(extra)
            em.op("pe", lambda e: e.matmul(ps[:, :], lhsT=kT[0][:, c * 128:(c + 1) * 128], rhs=q_sb[0][:, :], start=True, stop=(nm == 0)),
                  reads=[kT[1], q_sb[1]], writes=[psb], inc=(nm == 0))
            for kk_, (l_ap, l_b, r_ap, r_b) in enumerate(extra):
                em.op("pe", lambda e, l_ap=l_ap, r_ap=r_ap, kk_=kk_: e.matmul(ps[:, :], lhsT=l_ap, rhs=r_ap, start=False, stop=(kk_ == nm - 1)),
                      reads=[l_b, r_b], writes=[psb], inc=(kk_ == nm - 1))

        def job_finish(job):
            ps, psb = job["ps"]; eT = job["eT"]; po = job["po"]
            em.op("act", lambda e: e.activation(out=eT[0][:], in_=ps[:, :], func=AF.Exp, scale=0.125), reads=[psb], writes=[eT[1]])
            em.op("pe", lambda e: e.matmul(po[0][0:65, :], lhsT=job["v"], rhs=eT[0][:], start=job["first"], stop=job["last"]),
                  reads=[job["vb"], eT[1]], writes=[po[1]], inc=True)
            if job.get("imp") is not None:
                jc, njc = job["imp"]
                for r in range(4):
                    em.op("pe", lambda e, r=r: e.matmul(psum[5][0][:, r * 128:(r + 1) * 128], lhsT=eT[0][:, r * 128:(r + 1) * 128], rhs=c2s[0][:, jc, :],
                                                        start=(jc == 0), stop=(jc == njc - 1)), reads=[eT[1], c2s[1]], writes=[psum[5][1]], inc=(r == 3))
            if job.get("after") is not None:
                job["after"]()

        def finish(po, br, first_branch, Q):
            F = FS[st["n_f"] % 3]
            st["n_f"] += 1
            oT_sb, otm, rz, wz = F["oT_sb"], F["otm"], F["rz"], F["wz"]
            acc, gts = Q["acc"], Q["gts"]
            em.op("act", lambda e: e.copy(out=oT_sb[0][:], in_=po[0][0:65, :]), reads=[po[1]], writes=[oT_sb[1]])
            p6, p6b = psum[6]
            for r in range(4):
                em.op("pe", lambda e, r=r: e.transpose(out=p6[:, r * 65:(r + 1) * 65], in_=oT_sb[0][:, r * 128:(r + 1) * 128], identity=ident_f[0][0:65, 0:65]),
                      reads=[oT_sb[1], ident_f[1]], writes=[p6b], inc=(r == 3))
            em.op("dve", lambda e: e.tensor_copy(out=otm[0][:], in_=p6[:, 0:260].rearrange("p (r d) -> p r d", r=4)), reads=[p6b], writes=[otm[1]])
            em.op("dve", lambda e: e.tensor_scalar(out=rz[0][:], in0=otm[0][:, :, 64], scalar1=1e-30, scalar2=None, op0=ALU.max), reads=[otm[1]], writes=[rz[1]])
            em.op("dve", lambda e: e.reciprocal(out=rz[0][:], in_=rz[0][:]), reads=[rz[1]], writes=[rz[1]])
            em.op("dve", lambda e: e.tensor_tensor(out=wz[0][:], in0=rz[0][:], in1=gts[0][:, br:12:3], op=ALU.mult), reads=[rz[1], gts[1]], writes=[wz[1]])
            for r in range(4):
                if first_branch:
                    em.op("dve", lambda e, r=r: e.tensor_scalar(out=acc[0][:, r * 64:(r + 1) * 64], in0=otm[0][:, r, 0:64], scalar1=wz[0][:, r:r + 1], scalar2=None, op0=ALU.mult),
                          reads=[otm[1], wz[1]], writes=[acc[1]])
                else:
                    em.op("dve", lambda e, r=r: e.scalar_tensor_tensor(out=acc[0][:, r * 64:(r + 1) * 64], in0=otm[0][:, r, 0:64], scalar=wz[0][:, r:r + 1],
                                                                     in1=acc[0][:, r * 64:(r + 1) * 64], op0=ALU.mult, op1=ALU.add),
                          reads=[otm[1], wz[1], acc[1]], writes=[acc[1]])
            return F

        def topk(i, F):
            rz = F["rz"]
            pimp = psum[5]
            for r in range(4):
                if r == 0:
                    em.op("dve", lambda e: e.tensor_scalar(out=imp[0][:], in0=pimp[0][:, 0:128], scalar1=rz[0][:, 0:1], scalar2=None, op0=ALU.mult),
                          reads=[pimp[1], rz[1]], writes=[imp[1]])
                else:
                    em.op("dve", lambda e, r=r: e.scalar_tensor_tensor(out=imp[0][:], in0=pimp[0][:, r * 128:(r + 1) * 128], scalar=rz[0][:, r:r + 1], in1=imp[0][:],
                                                                       op0=ALU.mult, op1=ALU.add), reads=[pimp[1], rz[1], imp[1]], writes=[imp[1]])
            o0 = 126 - 2 * i
            em.op("dve", lambda e: e.tensor_tensor(out=imp[0][:], in0=imp[0][:], in1=keepb[0][:, o0:o0 + 128], op=ALU.mult), reads=[imp[1], keepb[1]], writes=[imp[1]])
            em.op("dve", lambda e: e.tensor_tensor(out=imp[0][:], in0=imp[0][:], in1=addb[0][:, o0:o0 + 128], op=ALU.add), reads=[imp[1], addb[1]], writes=[imp[1]])
            em.op("dve", lambda e: e.memset(imp[0][:, 0:1], 1000.0), reads=[imp[1]], writes=[imp[1]])
            em.op("dve", lambda e: e.max(out=mx[0][:], in_=imp[0][:]), reads=[imp[1]], writes=[mx[1]])
            em.op("dve", lambda e: e.match_replace(out=rp[0][:], in_to_replace=mx[0][:], in_values=imp[0][:], imm_value=-1e30), reads=[imp[1], mx[1]], writes=[rp[1]])
            em.op("dve", lambda e: e.max(out=mx[0][:], in_=rp[0][:]), reads=[rp[1]], writes=[mx[1]])
            em.op("dve", lambda e: e.tensor_reduce(out=thr[0][:], in_=mx[0][:], axis=AX.X, op=ALU.min), reads=[mx[1]], writes=[thr[1]])
            em.op("dve", lambda e: e.tensor_scalar(out=rp[0][:], in0=imp[0][:], scalar1=thr[0][:, 0:1], scalar2=None, op0=ALU.is_ge), reads=[imp[1], thr[1]], writes=[rp[1]])
            em.op("dve", lambda e: e.tensor_scalar(out=nsel[0][:], in0=rp[0][:], scalar1=-1.0, scalar2=-NEG, op0=ALU.add, op1=ALU.mult), reads=[rp[1]], writes=[nsel[1]])
            em.op("pe", lambda e: e.transpose(out=pst[:, 0, :], in_=nsel[0][:], identity=ident_bf[0][:]), reads=[nsel[1], ident_bf[1]], writes=[pstb])
            em.op("dve", lambda e: e.tensor_copy(out=nselT4[0][:].rearrange("p (r t) -> p r t", r=4), in_=pst[:, 0:1, :].to_broadcast([128, 4, 128])),
                  reads=[pstb], writes=[nselT4[1]])

        prep_a(0)
        prep_b(0)
        for i in range(NQ):
            t0 = i * 128
            Q = QS[i % 2]
            qTn, qTr = Q["qTn"], Q["qTr"]
            jobs = []
            njc = (8 * i + 6) // 128 + 1
            for jc in range(njc):
                d = 8 * i - 128 * jc
                extra = []
                if 0 <= d <= 128:
                    extra.append((ident_bf[0][:], ident_bf[1], cmpmask[0][:, d // 8, :], cmpmask[1]))
                jobs.append(dict(kT=kcT, c=jc, q=qTn, extra=extra, v=vc[0][:, jc, :], vb=vc[1], po=psum[2], first=(jc == 0), last=(jc == njc - 1),
                                 imp=(jc, njc)))

            def after_cmp(i=i, Q=Q):
                F = finish(psum[2], 0, True, Q)
                topk(i, F)
            jobs[-1]["after"] = after_cmp
            cl = max(0, i - 4)
            for c in range(cl, i + 1):
                extra = []
                if c == i:
                    extra.append((ident_bf[0][:], ident_bf[1], causneg[0][:], causneg[1]))
                if c == i - 4:
                    extra.append((ident_bf[0][:], ident_bf[1], anticaus[0][:], anticaus[1]))
                jobs.append(dict(kT=kTw, c=c, q=qTr, extra=extra, v=vw[0][:, c, :], vb=vw[1], po=psum[4], first=(c == cl), last=(c == i)))
            jobs[-1]["after"] = (lambda Q=Q: finish(psum[4], 2, False, Q))
            for c in range(i + 1):
                extra = [(ebig[0][:, c * 128:(c + 1) * 128], ebig[1], nselT4[0][:], nselT4[1])]
                if c == i:
                    extra.append((ident_bf[0][:], ident_bf[1], causneg[0][:], causneg[1]))
                jobs.append(dict(kT=kTs, c=c, q=qTr, extra=extra, v=vs[0][:, c, :], vb=vs[1], po=psum[3], first=(c == 0), last=(c == i)))

            def after_sel(i=i, Q=Q, t0=t0):
                finish(psum[3], 1, False, Q)
                em.dyn_store(y_dram[t0:t0 + 128, YOFF:YOFF + 256], Q["acc"][0][:], [Q["acc"][1]])
            jobs[-1]["after"] = after_sel
            job_scores(jobs[0])
            if i + 1 < NQ:
                prep_a(i + 1)
            kb = max(0, len(jobs) - 3)
            for k in range(len(jobs)):
                if k + 1 < len(jobs):
                    job_scores(jobs[k + 1])
                if k == kb and i + 1 < NQ:
                    prep_b(i + 1)
                job_finish(jobs[k])


def stage_merge(em, NTOK, x_rows, y_rows, prm, cst, x1_dram, psum, halo_flag=None):
    nc = em.nc
    ident = cst["ident_bf"]
    with ExitStack() as es:
        g_sb = mkT(es, nc, "mg_g", [128, 8]); ld(em, g_sb, prm["attn_g"])
        stage = [mkT(es, nc, "mg_stg%d" % k, [128, 1024]) for k in range(2)]
        wg = load_weight_bf16(em, es, "mg_wg", prm["w_gate"], 1024, 3072, scale_sb=g_sb, stage=stage)
        wbr = load_weight_bf16(em, es, "mg_wbr", prm["w_br"], 2048, 1024, stage=stage)
        wo = load_weight_bf16(em, es, "mg_wo", prm["w_out"], 1024, 1024, stage=stage)
        xts = [mkT(es, nc, "mg_x%d" % k, [128, 1024]) for k in range(2)]
        yt = mkT(es, nc, "mg_y", [128, 2048]); ytb = mkT(es, nc, "mg_yb", [128, 2048], BF16)
        scr = {"sq": mkT(es, nc, "mg_sq", [128, 1024], BF16), "ss": mkT(es, nc, "mg_ss", [128, 4]), "hb": mkT(es, nc, "mg_hb", [128, 1024], BF16)}
        hTs = [mkT(es, nc, "mg_hT%d" % k, [128, 8, 128], BF16) for k in range(2)]
        yTs = [mkT(es, nc, "mg_yT%d" % k, [128, 16, 128], BF16) for k in range(2)]
        gsb = mkT(es, nc, "mg_gs", [128, 3072])
        mrg = mkT(es, nc, "mg_m", [128, 1024]); tmp = mkT(es, nc, "mg_t", [128, 512]); mrgb = mkT(es, nc, "mg_mb", [128, 1024], BF16)
        mT = mkT(es, nc, "mg_mT", [128, 8, 128], BF16)
        x1 = mkT(es, nc, "mg_x1", [128, 1024])
        pst, pstb = psum[7]
        st = {"n_ps": 0}
        NTI = NTOK // 128

        def front(ti):
            t0 = ti * 128
            xt, hT, yT = xts[ti % 2], hTs[ti % 2], yTs[ti % 2]
            em.dyn_load(xt[0][:], [xt[1]], x_rows(t0))
            for (dst_fn, src) in y_rows(t0):
                em.dyn_load(dst_fn(yt[0]), [yt[1]], src)
            if ti == 0 and halo_flag is not None:
                em.op("dve", lambda e: e.tensor_scalar(out=xt[0][:], in0=xt[0][:], scalar1=halo_flag[0][:, 0:1], scalar2=None, op0=ALU.mult),
                      reads=[xt[1], halo_flag[1]], writes=[xt[1]])
                em.op("dve", lambda e: e.tensor_scalar(out=yt[0][:], in0=yt[0][:], scalar1=halo_flag[0][:, 0:1], scalar2=None, op0=ALU.mult),
                      reads=[yt[1], halo_flag[1]], writes=[yt[1]])
            rmsnorm_T(em, xt[0], xt[1], hT[0], hT[1], 0, scr, ident, psum[7])
            em.op("dve", lambda e: e.tensor_copy(out=ytb[0][:], in_=yt[0][:]), reads=[yt[1]], writes=[ytb[1]])
            for half in range(2):
                for kc in range(8):
                    em.op("pe", lambda e, kc=kc, half=half: e.transpose(out=pst[:, kc, :], in_=ytb[0][:, (half * 8 + kc) * 128:(half * 8 + kc + 1) * 128],
                                                                       identity=ident[0][:]), reads=[ytb[1], ident[1]], writes=[pstb], inc=(kc == 7))
                em.op("dve", lambda e, half=half: e.tensor_copy(out=yT[0][:, half * 8:(half + 1) * 8, :], in_=pst[:, :, :]), reads=[pstb], writes=[yT[1]])

        def back(ti):
            t0 = ti * 128
            xt, hT, yT = xts[ti % 2], hTs[ti % 2], yTs[ti % 2]
            for cb in range(6):
                ps, psb = psum[st["n_ps"] % 6]; st["n_ps"] += 1
                for kc in range(8):
                    em.op("pe", lambda e, kc=kc, cb=cb, ps=ps: e.matmul(ps[:, :], lhsT=hT[0][:, kc, :], rhs=wg[0][:, kc, cb * 512:(cb + 1) * 512],
                                                                       start=(kc == 0), stop=(kc == 7)), reads=[hT[1], wg[1]], writes=[psb], inc=(kc == 7))
                em.op("act", lambda e, cb=cb, ps=ps: e.activation(out=gsb[0][:, cb * 512:(cb + 1) * 512], in_=ps[:, :], func=AF.Sigmoid), reads=[psb], writes=[gsb[1]])
            for m, (k0, k1) in enumerate(((0, 4), (4, 8), (8, 16))):
                for nb in range(2):
                    ps, psb = psum[st["n_ps"] % 6]; st["n_ps"] += 1
                    for kc in range(k0, k1):
                        em.op("pe", lambda e, kc=kc, nb=nb, ps=ps: e.matmul(ps[:, :], lhsT=yT[0][:, kc, :], rhs=wbr[0][:, kc, nb * 512:(nb + 1) * 512],
                                                                           start=(kc == k0), stop=(kc == k1 - 1)), reads=[yT[1], wbr[1]], writes=[psb], inc=(kc == k1 - 1))
                    gsl = gsb[0][:, m * 1024 + nb * 512:m * 1024 + (nb + 1) * 512]
                    if m == 0:
                        em.op("dve", lambda e, nb=nb, ps=ps, gsl=gsl: e.tensor_tensor(out=mrg[0][:, nb * 512:(nb + 1) * 512], in0=ps[:, :], in1=gsl, op=ALU.mult),
                              reads=[psb, gsb[1]], writes=[mrg[1]])
                    else:
                        em.op("dve", lambda e, ps=ps, gsl=gsl: e.tensor_tensor(out=tmp[0][:], in0=ps[:, :], in1=gsl, op=ALU.mult), reads=[psb, gsb[1]], writes=[tmp[1]])
                        em.op("dve", lambda e, nb=nb: e.tensor_tensor(out=mrg[0][:, nb * 512:(nb + 1) * 512], in0=mrg[0][:, nb * 512:(nb + 1) * 512], in1=tmp[0][:], op=ALU.add),
                              reads=[mrg[1], tmp[1]], writes=[mrg[1]])
            em.op("act", lambda e: e.copy(out=mrgb[0][:], in_=mrg[0][:]), reads=[mrg[1]], writes=[mrgb[1]])
            for kc in range(8):
                em.op("pe", lambda e, kc=kc: e.transpose(out=pst[:, kc, :], in_=mrgb[0][:, kc * 128:(kc + 1) * 128], identity=ident[0][:]),
                      reads=[mrgb[1], ident[1]], writes=[pstb], inc=(kc == 7))
            em.op("act", lambda e: e.copy(out=mT[0][:], in_=pst[:, :, :]), reads=[pstb], writes=[mT[1]])
            for nb in range(2):
                ps, psb = psum[st["n_ps"] % 6]; st["n_ps"] += 1
                for kc in range(8):
                    em.op("pe", lambda e, kc=kc, nb=nb, ps=ps: e.matmul(ps[:, :], lhsT=mT[0][:, kc, :], rhs=wo[0][:, kc, nb * 512:(nb + 1) * 512],
                                                                       start=(kc == 0), stop=(kc == 7)), reads=[mT[1], wo[1]], writes=[psb], inc=(kc == 7))
                em.op("dve", lambda e, nb=nb, ps=ps: e.tensor_tensor(out=x1[0][:, nb * 512:(nb + 1) * 512], in0=ps[:, :], in1=xt[0][:, nb * 512:(nb + 1) * 512], op=ALU.add),
                      reads=[psb, xt[1]], writes=[x1[1]])
            em.dma(STORE_Q, x1_dram[t0:t0 + 128, :], x1[0][:], reads=[x1[1]])

        front(0)
        for ti in range(NTI):
            if ti + 1 < NTI:
                front(ti + 1)
            back(ti)


def stage_ffn(em, NTOK, x1_dram, prm, cst, out_rows, psum, FF=2816):
    nc = em.nc
    ident = cst["ident_bf"]
    NCB = FF // 128
    with ExitStack() as es:
        g_sb = mkT(es, nc, "ff_g", [128, 8]); ld(em, g_sb, prm["ffn_g"])
        stage = [mkT(es, nc, "ff_stg%d" % k, [128, 1024]) for k in range(2)]
        wup = load_weight_bf16(em, es, "ff_wup", prm["w_up"], 1024, 2 * FF, scale_sb=g_sb, stage=stage)
        wdn = load_weight_bf16(em, es, "ff_wdn", prm["w_down"], FF, 1024, stage=stage)
        fcw = mkT(es, nc, "ff_cw", [128, 2 * NCB, 3]); ld(em, fcw, prm["ffn_cw"])
        fcb = mkT(es, nc, "ff_cb", [128, 2 * NCB]); ld(em, fcb, prm["ffn_cb"])
        xts = [mkT(es, nc, "ff_x%d" % k, [128, 1024]) for k in range(2)]
        scr = {"sq": mkT(es, nc, "ff_sq", [128, 1024], BF16), "ss": mkT(es, nc, "ff_ss", [128, 4]), "hb": mkT(es, nc, "ff_hb", [128, 1024], BF16)}
        hTs = [mkT(es, nc, "ff_hT%d" % k, [128, 8, 256], BF16) for k in range(2)]
        ucar = mkT(es, nc, "ff_car", [128, 2 * NCB, 2])
        em.op("dve", lambda e: e.memset(ucar[0][:], 0.0), writes=[ucar[1]])
        ubuf = [mkT(es, nc, "ff_ub%d" % k, [128, 258]) for k in range(4)]
        ubufc = [Buf("ff_ubc%d" % k) for k in range(4)]
        acc = [mkT(es, nc, "ff_acc%d" % k, [128, 256]) for k in range(4)]
        gact = [mkT(es, nc, "ff_ga%d" % k, [128, 256]) for k in range(2)]
        ptmp = [mkT(es, nc, "ff_pt%d" % k, [128, 256]) for k in range(2)]
        actT = mkT(es, nc, "ff_aT", [128, NCB, 256], BF16)
        xo = mkT(es, nc, "ff_xo", [128, 1024])
        xr = mkT(es, nc, "ff_xr", [128, 1024])
        st = {"n_ps": 0, "n_ub": 0}
        segs = [(0, 128, True)] + [(128 + k * 256, 256, False) for k in range((NTOK - 128) // 256)]

        def front(si):
            s0, ntok, halo = segs[si]
            hT = hTs[si % 2]
            for k in range(ntok // 128):
                em.dma("sp", xts[k][0][:], x1_dram[s0 + k * 128:s0 + (k + 1) * 128, :], writes=[xts[k][1]])
                rmsnorm_T(em, xts[k][0], xts[k][1], hT[0], hT[1], k * 128, scr, ident, psum[7])

        def back(si):
            s0, ntok, halo = segs[si]
            hT = hTs[si % 2]
            nt = ntok // 128
            for cb in range(NCB):
                blks = (cb, cb + NCB)
                pss = []
                for gi, blk in enumerate(blks):
                    ps, psb = psum[st["n_ps"] % 4]; st["n_ps"] += 1
                    pss.append((ps, psb))
                    for kc in range(8):
                        em.op("pe", lambda e, kc=kc, blk=blk, ps=ps: e.matmul(ps[:, 0:ntok], lhsT=wup[0][:, kc, blk * 128:(blk + 1) * 128], rhs=hT[0][:, kc, 0:ntok],
                                                                             start=(kc == 0), stop=(kc == 7)), reads=[hT[1], wup[1]], writes=[psb], inc=(kc == 7))
                r = st["n_ub"] % 2; st["n_ub"] += 1
                ubs = [ubuf[2 * r], ubuf[2 * r + 1]]
                ubcs = [ubufc[2 * r], ubufc[2 * r + 1]]
                acs = [acc[2 * r], acc[2 * r + 1]]
                for gi, blk in enumerate(blks):
                    em.op("dve", lambda e, blk=blk, gi=gi: e.tensor_copy(out=ubs[gi][0][:, 0:2], in_=ucar[0][:, blk, :]), reads=[ucar[1]], writes=[ubcs[gi]])
                for gi, blk in enumerate(blks):
                    ps, psb = pss[gi]
                    em.op("act", lambda e, ps=ps, gi=gi: e.copy(out=ubs[gi][0][:, 2:2 + ntok], in_=ps[:, 0:ntok]), reads=[psb], writes=[ubs[gi][1]])
                for gi, blk in enumerate(blks):
                    em.op("dve", lambda e, blk=blk, gi=gi: e.tensor_copy(out=ucar[0][:, blk, :], in_=ubs[gi][0][:, ntok:ntok + 2]), reads=[ubs[gi][1], ubcs[gi]], writes=[ucar[1]])
                if halo:
                    continue
                for k in range(3):
                    for gi, blk in enumerate(blks):
                        eng = "dve"
                        ac, ub, ubc = acs[gi], ubs[gi], ubcs[gi]
                        if k == 0:
                            em.op(eng, lambda e, blk=blk, ub=ub, ac=ac: e.tensor_scalar(out=ac[0][:, 0:ntok], in0=ub[0][:, 0:ntok], scalar1=fcw[0][:, blk, 0:1], scalar2=None, op0=ALU.mult),
                                  reads=[ub[1], ubc, fcw[1]], writes=[ac[1]])
                        elif eng == "dve":
                            em.op(eng, lambda e, blk=blk, ub=ub, ac=ac, k=k: e.scalar_tensor_tensor(out=ac[0][:, 0:ntok], in0=ub[0][:, k:k + ntok], scalar=fcw[0][:, blk, k:k + 1],
                                                                                                  in1=ac[0][:, 0:ntok], op0=ALU.mult, op1=ALU.add),
                                  reads=[ub[1], ubc, fcw[1], ac[1]], writes=[ac[1]])
                        else:
                            tp = ptmp[r]
                            em.op(eng, lambda e, blk=blk, ub=ub, k=k, tp=tp: e.tensor_scalar(out=tp[0][:, 0:ntok], in0=ub[0][:, k:k + ntok], scalar1=fcw[0][:, blk, k:k + 1], scalar2=None, op0=ALU.mult),
                                  reads=[ub[1], ubc, fcw[1]], writes=[tp[1]])
                            em.op(eng, lambda e, ac=ac, tp=tp: e.tensor_tensor(out=ac[0][:, 0:ntok], in0=ac[0][:, 0:ntok], in1=tp[0][:, 0:ntok], op=ALU.add),
                                  reads=[ac[1], tp[1]], writes=[ac[1]])
                em.op("act", lambda e: e.activation(out=gact[r][0][:, 0:ntok], in_=acs[0][0][:, 0:ntok], func=AF.Silu, bias=fcb[0][:, blks[0]:blks[0] + 1]),
                      reads=[acs[0][1], fcb[1]], writes=[gact[r][1]])
                em.op("dve", lambda e: e.scalar_tensor_tensor(out=actT[0][:, cb, 0:ntok], in0=acs[1][0][:, 0:ntok], scalar=fcb[0][:, blks[1]:blks[1] + 1],
                                                              in1=gact[r][0][:, 0:ntok], op0=ALU.add, op1=ALU.mult),
                      reads=[acs[1][1], fcb[1], gact[r][1]], writes=[actT[1]])
            if halo:
                return
            for k in range(nt):
                em.dma("sp", xr[0][:], x1_dram[s0 + k * 128:s0 + (k + 1) * 128, :], writes=[xr[1]])
                for nb in range(2):
                    ps, psb = psum[4 + (st["n_ps"] % 2)]; st["n_ps"] += 1
                    for cb in range(NCB):
                        em.op("pe", lambda e, cb=cb, nb=nb, ps=ps, k=k: e.matmul(ps[:, :], lhsT=actT[0][:, cb, k * 128:(k + 1) * 128], rhs=wdn[0][:, cb, nb * 512:(nb + 1) * 512],
                                                                                start=(cb == 0), stop=(cb == NCB - 1)), reads=[actT[1], wdn[1]], writes=[psb], inc=(cb == NCB - 1))
                    em.op("dve", lambda e, nb=nb, ps=ps: e.tensor_tensor(out=xo[0][:, nb * 512:(nb + 1) * 512], in0=ps[:, :], in1=xr[0][:, nb * 512:(nb + 1) * 512], op=ALU.add),
                          reads=[psb, xr[1]], writes=[xo[1]])
                o0 = s0 - 128 + k * 128
                em.dyn_store(out_rows(o0), xo[0][:], [xo[1]])

        front(0)
        for si in range(len(segs)):
            if si + 1 < len(segs):
                front(si + 1)
            back(si)

import ml_dtypes

T_SEQ = 8192
NT_TM = 1556
NC_CM = 1408
OFF = dict(q=0, kv=512, gate=1280, hg_q=1304, hg_f=1816, hg_i=2328, hg_g=2840, z=3352, xbc=4376, dt=5912, merge=5928)


def group_cols(g):
    ar = np.arange
    kv = lambda br, kvi: OFF["kv"] + ((br * 2 + kvi) * 2 + g) * 64 + ar(64)
    tm = np.concatenate([
        OFF["q"] + g * 256 + ar(256), kv(1, 0), kv(1, 1), kv(2, 0), kv(2, 1),
        OFF["gate"] + g * 12 + ar(12), OFF["hg_i"] + g * 256 + ar(256), OFF["hg_g"] + g * 256 + ar(256),
        OFF["z"] + g * 512 + ar(512), OFF["dt"] + g * 8 + ar(8)])
    cm = np.concatenate([
        kv(0, 0), kv(0, 1), OFF["hg_q"] + g * 256 + ar(256), OFF["hg_f"] + g * 256 + ar(256),
        OFF["xbc"] + g * 512 + ar(512), OFF["xbc"] + 1024 + g * 128 + ar(128), OFF["xbc"] + 1280 + g * 128 + ar(128)])
    assert tm.size == NT_TM and cm.size == NC_CM
    return tm, cm


_CONST_CACHE = {}


def consts():
    if _CONST_CACHE:
        return _CONST_CACHE
    bf = ml_dtypes.bfloat16
    f32 = np.float32
    i = np.arange(128)
    c = {}
    c["ident_bf"] = np.eye(128).astype(bf)
    c["ident_f"] = np.eye(128).astype(f32)
    c["triu_f"] = (i[:, None] <= i[None, :]).astype(f32)
    c["ones_f"] = np.ones((128, 128), f32)
    c["mstrict_f"] = (i[:, None] > i[None, :]).astype(f32)
    c["mcausT_f"] = (i[None, :] >= i[:, None]).astype(f32)
    c["mask2_f"] = ((i[:, None] // 64 == i[None, :] // 64) & (i[:, None] <= i[None, :])).astype(f32)
    theta = np.float32(500000.0)
    invf = (theta ** (-np.arange(0, 16, 2, dtype=np.float32) / np.float32(16))).astype(f32)
    c["c_invf"] = np.broadcast_to(invf[None, :], (128, 8)).copy()
    NEG_ = -30000.0
    ds = list(range(0, 128, 8)) + [128]
    cm = np.zeros((128, 17, 4, 128), f32)
    for k, d in enumerate(ds):
        vis = (16 * (i[:, None] - d) + 31) <= i[None, :]
        cm[:, k, :, :] = np.where(vis, 0.0, NEG_)[:, None, :]
    c["c_cmpmask"] = cm.reshape(128, 17, 512).astype(bf)
    n_cmp = (T_SEQ - 32) // 16 + 1
    cs = np.arange(n_cmp) * 16
    s_start = np.arange(128) * 64
    ov = np.clip(np.minimum(cs[:, None] + 32, s_start[None, :] + 64) - np.maximum(cs[:, None], s_start[None, :]), 0, None) / 32.0
    c2s = np.zeros((512, 128), f32)
    c2s[:n_cmp] = ov
    c["c_c2s"] = np.ascontiguousarray(c2s.reshape(4, 128, 128).transpose(1, 0, 2)).astype(bf)
    keys = np.arange(T_SEQ)
    c["c_ebig"] = (i[:, None] == (keys[None, :] // 64)).astype(bf)
    caus = np.where(i[:, None] <= i[None, :], 0.0, NEG_)
    c["c_causneg"] = np.tile(caus, (1, 4)).astype(bf)
    anti = np.where(i[:, None] > i[None, :], 0.0, NEG_)
    c["c_anticaus"] = np.tile(anti, (1, 4)).astype(bf)
    u = np.arange(254) - 126
    curp = (i >= 64).astype(np.int64)[:, None]
    forced = (u[None, :] == curp) | (u[None, :] == curp - 1)
    invalid = u[None, :] > curp
    c["c_keep"] = (~forced & ~invalid).astype(f32)
    c["c_add"] = np.where(forced, 200.0 + u[None, :], np.where(invalid, -(300.0 + u[None, :]), 0.0)).astype(f32)
    _CONST_CACHE.update(c)
    return c


CONST_SB = ["ident_bf", "ident_f", "triu_f", "ones_f", "mstrict_f", "mcausT_f", "mask2_f"]
CONST_NSA = ["c_invf", "c_cmpmask", "c_c2s", "c_ebig", "c_causneg", "c_anticaus", "c_keep", "c_add"]
PRM_B = {
    "ns_gq": ([128, 256], F32), "ns_gk": ([128, 2, 64], F32), "ns_gk0": ([64, 1], F32), "ns_posT": ([128, 32], F32),
    "ns_w1kv": ([128, 32, 256], F32), "ns_w2k": ([128, 2, 64], F32), "ns_w2v": ([128, 2, 64], F32), "ns_pos": ([128, 64], I32),
    "hg_lbl": ([128, 2, 2], F32), "hg_lflag": ([128, 2], F32), "hg_ng": ([128, 128], F32),
    "m2_cw": ([128, 6, 4], F32), "m2_cb": ([128, 6], F32), "m2_dtb": ([128, 8], F32), "m2_alog": ([128, 8], F32),
    "m2_dsk": ([128, 8], F32), "m2_ng": ([128, 512], F32),
}
CONST_SHAPES = {
    "ident_bf": ([128, 128], BF16), "ident_f": ([128, 128], F32), "triu_f": ([128, 128], F32), "ones_f": ([128, 128], F32),
    "mstrict_f": ([128, 128], F32), "mcausT_f": ([128, 128], F32), "mask2_f": ([128, 128], F32),
    "c_invf": ([128, 8], F32), "c_cmpmask": ([128, 17, 512], BF16), "c_c2s": ([128, 4, 128], BF16), "c_ebig": ([128, T_SEQ], BF16),
    "c_causneg": ([128, 512], BF16), "c_anticaus": ([128, 512], BF16), "c_keep": ([128, 254], F32), "c_add": ([128, 254], F32),
}


def bcast(v, rows=128):
    v = np.asarray(v, np.float32).reshape(1, -1)
    return np.ascontiguousarray(np.broadcast_to(v, (rows, v.shape[1])))


def prep_B(inp, l, b, g, x_b):
    tm, cm = group_cols(g)
    w_in = inp["w_in"][l]
    m = {"x": (None if x_b is None else np.ascontiguousarray(x_b)), "g_attn": np.ascontiguousarray(inp["attn_norm_g"][l].reshape(8, 128).T),
         "w_tm": np.ascontiguousarray(w_in[:, tm]), "w_cm": np.ascontiguousarray(w_in[:, cm])}
    m.update(consts())
    m["ns_gq"] = bcast(np.tile(inp["nsa_q_norm_g"][l], 4))
    m["ns_gk"] = np.ascontiguousarray(np.broadcast_to(inp["nsa_k_norm_g"][l][1:3][None], (128, 2, 64))).astype(np.float32)
    m["ns_gk0"] = np.ascontiguousarray(inp["nsa_k_norm_g"][l][0].reshape(64, 1))
    m["ns_posT"] = np.ascontiguousarray(np.concatenate([inp["nsa_cmp_pos_k"][l].T, inp["nsa_cmp_pos_v"][l].T], 0))
    w1k = inp["nsa_cmp_k_w1"][l].reshape(32, 64, 256).transpose(1, 0, 2)
    w1v = inp["nsa_cmp_v_w1"][l].reshape(32, 64, 256).transpose(1, 0, 2)
    m["ns_w1kv"] = np.ascontiguousarray(np.concatenate([w1k, w1v], 0))
    m["ns_w2k"] = np.ascontiguousarray(inp["nsa_cmp_k_w2"][l].reshape(2, 128, 64).transpose(1, 0, 2))
    m["ns_w2v"] = np.ascontiguousarray(inp["nsa_cmp_v_w2"][l].reshape(2, 128, 64).transpose(1, 0, 2))
    m["ns_pos"] = np.ascontiguousarray(inp["positions"][b].reshape(64, 128).T.astype(np.int32))
    lbl = inp["hgrn_lb_logits"][:, g * 256:(g + 1) * 256].reshape(2, 2, 128)
    m["hg_lbl"] = np.ascontiguousarray(lbl.transpose(2, 1, 0))
    m["hg_ng"] = bcast(inp["hgrn_norm_g"][l])
    m["hg_lflag"] = np.full((128, 2), float(l), np.float32)
    chs = np.concatenate([g * 512 + np.arange(512), 1024 + g * 128 + np.arange(128), 1280 + g * 128 + np.arange(128)])
    m["m2_cw"] = np.ascontiguousarray(inp["m2_conv_w"][l][:, chs].reshape(4, 6, 128).transpose(2, 1, 0))
    m["m2_cb"] = np.ascontiguousarray(inp["m2_conv_b"][l][chs].reshape(6, 128).T)
    m["m2_dtb"] = bcast(inp["m2_dt_bias"][l][g * 8:(g + 1) * 8])
    m["m2_alog"] = bcast(inp["m2_a_log"][l][g * 8:(g + 1) * 8])
    m["m2_dsk"] = bcast(inp["m2_d_skip"][l][g * 8:(g + 1) * 8])
    m["m2_ng"] = bcast(inp["m2_norm_g"][l][g * 512:(g + 1) * 512])
    return m


def build_B(layer, stages=("proj", "m2", "hg", "nsa"), T=T_SEQ, debug_out=False):
    nc = bass.Bass("TRN2", target_bir_lowering=False)
    D = lambda name, shape, dt=F32, kind="ExternalInput": nc.dram_tensor(name, shape, dt, kind=kind).ap()
    x = D("x", [T, 1024]); g_attn = D("g_attn", [128, 8]); w_tm_d = D("w_tm", [1024, NT_TM]); w_cm_d = D("w_cm", [1024, NC_CM])
    cd = {k: D(k, *CONST_SHAPES[k]) for k in CONST_SHAPES}
    prm = {k: D(k, *PRM_B[k]) for k in PRM_B}
    prm.update({k: cd[k] for k in CONST_NSA})
    y = D("y", [T, 1024], kind="ExternalOutput")
    if debug_out:
        p_tm = D("p_tm", [T, NT_TM], kind="ExternalOutput"); p_cm = D("p_cm", [NC_CM, T], kind="ExternalOutput")
    else:
        p_tm = D("p_tm", [T, NT_TM], kind="Internal"); p_cm = D("p_cm", [NC_CM, T], kind="Internal")
    with ExitStack() as es:
        em = Em(nc, es)
        psum = [(es.enter_context(nc.psum_tensor("ps%d" % i, [128, 512], F32)), Buf()) for i in range(7)]
        psum.append((es.enter_context(nc.psum_tensor("pst", [128, 8, 128], BF16)), Buf()))
        cst = {}
        for k in CONST_SB:
            cst[k] = mkT(es, nc, "k_" + k, *CONST_SHAPES[k])
            ld(em, cst[k], cd[k])
        em.dyn_init(es)
        if "proj" in stages:
            with ExitStack() as es2:
                g_sb = mkT(es2, nc, "g_sb", [128, 8]); ld(em, g_sb, g_attn)
                w_tm = load_weight_bf16(em, es2, "w_tm_sb", w_tm_d, 1024, NT_TM, scale_sb=g_sb)
                w_cm = load_weight_bf16(em, es2, "w_cm_sb", w_cm_d, 1024, NC_CM, scale_sb=g_sb)
                stage_proj(em, T, x, g_sb, w_tm, w_cm, NT_TM, NC_CM, p_tm, p_cm, cst["ident_bf"], psum)
            em.drain()
        if "m2hg" in stages:
            run_m2_hg(em, T, layer, p_tm, p_cm, prm, cst, y, psum)
            em.drain()
        if "m2" in stages:
            stage_mamba(em, T, p_tm, p_cm, prm, cst, y, psum)
            em.drain()
        if "hg" in stages:
            stage_hgrn(em, T, layer, p_tm, p_cm, prm, cst, y, psum)
            em.drain()
        if "nsa" in stages:
            stage_nsa(em, T, p_tm, p_cm, prm, cst, y, psum)
        em.drain()
        print("build_B n_inst", em.n_inst, flush=True)
    return nc


PRM_C = {
    "attn_g": ([128, 8], F32), "w_gate": ([1024, 3072], F32), "w_br": ([2048, 1024], F32), "w_out": ([1024, 1024], F32),
    "ffn_g": ([128, 8], F32), "w_up": ([1024, 5632], F32), "w_down": ([2816, 1024], F32), "ffn_cw": ([128, 44, 3], F32), "ffn_cb": ([128, 44], F32),
}
NTOK_C = 4096 + 128
CSTAGES = "both"


def prep_C(inp, l, xh, yh):
    m = {"xh": xh, "yh": yh}
    m["ident_bf"] = consts()["ident_bf"]
    m["attn_g"] = np.ascontiguousarray(inp["attn_norm_g"][l].reshape(8, 128).T)
    m["w_gate"] = np.ascontiguousarray(inp["w_in"][l][:, OFF["merge"]:OFF["merge"] + 3072])
    m["w_br"] = np.ascontiguousarray(np.concatenate([inp["w_branch_nsa"][l], inp["w_branch_hgrn"][l], inp["w_branch_m2"][l]], 0))
    m["w_out"] = np.ascontiguousarray(inp["w_out"][l])
    m["ffn_g"] = np.ascontiguousarray(inp["ffn_norm_g"][l].reshape(8, 128).T)
    m["w_up"] = np.ascontiguousarray(inp["ffn_w_up"][l])
    m["w_down"] = np.ascontiguousarray(inp["ffn_w_down"][l])
    m["ffn_cw"] = np.ascontiguousarray(inp["ffn_conv_w"][l].reshape(3, 44, 128).transpose(2, 1, 0))
    m["ffn_cb"] = np.ascontiguousarray(inp["ffn_conv_b"][l].reshape(44, 128).T)
    return m


def build_C(NTOK=NTOK_C):
    nc = bass.Bass("TRN2", target_bir_lowering=False)
    D = lambda name, shape, dt=F32, kind="ExternalInput": nc.dram_tensor(name, shape, dt, kind=kind).ap()
    xh = D("xh", [NTOK, 1024]); yh = D("yh", [NTOK, 2048])
    idd = D("ident_bf", [128, 128], BF16)
    prm = {k: D(k, *PRM_C[k]) for k in PRM_C}
    out = D("out", [NTOK - 128, 1024], kind="ExternalOutput")
    x1d = D("x1d", [NTOK, 1024], kind="Internal")
    with ExitStack() as es:
        em = Em(nc, es)
        psum = [(es.enter_context(nc.psum_tensor("ps%d" % i, [128, 512], F32)), Buf()) for i in range(7)]
        psum.append((es.enter_context(nc.psum_tensor("pst", [128, 8, 128], BF16)), Buf()))
        cst = {"ident_bf": mkT(es, nc, "k_ident", [128, 128], BF16)}
        ld(em, cst["ident_bf"], idd)
        em.dyn_init(es)
        if CSTAGES in ("both", "merge"):
            stage_merge(em, NTOK, lambda t0: xh[t0:t0 + 128, :], lambda t0: [(lambda yt: yt[:, 0:2048], yh[t0:t0 + 128, :])], prm, cst, x1d, psum)
        em.drain()
        if CSTAGES in ("both", "ffn"):
            stage_ffn(em, NTOK, x1d, prm, cst, lambda o0: out[o0:o0 + 128, :], psum)
        em.drain()
    return nc


def run_m2_hg(em, T, layer, p_tm, p_cm, prm, cst, y_dst, psum):
    with ExitStack() as es_sh:
        gm = stage_mamba_gen(em, T, p_tm, p_cm, prm, cst, y_dst, psum, es_outer=es_sh)
        gh = stage_hgrn_gen(em, T, layer, p_tm, p_cm, prm, cst, y_dst, psum, es_outer=es_sh)
        gens = [[gm, 2], [gh, 1]]
        while gens:
            for ent in list(gens):
                for _ in range(ent[1]):
                    try:
                        next(ent[0])
                    except StopIteration:
                        gens.remove(ent)
                        break
        em.drain()


def prep_fused(inp, b, s):
    m = {"x": np.ascontiguousarray(inp["x"][b]), "halo_flag": np.full((128, 1), float(s), np.float32)}
    m.update(consts())
    for l in range(2):
        mb = prep_B(inp, l, b, s, None)
        mc = prep_C(inp, l, None, None)
        for k, v in list(mb.items()) + list(mc.items()):
            if k in CONST_SHAPES or k in ("x", "xh", "yh") or v is None:
                continue
            m["%s_l%d" % (k, l)] = v
    return m


class YDst:
    def __init__(self, fams, halos, par, ntile_half):
        self.fams = fams; self.halos = halos; self.par = par; self.nth = ntile_half

    def __getitem__(self, key):
        rows, cols = key
        i = rows.start // 128
        h, ti = i // self.nth, i % self.nth + 1
        c0, w = cols.start, cols.stop - cols.start
        if c0 < 256:
            fam, coff = "n", c0
        elif c0 < 512:
            fam, coff = "h", c0 - 256
        else:
            fam, coff = "m", c0 - 512
        out = [self.fams[fam][ti][h, bass.ds(self.par, 1)][0, :, coff:coff + w]]
        if i == self.nth - 1:
            out.append(self.halos[fam][bass.ds(self.par, 1)][0, :, coff:coff + w])
        return out


def build_fused(T=T_SEQ, n_layers=2):
    nc = bass.Bass("TRN2", target_bir_lowering=False, num_devices=8)
    D = lambda name, shape, dt=F32, kind="ExternalInput", **kw: nc.dram_tensor(name, shape, dt, kind=kind, **kw).ap()
    SH = lambda name, shape: D(name, shape, kind="Internal", addr_space="Shared")
    HALF = T // 2
    NTH = HALF // 128
    x_ext = D("x", [T, 1024])
    hflag_d = D("halo_flag", [128, 1])
    cd = {k: D(k, *CONST_SHAPES[k]) for k in CONST_SHAPES}
    out_ext = D("out", [HALF, 1024], kind="ExternalOutput")
    XT = [[None] + [SH("xt%d_%d" % (bf, ti), [2, 128, 1024]) for ti in range(1, NTH + 1)] for bf in range(2)]
    XH = [SH("xh%d" % bf, [2, 128, 1024]) for bf in range(2)]
    WID = {"m": 512, "h": 256, "n": 256}
    FAM = {f: [None] + [SH("y%s_%d" % (f, ti), [2, 2, 128, WID[f]]) for ti in range(1, NTH + 1)] for f in WID}
    FAMH = {f: SH("y%s_halo" % f, [2, 128, WID[f]]) for f in WID}
    p_tm = D("p_tm", [T, NT_TM], kind="Internal"); p_cm = D("p_cm", [NC_CM, T], kind="Internal")
    x1d = D("x1d", [HALF + 128, 1024], kind="Internal")
    LP = []
    for l in range(n_layers):
        d = {"g_attn": D("g_attn_l%d" % l, [128, 8]), "w_tm": D("w_tm_l%d" % l, [1024, NT_TM]), "w_cm": D("w_cm_l%d" % l, [1024, NC_CM])}
        d.update({k: D("%s_l%d" % (k, l), *PRM_B[k]) for k in PRM_B})
        d.update({k: D("%s_l%d" % (k, l), *PRM_C[k]) for k in PRM_C})
        d.update({k: cd[k] for k in CONST_NSA})
        LP.append(d)
    with ExitStack() as es:
        em = Em(nc, es)
        psum = [(es.enter_context(nc.psum_tensor("ps%d" % i, [128, 512], F32)), Buf()) for i in range(7)]
        psum.append((es.enter_context(nc.psum_tensor("pst", [128, 8, 128], BF16)), Buf()))
        cst = {}
        for k in CONST_SB:
            cst[k] = mkT(es, nc, "k_" + k, *CONST_SHAPES[k])
            ld(em, cst[k], cd[k])
        hflag = mkT(es, nc, "hflag", [128, 1]); ld(em, hflag, hflag_d)
        for i in range(T // 128):
            h, ti = i // NTH, i % NTH + 1
            em.dma("sp", XT[0][ti][h, :, :], x_ext[i * 128:(i + 1) * 128, :])
        em.dma("sp", XH[0][0, :, :], x_ext[HALF - 128:HALF, :])
        par = nc.sync.partition_id() % 2
        par_st = (nc.gpsimd.partition_id() % 2) if STORE_Q == "pool" else par

        def x_tile_static(bf):
            def f(t0):
                i = t0 // 128
                return XT[bf][i % NTH + 1][i // NTH, :, :]
            return f

        for l in range(n_layers):
            prm = LP[l]
            bf, bn = l % 2, (l + 1) % 2
            y_dst = YDst(FAM, FAMH, par_st, NTH)
            with ExitStack() as es2:
                g_sb = mkT(es2, nc, "g_sb%d" % l, [128, 8]); ld(em, g_sb, prm["g_attn"])
                w_tm = load_weight_bf16(em, es2, "w_tm_sb%d" % l, prm["w_tm"], 1024, NT_TM, scale_sb=g_sb)
                w_cm = load_weight_bf16(em, es2, "w_cm_sb%d" % l, prm["w_cm"], 1024, NC_CM, scale_sb=g_sb)
                stage_proj(em, T, (x_ext if l == 0 else x_tile_static(bf)), g_sb, w_tm, w_cm, NT_TM, NC_CM, p_tm, p_cm, cst["ident_bf"], psum)
                em.drain()
            run_m2_hg(em, T, l, p_tm, p_cm, prm, cst, y_dst, psum)
            em.drain()
            stage_nsa(em, T, p_tm, p_cm, prm, cst, y_dst, psum)
            em.drain()
            nc.all_core_barrier()

            def x_rows(t0, bf=bf):
                ti = t0 // 128
                if ti == 0:
                    return XH[bf][0, :, :]
                return XT[bf][ti][bass.ds(par, 1)][0, :, :]

            def y_rows(t0):
                ti = t0 // 128
                res = []
                for f, c0 in (("n", 0), ("h", 512), ("m", 1024)):
                    w = WID[f]
                    dst_fn = (lambda yt, c0=c0, w=w: yt[:, c0:c0 + 2 * w].rearrange("p (g c) -> p g c", g=2))
                    if ti == 0:
                        src = FAMH[f].rearrange("g r c -> r g c")
                    else:
                        src = FAM[f][ti][bass.ds(par, 1)][0].rearrange("g r c -> r g c")
                    res.append((dst_fn, src))
                return res
            stage_merge(em, HALF + 128, x_rows, y_rows, prm, cst, x1d, psum, halo_flag=hflag)
            em.drain()
            if l == n_layers - 1:
                out_rows = lambda o0: out_ext[o0:o0 + 128, :]
            else:
                def out_rows(o0, bn=bn):
                    ti = o0 // 128 + 1
                    r = [XT[bn][ti][bass.ds(par_st, 1)][0, :, :]]
                    if ti == NTH:
                        r.append(XH[bn][bass.ds(par_st, 1)][0, :, :])
                    return r
            stage_ffn(em, HALF + 128, x1d, prm, cst, out_rows, psum)
            em.drain()
            if l != n_layers - 1:
                nc.all_core_barrier()
        print("build_fused n_inst", em.n_inst, flush=True)
    return nc


_NC_CACHE = {}


def kernel(**inputs):
    inp = {k: np.asarray(v) for k, v in inputs.items()}
    B, T, Dm = inp["x"].shape
    if "F" not in _NC_CACHE:
        _NC_CACHE["F"] = build_fused()
    maps = [prep_fused(inp, b, s) for b in range(B) for s in range(2)]
    res = run_bass_kernel_spmd(_NC_CACHE["F"], maps, core_ids=list(range(8)))
    out = np.empty((B, T, Dm), np.float32)
    for b in range(B):
        for s in range(2):
            out[b, s * (T // 2):(s + 1) * (T // 2)] = np.asarray(res.results[2 * b + s]["out"])
    return out
```
